# Optimizing a Trainium2 kernel written in Bass

```python
import jax, jax.numpy as jnp
from jax import lax
import numpy as np

D_MODEL = 1024
BATCH = 8
SEQ = 2048
DEPTH = 1

DSW_GROUPS = ((128, 1), (512, 4), (2048, 16))
N_DSW_GROUPS = 3
DSW_HEADS_PER_GROUP = 4
DSW_HEAD_DIM = 64
DSW_BLOCK = 128
DSW_QKV_WIDTH = N_DSW_GROUPS * DSW_HEADS_PER_GROUP * DSW_HEAD_DIM
DSW_OUT_WIDTH = DSW_HEADS_PER_GROUP * DSW_HEAD_DIM
ROPE_THETA = 10000.0

GDN_HEADS = 8
GDN_HEAD_DIM = 128
GDN_WIDTH = GDN_HEADS * GDN_HEAD_DIM
GDN_CONV = 4
GDN_CHUNK = 64

D_FF = 2816
EPS = 1e-6

IN_WIDTHS = (DSW_QKV_WIDTH, DSW_QKV_WIDTH, DSW_QKV_WIDTH,
             3 * GDN_WIDTH,
             GDN_HEADS, GDN_HEADS,
             GDN_WIDTH,
             D_MODEL, D_MODEL)
D_IN = sum(IN_WIDTHS)

kernel_name = "hybrid_dilated_swa_gated_deltanet_macaron"


def rmsnorm(x, g):
    xf = x.astype(jnp.float32)
    y = xf * lax.rsqrt(jnp.mean(xf * xf, axis=-1, keepdims=True) + EPS)
    return (y * g.astype(jnp.float32)).astype(x.dtype)


def swiglu(x, w_gate, w_up, w_down):
    return (jax.nn.silu(x @ w_gate) * (x @ w_up)) @ w_down


def rope(x, pos):
    half = x.shape[-1] // 2
    inv_freq = ROPE_THETA ** (-jnp.arange(half, dtype=jnp.float32) / half)
    ang = pos.astype(jnp.float32)[:, None] * inv_freq[None, :]
    cos = jnp.cos(ang)[None, :, None, :]
    sin = jnp.sin(ang)[None, :, None, :]
    xf = x.astype(jnp.float32)
    x1, x2 = xf[..., :half], xf[..., half:]
    return jnp.concatenate([x1 * cos - x2 * sin, x2 * cos + x1 * sin], axis=-1).astype(x.dtype)


def dilated_window_attention(q, k, v, window, dilation):
    B, S, H, Dh = q.shape
    L = S // dilation
    span = window // dilation
    nb = -(-L // DSW_BLOCK)
    Lp = nb * DSW_BLOCK

    def to_blocks(t):
        t = t.reshape(B, L, dilation, H, Dh).transpose(0, 2, 1, 3, 4)
        t = jnp.pad(t, ((0, 0), (0, 0), (0, Lp - L), (0, 0), (0, 0)))
        return t.reshape(B, dilation, nb, DSW_BLOCK, H, Dh)

    def with_prev(t):
        prev = jnp.pad(t, ((0, 0), (0, 0), (1, 0), (0, 0), (0, 0), (0, 0)))[:, :, :-1]
        return jnp.concatenate([prev, t], axis=3)

    qb = to_blocks(q)
    kw = with_prev(to_blocks(k))
    vw = with_prev(to_blocks(v))
    s = jnp.einsum('brnqhd,brnkhd->brnhqk', qb, kw).astype(jnp.float32) * (Dh ** -0.5)
    qi = jnp.arange(nb)[:, None] * DSW_BLOCK + jnp.arange(DSW_BLOCK)[None, :]
    ki = (jnp.arange(nb)[:, None] - 1) * DSW_BLOCK + jnp.arange(2 * DSW_BLOCK)[None, :]
    dist = qi[:, :, None] - ki[:, None, :]
    valid = (dist >= 0) & (dist <= span) & (ki[:, None, :] >= 0)
    s = jnp.where(valid[None, None, :, None], s, -jnp.inf)
    m = jnp.max(s, axis=-1, keepdims=True)
    p = jnp.exp(s - m)
    l = jnp.sum(p, axis=-1, keepdims=True)
    o = jnp.einsum('brnhqk,brnkhd->brnqhd', (p / l).astype(v.dtype), vw)
    lse = (m + jnp.log(l))[..., 0]
    o = o.reshape(B, dilation, Lp, H, Dh)[:, :, :L].transpose(0, 2, 1, 3, 4).reshape(B, S, H, Dh)
    lse = lse.transpose(0, 1, 2, 4, 3).reshape(B, dilation, Lp, H)[:, :, :L]
    lse = lse.transpose(0, 2, 1, 3).reshape(B, S, H)
    return o, lse


def causal_depthwise_conv(x, w):
    K = w.shape[0]
    S = x.shape[1]
    xp = jnp.pad(x, ((0, 0), (K - 1, 0), (0, 0)))
    y = xp[:, 0:S] * w[0]
    for i in range(1, K):
        y = y + xp[:, i:i + S] * w[i]
    return y


def l2norm(x):
    return x * lax.rsqrt(jnp.sum(x * x, axis=-1, keepdims=True) + EPS)


def gated_delta_rule(q, k, v, beta, g):
    B, S, H, Dk = q.shape
    Dv = v.shape[-1]
    C = GDN_CHUNK
    n = S // C
    q = q * (Dk ** -0.5)
    qc, kc, vc = [t.transpose(0, 2, 1, 3).reshape(B, H, n, C, t.shape[-1]) for t in (q, k, v)]
    bc, gc = [t.transpose(0, 2, 1).reshape(B, H, n, C) for t in (beta, g)]
    gcum = jnp.cumsum(gc, axis=-1)
    tri = jnp.tril(jnp.ones((C, C), dtype=bool))
    tri_strict = jnp.tril(jnp.ones((C, C), dtype=bool), -1)
    decay = jnp.exp(jnp.where(tri, gcum[..., :, None] - gcum[..., None, :], -jnp.inf))
    kbeta = kc * bc[..., None]
    vbeta = vc * bc[..., None]
    M = jnp.where(tri_strict, jnp.einsum('bhnid,bhnjd->bhnij', kbeta, kc) * decay, 0.0)
    A = M + jnp.eye(C, dtype=M.dtype)
    rhs = jnp.concatenate([vbeta, kbeta * jnp.exp(gcum)[..., None]], axis=-1)
    sol = lax.linalg.triangular_solve(A, rhs, left_side=True, lower=True, unit_diagonal=True)
    u, w = sol[..., :Dv], sol[..., Dv:]
    a_qk = jnp.einsum('bhnid,bhnjd->bhnij', qc, kc) * decay
    q_dec = qc * jnp.exp(gcum)[..., None]
    g_last = gcum[..., -1]
    k_dec = kc * jnp.exp(g_last[..., None] - gcum)[..., None]
    state_dec = jnp.exp(g_last)

    def step(state, inp):
        q_i, k_i, u_i, w_i, a_i, sd_i = inp
        v_new = u_i - jnp.einsum('bhck,bhkv->bhcv', w_i, state)
        o_i = jnp.einsum('bhck,bhkv->bhcv', q_i, state) + jnp.einsum('bhij,bhjv->bhiv', a_i, v_new)
        state = state * sd_i[..., None, None] + jnp.einsum('bhck,bhcv->bhkv', k_i, v_new)
        return state, o_i

    xs = tuple(jnp.moveaxis(t, 2, 0) for t in (q_dec, k_dec, u, w, a_qk, state_dec))
    state0 = jnp.zeros((B, H, Dk, Dv), dtype=jnp.float32)
    _, o = lax.scan(step, state0, xs)
    return jnp.moveaxis(o, 0, 2).reshape(B, H, S, Dv).transpose(0, 2, 1, 3)


def hybrid_mixer(h, pos, w_in, gdn_conv_w, gdn_a_log, gdn_dt_bias, gdn_out_norm,
                 w_branch_a, w_branch_b, w_out):
    B, S, _ = h.shape
    offsets = []
    acc = 0
    for wdt in IN_WIDTHS[:-1]:
        acc += wdt
        offsets.append(acc)
    qa, ka, va, qkv_b, beta_raw, decay_raw, gdn_gate, gate_a, gate_b = jnp.split(h @ w_in, offsets, axis=-1)

    n_a = N_DSW_GROUPS * DSW_HEADS_PER_GROUP
    qa = rope(qa.reshape(B, S, n_a, DSW_HEAD_DIM), pos)
    ka = rope(ka.reshape(B, S, n_a, DSW_HEAD_DIM), pos)
    va = va.reshape(B, S, n_a, DSW_HEAD_DIM)
    outs, lses = [], []
    for gi, (window, dilation) in enumerate(DSW_GROUPS):
        sl = slice(gi * DSW_HEADS_PER_GROUP, (gi + 1) * DSW_HEADS_PER_GROUP)
        o, lse = dilated_window_attention(qa[:, :, sl], ka[:, :, sl], va[:, :, sl], window, dilation)
        outs.append(o)
        lses.append(lse)
    wts = jax.nn.softmax(jnp.stack(lses, axis=0), axis=0)
    ya = jnp.einsum('gbsh,gbshd->bshd', wts.astype(h.dtype), jnp.stack(outs, axis=0))
    ya = ya.reshape(B, S, DSW_OUT_WIDTH)

    qkv = jax.nn.silu(causal_depthwise_conv(qkv_b, gdn_conv_w)).astype(jnp.float32)
    qb, kb, vb = jnp.split(qkv, 3, axis=-1)
    qb = l2norm(qb.reshape(B, S, GDN_HEADS, GDN_HEAD_DIM))
    kb = l2norm(kb.reshape(B, S, GDN_HEADS, GDN_HEAD_DIM))
    vb = vb.reshape(B, S, GDN_HEADS, GDN_HEAD_DIM)
    beta = jax.nn.sigmoid(beta_raw.astype(jnp.float32))
    g = -jnp.exp(gdn_a_log.astype(jnp.float32)) * jax.nn.softplus(
        decay_raw.astype(jnp.float32) + gdn_dt_bias.astype(jnp.float32))
    ob = gated_delta_rule(qb, kb, vb, beta, g)
    ob = rmsnorm(ob, gdn_out_norm) * jax.nn.silu(
        gdn_gate.astype(jnp.float32).reshape(B, S, GDN_HEADS, GDN_HEAD_DIM))
    yb = ob.reshape(B, S, GDN_WIDTH).astype(h.dtype)

    merged = jax.nn.sigmoid(gate_a) * (ya @ w_branch_a) + jax.nn.sigmoid(gate_b) * (yb @ w_branch_b)
    return merged @ w_out


def setup_inputs(seed: int = 0) -> dict:
    key = jax.random.key(seed)
    ks = jax.random.split(key, 20)
    f32 = jnp.float32

    def nrm(k, shape, fan_in):
        return jax.random.normal(k, shape, f32) * (fan_in ** -0.5)

    def gain(k, shape):
        return 1.0 + 0.01 * jax.random.normal(k, shape, f32)

    dt = jnp.exp(jax.random.uniform(ks[9], (DEPTH, GDN_HEADS), f32, np.log(1e-3), np.log(1e-1)))
    return {
        "x": jax.random.normal(ks[0], (BATCH, SEQ, D_MODEL), f32),
        "ffn1_norm": gain(ks[1], (DEPTH, D_MODEL)),
        "ffn1_w_gate": nrm(ks[2], (DEPTH, D_MODEL, D_FF), D_MODEL),
        "ffn1_w_up": nrm(ks[3], (DEPTH, D_MODEL, D_FF), D_MODEL),
        "ffn1_w_down": nrm(ks[4], (DEPTH, D_FF, D_MODEL), D_FF),
        "mix_norm": gain(ks[5], (DEPTH, D_MODEL)),
        "w_in": nrm(ks[6], (DEPTH, D_MODEL, D_IN), D_MODEL),
        "gdn_conv_w": nrm(ks[7], (DEPTH, GDN_CONV, 3 * GDN_WIDTH), GDN_CONV),
        "gdn_a_log": jnp.log(jax.random.uniform(ks[8], (DEPTH, GDN_HEADS), f32, 1.0, 16.0)),
        "gdn_dt_bias": dt + jnp.log(-jnp.expm1(-dt)),
        "gdn_out_norm": gain(ks[10], (DEPTH, GDN_HEAD_DIM)),
        "w_branch_a": nrm(ks[11], (DEPTH, DSW_OUT_WIDTH, D_MODEL), DSW_OUT_WIDTH),
        "w_branch_b": nrm(ks[12], (DEPTH, GDN_WIDTH, D_MODEL), GDN_WIDTH),
        "w_out": nrm(ks[13], (DEPTH, D_MODEL, D_MODEL), D_MODEL),
        "ffn2_norm": gain(ks[14], (DEPTH, D_MODEL)),
        "ffn2_w_gate": nrm(ks[15], (DEPTH, D_MODEL, D_FF), D_MODEL),
        "ffn2_w_up": nrm(ks[16], (DEPTH, D_MODEL, D_FF), D_MODEL),
        "ffn2_w_down": nrm(ks[17], (DEPTH, D_FF, D_MODEL), D_FF),
        "final_norm": gain(ks[18], (D_MODEL,)),
    }


def reference(x, ffn1_norm, ffn1_w_gate, ffn1_w_up, ffn1_w_down, mix_norm, w_in, gdn_conv_w,
              gdn_a_log, gdn_dt_bias, gdn_out_norm, w_branch_a, w_branch_b, w_out,
              ffn2_norm, ffn2_w_gate, ffn2_w_up, ffn2_w_down, final_norm):
    pos = jnp.arange(x.shape[1])
    for layer in range(DEPTH):
        x = x + 0.5 * swiglu(rmsnorm(x, ffn1_norm[layer]),
                             ffn1_w_gate[layer], ffn1_w_up[layer], ffn1_w_down[layer])
        h = rmsnorm(x, mix_norm[layer])
        x = x + hybrid_mixer(h, pos, w_in[layer], gdn_conv_w[layer], gdn_a_log[layer],
                             gdn_dt_bias[layer], gdn_out_norm[layer], w_branch_a[layer],
                             w_branch_b[layer], w_out[layer])
        x = x + 0.5 * swiglu(rmsnorm(x, ffn2_norm[layer]),
                             ffn2_w_gate[layer], ffn2_w_up[layer], ffn2_w_down[layer])
    return rmsnorm(x, final_norm)
```

```python
import contextlib
import numpy as np
import concourse.bass as bass
import concourse.mybir as mybir
from concourse.bass_utils import run_bass_kernel_spmd

F32 = mybir.dt.float32
BF16 = mybir.dt.bfloat16
AF = mybir.ActivationFunctionType
ALU = mybir.AluOpType
AX = mybir.AxisListType

S = 2048
D = 1024
NCH = 8
DFF = 2816
NF = 22
NB = 4
EPS = 1e-6
D_IN = 8464
NDS = 16
import os
ROPE_ADD_ENG = os.environ.get("ROPE_ADD_ENG", "pool")


class _Stop(Exception):
    pass


class Buf:
    __slots__ = ("w", "r", "excl")

    def __init__(self, excl=False):
        self.w = None
        self.r = []
        self.excl = excl


class Trk:
    def __init__(self, nc, es):
        self.nc = nc
        self.eng = {"pe": nc.tensor, "act": nc.scalar, "dve": nc.vector, "pool": nc.gpsimd, "sp": nc.sync}
        self.sem = {k: es.enter_context(nc.semaphore("sem_" + k)) for k in self.eng}
        self.cnt = {k: 0 for k in self.eng}
        self.seen = {k: {} for k in self.eng}
        self.dsem = [es.enter_context(nc.semaphore("dsem%d" % i)) for i in range(NDS)]
        self.dcnt = [0] * NDS
        self.dnext = 0
        self.same = {"pe": False, "act": True, "dve": True, "pool": True, "sp": True}

    def _semof(self, k):
        return self.sem[k] if isinstance(k, str) else self.dsem[k[1]]

    def _wait(self, e, deps):
        best = {}
        for (k, v) in deps:
            if v > best.get(k, 0):
                best[k] = v
        for k, v in best.items():
            if k == e and not self.same[e]:
                continue
            if self.seen[e].get(k, 0) >= v:
                continue
            self.eng[e].wait_ge(self._semof(k), v)
            self.seen[e][k] = v

    @staticmethod
    def _deps(reads, writes):
        deps = []
        for b in reads:
            if b.w is not None:
                deps.append(b.w)
        for b in writes:
            if b.w is not None:
                deps.append(b.w)
            deps.extend(b.r)
        return deps

    @staticmethod
    def _upd(d, reads, writes):
        for b in reads:
            b.r.append(d)
            if len(b.r) > 64:
                best = {}
                for (k, v) in b.r:
                    if v > best.get(k, 0):
                        best[k] = v
                b.r = list(best.items())
        for b in writes:
            b.w = d
            b.r = []

    def op(self, e, fn, reads=(), writes=()):
        if any(b.excl for b in reads):
            writes = list(writes) + [b for b in reads if b.excl]
            reads = [b for b in reads if not b.excl]
        self._wait(e, self._deps(reads, writes))
        ins = fn(self.eng[e])
        self.cnt[e] += 1
        ins.then_inc(self.sem[e], 1)
        self._upd((e, self.cnt[e]), reads, writes)

    def dma(self, q, out, in_, reads=(), writes=()):
        deps = self._deps(reads, writes)
        i = self.dnext
        self.dnext = (i + 1) % NDS
        if self.dcnt[i] > 0:
            deps.append((("d", i), self.dcnt[i]))
        self._wait(q, deps)
        self.eng[q].dma_start(out=out, in_=in_).then_inc(self.dsem[i], 16)
        self.dcnt[i] += 16
        self._upd((("d", i), self.dcnt[i]), reads, writes)

    def barrier(self):
        deps = [(k, v) for k, v in self.cnt.items() if v > 0]
        deps += [(("d", i), c) for i, c in enumerate(self.dcnt) if c > 0]
        for e in self.eng:
            self._wait(e, [d for d in deps if d[0] != e])


def build_program(stop_after="all", debug=False):
    nc = bass.Bass("TRN2", target_bir_lowering=False)
    dt_in = {}

    def din(name, shape):
        dt_in[name] = nc.dram_tensor(name, list(shape), F32, kind="ExternalInput").ap()
        return dt_in[name]

    xT_d = din("xT", [D, S])
    n1_d = din("ffn1_norm", [128, NCH])
    n2_d = din("ffn2_norm", [128, NCH])
    nm_d = din("mix_norm", [128, NCH])
    nf_d = din("final_norm", [128, NCH])
    ffn_w = []
    for i in (1, 2):
        ffn_w.append((din("wg%d" % i, [11, 128, NCH * 256]), din("wu%d" % i, [11, 128, NCH * 256]),
                      din("wd%d" % i, [8, 128, NF * 128])))
    din("wqkv_a", [6, 3, 128, NCH * 128])
    din("rope_cos", [128, S]); din("rope_sin", [128, S]); din("rm", [128, 128]); din("amask", [128, 512]); din("em", [128, 64])
    din("wgdn", [8, 3, 128, NCH * 128]); din("wgg", [8, 128, NCH * 128]); din("wbd", [128, NCH * 16])
    din("convw", [128, 96]); din("alog", [128, 8]); din("dtb", [128, 8]); din("onorm", [128, 1])
    din("triu", [128, 128]); din("smask", [128, 128]); din("odiv", [128, 128])
    din("wgab", [8, 128, 2 * NCH * 128]); din("wba", [8, 64, 4 * 128]); din("wbb", [8, 128, 8 * 128]); din("wout", [8, 128, 8 * 128])
    ident_d = din("ident_f", [128, 128])
    ones_d = din("ones_f", [128, 128])
    out_d = nc.dram_tensor("outT", [D, S], F32, kind="ExternalOutput").ap()
    dbg = {}

    with contextlib.ExitStack() as es:
        T = Trk(nc, es)

        def sb(name, shape, dt):
            return es.enter_context(nc.sbuf_tensor(name, list(shape), dt))

        xT = sb("xT_sb", [128, NCH, S], F32)
        xB = [Buf() for _ in range(NB)]
        psum = es.enter_context(nc.psum_tensor("psum", [128, 8, 512], F32))
        pB = [Buf(True) for _ in range(8)]
        pstate = {"n": 0}

        def bank():
            b = pstate["n"]
            pstate["n"] = (b + 1) % 8
            return b

        def bank2():
            b = pstate["n"]
            if b % 2:
                b = (b + 1) % 8
            pstate["n"] = (b + 2) % 8
            return b

        ident_f = sb("ident_f_sb", [128, 128], F32)
        ident_b = sb("ident_b_sb", [128, 128], BF16)
        ones_b = sb("ones_b_sb", [128, 128], BF16)
        ones_f = sb("ones_f_sb", [128, 128], F32)
        ident2_f = sb("ident2_f_sb", [128, 128], F32)
        cB = Buf()
        norms = sb("norms_sb", [128, 4, NCH], F32)
        eps_t = sb("eps_sb", [128, 2], F32)
        T.op("dve", lambda e: e.memset(eps_t[:, 0:1], EPS), writes=[cB])
        T.op("dve", lambda e: e.memset(eps_t[:, 1:2], 1.0), writes=[cB])
        T.dma("sp", ident_f[:], ident_d, writes=[cB])
        T.dma("sp", ones_f[:], ones_d, writes=[cB])
        T.op("dve", lambda e: e.tensor_single_scalar(out=ident2_f[:, :], in_=ident_f[:, :], scalar=2.0, op=ALU.mult), reads=[cB], writes=[cB])
        T.dma("pool", ident_b[:], ident_d, writes=[cB])
        T.dma("pool", ones_b[:], ones_d, writes=[cB])
        for i, nd in enumerate((n1_d, nm_d, n2_d, nf_d)):
            T.dma("sp", norms[:, i, :], nd, writes=[cB])
        xT_dv = xT_d.rearrange("(c p) t -> p c t", p=128)
        for n in range(NB):
            T.dma("sp", xT[:, :, n * 512:(n + 1) * 512], xT_dv[:, :, n * 512:(n + 1) * 512], writes=[xB[n]])

        def rmsnorm_block(n, gi, out_ap_fn, out_buf, sq_t, sq_b, rs_t, rs_b):
            tok = slice(n * 512, (n + 1) * 512)
            b = bank()
            for c in range(NCH):
                T.op("act", lambda e, c=c: e.activation(out=sq_t[:, c, :], in_=xT[:, c, tok], func=AF.Square),
                     reads=[xB[n]], writes=[sq_b[c]])
            def mm(e):
                ins = None
                for c in range(NCH):
                    ins = e.matmul(psum[:, b, :], lhsT=ones_b[:, :], rhs=sq_t[:, c, :], start=(c == 0), stop=(c == NCH - 1))
                return ins
            T.op("pe", mm, reads=sq_b + [cB], writes=[pB[b]])
            T.op("act", lambda e: e.activation(out=rs_t[:, :], in_=psum[:, b, :], func=AF.Sqrt, scale=1.0 / D, bias=eps_t[:, 0:1]),
                 reads=[pB[b], cB], writes=[rs_b])
            T.op("dve", lambda e: e.reciprocal(out=rs_t[:, :], in_=rs_t[:, :]), reads=[rs_b], writes=[rs_b])
            for c in range(NCH):
                T.op("dve", lambda e, c=c: e.scalar_tensor_tensor(out=out_ap_fn(c), in0=xT[:, c, tok],
                                                                  scalar=norms[:, gi, c:c + 1], in1=rs_t[:, :],
                                                                  op0=ALU.mult, op1=ALU.mult),
                     reads=[xB[n], rs_b, cB], writes=[out_buf])

        def ffn(which, gi):
            wg_d, wu_d, wd_d = ffn_w[which]
            with contextlib.ExitStack() as fs:
                def fsb(name, shape, dt):
                    return fs.enter_context(nc.sbuf_tensor("f%d_%s" % (which, name), list(shape), dt))
                hT = fsb("hT", [128, NCH, 1024], BF16)
                hB = [Buf(), Buf()]
                actT = fsb("actT", [128, NF, 1024], BF16)
                aB = [[Buf(), Buf()] for _ in range(NF)]
                sq_t = fsb("sq", [128, NCH, 512], BF16)
                sq_b = [Buf() for _ in range(NCH)]
                rs_t = fsb("rs", [128, 512], F32)
                rs_b = Buf()
                NWB = 3
                wgb = [fsb("wg%d" % i, [128, NCH, 256], BF16) for i in range(NWB)]
                wub = [fsb("wu%d" % i, [128, NCH, 256], BF16) for i in range(NWB)]
                wgB = [Buf() for _ in range(NWB)]
                wuB = [Buf() for _ in range(NWB)]
                wdb = [fsb("wd%d" % i, [128, NF, 128], BF16) for i in range(2)]
                wdB = [Buf() for _ in range(2)]
                sg = [fsb("sg%d" % i, [128, 512], F32) for i in range(3)]
                sgB = [Buf() for _ in range(3)]
                sgn = 0

                def load_gu(g):
                    i = g % NWB
                    T.dma("pool", wgb[i][:].rearrange("p c f -> p (c f)"), wg_d[g], writes=[wgB[i]])
                    T.dma("pool", wub[i][:].rearrange("p c f -> p (c f)"), wu_d[g], writes=[wuB[i]])

                def load_d(dt):
                    i = dt % 2
                    T.dma("pool", wdb[i][:].rearrange("p f j -> p (f j)"), wd_d[dt], writes=[wdB[i]])

                for half in range(2):
                    load_gu(0)
                    load_gu(1)
                    for nn in range(2):
                        n = half * 2 + nn
                        rmsnorm_block(n, gi, lambda c, nn=nn: hT[:, c, nn * 512:(nn + 1) * 512], hB[nn],
                                      sq_t, sq_b, rs_t, rs_b)
                    for g in range(11):
                        if g + 2 < 11:
                            load_gu(g + 2)
                        if g == 9:
                            load_d(0)
                        if g == 10:
                            load_d(1)
                        wi = g % NWB
                        for j in range(2):
                            f = 2 * g + j
                            for nn in range(2):
                                tk = slice(nn * 512, (nn + 1) * 512)
                                bg = bank()
                                bu = bank()

                                def mmg(e, w=wgb[wi], b=bg, j=j, tk=tk):
                                    ins = None
                                    for c in range(NCH):
                                        ins = e.matmul(psum[:, b, :], lhsT=w[:, c, j * 128:(j + 1) * 128],
                                                       rhs=hT[:, c, tk], start=(c == 0), stop=(c == NCH - 1))
                                    return ins
                                T.op("pe", mmg, reads=[wgB[wi], hB[nn]], writes=[pB[bg]])
                                T.op("pe", lambda e, b=bu, j=j, tk=tk: mmg(e, wub[wi], b, j, tk),
                                     reads=[wuB[wi], hB[nn]], writes=[pB[bu]])
                                si = sgn % 3
                                sgn += 1
                                T.op("act", lambda e, si=si, b=bg: e.activation(out=sg[si][:, :], in_=psum[:, b, :],
                                                                               func=AF.Silu),
                                     reads=[pB[bg]], writes=[sgB[si]])
                                T.op("dve", lambda e, si=si, b=bu, f=f, tk=tk: e.tensor_tensor(
                                    out=actT[:, f, tk], in0=psum[:, b, :], in1=sg[si][:, :], op=ALU.mult),
                                     reads=[pB[bu], sgB[si]], writes=[aB[f][nn]])
                    for dt in range(8):
                        if dt + 2 < 8:
                            pass
                        wi = dt % 2
                        for nn in range(2):
                            n = half * 2 + nn
                            tk = slice(nn * 512, (nn + 1) * 512)
                            b = bank()

                            def mmd(e, b=b, wi=wi, tk=tk):
                                ins = None
                                for f in range(NF):
                                    ins = e.matmul(psum[:, b, :], lhsT=wdb[wi][:, f, :], rhs=actT[:, f, tk],
                                                   start=(f == 0), stop=(f == NF - 1))
                                return ins
                            T.op("pe", mmd, reads=[wdB[wi]] + [aB[f][nn] for f in range(NF)], writes=[pB[b]])
                            T.op("dve", lambda e, b=b, dt=dt, n=n: e.scalar_tensor_tensor(
                                out=xT[:, dt, n * 512:(n + 1) * 512], in0=psum[:, b, :], scalar=0.5,
                                in1=xT[:, dt, n * 512:(n + 1) * 512], op0=ALU.mult, op1=ALU.add),
                                 reads=[pB[b]], writes=[xB[n]])
                        if dt + 2 < 8:
                            load_d(dt + 2)
                T.barrier()

        def mixer():
            with contextlib.ExitStack() as ms:
                def msb(name, shape, dt):
                    return ms.enter_context(nc.sbuf_tensor("m_" + name, list(shape), dt))
                hT = msb("hT", [128, NCH, S], BF16)
                hB = [Buf() for _ in range(NB)]
                yaT = msb("yaT", [64, 4, S], BF16)
                yaB = [Buf() for _ in range(4)]
                with contextlib.ExitStack() as ns:
                    sq_t = ns.enter_context(nc.sbuf_tensor("m_sq", [128, NCH, 512], BF16))
                    sq_b = [Buf() for _ in range(NCH)]
                    rs_t = ns.enter_context(nc.sbuf_tensor("m_rs", [128, 512], F32))
                    rs_b = Buf()
                    for n in range(NB):
                        rmsnorm_block(n, 1, lambda c, n=n: hT[:, c, n * 512:(n + 1) * 512], hB[n], sq_t, sq_b, rs_t, rs_b)
                    T.barrier()
                if stop_after == "dbg_h":
                    dbg_dump([(hT[:, c, :], hB, c * 128, 128) for c in range(NCH)])
                    return True
                if attention(hT, hB, yaT, yaB):
                    return True
                T.barrier()
                if stop_after == "dbg_ya":
                    dbg_dump([(yaT[:, c, :], yaB, c * 64, 64) for c in range(4)])
                    return True
                ybT = msb("ybT", [128, 8, S], BF16)
                ybB = [Buf() for _ in range(8)]
                gdn(hT, hB, ybT, ybB)
                T.barrier()
                if stop_after == "dbg_yb":
                    dbg_dump([(ybT[:, c, :], ybB, c * 128, 128) for c in range(8)])
                    return True
                merge(hT, hB, yaT, yaB, ybT, ybB)
                T.barrier()
            return None

        def nset(lo, hi):
            return list(range(lo // 512, hi // 512 + 1))

        def attention(hT, hB, yaT, yaB):
            wqkv_d = dt_in["wqkv_a"]
            with contextlib.ExitStack() as as_:
                def asb(name, shape, dt):
                    return as_.enter_context(nc.sbuf_tensor("a_" + name, list(shape), dt))
                cos_t = asb("cos", [128, S], F32)
                sin_t = asb("sin", [128, S], F32)
                rm_t = asb("rm", [128, 128], BF16)
                am_t = asb("am", [128, 512], BF16)
                em_t = asb("em", [128, 64], F32)
                acB = Buf()
                T.dma("sp", cos_t[:], dt_in["rope_cos"], writes=[acB])
                T.dma("sp", sin_t[:], dt_in["rope_sin"], writes=[acB])
                T.dma("pool", rm_t[:], dt_in["rm"], writes=[acB])
                T.dma("pool", am_t[:], dt_in["amask"], writes=[acB])
                T.dma("sp", em_t[:], dt_in["em"], writes=[acB])
                wq = [asb("w%d" % i, [128, NCH, 128], BF16) for i in range(3)]
                wB = [Buf() for _ in range(3)]
                qk = [asb("qk%d" % i, [128, S], BF16) for i in range(2)]
                qkB = [[Buf() for _ in range(NB)] for _ in range(2)]
                Vt = asb("Vt", [128, 16, 2, 65], BF16)
                VtB = [Buf() for _ in range(16)]
                VoneB = Buf()
                T.op(ROPE_ADD_ENG, lambda e: e.memset(Vt[:, :, :, 64:65], 1.0), writes=[VoneB])
                acc = asb("acc", [128, 2, S], F32)
                accB = Buf()
                qb = asb("qb", [128, 512], BF16)
                qbB = Buf()
                t1 = asb("t1", [128, 512], F32)
                t2 = asb("t2", [128, 512], F32)
                t1B, t2B = Buf(), Buf()
                ex = [asb("ex%d" % i, [128, 512], BF16) for i in range(2)]
                exB = [Buf(), Buf()]
                exm = [asb("exm%d" % i, [128, 512], BF16) for i in range(2)]
                exmB = [Buf(), Buf()]
                rl = asb("rl", [64, 512], F32)
                rlB = Buf()
                exn = 0
                if stop_after == "dbg_att0":
                    dbg_dump([(cos_t[:, :], [acB], 0, 128), (sin_t[:, :], [acB, VoneB], 128, 128)])
                    return True
                for hp in range(2):
                    for g in range(3):
                        dil = (1, 4, 16)[g]
                        nbk = 16 // dil
                        pi = g * 2 + hp
                        for i in range(3):
                            T.dma("pool", wq[i][:].rearrange("p c f -> p (c f)"), wqkv_d[pi, i], writes=[wB[i]])
                        for i in range(2):
                            for n in range(NB):
                                tk = slice(n * 512, (n + 1) * 512)
                                b = bank()

                                def mm(e):
                                    ins = None
                                    for c in range(NCH):
                                        ins = e.matmul(psum[:, b, :], lhsT=wq[i][:, c, :], rhs=hT[:, c, tk],
                                                       start=(c == 0), stop=(c == NCH - 1))
                                    return ins
                                CUT = int(os.environ.get("CUT", "9"))
                                if CUT >= 1:
                                    T.op("pe", mm, reads=[wB[i], hB[n]], writes=[pB[b]])
                                if CUT >= 2:
                                    T.op("act", lambda e: e.activation(out=qb[:, :], in_=psum[:, b, :], func=AF.Copy),
                                         reads=[pB[b]], writes=[qbB])
                                b2 = bank()
                                if CUT >= 3:
                                    T.op("pe", lambda e: e.matmul(psum[:, b2, :], lhsT=rm_t[:, :], rhs=qb[:, :], start=True, stop=True),
                                         reads=[qbB, acB], writes=[pB[b2]])
                                if CUT >= 4:
                                    T.op("dve", lambda e: e.tensor_tensor(out=t1[:, :], in0=psum[:, b, :], in1=cos_t[:, tk], op=ALU.mult),
                                         reads=[pB[b], acB], writes=[t1B])
                                if CUT >= 5:
                                    T.op("dve", lambda e: e.tensor_tensor(out=t2[:, :], in0=psum[:, b2, :], in1=sin_t[:, tk], op=ALU.mult),
                                         reads=[pB[b2], acB], writes=[t2B])
                                if CUT >= 6:
                                    T.op(ROPE_ADD_ENG, lambda e: e.tensor_tensor(out=qk[i][:, tk], in0=t1[:, :], in1=t2[:, :], op=ALU.add),
                                         reads=[t1B, t2B], writes=[qkB[i][n]])
                                if stop_after == "dbg_att1a":
                                    T.barrier()
                                    with contextlib.ExitStack() as ds:
                                        dtl = ds.enter_context(nc.sbuf_tensor("dbgx", [128, 512], F32))
                                        dxB = Buf()
                                        T.op("act", lambda e: e.activation(out=dtl[:, :], in_=qk[0][:, 0:512], func=AF.Copy), reads=[qkB[0][0]], writes=[dxB])
                                        T.dma("sp", out_d[0:128, 0:512], t1[:, :], reads=[t1B])
                                        T.dma("sp", out_d[128:256, 0:512], t2[:, :], reads=[t2B])
                                        T.dma("sp", out_d[256:384, 0:512], dtl[:, :], reads=[dxB])
                                        T.barrier()
                                    return True
                        if stop_after == "dbg_att1":
                            dbg_dump([(qk[0][:, :], qkB[0], 0, 128), (qk[1][:, :], qkB[1], 128, 128)])
                            return True
                        def tokslice(bi):
                            r, nk = bi // nbk, bi % nbk
                            st = r + dil * 128 * nk
                            return st, slice(st, st + 127 * dil + 1, dil)
                        for b4 in range(4):
                            b = bank()

                            def mmv(e):
                                ins = None
                                for k in range(4):
                                    st, sl = tokslice(b4 * 4 + k)
                                    for c in range(NCH):
                                        ins = e.matmul(psum[:, b, k * 128:(k + 1) * 128], lhsT=hT[:, c, sl], rhs=wq[2][:, c, :],
                                                       start=(c == 0), stop=(c == NCH - 1))
                                return ins
                            T.op("pe", mmv, reads=[wB[2]] + hB, writes=[pB[b]])
                            T.op("act", lambda e: e.activation(
                                out=Vt[:, b4 * 4:(b4 + 1) * 4, :, 0:64],
                                in_=psum[:, b, :].rearrange("p (k e d) -> p k e d", k=4, e=2), func=AF.Copy),
                                 reads=[pB[b]], writes=[VtB[b4 * 4 + k] for k in range(4)])
                        if stop_after == "dbg_att2":
                            dbg_dump([(Vt[:, :, :, :].rearrange("p a b c -> p (a b c)")[:, 0:2048], VtB + [VoneB], 0, 128)])
                            return True
                        for bi in range(16):
                            r, nk = bi // nbk, bi % nbk
                            st, qsl = tokslice(bi)
                            hi_tok = st + 127 * dil
                            qn_set = nset(st, hi_tok)
                            b = bank2()
                            if nk > 0:
                                stp, ksl_p = tokslice(bi - 1)
                                kn_set = nset(stp, hi_tok)
                            else:
                                kn_set = qn_set

                            def mms(e):
                                ins = None
                                for hd in range(2):
                                    hs = slice(hd * 64, (hd + 1) * 64)
                                    if nk > 0:
                                        ins = e.matmul(psum[:, b + hd, 0:128], lhsT=qk[1][hs, ksl_p], rhs=qk[0][hs, qsl],
                                                       start=True, stop=True)
                                    ins = e.matmul(psum[:, b + hd, 128:256], lhsT=qk[1][hs, qsl], rhs=qk[0][hs, qsl],
                                                   start=True, stop=True)
                                return ins
                            T.op("pe", mms, reads=[qkB[0][n] for n in qn_set] + [qkB[1][n] for n in kn_set], writes=[pB[b], pB[b + 1]])
                            xi = exn % 2
                            exn += 1
                            lo = 0 if nk > 0 else 128
                            src = psum[:, b:b + 2, lo:256]
                            dst = ex[xi][:, :].rearrange("p (e w) -> p e w", e=2)[:, :, lo:256]
                            dstm = exm[xi][:, :].rearrange("p (e w) -> p e w", e=2)[:, :, lo:256]
                            msk = am_t[:, :].rearrange("p (e w) -> p e w", e=2)[:, :, lo:256]
                            T.op("act", lambda e: e.activation(out=dst, in_=src, func=AF.Exp, scale=0.125),
                                 reads=[pB[b], pB[b + 1]], writes=[exB[xi]])
                            T.op("pool", lambda e: e.tensor_tensor(out=dstm, in0=dst, in1=msk, op=ALU.mult),
                                 reads=[exB[xi], acB], writes=[exmB[xi]])
                            b2 = bank()

                            def mmu(e):
                                ins = None
                                for hd in range(2):
                                    o = psum[0:65, b2, hd * 128:(hd + 1) * 128]
                                    if nk > 0:
                                        e.matmul(o, lhsT=Vt[:, bi - 1, hd, :], rhs=exm[xi][:, hd * 256:hd * 256 + 128], start=True, stop=False)
                                    ins = e.matmul(o, lhsT=Vt[:, bi, hd, :], rhs=exm[xi][:, hd * 256 + 128:hd * 256 + 256],
                                                   start=(nk == 0), stop=True)
                                return ins
                            T.op("pe", mmu, reads=[exmB[xi], VtB[bi], VoneB] + ([VtB[bi - 1]] if nk > 0 else []), writes=[pB[b2]])
                            src2 = psum[0:65, b2, 0:256].rearrange("p (e q) -> p e q", e=2)
                            if g == 0:
                                T.op("dve", lambda e: e.tensor_copy(out=acc[0:65, :, qsl], in_=src2), reads=[pB[b2]], writes=[accB])
                            else:
                                T.op("dve", lambda e: e.tensor_tensor(out=acc[0:65, :, qsl], in0=src2, in1=acc[0:65, :, qsl], op=ALU.add),
                                     reads=[pB[b2]], writes=[accB])
                        if stop_after == "dbg_att3":
                            dbg_dump([(acc[:, 0, :], [accB], 0, 128), (acc[:, 1, :], [accB], 128, 128)])
                            return True
                    for hd in range(2):
                        for n in range(NB):
                            tk = slice(n * 512, (n + 1) * 512)
                            b = bank()
                            T.op("pe", lambda e: e.matmul(psum[0:64, b, :], lhsT=em_t[0:65, :], rhs=acc[0:65, hd, tk], start=True, stop=True),
                                 reads=[accB, acB], writes=[pB[b]])
                            T.op("dve", lambda e: e.reciprocal(out=rl[:, :], in_=psum[0:64, b, :]), reads=[pB[b]], writes=[rlB])
                            T.op("dve", lambda e: e.tensor_tensor(out=yaT[:, 2 * hp + hd, tk], in0=acc[0:64, hd, tk], in1=rl[:, :], op=ALU.mult),
                                 reads=[rlB, accB], writes=[yaB[2 * hp + hd]])
                T.barrier()

        def gdn(hT, hB, ybT, ybB):
            with contextlib.ExitStack() as gs:
                def gsb(name, shape, dt):
                    return gs.enter_context(nc.sbuf_tensor("g_" + name, list(shape), dt))
                gcB = Buf()
                triu = gsb("triu", [128, 128], F32)
                smask = gsb("smask", [128, 128], F32)
                odiv = gsb("odiv", [128, 128], BF16)
                convw = gsb("convw", [128, 8, 3, 4], F32)
                alog = gsb("alog", [128, 8], F32)
                dtb = gsb("dtb", [128, 8], F32)
                onorm = gsb("onorm", [128, 1], F32)
                wbd = gsb("wbd", [128, NCH, 16], BF16)
                T.dma("sp", triu[:], dt_in["triu"], writes=[gcB])
                T.dma("sp", smask[:], dt_in["smask"], writes=[gcB])
                T.dma("pool", odiv[:], dt_in["odiv"], writes=[gcB])
                T.dma("sp", convw[:].rearrange("p h w t -> p (h w t)"), dt_in["convw"], writes=[gcB])
                T.dma("sp", alog[:], dt_in["alog"], writes=[gcB])
                T.dma("sp", dtb[:], dt_in["dtb"], writes=[gcB])
                T.dma("sp", onorm[:], dt_in["onorm"], writes=[gcB])
                T.dma("pool", wbd[:].rearrange("p c f -> p (c f)"), dt_in["wbd"], writes=[gcB])
                braw = gsb("braw", [128, 16, 16], F32)
                beta = gsb("beta", [128, 16, 8], F32)
                negb = gsb("negb", [128, 16, 8], F32)
                gt = gsb("gt", [128, 16, 8], F32)
                gcum = gsb("gcum", [128, 16, 8], F32)
                glast = gsb("glast", [128, 16, 8], F32)
                kds = gsb("kds", [128, 16, 8], F32)
                sdec = gsb("sdec", [128, 16, 8], F32)
                tmpg = gsb("tmpg", [128, 16, 8], F32)
                negA = gsb("negA", [128, 8], F32)
                gB = Buf()
                b = bank()

                def mmb(e):
                    ins = None
                    for t in range(16):
                        for c in range(NCH):
                            ins = e.matmul(psum[:, b, t * 16:(t + 1) * 16], lhsT=hT[:, c, t * 128:(t + 1) * 128], rhs=wbd[:, c, :],
                                           start=(c == 0), stop=(c == NCH - 1))
                    return ins
                T.op("pe", mmb, reads=hB + [gcB], writes=[pB[b]])
                T.op("act", lambda e: e.activation(out=braw[:].rearrange("p t k -> p (t k)"), in_=psum[:, b, 0:256], func=AF.Copy),
                     reads=[pB[b]], writes=[gB])
                T.op("act", lambda e: e.activation(out=tmpg[:], in_=braw[:, :, 0:8], func=AF.Exp, scale=-1.0), reads=[gB], writes=[gB])
                T.op("dve", lambda e: e.tensor_single_scalar(out=tmpg[:], in_=tmpg[:], scalar=1.0, op=ALU.add), reads=[gB], writes=[gB])
                T.op("dve", lambda e: e.reciprocal(out=beta[:], in_=tmpg[:]), reads=[gB], writes=[gB])
                T.op("dve", lambda e: e.tensor_single_scalar(out=negb[:], in_=beta[:], scalar=-1.0, op=ALU.mult), reads=[gB], writes=[gB])
                T.op("dve", lambda e: e.tensor_tensor(out=tmpg[:], in0=braw[:, :, 8:16], in1=dtb[:].unsqueeze(1).to_broadcast([128, 16, 8]), op=ALU.add),
                     reads=[gB, gcB], writes=[gB])
                T.op("act", lambda e: e.activation(out=tmpg[:], in_=tmpg[:], func=AF.Exp), reads=[gB], writes=[gB])
                T.op("act", lambda e: e.activation(out=tmpg[:], in_=tmpg[:], func=AF.Ln, bias=eps_t[:, 1:2]), reads=[gB, cB], writes=[gB])
                T.op("act", lambda e: e.activation(out=negA[:], in_=alog[:], func=AF.Exp), reads=[gcB], writes=[gB])
                T.op("dve", lambda e: e.tensor_single_scalar(out=negA[:], in_=negA[:], scalar=-1.0, op=ALU.mult), reads=[gB], writes=[gB])
                T.op("dve", lambda e: e.tensor_tensor(out=gt[:], in0=tmpg[:], in1=negA[:].unsqueeze(1).to_broadcast([128, 16, 8]), op=ALU.mult),
                     reads=[gB], writes=[gB])
                b = bank()
                T.op("pe", lambda e: e.matmul(psum[:, b, 0:128], lhsT=triu[:, :], rhs=gt[:].rearrange("p t k -> p (t k)"), start=True, stop=True),
                     reads=[gB, gcB], writes=[pB[b]])
                T.op("act", lambda e: e.activation(out=gcum[:].rearrange("p t k -> p (t k)"), in_=psum[:, b, 0:128], func=AF.Copy),
                     reads=[pB[b]], writes=[gB])
                b = bank()
                T.op("pe", lambda e: e.matmul(psum[:, b, 0:128], lhsT=ones_f[:, :], rhs=gt[:].rearrange("p t k -> p (t k)"), start=True, stop=True),
                     reads=[gB, cB], writes=[pB[b]])
                T.op("act", lambda e: e.activation(out=glast[:].rearrange("p t k -> p (t k)"), in_=psum[:, b, 0:128], func=AF.Copy),
                     reads=[pB[b]], writes=[gB])
                T.op("dve", lambda e: e.tensor_tensor(out=tmpg[:], in0=glast[:], in1=gcum[:], op=ALU.subtract), reads=[gB], writes=[gB])
                T.op("act", lambda e: e.activation(out=kds[:], in_=tmpg[:], func=AF.Exp), reads=[gB], writes=[gB])
                T.op("act", lambda e: e.activation(out=sdec[:], in_=glast[:], func=AF.Exp), reads=[gB], writes=[gB])

                wh = [gsb("wh%d" % i, [128, NCH, 128], BF16) for i in range(4)]
                whB = [Buf() for _ in range(4)]
                Dg = gsb("Dg", [128, 3, 4, 128], BF16)
                DgB = Buf()
                xb = gsb("xb", [128, 4 + S], BF16)
                xbB = [Buf() for _ in range(NB)]
                xb0B = Buf()
                T.op("pool", lambda e: e.memset(xb[:, 0:4], 0.0), writes=[xb0B])
                qkv = [gsb("qkv%d" % i, [128, S], BF16) for i in range(3)]
                qkvB = [[Buf() for _ in range(NB)] for _ in range(3)]
                gsil = gsb("gsil", [128, S], BF16)
                gsilB = [Buf() for _ in range(NB)]
                sl = gsb("sl", [128, 512], F32)
                slB = Buf()
                sqb = gsb("sqb", [128, 512], BF16)
                sqbB = Buf()
                rt = gsb("rt", [128, 512], F32)
                rtB = Buf()
                def pp(name, shape, dt):
                    return [gsb("%s%d" % (name, i), shape, dt) for i in range(2)], [Buf(), Buf()]
                DT, DTB = pp("DT", [128, 128], F32)
                decS, decSB = pp("decS", [128, 128], F32)
                decF, decFB = pp("decF", [128, 128], F32)
                EG, EGB = pp("EG", [128, 128], F32)
                PQ = [[gsb("PQ%d_%d" % (i, k), [128, 256], BF16) for k in range(2)] for i in range(2)]
                PQB = [[Buf(), Buf()] for _ in range(2)]
                IP, IPB = pp("IP", [128, 128], BF16)
                TT = [[gsb("TT%d_%d" % (i, k), [128, 128], BF16) for k in range(2)] for i in range(2)]
                TTB = [[Buf(), Buf()] for _ in range(2)]
                Aqk, AqkB = pp("Aqk", [128, 128], BF16)
                kdec, kdecB = pp("kdec", [128, 128], BF16)
                vtok, vtokB = pp("vtok", [128, 128], F32)
                kgT, kgTB = pp("kgT", [128, 128], BF16)
                qdT, qdTB = pp("qdT", [128, 128], BF16)
                Xt, XtB = pp("Xt", [128, 128], BF16)
                vnew, vnewB = pp("vnew", [128, 128], BF16)
                osq, osqB = pp("osq", [128, 128], BF16)
                orst, orstB = pp("orst", [128, 128], F32)
                otmp, otmpB = pp("otmp", [128, 128], F32)
                Sf = gsb("Sf", [128, 128], F32)
                Sb = gsb("Sb", [128, 128], BF16)
                SfB, SbB = Buf(), Buf()
                QSCALE = 128.0 ** -0.5

                for h in range(8):
                    for i in range(3):
                        T.dma("pool", wh[i][:].rearrange("p c f -> p (c f)"), dt_in["wgdn"][h, i], writes=[whB[i]])
                    T.dma("pool", wh[3][:].rearrange("p c f -> p (c f)"), dt_in["wgg"][h], writes=[whB[3]])
                    for w3 in range(3):
                        for tp in range(4):
                            T.op("dve", lambda e: e.tensor_scalar_mul(out=Dg[:, w3, tp, :], in0=ident_f[:, :], scalar1=convw[:, h, w3, tp:tp + 1]),
                                 reads=[gcB, cB], writes=[DgB])
                    for w3 in range(3):
                        for n in range(NB):
                            tk = slice(n * 512, (n + 1) * 512)
                            b = bank()

                            def mm(e):
                                ins = None
                                for c in range(NCH):
                                    ins = e.matmul(psum[:, b, :], lhsT=wh[w3][:, c, :], rhs=hT[:, c, tk], start=(c == 0), stop=(c == NCH - 1))
                                return ins
                            T.op("pe", mm, reads=[whB[w3], hB[n]], writes=[pB[b]])
                            T.op("act", lambda e: e.activation(out=xb[:, 4 + n * 512:4 + (n + 1) * 512], in_=psum[:, b, :], func=AF.Copy),
                                 reads=[pB[b]], writes=[xbB[n]])
                        for n in range(NB):
                            tk = slice(n * 512, (n + 1) * 512)
                            b = bank()

                            def mmc(e):
                                ins = None
                                for tp in range(4):
                                    ins = e.matmul(psum[:, b, :], lhsT=Dg[:, w3, tp, :], rhs=xb[:, n * 512 + 1 + tp:n * 512 + 1 + tp + 512],
                                                   start=(tp == 0), stop=(tp == 3))
                                return ins
                            T.op("pe", mmc, reads=[DgB, xbB[n], xb0B] + ([xbB[n - 1]] if n > 0 else []), writes=[pB[b]])
                            if w3 == 2:
                                T.op("act", lambda e: e.activation(out=qkv[2][:, tk], in_=psum[:, b, :], func=AF.Silu),
                                     reads=[pB[b]], writes=[qkvB[2][n]])
                            else:
                                T.op("act", lambda e: e.activation(out=sl[:, :], in_=psum[:, b, :], func=AF.Silu), reads=[pB[b]], writes=[slB])
                                T.op("act", lambda e: e.activation(out=sqb[:, :], in_=sl[:, :], func=AF.Square), reads=[slB], writes=[sqbB])
                                b2 = bank()
                                T.op("pe", lambda e: e.matmul(psum[:, b2, :], lhsT=ones_b[:, :], rhs=sqb[:, :], start=True, stop=True),
                                     reads=[sqbB, cB], writes=[pB[b2]])
                                T.op("act", lambda e: e.activation(out=rt[:, :], in_=psum[:, b2, :], func=AF.Sqrt, bias=eps_t[:, 0:1]),
                                     reads=[pB[b2], cB], writes=[rtB])
                                T.op("dve", lambda e: e.reciprocal(out=rt[:, :], in_=rt[:, :]), reads=[rtB], writes=[rtB])
                                T.op("dve", lambda e: e.scalar_tensor_tensor(out=qkv[w3][:, tk], in0=sl[:, :], scalar=(QSCALE if w3 == 0 else 1.0),
                                                                             in1=rt[:, :], op0=ALU.mult, op1=ALU.mult),
                                     reads=[slB, rtB], writes=[qkvB[w3][n]])
                    for n in range(NB):
                        tk = slice(n * 512, (n + 1) * 512)
                        b = bank()

                        def mm(e):
                            ins = None
                            for c in range(NCH):
                                ins = e.matmul(psum[:, b, :], lhsT=wh[3][:, c, :], rhs=hT[:, c, tk], start=(c == 0), stop=(c == NCH - 1))
                            return ins
                        T.op("pe", mm, reads=[whB[3], hB[n]], writes=[pB[b]])
                        T.op("act", lambda e: e.activation(out=sl[:, :], in_=psum[:, b, :], func=AF.Silu), reads=[pB[b]], writes=[slB])
                        T.op("dve", lambda e: e.tensor_scalar_mul(out=gsil[:, tk], in0=sl[:, :], scalar1=onorm[:, 0:1]),
                             reads=[slB, gcB], writes=[gsilB[n]])
                    for t in range(16):
                        pz = t % 2
                        n = t // 4
                        ck = slice(t * 128, (t + 1) * 128)
                        gc = gcum[:, t, h:h + 1]
                        b = bank()
                        T.op("pe", lambda e: e.matmul(psum[:, b, 0:128], lhsT=gt[:, t, h:h + 1].to_broadcast([128, 128]), rhs=triu[:, :],
                                                      start=True, stop=True), reads=[gB, gcB], writes=[pB[b]])
                        T.op("dve", lambda e: e.scalar_tensor_tensor(out=DT[pz][:, :], in0=psum[:, b, 0:128], scalar=gc, in1=smask[:, :],
                                                                     op0=ALU.subtract, op1=ALU.add), reads=[pB[b], gB, gcB], writes=[DTB[pz]])
                        T.op("act", lambda e: e.activation(out=EG[pz][:, :], in_=psum[:, b, 0:128], func=AF.Exp), reads=[pB[b]], writes=[EGB[pz]])
                        T.op("act", lambda e: e.activation(out=decS[pz][:, :], in_=DT[pz][:, :], func=AF.Exp), reads=[DTB[pz]], writes=[decSB[pz]])
                        T.op("pool", lambda e: e.tensor_tensor(out=decF[pz][:, :], in0=decS[pz][:, :], in1=ident_f[:, :], op=ALU.add),
                             reads=[decSB[pz], cB], writes=[decFB[pz]])
                        b = bank()
                        T.op("pe", lambda e: (e.matmul(psum[:, b, 0:128], lhsT=qkv[1][:, ck], rhs=qkv[1][:, ck], start=True, stop=True),
                                              e.matmul(psum[:, b, 128:256], lhsT=qkv[1][:, ck], rhs=qkv[0][:, ck], start=True, stop=True))[1],
                             reads=[qkvB[0][n], qkvB[1][n]], writes=[pB[b]])
                        T.op("dve", lambda e: e.scalar_tensor_tensor(out=PQ[pz][0][:, 128:256], in0=psum[:, b, 0:128], scalar=negb[:, t, h:h + 1],
                                                                     in1=decS[pz][:, :], op0=ALU.mult, op1=ALU.mult),
                             reads=[pB[b], gB, decSB[pz]], writes=[PQB[pz][0]])
                        T.op("dve", lambda e: e.tensor_tensor(out=Aqk[pz][:, :], in0=psum[:, b, 128:256], in1=decF[pz][:, :], op=ALU.mult),
                             reads=[pB[b], decFB[pz]], writes=[AqkB[pz]])
                        b = bank()
                        T.op("pe", lambda e: (e.matmul(psum[:, b, 0:128], lhsT=PQ[pz][0][:, 128:256], rhs=ident_b[:, :], start=True, stop=True),
                                              e.matmul(psum[:, b, 128:256], lhsT=qkv[1][:, ck], rhs=ident_b[:, :], start=True, stop=True),
                                              e.matmul(psum[:, b, 256:384], lhsT=qkv[2][:, ck], rhs=ident_b[:, :], start=True, stop=True))[2],
                             reads=[PQB[pz][0], qkvB[1][n], qkvB[2][n], cB], writes=[pB[b]])
                        T.op("act", lambda e: e.activation(out=PQ[pz][0][:, 0:128], in_=psum[:, b, 0:128], func=AF.Copy),
                             reads=[pB[b]], writes=[PQB[pz][0]])
                        T.op("dve", lambda e: e.tensor_scalar_mul(out=kdec[pz][:, :], in0=psum[:, b, 128:256], scalar1=kds[:, t, h:h + 1]),
                             reads=[pB[b], gB], writes=[kdecB[pz]])
                        T.op("act", lambda e: e.activation(out=vtok[pz][:, :], in_=psum[:, b, 256:384], func=AF.Copy),
                             reads=[pB[b]], writes=[vtokB[pz]])
                        T.op("pool", lambda e: e.tensor_tensor(out=kgT[pz][:, :], in0=qkv[1][:, ck], in1=EG[pz][:, :], op=ALU.mult),
                             reads=[qkvB[1][n], EGB[pz]], writes=[kgTB[pz]])
                        T.op("pool", lambda e: e.tensor_tensor(out=qdT[pz][:, :], in0=qkv[0][:, ck], in1=EG[pz][:, :], op=ALU.mult),
                             reads=[qkvB[0][n], EGB[pz]], writes=[qdTB[pz]])
                        T.op("pool", lambda e: e.tensor_tensor(out=IP[pz][:, :], in0=ident_b[:, :], in1=PQ[pz][0][:, 128:256], op=ALU.subtract),
                             reads=[PQB[pz][0], cB], writes=[IPB[pz]])
                        T.op("pool", lambda e: e.tensor_tensor(out=PQ[pz][1][:, :].rearrange("p (a b) -> p a b", a=2),
                                                                in0=PQ[pz][0][:, :].rearrange("p (a b) -> p a b", a=2),
                                                                in1=ident_b[:, :].unsqueeze(1).to_broadcast([128, 2, 128]), op=ALU.add),
                             reads=[PQB[pz][0], cB], writes=[PQB[pz][1]])
                        cur = 1
                        for k in range(2, 8):
                            nxt = 1 - cur
                            last = (k == 7)
                            b = bank()
                            T.op("pe", lambda e: e.matmul(psum[:, b, 0:128], lhsT=IP[pz][:, :], rhs=PQ[pz][cur][:, 0:128], start=True, stop=True),
                                 reads=[IPB[pz], PQB[pz][cur]], writes=[pB[b]])
                            T.op("dve", lambda e: e.scalar_tensor_tensor(out=TT[pz][0][:, :], in0=psum[:, b, 0:128], scalar=-1.0, in1=ident2_f[:, :],
                                                                         op0=ALU.mult, op1=ALU.add), reads=[pB[b], cB], writes=[TTB[pz][0]])
                            b2 = bank()

                            def mmn(e):
                                ins = e.matmul(psum[:, b2, 128:256], lhsT=TT[pz][0][:, :], rhs=PQ[pz][cur][:, 128:256], start=True, stop=True)
                                if not last:
                                    ins = e.matmul(psum[:, b2, 0:128], lhsT=PQ[pz][cur][:, 128:256], rhs=TT[pz][0][:, :], start=True, stop=True)
                                return ins
                            T.op("pe", mmn, reads=[TTB[pz][0], PQB[pz][cur]], writes=[pB[b2]])
                            if last:
                                T.op("act", lambda e: e.activation(out=TT[pz][1][:, :], in_=psum[:, b2, 128:256], func=AF.Copy),
                                     reads=[pB[b2]], writes=[TTB[pz][1]])
                            else:
                                T.op("act", lambda e: e.activation(out=PQ[pz][nxt][:, :], in_=psum[:, b2, 0:256], func=AF.Copy),
                                     reads=[pB[b2]], writes=[PQB[pz][nxt]])
                            cur = nxt
                        TTf, TTfB = TT[pz][1], TTB[pz][1]
                        if t == 0:
                            T.op("act", lambda e: e.activation(out=Xt[pz][:, :], in_=vtok[pz][:, :], func=AF.Copy), reads=[vtokB[pz]], writes=[XtB[pz]])
                        else:
                            b = bank()
                            T.op("pe", lambda e: e.matmul(psum[:, b, 0:128], lhsT=kgT[pz][:, :], rhs=Sb[:, :], start=True, stop=True),
                                 reads=[kgTB[pz], SbB], writes=[pB[b]])
                            T.op("dve", lambda e: e.tensor_tensor(out=Xt[pz][:, :], in0=vtok[pz][:, :], in1=psum[:, b, 0:128], op=ALU.subtract),
                                 reads=[pB[b], vtokB[pz]], writes=[XtB[pz]])
                        b = bank()
                        T.op("pe", lambda e: e.matmul(psum[:, b, 0:128], lhsT=TTf[:, :], rhs=Xt[pz][:, :], start=True, stop=True),
                             reads=[TTfB, XtB[pz]], writes=[pB[b]])
                        T.op("dve", lambda e: e.tensor_scalar_mul(out=vnew[pz][:, :], in0=psum[:, b, 0:128], scalar1=beta[:, t, h:h + 1]),
                             reads=[pB[b], gB], writes=[vnewB[pz]])
                        b = bank()

                        def mmo(e):
                            if t > 0:
                                e.matmul(psum[:, b, 0:128], lhsT=Sb[:, :], rhs=qdT[pz][:, :], start=True, stop=False)
                            return e.matmul(psum[:, b, 0:128], lhsT=vnew[pz][:, :], rhs=Aqk[pz][:, :], start=(t == 0), stop=True)
                        T.op("pe", mmo, reads=[SbB, qdTB[pz], vnewB[pz], AqkB[pz]], writes=[pB[b]])
                        T.op("act", lambda e: e.activation(out=osq[pz][:, :], in_=psum[:, b, 0:128], func=AF.Square), reads=[pB[b]], writes=[osqB[pz]])
                        b2 = bank()
                        T.op("pe", lambda e: e.matmul(psum[:, b2, 0:128], lhsT=odiv[:, :], rhs=osq[pz][:, :], start=True, stop=True),
                             reads=[osqB[pz], gcB], writes=[pB[b2]])
                        T.op("act", lambda e: e.activation(out=orst[pz][:, :], in_=psum[:, b2, 0:128], func=AF.Sqrt, bias=eps_t[:, 0:1]),
                             reads=[pB[b2], cB], writes=[orstB[pz]])
                        T.op("dve", lambda e: e.reciprocal(out=orst[pz][:, :], in_=orst[pz][:, :]), reads=[orstB[pz]], writes=[orstB[pz]])
                        T.op("dve", lambda e: e.tensor_tensor(out=otmp[pz][:, :], in0=psum[:, b, 0:128], in1=orst[pz][:, :], op=ALU.mult),
                             reads=[pB[b], orstB[pz]], writes=[otmpB[pz]])
                        T.op("pool", lambda e: e.tensor_tensor(out=ybT[:, h, ck], in0=otmp[pz][:, :], in1=gsil[:, ck], op=ALU.mult),
                             reads=[otmpB[pz], gsilB[n]], writes=[ybB[h]])
                        if t < 15:
                            b = bank()
                            T.op("pe", lambda e: e.matmul(psum[:, b, 0:128], lhsT=kdec[pz][:, :], rhs=vnew[pz][:, :], start=True, stop=True),
                                 reads=[kdecB[pz], vnewB[pz]], writes=[pB[b]])
                            if t == 0:
                                T.op("dve", lambda e: e.tensor_copy(out=Sf[:, :], in_=psum[:, b, 0:128]), reads=[pB[b]], writes=[SfB])
                            else:
                                T.op("dve", lambda e: e.scalar_tensor_tensor(out=Sf[:, :], in0=Sf[:, :], scalar=sdec[:, t, h:h + 1], in1=psum[:, b, 0:128],
                                                                             op0=ALU.mult, op1=ALU.add), reads=[pB[b], gB, SfB], writes=[SfB])
                            T.op("act", lambda e: e.activation(out=Sb[:, :], in_=Sf[:, :], func=AF.Copy), reads=[SfB], writes=[SbB])
                T.barrier()

        def merge(hT, hB, yaT, yaB, ybT, ybB):
            with contextlib.ExitStack() as gs:
                def gsb(name, shape, dt):
                    return gs.enter_context(nc.sbuf_tensor("x_" + name, list(shape), dt))
                mg = gsb("mg", [128, 8, S], BF16)
                mgB = [[Buf() for _ in range(NB)] for _ in range(8)]
                wga = [gsb("wga%d" % i, [128, 2, NCH, 128], BF16) for i in range(2)]
                wba = [gsb("wba%d" % i, [64, 4, 128], BF16) for i in range(2)]
                wbb = [gsb("wbb%d" % i, [128, 8, 128], BF16) for i in range(2)]
                wB = [Buf(), Buf()]
                wo = [gsb("wo%d" % i, [128, 8, 128], BF16) for i in range(2)]
                woB = [Buf(), Buf()]
                one_c = eps_t[:, 1:2]
                sa = gsb("sa", [128, 512], F32)
                sb_ = gsb("sb", [128, 512], F32)
                ta = gsb("ta", [128, 512], F32)
                tb = gsb("tb", [128, 512], F32)
                saB, sbB, taB, tbB = Buf(), Buf(), Buf(), Buf()

                def loadw(dt):
                    i = dt % 2
                    T.dma("pool", wga[i][:].rearrange("p w c f -> p (w c f)"), dt_in["wgab"][dt], writes=[wB[i]])
                    T.dma("pool", wba[i][:].rearrange("p s f -> p (s f)"), dt_in["wba"][dt], writes=[wB[i]])
                    T.dma("pool", wbb[i][:].rearrange("p s f -> p (s f)"), dt_in["wbb"][dt], writes=[wB[i]])
                loadw(0)
                loadw(1)
                for dt in range(8):
                    wi = dt % 2
                    for n in range(NB):
                        tk = slice(n * 512, (n + 1) * 512)
                        bga, bgb, ba, bb = bank(), bank(), bank(), bank()

                        def mmg(e, which, b):
                            ins = None
                            for c in range(NCH):
                                ins = e.matmul(psum[:, b, :], lhsT=wga[wi][:, which, c, :], rhs=hT[:, c, tk], start=(c == 0), stop=(c == NCH - 1))
                            return ins
                        T.op("pe", lambda e: mmg(e, 0, bga), reads=[wB[wi], hB[n]], writes=[pB[bga]])
                        T.op("pe", lambda e: mmg(e, 1, bgb), reads=[wB[wi], hB[n]], writes=[pB[bgb]])

                        def mma(e):
                            ins = None
                            for s4 in range(4):
                                ins = e.matmul(psum[:, ba, :], lhsT=wba[wi][:, s4, :], rhs=yaT[:, s4, tk], start=(s4 == 0), stop=(s4 == 3))
                            return ins
                        T.op("pe", mma, reads=[wB[wi]] + yaB, writes=[pB[ba]])

                        def mmb(e):
                            ins = None
                            for h in range(8):
                                ins = e.matmul(psum[:, bb, :], lhsT=wbb[wi][:, h, :], rhs=ybT[:, h, tk], start=(h == 0), stop=(h == 7))
                            return ins
                        T.op("pe", mmb, reads=[wB[wi]] + ybB, writes=[pB[bb]])
                        T.op("act", lambda e: e.activation(out=sa[:, :], in_=psum[:, bga, :], func=AF.Exp, scale=-1.0), reads=[pB[bga]], writes=[saB])
                        T.op("act", lambda e: e.activation(out=sb_[:, :], in_=psum[:, bgb, :], func=AF.Exp, scale=-1.0), reads=[pB[bgb]], writes=[sbB])
                        T.op("pool", lambda e: e.tensor_single_scalar(out=sa[:, :], in_=sa[:, :], scalar=1.0, op=ALU.add), reads=[saB], writes=[saB])
                        T.op("pool", lambda e: e.tensor_single_scalar(out=sb_[:, :], in_=sb_[:, :], scalar=1.0, op=ALU.add), reads=[sbB], writes=[sbB])
                        T.op("dve", lambda e: e.reciprocal(out=sa[:, :], in_=sa[:, :]), reads=[saB], writes=[saB])
                        T.op("dve", lambda e: e.reciprocal(out=sb_[:, :], in_=sb_[:, :]), reads=[sbB], writes=[sbB])
                        T.op("dve", lambda e: e.tensor_tensor(out=ta[:, :], in0=psum[:, ba, :], in1=sa[:, :], op=ALU.mult), reads=[pB[ba], saB], writes=[taB])
                        T.op("dve", lambda e: e.tensor_tensor(out=tb[:, :], in0=psum[:, bb, :], in1=sb_[:, :], op=ALU.mult), reads=[pB[bb], sbB], writes=[tbB])
                        T.op("pool", lambda e: e.tensor_tensor(out=mg[:, dt, tk], in0=ta[:, :], in1=tb[:, :], op=ALU.add), reads=[taB, tbB], writes=[mgB[dt][n]])
                    if dt + 2 < 8:
                        loadw(dt + 2)

                def loado(dt):
                    T.dma("pool", wo[dt % 2][:].rearrange("p s f -> p (s f)"), dt_in["wout"][dt], writes=[woB[dt % 2]])
                loado(0)
                loado(1)
                for dt in range(8):
                    wi = dt % 2
                    for n in range(NB):
                        tk = slice(n * 512, (n + 1) * 512)
                        b = bank()

                        def mmo(e):
                            ins = None
                            for k in range(8):
                                ins = e.matmul(psum[:, b, :], lhsT=wo[wi][:, k, :], rhs=mg[:, k, tk], start=(k == 0), stop=(k == 7))
                            return ins
                        T.op("pe", mmo, reads=[woB[wi]] + [mgB[k][n] for k in range(8)], writes=[pB[b]])
                        T.op("dve", lambda e: e.tensor_tensor(out=xT[:, dt, tk], in0=psum[:, b, :], in1=xT[:, dt, tk], op=ALU.add),
                             reads=[pB[b]], writes=[xB[n]])
                    if dt + 2 < 8:
                        loado(dt + 2)
                T.barrier()

        def dbg_dump(slabs):
            with contextlib.ExitStack() as ds:
                dtile = [ds.enter_context(nc.sbuf_tensor("dbg%d" % i, [128, S], F32)) for i in range(2)]
                dB = [Buf(), Buf()]
                for k, (ap, bufs, row0, P) in enumerate(slabs):
                    i = k % 2
                    T.op("act", lambda e: e.activation(out=dtile[i][0:P, :], in_=ap, func=AF.Copy), reads=bufs, writes=[dB[i]])
                    T.dma("sp", out_d[row0:row0 + P, :], dtile[i][0:P, :], reads=[dB[i]])
                T.barrier()

        ffn(0, 0)
        done = False
        if stop_after not in ("ffn1_raw", "ffn1"):
            r = mixer()
            if r is not None:
                done = True
        if stop_after in ("all", "x3_raw"):
            ffn(1, 2)
        if not done:
          with contextlib.ExitStack() as fs:
            yT = [fs.enter_context(nc.sbuf_tensor("yT%d" % i, [128, NCH, 512], F32)) for i in range(2)]
            yB = [Buf(), Buf()]
            sq_t = fs.enter_context(nc.sbuf_tensor("fsq", [128, NCH, 512], BF16))
            sq_b = [Buf() for _ in range(NCH)]
            rs_t = fs.enter_context(nc.sbuf_tensor("frs", [128, 512], F32))
            rs_b = Buf()
            out_dv = out_d.rearrange("(c p) t -> p c t", p=128)
            for n in range(NB):
                i = n % 2
                if stop_after.endswith("_raw"):
                    for c in range(NCH):
                        T.op("dve", lambda e: e.tensor_copy(out=yT[i][:, c, :], in_=xT[:, c, n * 512:(n + 1) * 512]),
                             reads=[xB[n]], writes=[yB[i]])
                else:
                    rmsnorm_block(n, 3, lambda c: yT[i][:, c, :], yB[i], sq_t, sq_b, rs_t, rs_b)
                T.dma("sp", out_dv[:, :, n * 512:(n + 1) * 512], yT[i][:], reads=[yB[i]])
            T.barrier()
    return nc, set(dt_in.keys())


def _consts():
    c = {}
    c["ident_f"] = np.eye(128, dtype=np.float32)
    c["ones_f"] = np.ones((128, 128), dtype=np.float32)
    half = 32
    inv_freq = (10000.0 ** (-np.arange(half, dtype=np.float32) / half)).astype(np.float32)
    pos = np.arange(S, dtype=np.float32)
    ang = (pos[None, :] * inv_freq[:, None]).astype(np.float32)
    cosv, sinv = np.cos(ang).astype(np.float32), np.sin(ang).astype(np.float32)
    cos_t = np.zeros((128, S), np.float32)
    sin_t = np.zeros((128, S), np.float32)
    rm = np.zeros((128, 128), np.float32)
    for p in range(128):
        d = p % 64
        cos_t[p] = cosv[d % 32]
        sin_t[p] = -sinv[d % 32] if d < 32 else sinv[d % 32]
        partner = p + 32 if d < 32 else p - 32
        rm[partner, p] = 1.0
    c["rope_cos"], c["rope_sin"], c["rm"] = cos_t, sin_t, rm
    j = np.arange(128)[:, None]
    i = np.arange(128)[None, :]
    mprev = (j >= i).astype(np.float32)
    mcur = (j <= i).astype(np.float32)
    c["amask"] = np.concatenate([mprev, mcur, mprev, mcur], axis=1)
    em = np.zeros((128, 64), np.float32)
    em[64, :] = 1.0
    c["em"] = em
    c["triu"] = (j <= i).astype(np.float32)
    c["smask"] = np.where(i > j, 0.0, -30000.0).astype(np.float32)
    c["odiv"] = np.full((128, 128), 1.0 / 128.0, np.float32)
    return c


def _prep_inputs(inp):
    f = lambda a: np.ascontiguousarray(a, dtype=np.float32)
    shared = {}

    def ncol(v):
        return f(np.asarray(v).reshape(NCH, 128).T)
    shared["ffn1_norm"] = ncol(inp["ffn1_norm"][0])
    shared["ffn2_norm"] = ncol(inp["ffn2_norm"][0])
    shared["mix_norm"] = ncol(inp["mix_norm"][0])
    shared["final_norm"] = ncol(inp["final_norm"])
    for i in (1, 2):
        wg = np.asarray(inp["ffn%d_w_gate" % i][0])
        wu = np.asarray(inp["ffn%d_w_up" % i][0])
        wd = np.asarray(inp["ffn%d_w_down" % i][0])
        shared["wg%d" % i] = f(wg.reshape(NCH, 128, 11, 256).transpose(2, 1, 0, 3).reshape(11, 128, NCH * 256))
        shared["wu%d" % i] = f(wu.reshape(NCH, 128, 11, 256).transpose(2, 1, 0, 3).reshape(11, 128, NCH * 256))
        shared["wd%d" % i] = f(wd.reshape(NF, 128, 8, 128).transpose(2, 1, 0, 3).reshape(8, 128, NF * 128))
    w_in = np.asarray(inp["w_in"][0])

    def cols(c0, n):
        return w_in[:, c0:c0 + n].reshape(NCH, 128, n).transpose(1, 0, 2).reshape(128, NCH * n)
    wa = np.zeros((6, 3, 128, NCH * 128), np.float32)
    for g in range(3):
        for hp in range(2):
            for i in range(3):
                wa[g * 2 + hp, i] = cols(i * 768 + (4 * g + 2 * hp) * 64, 128)
    shared["wqkv_a"] = wa
    wg_ = np.zeros((8, 3, 128, NCH * 128), np.float32)
    wgg = np.zeros((8, 128, NCH * 128), np.float32)
    for h in range(8):
        for i in range(3):
            wg_[h, i] = cols(2304 + i * 1024 + h * 128, 128)
        wgg[h] = cols(5392 + h * 128, 128)
    shared["wgdn"] = wg_
    shared["wgg"] = wgg
    shared["wbd"] = f(cols(5376, 16))
    wgab = np.zeros((8, 128, 2 * NCH * 128), np.float32)
    for dt in range(8):
        a = cols(6416 + dt * 128, 128).reshape(128, 1, NCH * 128)
        b = cols(7440 + dt * 128, 128).reshape(128, 1, NCH * 128)
        wgab[dt] = np.concatenate([a, b], axis=1).reshape(128, 2 * NCH * 128)
    shared["wgab"] = wgab
    wba = np.asarray(inp["w_branch_a"][0])
    shared["wba"] = f(wba.reshape(4, 64, 8, 128).transpose(2, 1, 0, 3).reshape(8, 64, 4 * 128))
    wbb = np.asarray(inp["w_branch_b"][0])
    shared["wbb"] = f(wbb.reshape(8, 128, 8, 128).transpose(2, 1, 0, 3).reshape(8, 128, 8 * 128))
    wo = np.asarray(inp["w_out"][0])
    shared["wout"] = f(wo.reshape(8, 128, 8, 128).transpose(2, 1, 0, 3).reshape(8, 128, 8 * 128))
    cw = np.asarray(inp["gdn_conv_w"][0])
    shared["convw"] = f(cw.reshape(4, 3, 8, 128).transpose(3, 2, 1, 0).reshape(128, 96))
    shared["alog"] = f(np.broadcast_to(np.asarray(inp["gdn_a_log"][0])[None, :], (128, 8)))
    shared["dtb"] = f(np.broadcast_to(np.asarray(inp["gdn_dt_bias"][0])[None, :], (128, 8)))
    shared["onorm"] = f(np.asarray(inp["gdn_out_norm"][0]).reshape(128, 1))
    shared.update(_consts())
    x = np.asarray(inp["x"])
    maps = []
    for b in range(8):
        m = dict(shared)
        m["xT"] = f(x[b].T)
        maps.append(m)
    return maps


_NC_CACHE = {}


def kernel(**inputs):
    stop_after = inputs.pop("_stop_after", "all")
    if stop_after not in _NC_CACHE:
        _NC_CACHE[stop_after] = build_program(stop_after)
    nc, names = _NC_CACHE[stop_after]
    maps = _prep_inputs(inputs)
    maps = [{k: v for k, v in m.items() if k in names} for m in maps]
    res = run_bass_kernel_spmd(nc, maps, core_ids=list(range(8)))
    out = np.stack([np.ascontiguousarray(r["outT"].T) for r in res.results], axis=0)
    return out.astype(np.float32)
```

```python
import contextlib
import numpy as np
import concourse.bass as bass
import concourse.mybir as mybir
from concourse.bass_utils import run_bass_kernel_spmd

F32 = mybir.dt.float32
BF16 = mybir.dt.bfloat16
AF = mybir.ActivationFunctionType
ALU = mybir.AluOpType
AX = mybir.AxisListType

S = 2048
D = 1024
NCH = 8
DFF = 2816
NF = 22
NB = 4
EPS = 1e-6
D_IN = 8464
NDS = 16
import os
ROPE_ADD_ENG = os.environ.get("ROPE_ADD_ENG", "pool")


class _Stop(Exception):
    pass


class Buf:
    __slots__ = ("w", "r", "excl")

    def __init__(self, excl=False):
        self.w = None
        self.r = []
        self.excl = excl


class Trk:
    def __init__(self, nc, es):
        self.nc = nc
        self.eng = {"pe": nc.tensor, "act": nc.scalar, "dve": nc.vector, "pool": nc.gpsimd, "sp": nc.sync}
        self.sem = {k: es.enter_context(nc.semaphore("sem_" + k)) for k in self.eng}
        self.cnt = {k: 0 for k in self.eng}
        self.seen = {k: {} for k in self.eng}
        self.dsem = [es.enter_context(nc.semaphore("dsem%d" % i)) for i in range(NDS)]
        self.dcnt = [0] * NDS
        self.dnext = 0
        self.same = {"pe": False, "act": True, "dve": True, "pool": True, "sp": True}

    def _semof(self, k):
        return self.sem[k] if isinstance(k, str) else self.dsem[k[1]]

    def _wait(self, e, deps):
        best = {}
        for (k, v) in deps:
            if v > best.get(k, 0):
                best[k] = v
        for k, v in best.items():
            if k == e and not self.same[e]:
                continue
            if self.seen[e].get(k, 0) >= v:
                continue
            self.eng[e].wait_ge(self._semof(k), v)
            self.seen[e][k] = v

    @staticmethod
    def _deps(reads, writes):
        deps = []
        for b in reads:
            if b.w is not None:
                deps.append(b.w)
        for b in writes:
            if b.w is not None:
                deps.append(b.w)
            deps.extend(b.r)
        return deps

    @staticmethod
    def _upd(d, reads, writes):
        for b in reads:
            b.r.append(d)
            if len(b.r) > 64:
                best = {}
                for (k, v) in b.r:
                    if v > best.get(k, 0):
                        best[k] = v
                b.r = list(best.items())
        for b in writes:
            b.w = d
            b.r = []

    def op(self, e, fn, reads=(), writes=()):
        if any(b.excl for b in reads):
            writes = list(writes) + [b for b in reads if b.excl]
            reads = [b for b in reads if not b.excl]
        self._wait(e, self._deps(reads, writes))
        ins = fn(self.eng[e])
        self.cnt[e] += 1
        ins.then_inc(self.sem[e], 1)
        self._upd((e, self.cnt[e]), reads, writes)

    def dma(self, q, out, in_, reads=(), writes=()):
        deps = self._deps(reads, writes)
        i = self.dnext
        self.dnext = (i + 1) % NDS
        if self.dcnt[i] > 0:
            deps.append((("d", i), self.dcnt[i]))
        self._wait(q, deps)
        self.eng[q].dma_start(out=out, in_=in_).then_inc(self.dsem[i], 16)
        self.dcnt[i] += 16
        self._upd((("d", i), self.dcnt[i]), reads, writes)

    def barrier(self):
        deps = [(k, v) for k, v in self.cnt.items() if v > 0]
        deps += [(("d", i), c) for i, c in enumerate(self.dcnt) if c > 0]
        for e in self.eng:
            self._wait(e, [d for d in deps if d[0] != e])


def build_program(stop_after="all", debug=False):
    nc = bass.Bass("TRN2", target_bir_lowering=False)
    dt_in = {}

    def din(name, shape):
        dt_in[name] = nc.dram_tensor(name, list(shape), F32, kind="ExternalInput").ap()
        return dt_in[name]

    xT_d = din("xT", [D, S])
    n1_d = din("ffn1_norm", [128, NCH])
    n2_d = din("ffn2_norm", [128, NCH])
    nm_d = din("mix_norm", [128, NCH])
    nf_d = din("final_norm", [128, NCH])
    ffn_w = []
    for i in (1, 2):
        ffn_w.append((din("wg%d" % i, [11, 128, NCH * 256]), din("wu%d" % i, [11, 128, NCH * 256]),
                      din("wd%d" % i, [8, 128, NF * 128])))
    din("wqkv_a", [6, 3, 128, NCH * 128])
    din("rope_cos", [128, S]); din("rope_sin", [128, S]); din("rm", [128, 128]); din("amask", [128, 512]); din("em", [128, 64])
    din("wgdn", [8, 3, 128, NCH * 128]); din("wgg", [8, 128, NCH * 128]); din("wbd", [128, NCH * 16])
    din("convw", [128, 96]); din("alog", [128, 8]); din("dtb", [128, 8]); din("onorm", [128, 1])
    din("triu", [128, 128]); din("smask", [128, 128]); din("odiv", [128, 128])
    din("wgab", [8, 128, 2 * NCH * 128]); din("wba", [8, 64, 4 * 128]); din("wbb", [8, 128, 8 * 128]); din("wout", [8, 128, 8 * 128])
    ident_d = din("ident_f", [128, 128])
    ones_d = din("ones_f", [128, 128])
    out_d = nc.dram_tensor("outT", [D, S], F32, kind="ExternalOutput").ap()
    dbg = {}

    with contextlib.ExitStack() as es:
        T = Trk(nc, es)

        def sb(name, shape, dt):
            return es.enter_context(nc.sbuf_tensor(name, list(shape), dt))

        xT = sb("xT_sb", [128, NCH, S], F32)
        xB = [Buf() for _ in range(NB)]
        psum = es.enter_context(nc.psum_tensor("psum", [128, 8, 512], F32))
        pB = [Buf(True) for _ in range(8)]
        pstate = {"n": 0}

        def bank():
            b = pstate["n"]
            pstate["n"] = (b + 1) % 8
            return b

        def bank2():
            b = pstate["n"]
            if b % 2:
                b = (b + 1) % 8
            pstate["n"] = (b + 2) % 8
            return b

        ident_f = sb("ident_f_sb", [128, 128], F32)
        ident_b = sb("ident_b_sb", [128, 128], BF16)
        ones_b = sb("ones_b_sb", [128, 128], BF16)
        ones_f = sb("ones_f_sb", [128, 128], F32)
        ident2_f = sb("ident2_f_sb", [128, 128], F32)
        cB = Buf()
        norms = sb("norms_sb", [128, 4, NCH], F32)
        eps_t = sb("eps_sb", [128, 2], F32)
        T.op("dve", lambda e: e.memset(eps_t[:, 0:1], EPS), writes=[cB])
        T.op("dve", lambda e: e.memset(eps_t[:, 1:2], 1.0), writes=[cB])
        T.dma("sp", ident_f[:], ident_d, writes=[cB])
        T.dma("sp", ones_f[:], ones_d, writes=[cB])
        T.op("dve", lambda e: e.tensor_single_scalar(out=ident2_f[:, :], in_=ident_f[:, :], scalar=2.0, op=ALU.mult), reads=[cB], writes=[cB])
        T.dma("pool", ident_b[:], ident_d, writes=[cB])
        T.dma("pool", ones_b[:], ones_d, writes=[cB])
        for i, nd in enumerate((n1_d, nm_d, n2_d, nf_d)):
            T.dma("sp", norms[:, i, :], nd, writes=[cB])
        xT_dv = xT_d.rearrange("(c p) t -> p c t", p=128)
        for n in range(NB):
            T.dma("sp", xT[:, :, n * 512:(n + 1) * 512], xT_dv[:, :, n * 512:(n + 1) * 512], writes=[xB[n]])

        def rmsnorm_block(n, gi, out_ap_fn, out_buf, sq_t, sq_b, rs_t, rs_b):
            tok = slice(n * 512, (n + 1) * 512)
            b = bank()
            for c in range(NCH):
                T.op("act", lambda e, c=c: e.activation(out=sq_t[:, c, :], in_=xT[:, c, tok], func=AF.Square),
                     reads=[xB[n]], writes=[sq_b[c]])
            def mm(e):
                ins = None
                for c in range(NCH):
                    ins = e.matmul(psum[:, b, :], lhsT=ones_b[:, :], rhs=sq_t[:, c, :], start=(c == 0), stop=(c == NCH - 1))
                return ins
            T.op("pe", mm, reads=sq_b + [cB], writes=[pB[b]])
            T.op("act", lambda e: e.activation(out=rs_t[:, :], in_=psum[:, b, :], func=AF.Ln, scale=1.0 / D, bias=eps_t[:, 0:1]),
                 reads=[pB[b], cB], writes=[rs_b])
            T.op("act", lambda e: e.activation(out=rs_t[:, :], in_=rs_t[:, :], func=AF.Exp, scale=-0.5), reads=[rs_b], writes=[rs_b])
            for c in range(NCH):
                T.op("dve", lambda e, c=c: e.scalar_tensor_tensor(out=out_ap_fn(c), in0=xT[:, c, tok],
                                                                  scalar=norms[:, gi, c:c + 1], in1=rs_t[:, :],
                                                                  op0=ALU.mult, op1=ALU.mult),
                     reads=[xB[n], rs_b, cB], writes=[out_buf])

        def ffn(which, gi):
            wg_d, wu_d, wd_d = ffn_w[which]
            with contextlib.ExitStack() as fs:
                def fsb(name, shape, dt):
                    return fs.enter_context(nc.sbuf_tensor("f%d_%s" % (which, name), list(shape), dt))
                hT = fsb("hT", [128, NCH, 1024], BF16)
                hB = [Buf(), Buf()]
                actT = fsb("actT", [128, NF, 1024], BF16)
                aB = [[Buf(), Buf()] for _ in range(NF)]
                sq_t = fsb("sq", [128, NCH, 512], BF16)
                sq_b = [Buf() for _ in range(NCH)]
                rs_t = fsb("rs", [128, 512], F32)
                rs_b = Buf()
                NWB = 3
                wgb = [fsb("wg%d" % i, [128, NCH, 256], BF16) for i in range(NWB)]
                wub = [fsb("wu%d" % i, [128, NCH, 256], BF16) for i in range(NWB)]
                wgB = [Buf() for _ in range(NWB)]
                wuB = [Buf() for _ in range(NWB)]
                wdb = [fsb("wd%d" % i, [128, NF, 128], BF16) for i in range(2)]
                wdB = [Buf() for _ in range(2)]
                sg = [fsb("sg%d" % i, [128, 512], F32) for i in range(3)]
                sgB = [Buf() for _ in range(3)]
                sgn = 0

                def load_gu(g):
                    i = g % NWB
                    T.dma("pool", wgb[i][:].rearrange("p c f -> p (c f)"), wg_d[g], writes=[wgB[i]])
                    T.dma("pool", wub[i][:].rearrange("p c f -> p (c f)"), wu_d[g], writes=[wuB[i]])

                def load_d(dt):
                    i = dt % 2
                    T.dma("pool", wdb[i][:].rearrange("p f j -> p (f j)"), wd_d[dt], writes=[wdB[i]])

                for half in range(2):
                    load_gu(0)
                    load_gu(1)
                    for nn in range(2):
                        n = half * 2 + nn
                        rmsnorm_block(n, gi, lambda c, nn=nn: hT[:, c, nn * 512:(nn + 1) * 512], hB[nn],
                                      sq_t, sq_b, rs_t, rs_b)
                    for g in range(11):
                        if g + 2 < 11:
                            load_gu(g + 2)
                        if g == 9:
                            load_d(0)
                        if g == 10:
                            load_d(1)
                        wi = g % NWB
                        for j in range(2):
                            f = 2 * g + j
                            for nn in range(2):
                                tk = slice(nn * 512, (nn + 1) * 512)
                                bg = bank()
                                bu = bank()

                                def mmg(e, w=wgb[wi], b=bg, j=j, tk=tk):
                                    ins = None
                                    for c in range(NCH):
                                        ins = e.matmul(psum[:, b, :], lhsT=w[:, c, j * 128:(j + 1) * 128],
                                                       rhs=hT[:, c, tk], start=(c == 0), stop=(c == NCH - 1))
                                    return ins
                                T.op("pe", mmg, reads=[wgB[wi], hB[nn]], writes=[pB[bg]])
                                T.op("pe", lambda e, b=bu, j=j, tk=tk: mmg(e, wub[wi], b, j, tk),
                                     reads=[wuB[wi], hB[nn]], writes=[pB[bu]])
                                si = sgn % 3
                                sgn += 1
                                T.op("act", lambda e, si=si, b=bg: e.activation(out=sg[si][:, :], in_=psum[:, b, :],
                                                                               func=AF.Silu),
                                     reads=[pB[bg]], writes=[sgB[si]])
                                T.op("dve", lambda e, si=si, b=bu, f=f, tk=tk: e.tensor_tensor(
                                    out=actT[:, f, tk], in0=psum[:, b, :], in1=sg[si][:, :], op=ALU.mult),
                                     reads=[pB[bu], sgB[si]], writes=[aB[f][nn]])
                    for dt in range(8):
                        if dt + 2 < 8:
                            pass
                        wi = dt % 2
                        for nn in range(2):
                            n = half * 2 + nn
                            tk = slice(nn * 512, (nn + 1) * 512)
                            b = bank()

                            def mmd(e, b=b, wi=wi, tk=tk):
                                ins = None
                                for f in range(NF):
                                    ins = e.matmul(psum[:, b, :], lhsT=wdb[wi][:, f, :], rhs=actT[:, f, tk],
                                                   start=(f == 0), stop=(f == NF - 1))
                                return ins
                            T.op("pe", mmd, reads=[wdB[wi]] + [aB[f][nn] for f in range(NF)], writes=[pB[b]])
                            T.op("dve", lambda e, b=b, dt=dt, n=n: e.scalar_tensor_tensor(
                                out=xT[:, dt, n * 512:(n + 1) * 512], in0=psum[:, b, :], scalar=0.5,
                                in1=xT[:, dt, n * 512:(n + 1) * 512], op0=ALU.mult, op1=ALU.add),
                                 reads=[pB[b]], writes=[xB[n]])
                        if dt + 2 < 8:
                            load_d(dt + 2)
                T.barrier()

        def mixer():
            with contextlib.ExitStack() as ms:
                def msb(name, shape, dt):
                    return ms.enter_context(nc.sbuf_tensor("m_" + name, list(shape), dt))
                hT = msb("hT", [128, NCH, S], BF16)
                hB = [Buf() for _ in range(NB)]
                yaT = msb("yaT", [64, 4, S], BF16)
                yaB = [Buf() for _ in range(4)]
                with contextlib.ExitStack() as ns:
                    sq_t = ns.enter_context(nc.sbuf_tensor("m_sq", [128, NCH, 512], BF16))
                    sq_b = [Buf() for _ in range(NCH)]
                    rs_t = ns.enter_context(nc.sbuf_tensor("m_rs", [128, 512], F32))
                    rs_b = Buf()
                    for n in range(NB):
                        rmsnorm_block(n, 1, lambda c, n=n: hT[:, c, n * 512:(n + 1) * 512], hB[n], sq_t, sq_b, rs_t, rs_b)
                    T.barrier()
                if stop_after == "dbg_h":
                    dbg_dump([(hT[:, c, :], hB, c * 128, 128) for c in range(NCH)])
                    return True
                if attention(hT, hB, yaT, yaB):
                    return True
                T.barrier()
                if stop_after == "dbg_ya":
                    dbg_dump([(yaT[:, c, :], yaB, c * 64, 64) for c in range(4)])
                    return True
                ybT = msb("ybT", [128, 8, S], BF16)
                ybB = [Buf() for _ in range(8)]
                gdn(hT, hB, ybT, ybB)
                T.barrier()
                if stop_after == "dbg_yb":
                    dbg_dump([(ybT[:, c, :], ybB, c * 128, 128) for c in range(8)])
                    return True
                merge(hT, hB, yaT, yaB, ybT, ybB)
                T.barrier()
            return None

        def nset(lo, hi):
            return list(range(lo // 512, hi // 512 + 1))

        def attention(hT, hB, yaT, yaB):
            wqkv_d = dt_in["wqkv_a"]
            with contextlib.ExitStack() as as_:
                def asb(name, shape, dt):
                    return as_.enter_context(nc.sbuf_tensor("a_" + name, list(shape), dt))
                cos_t = asb("cos", [128, S], F32)
                sin_t = asb("sin", [128, S], F32)
                rm_t = asb("rm", [128, 128], BF16)
                am_t = asb("am", [128, 512], BF16)
                em_t = asb("em", [128, 64], F32)
                acB = Buf()
                T.dma("sp", cos_t[:], dt_in["rope_cos"], writes=[acB])
                T.dma("sp", sin_t[:], dt_in["rope_sin"], writes=[acB])
                T.dma("pool", rm_t[:], dt_in["rm"], writes=[acB])
                T.dma("pool", am_t[:], dt_in["amask"], writes=[acB])
                T.dma("sp", em_t[:], dt_in["em"], writes=[acB])
                wq = [asb("w%d" % i, [128, NCH, 128], BF16) for i in range(3)]
                wB = [Buf() for _ in range(3)]
                qk = [asb("qk%d" % i, [128, S], BF16) for i in range(2)]
                qkB = [[Buf() for _ in range(NB)] for _ in range(2)]
                Vt = asb("Vt", [128, 16, 2, 65], BF16)
                VtB = [Buf() for _ in range(16)]
                VoneB = Buf()
                T.op(ROPE_ADD_ENG, lambda e: e.memset(Vt[:, :, :, 64:65], 1.0), writes=[VoneB])
                acc = asb("acc", [128, 2, S], F32)
                accB = Buf()
                qb = asb("qb", [128, 512], BF16)
                qbB = Buf()
                t1 = asb("t1", [128, 512], F32)
                t2 = asb("t2", [128, 512], F32)
                t1B, t2B = Buf(), Buf()
                ex = [asb("ex%d" % i, [128, 512], BF16) for i in range(3)]
                exB = [Buf(), Buf(), Buf()]
                exm = [asb("exm%d" % i, [128, 512], BF16) for i in range(3)]
                exmB = [Buf(), Buf(), Buf()]
                rl = asb("rl", [64, 512], F32)
                rlB = Buf()
                exn = 0
                if stop_after == "dbg_att0":
                    dbg_dump([(cos_t[:, :], [acB], 0, 128), (sin_t[:, :], [acB, VoneB], 128, 128)])
                    return True
                for hp in range(2):
                    for g in range(3):
                        dil = (1, 4, 16)[g]
                        nbk = 16 // dil
                        pi = g * 2 + hp
                        for i in range(3):
                            T.dma("pool", wq[i][:].rearrange("p c f -> p (c f)"), wqkv_d[pi, i], writes=[wB[i]])
                        for i in range(2):
                            for n in range(NB):
                                tk = slice(n * 512, (n + 1) * 512)
                                b = bank()

                                def mm(e):
                                    ins = None
                                    for c in range(NCH):
                                        ins = e.matmul(psum[:, b, :], lhsT=wq[i][:, c, :], rhs=hT[:, c, tk],
                                                       start=(c == 0), stop=(c == NCH - 1))
                                    return ins
                                CUT = int(os.environ.get("CUT", "9"))
                                if CUT >= 1:
                                    T.op("pe", mm, reads=[wB[i], hB[n]], writes=[pB[b]])
                                if CUT >= 2:
                                    T.op("act", lambda e: e.activation(out=qb[:, :], in_=psum[:, b, :], func=AF.Copy),
                                         reads=[pB[b]], writes=[qbB])
                                b2 = bank()
                                if CUT >= 3:
                                    T.op("pe", lambda e: e.matmul(psum[:, b2, :], lhsT=rm_t[:, :], rhs=qb[:, :], start=True, stop=True),
                                         reads=[qbB, acB], writes=[pB[b2]])
                                if CUT >= 4:
                                    T.op("dve", lambda e: e.tensor_tensor(out=t1[:, :], in0=psum[:, b, :], in1=cos_t[:, tk], op=ALU.mult),
                                         reads=[pB[b], acB], writes=[t1B])
                                if CUT >= 5:
                                    T.op("dve", lambda e: e.tensor_tensor(out=t2[:, :], in0=psum[:, b2, :], in1=sin_t[:, tk], op=ALU.mult),
                                         reads=[pB[b2], acB], writes=[t2B])
                                if CUT >= 6:
                                    T.op(ROPE_ADD_ENG, lambda e: e.tensor_tensor(out=qk[i][:, tk], in0=t1[:, :], in1=t2[:, :], op=ALU.add),
                                         reads=[t1B, t2B], writes=[qkB[i][n]])
                                if stop_after == "dbg_att1a":
                                    T.barrier()
                                    with contextlib.ExitStack() as ds:
                                        dtl = ds.enter_context(nc.sbuf_tensor("dbgx", [128, 512], F32))
                                        dxB = Buf()
                                        T.op("act", lambda e: e.activation(out=dtl[:, :], in_=qk[0][:, 0:512], func=AF.Copy), reads=[qkB[0][0]], writes=[dxB])
                                        T.dma("sp", out_d[0:128, 0:512], t1[:, :], reads=[t1B])
                                        T.dma("sp", out_d[128:256, 0:512], t2[:, :], reads=[t2B])
                                        T.dma("sp", out_d[256:384, 0:512], dtl[:, :], reads=[dxB])
                                        T.barrier()
                                    return True
                        if stop_after == "dbg_att1":
                            dbg_dump([(qk[0][:, :], qkB[0], 0, 128), (qk[1][:, :], qkB[1], 128, 128)])
                            return True
                        def tokslice(bi):
                            r, nk = bi // nbk, bi % nbk
                            st = r + dil * 128 * nk
                            return st, slice(st, st + 127 * dil + 1, dil)
                        for b4 in range(4):
                            b = bank()

                            def mmv(e):
                                ins = None
                                for k in range(4):
                                    st, sl = tokslice(b4 * 4 + k)
                                    for c in range(NCH):
                                        ins = e.matmul(psum[:, b, k * 128:(k + 1) * 128], lhsT=hT[:, c, sl], rhs=wq[2][:, c, :],
                                                       start=(c == 0), stop=(c == NCH - 1))
                                return ins
                            T.op("pe", mmv, reads=[wB[2]] + hB, writes=[pB[b]])
                            T.op("act", lambda e: e.activation(
                                out=Vt[:, b4 * 4:(b4 + 1) * 4, :, 0:64],
                                in_=psum[:, b, :].rearrange("p (k e d) -> p k e d", k=4, e=2), func=AF.Copy),
                                 reads=[pB[b]], writes=[VtB[b4 * 4 + k] for k in range(4)])
                        if stop_after == "dbg_att2":
                            dbg_dump([(Vt[:, :, :, :].rearrange("p a b c -> p (a b c)")[:, 0:2048], VtB + [VoneB], 0, 128)])
                            return True
                        def stage1(bi):
                            nonlocal exn
                            r, nk = bi // nbk, bi % nbk
                            st, qsl = tokslice(bi)
                            hi_tok = st + 127 * dil
                            qn_set = nset(st, hi_tok)
                            b = bank2()
                            if nk > 0:
                                stp, ksl_p = tokslice(bi - 1)
                                kn_set = nset(stp, hi_tok)
                            else:
                                kn_set = qn_set

                            def mms(e):
                                ins = None
                                for hd in range(2):
                                    hs = slice(hd * 64, (hd + 1) * 64)
                                    if nk > 0:
                                        ins = e.matmul(psum[:, b + hd, 0:128], lhsT=qk[1][hs, ksl_p], rhs=qk[0][hs, qsl],
                                                       start=True, stop=True)
                                    ins = e.matmul(psum[:, b + hd, 128:256], lhsT=qk[1][hs, qsl], rhs=qk[0][hs, qsl],
                                                   start=True, stop=True)
                                return ins
                            T.op("pe", mms, reads=[qkB[0][n] for n in qn_set] + [qkB[1][n] for n in kn_set], writes=[pB[b], pB[b + 1]])
                            xi = exn % 3
                            exn += 1
                            lo = 0 if nk > 0 else 128
                            src = psum[:, b:b + 2, lo:256]
                            dst = ex[xi][:, :].rearrange("p (e w) -> p e w", e=2)[:, :, lo:256]
                            dstm = exm[xi][:, :].rearrange("p (e w) -> p e w", e=2)[:, :, lo:256]
                            msk = am_t[:, :].rearrange("p (e w) -> p e w", e=2)[:, :, lo:256]
                            T.op("act", lambda e: e.activation(out=dst, in_=src, func=AF.Exp, scale=0.125),
                                 reads=[pB[b], pB[b + 1]], writes=[exB[xi]])
                            T.op("pool", lambda e: e.tensor_tensor(out=dstm, in0=dst, in1=msk, op=ALU.mult),
                                 reads=[exB[xi], acB], writes=[exmB[xi]])
                            return xi

                        def stage2(bi, xi):
                            r, nk = bi // nbk, bi % nbk
                            st, qsl = tokslice(bi)
                            b2 = bank()

                            def mmu(e):
                                ins = None
                                for hd in range(2):
                                    o = psum[0:65, b2, hd * 128:(hd + 1) * 128]
                                    if nk > 0:
                                        e.matmul(o, lhsT=Vt[:, bi - 1, hd, :], rhs=exm[xi][:, hd * 256:hd * 256 + 128], start=True, stop=False)
                                    ins = e.matmul(o, lhsT=Vt[:, bi, hd, :], rhs=exm[xi][:, hd * 256 + 128:hd * 256 + 256],
                                                   start=(nk == 0), stop=True)
                                return ins
                            T.op("pe", mmu, reads=[exmB[xi], VtB[bi], VoneB] + ([VtB[bi - 1]] if nk > 0 else []), writes=[pB[b2]])
                            src2 = psum[0:65, b2, 0:256].rearrange("p (e q) -> p e q", e=2)
                            if g == 0:
                                T.op("dve", lambda e: e.tensor_copy(out=acc[0:65, :, qsl], in_=src2), reads=[pB[b2]], writes=[accB])
                            else:
                                T.op("dve", lambda e: e.tensor_tensor(out=acc[0:65, :, qsl], in0=src2, in1=acc[0:65, :, qsl], op=ALU.add),
                                     reads=[pB[b2]], writes=[accB])
                        pend = []
                        for bi in range(16):
                            pend.append((bi, stage1(bi)))
                            if len(pend) > 1:
                                stage2(*pend.pop(0))
                        while pend:
                            stage2(*pend.pop(0))
                        if stop_after == "dbg_att3":
                            dbg_dump([(acc[:, 0, :], [accB], 0, 128), (acc[:, 1, :], [accB], 128, 128)])
                            return True
                    for hd in range(2):
                        for n in range(NB):
                            tk = slice(n * 512, (n + 1) * 512)
                            b = bank()
                            T.op("pe", lambda e: e.matmul(psum[0:64, b, :], lhsT=em_t[0:65, :], rhs=acc[0:65, hd, tk], start=True, stop=True),
                                 reads=[accB, acB], writes=[pB[b]])
                            T.op("dve", lambda e: e.reciprocal(out=rl[:, :], in_=psum[0:64, b, :]), reads=[pB[b]], writes=[rlB])
                            T.op("dve", lambda e: e.tensor_tensor(out=yaT[:, 2 * hp + hd, tk], in0=acc[0:64, hd, tk], in1=rl[:, :], op=ALU.mult),
                                 reads=[rlB, accB], writes=[yaB[2 * hp + hd]])
                T.barrier()

        def gdn(hT, hB, ybT, ybB):
            with contextlib.ExitStack() as gs:
                def gsb(name, shape, dt):
                    return gs.enter_context(nc.sbuf_tensor("g_" + name, list(shape), dt))
                gcB = Buf()
                triu = gsb("triu", [128, 128], F32)
                smask = gsb("smask", [128, 128], F32)
                odiv = gsb("odiv", [128, 128], BF16)
                convw = gsb("convw", [128, 8, 3, 4], F32)
                alog = gsb("alog", [128, 8], F32)
                dtb = gsb("dtb", [128, 8], F32)
                onorm = gsb("onorm", [128, 1], F32)
                wbd = gsb("wbd", [128, NCH, 16], BF16)
                T.dma("sp", triu[:], dt_in["triu"], writes=[gcB])
                T.dma("sp", smask[:], dt_in["smask"], writes=[gcB])
                T.dma("pool", odiv[:], dt_in["odiv"], writes=[gcB])
                T.dma("sp", convw[:].rearrange("p h w t -> p (h w t)"), dt_in["convw"], writes=[gcB])
                T.dma("sp", alog[:], dt_in["alog"], writes=[gcB])
                T.dma("sp", dtb[:], dt_in["dtb"], writes=[gcB])
                T.dma("sp", onorm[:], dt_in["onorm"], writes=[gcB])
                T.dma("pool", wbd[:].rearrange("p c f -> p (c f)"), dt_in["wbd"], writes=[gcB])
                braw = gsb("braw", [128, 16, 16], F32)
                beta = gsb("beta", [128, 16, 8], F32)
                negb = gsb("negb", [128, 16, 8], F32)
                gt = gsb("gt", [128, 16, 8], F32)
                gcum = gsb("gcum", [128, 16, 8], F32)
                glast = gsb("glast", [128, 16, 8], F32)
                kds = gsb("kds", [128, 16, 8], F32)
                sdec = gsb("sdec", [128, 16, 8], F32)
                tmpg = gsb("tmpg", [128, 16, 8], F32)
                negA = gsb("negA", [128, 8], F32)
                gB = Buf()
                b = bank()

                def mmb(e):
                    ins = None
                    for t in range(16):
                        for c in range(NCH):
                            ins = e.matmul(psum[:, b, t * 16:(t + 1) * 16], lhsT=hT[:, c, t * 128:(t + 1) * 128], rhs=wbd[:, c, :],
                                           start=(c == 0), stop=(c == NCH - 1))
                    return ins
                T.op("pe", mmb, reads=hB + [gcB], writes=[pB[b]])
                T.op("act", lambda e: e.activation(out=braw[:].rearrange("p t k -> p (t k)"), in_=psum[:, b, 0:256], func=AF.Copy),
                     reads=[pB[b]], writes=[gB])
                T.op("act", lambda e: e.activation(out=tmpg[:], in_=braw[:, :, 0:8], func=AF.Exp, scale=-1.0), reads=[gB], writes=[gB])
                T.op("dve", lambda e: e.tensor_single_scalar(out=tmpg[:], in_=tmpg[:], scalar=1.0, op=ALU.add), reads=[gB], writes=[gB])
                T.op("dve", lambda e: e.reciprocal(out=beta[:], in_=tmpg[:]), reads=[gB], writes=[gB])
                T.op("dve", lambda e: e.tensor_single_scalar(out=negb[:], in_=beta[:], scalar=-1.0, op=ALU.mult), reads=[gB], writes=[gB])
                T.op("dve", lambda e: e.tensor_tensor(out=tmpg[:], in0=braw[:, :, 8:16], in1=dtb[:].unsqueeze(1).to_broadcast([128, 16, 8]), op=ALU.add),
                     reads=[gB, gcB], writes=[gB])
                T.op("act", lambda e: e.activation(out=tmpg[:], in_=tmpg[:], func=AF.Exp), reads=[gB], writes=[gB])
                T.op("act", lambda e: e.activation(out=tmpg[:], in_=tmpg[:], func=AF.Ln, bias=eps_t[:, 1:2]), reads=[gB, cB], writes=[gB])
                T.op("act", lambda e: e.activation(out=negA[:], in_=alog[:], func=AF.Exp), reads=[gcB], writes=[gB])
                T.op("dve", lambda e: e.tensor_single_scalar(out=negA[:], in_=negA[:], scalar=-1.0, op=ALU.mult), reads=[gB], writes=[gB])
                T.op("dve", lambda e: e.tensor_tensor(out=gt[:], in0=tmpg[:], in1=negA[:].unsqueeze(1).to_broadcast([128, 16, 8]), op=ALU.mult),
                     reads=[gB], writes=[gB])
                b = bank()
                T.op("pe", lambda e: e.matmul(psum[:, b, 0:128], lhsT=triu[:, :], rhs=gt[:].rearrange("p t k -> p (t k)"), start=True, stop=True),
                     reads=[gB, gcB], writes=[pB[b]])
                T.op("act", lambda e: e.activation(out=gcum[:].rearrange("p t k -> p (t k)"), in_=psum[:, b, 0:128], func=AF.Copy),
                     reads=[pB[b]], writes=[gB])
                b = bank()
                T.op("pe", lambda e: e.matmul(psum[:, b, 0:128], lhsT=ones_f[:, :], rhs=gt[:].rearrange("p t k -> p (t k)"), start=True, stop=True),
                     reads=[gB, cB], writes=[pB[b]])
                T.op("act", lambda e: e.activation(out=glast[:].rearrange("p t k -> p (t k)"), in_=psum[:, b, 0:128], func=AF.Copy),
                     reads=[pB[b]], writes=[gB])
                T.op("dve", lambda e: e.tensor_tensor(out=tmpg[:], in0=glast[:], in1=gcum[:], op=ALU.subtract), reads=[gB], writes=[gB])
                T.op("act", lambda e: e.activation(out=kds[:], in_=tmpg[:], func=AF.Exp), reads=[gB], writes=[gB])
                T.op("act", lambda e: e.activation(out=sdec[:], in_=glast[:], func=AF.Exp), reads=[gB], writes=[gB])

                wh = [gsb("wh%d" % i, [128, NCH, 128], BF16) for i in range(4)]
                whB = [Buf() for _ in range(4)]
                Dg = gsb("Dg", [128, 3, 4, 128], BF16)
                DgB = Buf()
                xb = gsb("xb", [128, 4 + S], BF16)
                xbB = [Buf() for _ in range(NB)]
                xb0B = Buf()
                T.op("pool", lambda e: e.memset(xb[:, 0:4], 0.0), writes=[xb0B])
                qkv = [gsb("qkv%d" % i, [128, S], BF16) for i in range(3)]
                qkvB = [[Buf() for _ in range(NB)] for _ in range(3)]
                gsil = gsb("gsil", [128, S], BF16)
                gsilB = [Buf() for _ in range(NB)]
                sl = gsb("sl", [128, 512], F32)
                slB = Buf()
                sqb = gsb("sqb", [128, 512], BF16)
                sqbB = Buf()
                rt = gsb("rt", [128, 512], F32)
                rtB = Buf()
                def pp(name, shape, dt):
                    return [gsb("%s%d" % (name, i), shape, dt) for i in range(2)], [Buf(), Buf()]
                DT, DTB = pp("DT", [128, 128], F32)
                decS, decSB = pp("decS", [128, 128], F32)
                decF, decFB = pp("decF", [128, 128], F32)
                EG, EGB = pp("EG", [128, 128], F32)
                PQ = [[gsb("PQ%d_%d" % (i, k), [128, 256], BF16) for k in range(2)] for i in range(2)]
                PQB = [[Buf(), Buf()] for _ in range(2)]
                IP, IPB = pp("IP", [128, 128], BF16)
                TT = [[gsb("TT%d_%d" % (i, k), [128, 128], BF16) for k in range(2)] for i in range(2)]
                TTB = [[Buf(), Buf()] for _ in range(2)]
                Aqk, AqkB = pp("Aqk", [128, 128], BF16)
                kdec, kdecB = pp("kdec", [128, 128], BF16)
                vtok, vtokB = pp("vtok", [128, 128], F32)
                kgT, kgTB = pp("kgT", [128, 128], BF16)
                qdT, qdTB = pp("qdT", [128, 128], BF16)
                Xt, XtB = pp("Xt", [128, 128], BF16)
                vnew, vnewB = pp("vnew", [128, 128], BF16)
                osq, osqB = pp("osq", [128, 128], BF16)
                orst, orstB = pp("orst", [128, 128], F32)
                otmp, otmpB = pp("otmp", [128, 128], F32)
                Sf = gsb("Sf", [128, 128], F32)
                Sb = gsb("Sb", [128, 128], BF16)
                SfB, SbB = Buf(), Buf()
                QSCALE = 128.0 ** -0.5

                for h in range(8):
                    for i in range(3):
                        T.dma("pool", wh[i][:].rearrange("p c f -> p (c f)"), dt_in["wgdn"][h, i], writes=[whB[i]])
                    T.dma("pool", wh[3][:].rearrange("p c f -> p (c f)"), dt_in["wgg"][h], writes=[whB[3]])
                    for w3 in range(3):
                        for tp in range(4):
                            T.op("dve", lambda e: e.tensor_scalar_mul(out=Dg[:, w3, tp, :], in0=ident_f[:, :], scalar1=convw[:, h, w3, tp:tp + 1]),
                                 reads=[gcB, cB], writes=[DgB])
                    for w3 in range(3):
                        for n in range(NB):
                            tk = slice(n * 512, (n + 1) * 512)
                            b = bank()

                            def mm(e):
                                ins = None
                                for c in range(NCH):
                                    ins = e.matmul(psum[:, b, :], lhsT=wh[w3][:, c, :], rhs=hT[:, c, tk], start=(c == 0), stop=(c == NCH - 1))
                                return ins
                            T.op("pe", mm, reads=[whB[w3], hB[n]], writes=[pB[b]])
                            T.op("act", lambda e: e.activation(out=xb[:, 4 + n * 512:4 + (n + 1) * 512], in_=psum[:, b, :], func=AF.Copy),
                                 reads=[pB[b]], writes=[xbB[n]])
                        for n in range(NB):
                            tk = slice(n * 512, (n + 1) * 512)
                            b = bank()

                            def mmc(e):
                                ins = None
                                for tp in range(4):
                                    ins = e.matmul(psum[:, b, :], lhsT=Dg[:, w3, tp, :], rhs=xb[:, n * 512 + 1 + tp:n * 512 + 1 + tp + 512],
                                                   start=(tp == 0), stop=(tp == 3))
                                return ins
                            T.op("pe", mmc, reads=[DgB, xbB[n], xb0B] + ([xbB[n - 1]] if n > 0 else []), writes=[pB[b]])
                            if w3 == 2:
                                T.op("act", lambda e: e.activation(out=qkv[2][:, tk], in_=psum[:, b, :], func=AF.Silu),
                                     reads=[pB[b]], writes=[qkvB[2][n]])
                            else:
                                T.op("act", lambda e: e.activation(out=sl[:, :], in_=psum[:, b, :], func=AF.Silu), reads=[pB[b]], writes=[slB])
                                T.op("act", lambda e: e.activation(out=sqb[:, :], in_=sl[:, :], func=AF.Square), reads=[slB], writes=[sqbB])
                                b2 = bank()
                                T.op("pe", lambda e: e.matmul(psum[:, b2, :], lhsT=ones_b[:, :], rhs=sqb[:, :], start=True, stop=True),
                                     reads=[sqbB, cB], writes=[pB[b2]])
                                T.op("act", lambda e: e.activation(out=rt[:, :], in_=psum[:, b2, :], func=AF.Sqrt, bias=eps_t[:, 0:1]),
                                     reads=[pB[b2], cB], writes=[rtB])
                                T.op("dve", lambda e: e.reciprocal(out=rt[:, :], in_=rt[:, :]), reads=[rtB], writes=[rtB])
                                T.op("dve", lambda e: e.scalar_tensor_tensor(out=qkv[w3][:, tk], in0=sl[:, :], scalar=(QSCALE if w3 == 0 else 1.0),
                                                                             in1=rt[:, :], op0=ALU.mult, op1=ALU.mult),
                                     reads=[slB, rtB], writes=[qkvB[w3][n]])
                    for n in range(NB):
                        tk = slice(n * 512, (n + 1) * 512)
                        b = bank()

                        def mm(e):
                            ins = None
                            for c in range(NCH):
                                ins = e.matmul(psum[:, b, :], lhsT=wh[3][:, c, :], rhs=hT[:, c, tk], start=(c == 0), stop=(c == NCH - 1))
                            return ins
                        T.op("pe", mm, reads=[whB[3], hB[n]], writes=[pB[b]])
                        T.op("act", lambda e: e.activation(out=sl[:, :], in_=psum[:, b, :], func=AF.Silu), reads=[pB[b]], writes=[slB])
                        T.op("dve", lambda e: e.tensor_scalar_mul(out=gsil[:, tk], in0=sl[:, :], scalar1=onorm[:, 0:1]),
                             reads=[slB, gcB], writes=[gsilB[n]])
                    for t in range(16):
                        pz = t % 2
                        n = t // 4
                        ck = slice(t * 128, (t + 1) * 128)
                        gc = gcum[:, t, h:h + 1]
                        b = bank()
                        T.op("pe", lambda e: e.matmul(psum[:, b, 0:128], lhsT=gt[:, t, h:h + 1].to_broadcast([128, 128]), rhs=triu[:, :],
                                                      start=True, stop=True), reads=[gB, gcB], writes=[pB[b]])
                        T.op("dve", lambda e: e.scalar_tensor_tensor(out=DT[pz][:, :], in0=psum[:, b, 0:128], scalar=gc, in1=smask[:, :],
                                                                     op0=ALU.subtract, op1=ALU.add), reads=[pB[b], gB, gcB], writes=[DTB[pz]])
                        T.op("act", lambda e: e.activation(out=EG[pz][:, :], in_=psum[:, b, 0:128], func=AF.Exp), reads=[pB[b]], writes=[EGB[pz]])
                        T.op("act", lambda e: e.activation(out=decS[pz][:, :], in_=DT[pz][:, :], func=AF.Exp), reads=[DTB[pz]], writes=[decSB[pz]])
                        T.op("pool", lambda e: e.tensor_tensor(out=decF[pz][:, :], in0=decS[pz][:, :], in1=ident_f[:, :], op=ALU.add),
                             reads=[decSB[pz], cB], writes=[decFB[pz]])
                        b = bank()
                        T.op("pe", lambda e: (e.matmul(psum[:, b, 0:128], lhsT=qkv[1][:, ck], rhs=qkv[1][:, ck], start=True, stop=True),
                                              e.matmul(psum[:, b, 128:256], lhsT=qkv[1][:, ck], rhs=qkv[0][:, ck], start=True, stop=True))[1],
                             reads=[qkvB[0][n], qkvB[1][n]], writes=[pB[b]])
                        T.op("dve", lambda e: e.scalar_tensor_tensor(out=PQ[pz][0][:, 128:256], in0=psum[:, b, 0:128], scalar=negb[:, t, h:h + 1],
                                                                     in1=decS[pz][:, :], op0=ALU.mult, op1=ALU.mult),
                             reads=[pB[b], gB, decSB[pz]], writes=[PQB[pz][0]])
                        T.op("dve", lambda e: e.tensor_tensor(out=Aqk[pz][:, :], in0=psum[:, b, 128:256], in1=decF[pz][:, :], op=ALU.mult),
                             reads=[pB[b], decFB[pz]], writes=[AqkB[pz]])
                        b = bank()
                        T.op("pe", lambda e: (e.matmul(psum[:, b, 0:128], lhsT=PQ[pz][0][:, 128:256], rhs=ident_b[:, :], start=True, stop=True),
                                              e.matmul(psum[:, b, 128:256], lhsT=qkv[1][:, ck], rhs=ident_b[:, :], start=True, stop=True),
                                              e.matmul(psum[:, b, 256:384], lhsT=qkv[2][:, ck], rhs=ident_b[:, :], start=True, stop=True))[2],
                             reads=[PQB[pz][0], qkvB[1][n], qkvB[2][n], cB], writes=[pB[b]])
                        T.op("act", lambda e: e.activation(out=PQ[pz][0][:, 0:128], in_=psum[:, b, 0:128], func=AF.Copy),
                             reads=[pB[b]], writes=[PQB[pz][0]])
                        T.op("dve", lambda e: e.tensor_scalar_mul(out=kdec[pz][:, :], in0=psum[:, b, 128:256], scalar1=kds[:, t, h:h + 1]),
                             reads=[pB[b], gB], writes=[kdecB[pz]])
                        T.op("act", lambda e: e.activation(out=vtok[pz][:, :], in_=psum[:, b, 256:384], func=AF.Copy),
                             reads=[pB[b]], writes=[vtokB[pz]])
                        T.op("pool", lambda e: e.tensor_tensor(out=kgT[pz][:, :], in0=qkv[1][:, ck], in1=EG[pz][:, :], op=ALU.mult),
                             reads=[qkvB[1][n], EGB[pz]], writes=[kgTB[pz]])
                        T.op("pool", lambda e: e.tensor_tensor(out=qdT[pz][:, :], in0=qkv[0][:, ck], in1=EG[pz][:, :], op=ALU.mult),
                             reads=[qkvB[0][n], EGB[pz]], writes=[qdTB[pz]])
                        T.op("pool", lambda e: e.tensor_tensor(out=IP[pz][:, :], in0=ident_b[:, :], in1=PQ[pz][0][:, 128:256], op=ALU.subtract),
                             reads=[PQB[pz][0], cB], writes=[IPB[pz]])
                        T.op("pool", lambda e: e.tensor_tensor(out=PQ[pz][1][:, :].rearrange("p (a b) -> p a b", a=2),
                                                                in0=PQ[pz][0][:, :].rearrange("p (a b) -> p a b", a=2),
                                                                in1=ident_b[:, :].unsqueeze(1).to_broadcast([128, 2, 128]), op=ALU.add),
                             reads=[PQB[pz][0], cB], writes=[PQB[pz][1]])
                        cur = 1
                        for k in range(2, 8):
                            nxt = 1 - cur
                            last = (k == 7)
                            b = bank()
                            T.op("pe", lambda e: e.matmul(psum[:, b, 0:128], lhsT=IP[pz][:, :], rhs=PQ[pz][cur][:, 0:128], start=True, stop=True),
                                 reads=[IPB[pz], PQB[pz][cur]], writes=[pB[b]])
                            T.op("dve", lambda e: e.scalar_tensor_tensor(out=TT[pz][0][:, :], in0=psum[:, b, 0:128], scalar=-1.0, in1=ident2_f[:, :],
                                                                         op0=ALU.mult, op1=ALU.add), reads=[pB[b], cB], writes=[TTB[pz][0]])
                            b2 = bank()

                            def mmn(e):
                                ins = e.matmul(psum[:, b2, 128:256], lhsT=TT[pz][0][:, :], rhs=PQ[pz][cur][:, 128:256], start=True, stop=True)
                                if not last:
                                    ins = e.matmul(psum[:, b2, 0:128], lhsT=PQ[pz][cur][:, 128:256], rhs=TT[pz][0][:, :], start=True, stop=True)
                                return ins
                            T.op("pe", mmn, reads=[TTB[pz][0], PQB[pz][cur]], writes=[pB[b2]])
                            if last:
                                T.op("act", lambda e: e.activation(out=TT[pz][1][:, :], in_=psum[:, b2, 128:256], func=AF.Copy),
                                     reads=[pB[b2]], writes=[TTB[pz][1]])
                            else:
                                T.op("act", lambda e: e.activation(out=PQ[pz][nxt][:, :], in_=psum[:, b2, 0:256], func=AF.Copy),
                                     reads=[pB[b2]], writes=[PQB[pz][nxt]])
                            cur = nxt
                        TTf, TTfB = TT[pz][1], TTB[pz][1]
                        if t == 0:
                            T.op("act", lambda e: e.activation(out=Xt[pz][:, :], in_=vtok[pz][:, :], func=AF.Copy), reads=[vtokB[pz]], writes=[XtB[pz]])
                        else:
                            b = bank()
                            T.op("pe", lambda e: e.matmul(psum[:, b, 0:128], lhsT=kgT[pz][:, :], rhs=Sb[:, :], start=True, stop=True),
                                 reads=[kgTB[pz], SbB], writes=[pB[b]])
                            T.op("dve", lambda e: e.tensor_tensor(out=Xt[pz][:, :], in0=vtok[pz][:, :], in1=psum[:, b, 0:128], op=ALU.subtract),
                                 reads=[pB[b], vtokB[pz]], writes=[XtB[pz]])
                        b = bank()
                        T.op("pe", lambda e: e.matmul(psum[:, b, 0:128], lhsT=TTf[:, :], rhs=Xt[pz][:, :], start=True, stop=True),
                             reads=[TTfB, XtB[pz]], writes=[pB[b]])
                        T.op("dve", lambda e: e.tensor_scalar_mul(out=vnew[pz][:, :], in0=psum[:, b, 0:128], scalar1=beta[:, t, h:h + 1]),
                             reads=[pB[b], gB], writes=[vnewB[pz]])
                        b = bank()

                        def mmo(e):
                            if t > 0:
                                e.matmul(psum[:, b, 0:128], lhsT=Sb[:, :], rhs=qdT[pz][:, :], start=True, stop=False)
                            return e.matmul(psum[:, b, 0:128], lhsT=vnew[pz][:, :], rhs=Aqk[pz][:, :], start=(t == 0), stop=True)
                        T.op("pe", mmo, reads=[SbB, qdTB[pz], vnewB[pz], AqkB[pz]], writes=[pB[b]])
                        T.op("act", lambda e: e.activation(out=osq[pz][:, :], in_=psum[:, b, 0:128], func=AF.Square), reads=[pB[b]], writes=[osqB[pz]])
                        b2 = bank()
                        T.op("pe", lambda e: e.matmul(psum[:, b2, 0:128], lhsT=odiv[:, :], rhs=osq[pz][:, :], start=True, stop=True),
                             reads=[osqB[pz], gcB], writes=[pB[b2]])
                        T.op("act", lambda e: e.activation(out=orst[pz][:, :], in_=psum[:, b2, 0:128], func=AF.Sqrt, bias=eps_t[:, 0:1]),
                             reads=[pB[b2], cB], writes=[orstB[pz]])
                        T.op("dve", lambda e: e.reciprocal(out=orst[pz][:, :], in_=orst[pz][:, :]), reads=[orstB[pz]], writes=[orstB[pz]])
                        T.op("dve", lambda e: e.tensor_tensor(out=otmp[pz][:, :], in0=psum[:, b, 0:128], in1=orst[pz][:, :], op=ALU.mult),
                             reads=[pB[b], orstB[pz]], writes=[otmpB[pz]])
                        T.op("pool", lambda e: e.tensor_tensor(out=ybT[:, h, ck], in0=otmp[pz][:, :], in1=gsil[:, ck], op=ALU.mult),
                             reads=[otmpB[pz], gsilB[n]], writes=[ybB[h]])
                        if t < 15:
                            b = bank()
                            T.op("pe", lambda e: e.matmul(psum[:, b, 0:128], lhsT=kdec[pz][:, :], rhs=vnew[pz][:, :], start=True, stop=True),
                                 reads=[kdecB[pz], vnewB[pz]], writes=[pB[b]])
                            if t == 0:
                                T.op("dve", lambda e: e.tensor_copy(out=Sf[:, :], in_=psum[:, b, 0:128]), reads=[pB[b]], writes=[SfB])
                            else:
                                T.op("dve", lambda e: e.scalar_tensor_tensor(out=Sf[:, :], in0=Sf[:, :], scalar=sdec[:, t, h:h + 1], in1=psum[:, b, 0:128],
                                                                             op0=ALU.mult, op1=ALU.add), reads=[pB[b], gB, SfB], writes=[SfB])
                            T.op("act", lambda e: e.activation(out=Sb[:, :], in_=Sf[:, :], func=AF.Copy), reads=[SfB], writes=[SbB])
                T.barrier()

        def merge(hT, hB, yaT, yaB, ybT, ybB):
            with contextlib.ExitStack() as gs:
                def gsb(name, shape, dt):
                    return gs.enter_context(nc.sbuf_tensor("x_" + name, list(shape), dt))
                mg = gsb("mg", [128, 8, S], BF16)
                mgB = [[Buf() for _ in range(NB)] for _ in range(8)]
                wga = [gsb("wga%d" % i, [128, 2, NCH, 128], BF16) for i in range(2)]
                wba = [gsb("wba%d" % i, [64, 4, 128], BF16) for i in range(2)]
                wbb = [gsb("wbb%d" % i, [128, 8, 128], BF16) for i in range(2)]
                wB = [Buf(), Buf()]
                wo = [gsb("wo%d" % i, [128, 8, 128], BF16) for i in range(2)]
                woB = [Buf(), Buf()]
                one_c = eps_t[:, 1:2]
                sa = gsb("sa", [128, 512], F32)
                sb_ = gsb("sb", [128, 512], F32)
                ta = gsb("ta", [128, 512], F32)
                tb = gsb("tb", [128, 512], F32)
                saB, sbB, taB, tbB = Buf(), Buf(), Buf(), Buf()

                def loadw(dt):
                    i = dt % 2
                    T.dma("pool", wga[i][:].rearrange("p w c f -> p (w c f)"), dt_in["wgab"][dt], writes=[wB[i]])
                    T.dma("pool", wba[i][:].rearrange("p s f -> p (s f)"), dt_in["wba"][dt], writes=[wB[i]])
                    T.dma("pool", wbb[i][:].rearrange("p s f -> p (s f)"), dt_in["wbb"][dt], writes=[wB[i]])
                loadw(0)
                loadw(1)
                for dt in range(8):
                    wi = dt % 2
                    for n in range(NB):
                        tk = slice(n * 512, (n + 1) * 512)
                        bga, bgb, ba, bb = bank(), bank(), bank(), bank()

                        def mmg(e, which, b):
                            ins = None
                            for c in range(NCH):
                                ins = e.matmul(psum[:, b, :], lhsT=wga[wi][:, which, c, :], rhs=hT[:, c, tk], start=(c == 0), stop=(c == NCH - 1))
                            return ins
                        T.op("pe", lambda e: mmg(e, 0, bga), reads=[wB[wi], hB[n]], writes=[pB[bga]])
                        T.op("pe", lambda e: mmg(e, 1, bgb), reads=[wB[wi], hB[n]], writes=[pB[bgb]])

                        def mma(e):
                            ins = None
                            for s4 in range(4):
                                ins = e.matmul(psum[:, ba, :], lhsT=wba[wi][:, s4, :], rhs=yaT[:, s4, tk], start=(s4 == 0), stop=(s4 == 3))
                            return ins
                        T.op("pe", mma, reads=[wB[wi]] + yaB, writes=[pB[ba]])

                        def mmb(e):
                            ins = None
                            for h in range(8):
                                ins = e.matmul(psum[:, bb, :], lhsT=wbb[wi][:, h, :], rhs=ybT[:, h, tk], start=(h == 0), stop=(h == 7))
                            return ins
                        T.op("pe", mmb, reads=[wB[wi]] + ybB, writes=[pB[bb]])
                        T.op("act", lambda e: e.activation(out=sa[:, :], in_=psum[:, bga, :], func=AF.Sigmoid), reads=[pB[bga]], writes=[saB])
                        T.op("act", lambda e: e.activation(out=sb_[:, :], in_=psum[:, bgb, :], func=AF.Sigmoid), reads=[pB[bgb]], writes=[sbB])
                        T.op("dve", lambda e: e.tensor_tensor(out=ta[:, :], in0=psum[:, ba, :], in1=sa[:, :], op=ALU.mult), reads=[pB[ba], saB], writes=[taB])
                        T.op("dve", lambda e: e.tensor_tensor(out=tb[:, :], in0=psum[:, bb, :], in1=sb_[:, :], op=ALU.mult), reads=[pB[bb], sbB], writes=[tbB])
                        T.op("pool", lambda e: e.tensor_tensor(out=mg[:, dt, tk], in0=ta[:, :], in1=tb[:, :], op=ALU.add), reads=[taB, tbB], writes=[mgB[dt][n]])
                    if dt + 2 < 8:
                        loadw(dt + 2)

                def loado(dt):
                    T.dma("pool", wo[dt % 2][:].rearrange("p s f -> p (s f)"), dt_in["wout"][dt], writes=[woB[dt % 2]])
                loado(0)
                loado(1)
                for dt in range(8):
                    wi = dt % 2
                    for n in range(NB):
                        tk = slice(n * 512, (n + 1) * 512)
                        b = bank()

                        def mmo(e):
                            ins = None
                            for k in range(8):
                                ins = e.matmul(psum[:, b, :], lhsT=wo[wi][:, k, :], rhs=mg[:, k, tk], start=(k == 0), stop=(k == 7))
                            return ins
                        T.op("pe", mmo, reads=[woB[wi]] + [mgB[k][n] for k in range(8)], writes=[pB[b]])
                        T.op("dve", lambda e: e.tensor_tensor(out=xT[:, dt, tk], in0=psum[:, b, :], in1=xT[:, dt, tk], op=ALU.add),
                             reads=[pB[b]], writes=[xB[n]])
                    if dt + 2 < 8:
                        loado(dt + 2)
                T.barrier()

        def dbg_dump(slabs):
            with contextlib.ExitStack() as ds:
                dtile = [ds.enter_context(nc.sbuf_tensor("dbg%d" % i, [128, S], F32)) for i in range(2)]
                dB = [Buf(), Buf()]
                for k, (ap, bufs, row0, P) in enumerate(slabs):
                    i = k % 2
                    T.op("act", lambda e: e.activation(out=dtile[i][0:P, :], in_=ap, func=AF.Copy), reads=bufs, writes=[dB[i]])
                    T.dma("sp", out_d[row0:row0 + P, :], dtile[i][0:P, :], reads=[dB[i]])
                T.barrier()

        ffn(0, 0)
        done = False
        if stop_after not in ("ffn1_raw", "ffn1"):
            r = mixer()
            if r is not None:
                done = True
        if stop_after in ("all", "x3_raw"):
            ffn(1, 2)
        if not done:
          with contextlib.ExitStack() as fs:
            yT = [fs.enter_context(nc.sbuf_tensor("yT%d" % i, [128, NCH, 512], F32)) for i in range(2)]
            yB = [Buf(), Buf()]
            sq_t = fs.enter_context(nc.sbuf_tensor("fsq", [128, NCH, 512], BF16))
            sq_b = [Buf() for _ in range(NCH)]
            rs_t = fs.enter_context(nc.sbuf_tensor("frs", [128, 512], F32))
            rs_b = Buf()
            out_dv = out_d.rearrange("(c p) t -> p c t", p=128)
            for n in range(NB):
                i = n % 2
                if stop_after.endswith("_raw"):
                    for c in range(NCH):
                        T.op("dve", lambda e: e.tensor_copy(out=yT[i][:, c, :], in_=xT[:, c, n * 512:(n + 1) * 512]),
                             reads=[xB[n]], writes=[yB[i]])
                else:
                    rmsnorm_block(n, 3, lambda c: yT[i][:, c, :], yB[i], sq_t, sq_b, rs_t, rs_b)
                T.dma("sp", out_dv[:, :, n * 512:(n + 1) * 512], yT[i][:], reads=[yB[i]])
            T.barrier()
    return nc, set(dt_in.keys())


def _consts():
    c = {}
    c["ident_f"] = np.eye(128, dtype=np.float32)
    c["ones_f"] = np.ones((128, 128), dtype=np.float32)
    half = 32
    inv_freq = (10000.0 ** (-np.arange(half, dtype=np.float32) / half)).astype(np.float32)
    pos = np.arange(S, dtype=np.float32)
    ang = (pos[None, :] * inv_freq[:, None]).astype(np.float32)
    cosv, sinv = np.cos(ang).astype(np.float32), np.sin(ang).astype(np.float32)
    cos_t = np.zeros((128, S), np.float32)
    sin_t = np.zeros((128, S), np.float32)
    rm = np.zeros((128, 128), np.float32)
    for p in range(128):
        d = p % 64
        cos_t[p] = cosv[d % 32]
        sin_t[p] = -sinv[d % 32] if d < 32 else sinv[d % 32]
        partner = p + 32 if d < 32 else p - 32
        rm[partner, p] = 1.0
    c["rope_cos"], c["rope_sin"], c["rm"] = cos_t, sin_t, rm
    j = np.arange(128)[:, None]
    i = np.arange(128)[None, :]
    mprev = (j >= i).astype(np.float32)
    mcur = (j <= i).astype(np.float32)
    c["amask"] = np.concatenate([mprev, mcur, mprev, mcur], axis=1)
    em = np.zeros((128, 64), np.float32)
    em[64, :] = 1.0
    c["em"] = em
    c["triu"] = (j <= i).astype(np.float32)
    c["smask"] = np.where(i > j, 0.0, -30000.0).astype(np.float32)
    c["odiv"] = np.full((128, 128), 1.0 / 128.0, np.float32)
    return c


def _prep_inputs(inp):
    f = lambda a: np.ascontiguousarray(a, dtype=np.float32)
    shared = {}

    def ncol(v):
        return f(np.asarray(v).reshape(NCH, 128).T)
    shared["ffn1_norm"] = ncol(inp["ffn1_norm"][0])
    shared["ffn2_norm"] = ncol(inp["ffn2_norm"][0])
    shared["mix_norm"] = ncol(inp["mix_norm"][0])
    shared["final_norm"] = ncol(inp["final_norm"])
    ffn_in = {1: (inp["ffn1_w_gate"], inp["ffn1_w_up"], inp["ffn1_w_down"]),
              2: (inp["ffn2_w_gate"], inp["ffn2_w_up"], inp["ffn2_w_down"])}
    for i in (1, 2):
        wg = np.asarray(ffn_in[i][0][0])
        wu = np.asarray(ffn_in[i][1][0])
        wd = np.asarray(ffn_in[i][2][0])
        shared["wg%d" % i] = f(wg.reshape(NCH, 128, 11, 256).transpose(2, 1, 0, 3).reshape(11, 128, NCH * 256))
        shared["wu%d" % i] = f(wu.reshape(NCH, 128, 11, 256).transpose(2, 1, 0, 3).reshape(11, 128, NCH * 256))
        shared["wd%d" % i] = f(wd.reshape(NF, 128, 8, 128).transpose(2, 1, 0, 3).reshape(8, 128, NF * 128))
    w_in = np.asarray(inp["w_in"][0])

    def cols(c0, n):
        return w_in[:, c0:c0 + n].reshape(NCH, 128, n).transpose(1, 0, 2).reshape(128, NCH * n)
    wa = np.zeros((6, 3, 128, NCH * 128), np.float32)
    for g in range(3):
        for hp in range(2):
            for i in range(3):
                wa[g * 2 + hp, i] = cols(i * 768 + (4 * g + 2 * hp) * 64, 128)
    shared["wqkv_a"] = wa
    wg_ = np.zeros((8, 3, 128, NCH * 128), np.float32)
    wgg = np.zeros((8, 128, NCH * 128), np.float32)
    for h in range(8):
        for i in range(3):
            wg_[h, i] = cols(2304 + i * 1024 + h * 128, 128)
        wgg[h] = cols(5392 + h * 128, 128)
    shared["wgdn"] = wg_
    shared["wgg"] = wgg
    shared["wbd"] = f(cols(5376, 16))
    wgab = np.zeros((8, 128, 2 * NCH * 128), np.float32)
    for dt in range(8):
        a = cols(6416 + dt * 128, 128).reshape(128, 1, NCH * 128)
        b = cols(7440 + dt * 128, 128).reshape(128, 1, NCH * 128)
        wgab[dt] = np.concatenate([a, b], axis=1).reshape(128, 2 * NCH * 128)
    shared["wgab"] = wgab
    wba = np.asarray(inp["w_branch_a"][0])
    shared["wba"] = f(wba.reshape(4, 64, 8, 128).transpose(2, 1, 0, 3).reshape(8, 64, 4 * 128))
    wbb = np.asarray(inp["w_branch_b"][0])
    shared["wbb"] = f(wbb.reshape(8, 128, 8, 128).transpose(2, 1, 0, 3).reshape(8, 128, 8 * 128))
    wo = np.asarray(inp["w_out"][0])
    shared["wout"] = f(wo.reshape(8, 128, 8, 128).transpose(2, 1, 0, 3).reshape(8, 128, 8 * 128))
    cw = np.asarray(inp["gdn_conv_w"][0])
    shared["convw"] = f(cw.reshape(4, 3, 8, 128).transpose(3, 2, 1, 0).reshape(128, 96))
    shared["alog"] = f(np.broadcast_to(np.asarray(inp["gdn_a_log"][0])[None, :], (128, 8)))
    shared["dtb"] = f(np.broadcast_to(np.asarray(inp["gdn_dt_bias"][0])[None, :], (128, 8)))
    shared["onorm"] = f(np.asarray(inp["gdn_out_norm"][0]).reshape(128, 1))
    shared.update(_consts())
    x = np.asarray(inp["x"])
    maps = []
    for b in range(8):
        m = dict(shared)
        m["xT"] = f(x[b].T)
        maps.append(m)
    return maps


_NC_CACHE = {}


def kernel(**inputs):
    stop_after = inputs.pop("_stop_after", "all")
    if stop_after not in _NC_CACHE:
        _NC_CACHE[stop_after] = build_program(stop_after)
    nc, names = _NC_CACHE[stop_after]
    maps = _prep_inputs(inputs)
    maps = [{k: v for k, v in m.items() if k in names} for m in maps]
    res = run_bass_kernel_spmd(nc, maps, core_ids=list(range(8)))
    out = np.stack([np.ascontiguousarray(r["outT"].T) for r in res.results], axis=0)
    return out.astype(np.float32)
```

```python
import contextlib
import numpy as np
import concourse.bass as bass
import concourse.mybir as mybir
from concourse.bass_utils import run_bass_kernel_spmd

F32 = mybir.dt.float32
BF16 = mybir.dt.bfloat16
AF = mybir.ActivationFunctionType
ALU = mybir.AluOpType
AX = mybir.AxisListType

S = 2048
D = 1024
NCH = 8
DFF = 2816
NF = 22
NB = 4
EPS = 1e-6
D_IN = 8464
NDS = 16
import os
ROPE_ADD_ENG = os.environ.get("ROPE_ADD_ENG", "pool")


class _Stop(Exception):
    pass


class Buf:
    __slots__ = ("w", "r", "excl")

    def __init__(self, excl=False):
        self.w = None
        self.r = []
        self.excl = excl


class Trk:
    def __init__(self, nc, es):
        self.nc = nc
        self.eng = {"pe": nc.tensor, "act": nc.scalar, "dve": nc.vector, "pool": nc.gpsimd, "sp": nc.sync}
        self.sem = {k: es.enter_context(nc.semaphore("sem_" + k)) for k in self.eng}
        self.cnt = {k: 0 for k in self.eng}
        self.seen = {k: {} for k in self.eng}
        self.dsem = [es.enter_context(nc.semaphore("dsem%d" % i)) for i in range(NDS)]
        self.dcnt = [0] * NDS
        self.dnext = {"pool": 0, "sp": NDS // 2, "act": NDS // 2}
        self.same = {"pe": False, "act": True, "dve": True, "pool": True, "sp": True}

    def _semof(self, k):
        return self.sem[k] if isinstance(k, str) else self.dsem[k[1]]

    def _wait(self, e, deps):
        best = {}
        for (k, v) in deps:
            if v > best.get(k, 0):
                best[k] = v
        for k, v in best.items():
            if k == e and not self.same[e]:
                continue
            if self.seen[e].get(k, 0) >= v:
                continue
            self.eng[e].wait_ge(self._semof(k), v)
            self.seen[e][k] = v

    @staticmethod
    def _deps(reads, writes):
        deps = []
        for b in reads:
            if b.w is not None:
                deps.append(b.w)
        for b in writes:
            if b.w is not None:
                deps.append(b.w)
            deps.extend(b.r)
        return deps

    @staticmethod
    def _upd(d, reads, writes):
        for b in reads:
            b.r.append(d)
            if len(b.r) > 64:
                best = {}
                for (k, v) in b.r:
                    if v > best.get(k, 0):
                        best[k] = v
                b.r = list(best.items())
        for b in writes:
            b.w = d
            b.r = []

    def op(self, e, fn, reads=(), writes=()):
        if any(b.excl for b in reads):
            writes = list(writes) + [b for b in reads if b.excl]
            reads = [b for b in reads if not b.excl]
        self._wait(e, self._deps(reads, writes))
        ins = fn(self.eng[e])
        self.cnt[e] += 1
        ins.then_inc(self.sem[e], 1)
        self._upd((e, self.cnt[e]), reads, writes)

    def dma(self, q, out, in_, reads=(), writes=()):
        deps = self._deps(reads, writes)
        i = self.dnext[q]
        lo = 0 if q == "pool" else NDS // 2
        self.dnext[q] = lo + (i - lo + 1) % (NDS // 2)
        if self.dcnt[i] > 0:
            deps.append((("d", i), self.dcnt[i]))
        self._wait(q, deps)
        self.eng[q].dma_start(out=out, in_=in_).then_inc(self.dsem[i], 16)
        self.dcnt[i] += 16
        self._upd((("d", i), self.dcnt[i]), reads, writes)

    def barrier(self):
        deps = [(k, v) for k, v in self.cnt.items() if v > 0]
        deps += [(("d", i), c) for i, c in enumerate(self.dcnt) if c > 0]
        for e in self.eng:
            self._wait(e, [d for d in deps if d[0] != e])


def build_program(stop_after="all", debug=False):
    nc = bass.Bass("TRN2", target_bir_lowering=False)
    dt_in = {}

    def din(name, shape):
        dt_in[name] = nc.dram_tensor(name, list(shape), F32, kind="ExternalInput").ap()
        return dt_in[name]

    xT_d = din("xT", [D, S])
    n1_d = din("ffn1_norm", [128, NCH])
    n2_d = din("ffn2_norm", [128, NCH])
    nm_d = din("mix_norm", [128, NCH])
    nf_d = din("final_norm", [128, NCH])
    ffn_w = []
    for i in (1, 2):
        ffn_w.append((din("wg%d" % i, [11, 128, NCH * 256]), din("wu%d" % i, [11, 128, NCH * 256]),
                      din("wd%d" % i, [8, 128, NF * 128])))
    din("wqkv_a", [6, 3, 128, NCH * 128])
    din("rope_cos", [128, S]); din("rope_sin", [128, S]); din("rm", [128, 128]); din("amask", [128, 512]); din("em", [128, 64])
    din("wgdn", [8, 3, 128, NCH * 128]); din("wgg", [8, 128, NCH * 128]); din("wbd", [128, NCH * 16])
    din("convw", [128, 96]); din("alog", [128, 8]); din("dtb", [128, 8]); din("onorm", [128, 1])
    din("triu", [128, 128]); din("smask", [128, 128]); din("odiv", [128, 128])
    din("wgab", [8, 128, 2 * NCH * 128]); din("wba", [8, 64, 4 * 128]); din("wbb", [8, 128, 8 * 128]); din("wout", [8, 128, 8 * 128])
    ident_d = din("ident_f", [128, 128])
    ones_d = din("ones_f", [128, 128])
    out_d = nc.dram_tensor("outT", [D, S], F32, kind="ExternalOutput").ap()
    dbg = {}

    with contextlib.ExitStack() as es:
        T = Trk(nc, es)

        def sb(name, shape, dt):
            return es.enter_context(nc.sbuf_tensor(name, list(shape), dt))

        xT = sb("xT_sb", [128, NCH, S], F32)
        xB = [Buf() for _ in range(NB)]
        psum = es.enter_context(nc.psum_tensor("psum", [128, 8, 512], F32))
        pB = [Buf(True) for _ in range(8)]
        pstate = {"n": 0}

        def bank():
            b = pstate["n"]
            pstate["n"] = (b + 1) % 8
            return b

        def bank2():
            b = pstate["n"]
            if b % 2:
                b = (b + 1) % 8
            pstate["n"] = (b + 2) % 8
            return b

        ident_f = sb("ident_f_sb", [128, 128], F32)
        ident_b = sb("ident_b_sb", [128, 128], BF16)
        ones_b = sb("ones_b_sb", [128, 128], BF16)
        ones_f = sb("ones_f_sb", [128, 128], F32)
        ident2_f = sb("ident2_f_sb", [128, 128], F32)
        cB = Buf()
        norms = sb("norms_sb", [128, 4, NCH], F32)
        eps_t = sb("eps_sb", [128, 2], F32)
        T.op("dve", lambda e: e.memset(eps_t[:, 0:1], EPS), writes=[cB])
        T.op("dve", lambda e: e.memset(eps_t[:, 1:2], 1.0), writes=[cB])
        T.dma("sp", ident_f[:], ident_d, writes=[cB])
        T.dma("sp", ones_f[:], ones_d, writes=[cB])
        T.op("dve", lambda e: e.tensor_single_scalar(out=ident2_f[:, :], in_=ident_f[:, :], scalar=2.0, op=ALU.mult), reads=[cB], writes=[cB])
        T.dma("pool", ident_b[:], ident_d, writes=[cB])
        T.dma("pool", ones_b[:], ones_d, writes=[cB])
        for i, nd in enumerate((n1_d, nm_d, n2_d, nf_d)):
            T.dma("sp", norms[:, i, :], nd, writes=[cB])
        xT_dv = xT_d.rearrange("(c p) t -> p c t", p=128)
        for n in range(NB):
            T.dma("sp", xT[:, :, n * 512:(n + 1) * 512], xT_dv[:, :, n * 512:(n + 1) * 512], writes=[xB[n]])

        def rmsnorm_block(n, gi, out_ap_fn, out_buf, sq_t, sq_b, rs_t, rs_b):
            tok = slice(n * 512, (n + 1) * 512)
            b = bank()
            for c in range(NCH):
                T.op("act", lambda e, c=c: e.activation(out=sq_t[:, c, :], in_=xT[:, c, tok], func=AF.Square),
                     reads=[xB[n]], writes=[sq_b[c]])
            def mm(e):
                ins = None
                for c in range(NCH):
                    ins = e.matmul(psum[:, b, :], lhsT=ones_b[:, :], rhs=sq_t[:, c, :], start=(c == 0), stop=(c == NCH - 1))
                return ins
            T.op("pe", mm, reads=sq_b + [cB], writes=[pB[b]])
            T.op("act", lambda e: e.activation(out=rs_t[:, :], in_=psum[:, b, :], func=AF.Ln, scale=1.0 / D, bias=eps_t[:, 0:1]),
                 reads=[pB[b], cB], writes=[rs_b])
            T.op("act", lambda e: e.activation(out=rs_t[:, :], in_=rs_t[:, :], func=AF.Exp, scale=-0.5), reads=[rs_b], writes=[rs_b])
            for c in range(NCH):
                T.op("dve", lambda e, c=c: e.scalar_tensor_tensor(out=out_ap_fn(c), in0=xT[:, c, tok],
                                                                  scalar=norms[:, gi, c:c + 1], in1=rs_t[:, :],
                                                                  op0=ALU.mult, op1=ALU.mult),
                     reads=[xB[n], rs_b, cB], writes=[out_buf])

        def ffn(which, gi):
            wg_d, wu_d, wd_d = ffn_w[which]
            with contextlib.ExitStack() as fs:
                def fsb(name, shape, dt):
                    return fs.enter_context(nc.sbuf_tensor("f%d_%s" % (which, name), list(shape), dt))
                hT = fsb("hT", [128, NCH, 1024], BF16)
                hB = [Buf(), Buf()]
                actT = fsb("actT", [128, NF, 1024], BF16)
                aB = [[Buf(), Buf()] for _ in range(NF)]
                sq_t = fsb("sq", [128, NCH, 512], BF16)
                sq_b = [Buf() for _ in range(NCH)]
                rs_t = fsb("rs", [128, 512], F32)
                rs_b = Buf()
                NWB = 3
                wgb = [fsb("wg%d" % i, [128, NCH, 256], BF16) for i in range(NWB)]
                wub = [fsb("wu%d" % i, [128, NCH, 256], BF16) for i in range(NWB)]
                wgB = [Buf() for _ in range(NWB)]
                wuB = [Buf() for _ in range(NWB)]
                wdb = [fsb("wd%d" % i, [128, NF, 128], BF16) for i in range(2)]
                wdB = [Buf() for _ in range(2)]
                sg = [fsb("sg%d" % i, [128, 512], F32) for i in range(3)]
                sgB = [Buf() for _ in range(3)]
                sgn = 0

                def load_gu(g):
                    i = g % NWB
                    T.dma("pool", wgb[i][:].rearrange("p c f -> p (c f)"), wg_d[g], writes=[wgB[i]])
                    T.dma("pool", wub[i][:].rearrange("p c f -> p (c f)"), wu_d[g], writes=[wuB[i]])

                def load_d(dt):
                    i = dt % 2
                    T.dma("pool", wdb[i][:].rearrange("p f j -> p (f j)"), wd_d[dt], writes=[wdB[i]])

                for half in range(2):
                    load_gu(0)
                    load_gu(1)
                    for nn in range(2):
                        n = half * 2 + nn
                        rmsnorm_block(n, gi, lambda c, nn=nn: hT[:, c, nn * 512:(nn + 1) * 512], hB[nn],
                                      sq_t, sq_b, rs_t, rs_b)
                    for g in range(11):
                        if g + 2 < 11:
                            load_gu(g + 2)
                        if g == 9:
                            load_d(0)
                        if g == 10:
                            load_d(1)
                        wi = g % NWB
                        for j in range(2):
                            f = 2 * g + j
                            for nn in range(2):
                                tk = slice(nn * 512, (nn + 1) * 512)
                                bg = bank()
                                bu = bank()

                                def mmg(e, w=wgb[wi], b=bg, j=j, tk=tk):
                                    ins = None
                                    for c in range(NCH):
                                        ins = e.matmul(psum[:, b, :], lhsT=w[:, c, j * 128:(j + 1) * 128],
                                                       rhs=hT[:, c, tk], start=(c == 0), stop=(c == NCH - 1))
                                    return ins
                                T.op("pe", mmg, reads=[wgB[wi], hB[nn]], writes=[pB[bg]])
                                T.op("pe", lambda e, b=bu, j=j, tk=tk: mmg(e, wub[wi], b, j, tk),
                                     reads=[wuB[wi], hB[nn]], writes=[pB[bu]])
                                si = sgn % 3
                                sgn += 1
                                T.op("act", lambda e, si=si, b=bg: e.activation(out=sg[si][:, :], in_=psum[:, b, :],
                                                                               func=AF.Silu),
                                     reads=[pB[bg]], writes=[sgB[si]])
                                T.op("dve", lambda e, si=si, b=bu, f=f, tk=tk: e.tensor_tensor(
                                    out=actT[:, f, tk], in0=psum[:, b, :], in1=sg[si][:, :], op=ALU.mult),
                                     reads=[pB[bu], sgB[si]], writes=[aB[f][nn]])
                    for dt in range(8):
                        if dt + 2 < 8:
                            pass
                        wi = dt % 2
                        for nn in range(2):
                            n = half * 2 + nn
                            tk = slice(nn * 512, (nn + 1) * 512)
                            b = bank()

                            def mmd(e, b=b, wi=wi, tk=tk):
                                ins = None
                                for f in range(NF):
                                    ins = e.matmul(psum[:, b, :], lhsT=wdb[wi][:, f, :], rhs=actT[:, f, tk],
                                                   start=(f == 0), stop=(f == NF - 1))
                                return ins
                            T.op("pe", mmd, reads=[wdB[wi]] + [aB[f][nn] for f in range(NF)], writes=[pB[b]])
                            T.op("dve", lambda e, b=b, dt=dt, n=n: e.scalar_tensor_tensor(
                                out=xT[:, dt, n * 512:(n + 1) * 512], in0=psum[:, b, :], scalar=0.5,
                                in1=xT[:, dt, n * 512:(n + 1) * 512], op0=ALU.mult, op1=ALU.add),
                                 reads=[pB[b]], writes=[xB[n]])
                        if dt + 2 < 8:
                            load_d(dt + 2)
                T.barrier()

        def mixer():
            with contextlib.ExitStack() as ms:
                def msb(name, shape, dt):
                    return ms.enter_context(nc.sbuf_tensor("m_" + name, list(shape), dt))
                hT = msb("hT", [128, NCH, S], BF16)
                hB = [Buf() for _ in range(NB)]
                with contextlib.ExitStack() as ns:
                    sq_t = ns.enter_context(nc.sbuf_tensor("m_sq", [128, NCH, 512], BF16))
                    sq_b = [Buf() for _ in range(NCH)]
                    rs_t = ns.enter_context(nc.sbuf_tensor("m_rs", [128, 512], F32))
                    rs_b = Buf()
                    for n in range(NB):
                        rmsnorm_block(n, 1, lambda c, n=n: hT[:, c, n * 512:(n + 1) * 512], hB[n], sq_t, sq_b, rs_t, rs_b)
                    T.barrier()
                if stop_after == "dbg_h":
                    dbg_dump([(hT[:, c, :], hB, c * 128, 128) for c in range(NCH)])
                    return True
                ybT = msb("ybT", [128, 8, S], BF16)
                ybB = [Buf() for _ in range(8)]
                gdn(hT, hB, ybT, ybB)
                T.barrier()
                if stop_after == "dbg_yb":
                    dbg_dump([(ybT[:, c, :], ybB, c * 128, 128) for c in range(8)])
                    return True
                yaT = msb("yaT", [64, 4, S], BF16)
                yaB = [Buf() for _ in range(4)]
                if attention(hT, hB, yaT, yaB):
                    return True
                T.barrier()
                if stop_after == "dbg_ya":
                    dbg_dump([(yaT[:, c, :], yaB, c * 64, 64) for c in range(4)])
                    return True
                merge(hT, hB, yaT, yaB, ybT, ybB)
                T.barrier()
            return None

        def nset(lo, hi):
            return list(range(lo // 512, hi // 512 + 1))

        def attention(hT, hB, yaT, yaB):
            wqkv_d = dt_in["wqkv_a"]
            with contextlib.ExitStack() as as_:
                def asb(name, shape, dt):
                    return as_.enter_context(nc.sbuf_tensor("a_" + name, list(shape), dt))
                cos_t = asb("cos", [128, S], BF16)
                sin_t = asb("sin", [128, S], BF16)
                rm_t = asb("rm", [128, 128], BF16)
                am_t = asb("am", [128, 512], BF16)
                em_t = asb("em", [128, 64], F32)
                acB = Buf()
                T.dma("pool", cos_t[:], dt_in["rope_cos"], writes=[acB])
                T.dma("pool", sin_t[:], dt_in["rope_sin"], writes=[acB])
                T.dma("pool", rm_t[:], dt_in["rm"], writes=[acB])
                T.dma("pool", am_t[:], dt_in["amask"], writes=[acB])
                T.dma("sp", em_t[:], dt_in["em"], writes=[acB])
                wq = [asb("w%d" % i, [128, NCH, 128], BF16) for i in range(3)]
                wB = [Buf() for _ in range(3)]
                qk = [asb("qk%d" % i, [128, S], BF16) for i in range(2)]
                qkB = [[Buf() for _ in range(NB)] for _ in range(2)]
                Vt = asb("Vt", [128, 16, 2, 65], BF16)
                VtB = [Buf() for _ in range(16)]
                VoneB = Buf()
                T.op(ROPE_ADD_ENG, lambda e: e.memset(Vt[:, :, :, 64:65], 1.0), writes=[VoneB])
                acc = asb("acc", [128, 2, S], F32)
                accB = Buf()
                qb = asb("qb", [128, 512], BF16)
                qbB = Buf()
                t1 = asb("t1", [128, 512], F32)
                t2 = asb("t2", [128, 512], F32)
                t1B, t2B = Buf(), Buf()
                ex = [asb("ex%d" % i, [128, 512], BF16) for i in range(3)]
                exB = [Buf(), Buf(), Buf()]
                exm = [asb("exm%d" % i, [128, 512], BF16) for i in range(3)]
                exmB = [Buf(), Buf(), Buf()]
                rl = asb("rl", [64, 512], F32)
                rlB = Buf()
                exn = 0
                if stop_after == "dbg_att0":
                    dbg_dump([(cos_t[:, :], [acB], 0, 128), (sin_t[:, :], [acB, VoneB], 128, 128)])
                    return True
                for hp in range(2):
                    for g in range(3):
                        dil = (1, 4, 16)[g]
                        nbk = 16 // dil
                        pi = g * 2 + hp
                        for i in range(3):
                            T.dma("pool", wq[i][:].rearrange("p c f -> p (c f)"), wqkv_d[pi, i], writes=[wB[i]])
                        for i in range(2):
                            for n in range(NB):
                                tk = slice(n * 512, (n + 1) * 512)
                                b = bank()

                                def mm(e):
                                    ins = None
                                    for c in range(NCH):
                                        ins = e.matmul(psum[:, b, :], lhsT=wq[i][:, c, :], rhs=hT[:, c, tk],
                                                       start=(c == 0), stop=(c == NCH - 1))
                                    return ins
                                CUT = int(os.environ.get("CUT", "9"))
                                if CUT >= 1:
                                    T.op("pe", mm, reads=[wB[i], hB[n]], writes=[pB[b]])
                                if CUT >= 2:
                                    T.op("act", lambda e: e.activation(out=qb[:, :], in_=psum[:, b, :], func=AF.Copy),
                                         reads=[pB[b]], writes=[qbB])
                                b2 = bank()
                                if CUT >= 3:
                                    T.op("pe", lambda e: e.matmul(psum[:, b2, :], lhsT=rm_t[:, :], rhs=qb[:, :], start=True, stop=True),
                                         reads=[qbB, acB], writes=[pB[b2]])
                                if CUT >= 4:
                                    T.op("dve", lambda e: e.tensor_tensor(out=t1[:, :], in0=psum[:, b, :], in1=cos_t[:, tk], op=ALU.mult),
                                         reads=[pB[b], acB], writes=[t1B])
                                if CUT >= 5:
                                    T.op("dve", lambda e: e.tensor_tensor(out=t2[:, :], in0=psum[:, b2, :], in1=sin_t[:, tk], op=ALU.mult),
                                         reads=[pB[b2], acB], writes=[t2B])
                                if CUT >= 6:
                                    T.op(ROPE_ADD_ENG, lambda e: e.tensor_tensor(out=qk[i][:, tk], in0=t1[:, :], in1=t2[:, :], op=ALU.add),
                                         reads=[t1B, t2B], writes=[qkB[i][n]])
                                if stop_after == "dbg_att1a":
                                    T.barrier()
                                    with contextlib.ExitStack() as ds:
                                        dtl = ds.enter_context(nc.sbuf_tensor("dbgx", [128, 512], F32))
                                        dxB = Buf()
                                        T.op("act", lambda e: e.activation(out=dtl[:, :], in_=qk[0][:, 0:512], func=AF.Copy), reads=[qkB[0][0]], writes=[dxB])
                                        T.dma("sp", out_d[0:128, 0:512], t1[:, :], reads=[t1B])
                                        T.dma("sp", out_d[128:256, 0:512], t2[:, :], reads=[t2B])
                                        T.dma("sp", out_d[256:384, 0:512], dtl[:, :], reads=[dxB])
                                        T.barrier()
                                    return True
                        if stop_after == "dbg_att1":
                            dbg_dump([(qk[0][:, :], qkB[0], 0, 128), (qk[1][:, :], qkB[1], 128, 128)])
                            return True
                        def tokslice(bi):
                            r, nk = bi // nbk, bi % nbk
                            st = r + dil * 128 * nk
                            return st, slice(st, st + 127 * dil + 1, dil)
                        for b4 in range(4):
                            b = bank()

                            def mmv(e):
                                ins = None
                                for k in range(4):
                                    st, sl = tokslice(b4 * 4 + k)
                                    for c in range(NCH):
                                        ins = e.matmul(psum[:, b, k * 128:(k + 1) * 128], lhsT=hT[:, c, sl], rhs=wq[2][:, c, :],
                                                       start=(c == 0), stop=(c == NCH - 1))
                                return ins
                            T.op("pe", mmv, reads=[wB[2]] + hB, writes=[pB[b]])
                            T.op("act", lambda e: e.activation(
                                out=Vt[:, b4 * 4:(b4 + 1) * 4, :, 0:64],
                                in_=psum[:, b, :].rearrange("p (k e d) -> p k e d", k=4, e=2), func=AF.Copy),
                                 reads=[pB[b]], writes=[VtB[b4 * 4 + k] for k in range(4)])
                        if stop_after == "dbg_att2":
                            dbg_dump([(Vt[:, :, :, :].rearrange("p a b c -> p (a b c)")[:, 0:2048], VtB + [VoneB], 0, 128)])
                            return True
                        def stage1(bi):
                            nonlocal exn
                            r, nk = bi // nbk, bi % nbk
                            st, qsl = tokslice(bi)
                            hi_tok = st + 127 * dil
                            qn_set = nset(st, hi_tok)
                            b = bank2()
                            if nk > 0:
                                stp, ksl_p = tokslice(bi - 1)
                                kn_set = nset(stp, hi_tok)
                            else:
                                kn_set = qn_set

                            def mms(e):
                                ins = None
                                for hd in range(2):
                                    hs = slice(hd * 64, (hd + 1) * 64)
                                    if nk > 0:
                                        ins = e.matmul(psum[:, b + hd, 0:128], lhsT=qk[1][hs, ksl_p], rhs=qk[0][hs, qsl],
                                                       start=True, stop=True)
                                    ins = e.matmul(psum[:, b + hd, 128:256], lhsT=qk[1][hs, qsl], rhs=qk[0][hs, qsl],
                                                   start=True, stop=True)
                                return ins
                            T.op("pe", mms, reads=[qkB[0][n] for n in qn_set] + [qkB[1][n] for n in kn_set], writes=[pB[b], pB[b + 1]])
                            xi = exn % 3
                            exn += 1
                            lo = 0 if nk > 0 else 128
                            src = psum[:, b:b + 2, lo:256]
                            dst = ex[xi][:, :].rearrange("p (e w) -> p e w", e=2)[:, :, lo:256]
                            dstm = exm[xi][:, :].rearrange("p (e w) -> p e w", e=2)[:, :, lo:256]
                            msk = am_t[:, :].rearrange("p (e w) -> p e w", e=2)[:, :, lo:256]
                            T.op("act", lambda e: e.activation(out=dst, in_=src, func=AF.Exp, scale=0.125),
                                 reads=[pB[b], pB[b + 1]], writes=[exB[xi]])
                            T.op("pool", lambda e: e.tensor_tensor(out=dstm, in0=dst, in1=msk, op=ALU.mult),
                                 reads=[exB[xi], acB], writes=[exmB[xi]])
                            return xi

                        def stage2(bi, xi):
                            r, nk = bi // nbk, bi % nbk
                            st, qsl = tokslice(bi)
                            b2 = bank()

                            def mmu(e):
                                ins = None
                                for hd in range(2):
                                    o = psum[0:65, b2, hd * 128:(hd + 1) * 128]
                                    if nk > 0:
                                        e.matmul(o, lhsT=Vt[:, bi - 1, hd, :], rhs=exm[xi][:, hd * 256:hd * 256 + 128], start=True, stop=False)
                                    ins = e.matmul(o, lhsT=Vt[:, bi, hd, :], rhs=exm[xi][:, hd * 256 + 128:hd * 256 + 256],
                                                   start=(nk == 0), stop=True)
                                return ins
                            T.op("pe", mmu, reads=[exmB[xi], VtB[bi], VoneB] + ([VtB[bi - 1]] if nk > 0 else []), writes=[pB[b2]])
                            src2 = psum[0:65, b2, 0:256].rearrange("p (e q) -> p e q", e=2)
                            if g == 0:
                                T.op("dve", lambda e: e.tensor_copy(out=acc[0:65, :, qsl], in_=src2), reads=[pB[b2]], writes=[accB])
                            else:
                                T.op("dve", lambda e: e.tensor_tensor(out=acc[0:65, :, qsl], in0=src2, in1=acc[0:65, :, qsl], op=ALU.add),
                                     reads=[pB[b2]], writes=[accB])
                        pend = []
                        for bi in range(16):
                            pend.append((bi, stage1(bi)))
                            if len(pend) > 1:
                                stage2(*pend.pop(0))
                        while pend:
                            stage2(*pend.pop(0))
                        if stop_after == "dbg_att3":
                            dbg_dump([(acc[:, 0, :], [accB], 0, 128), (acc[:, 1, :], [accB], 128, 128)])
                            return True
                    for hd in range(2):
                        for n in range(NB):
                            tk = slice(n * 512, (n + 1) * 512)
                            b = bank()
                            T.op("pe", lambda e: e.matmul(psum[0:64, b, :], lhsT=em_t[0:65, :], rhs=acc[0:65, hd, tk], start=True, stop=True),
                                 reads=[accB, acB], writes=[pB[b]])
                            T.op("dve", lambda e: e.reciprocal(out=rl[:, :], in_=psum[0:64, b, :]), reads=[pB[b]], writes=[rlB])
                            T.op("dve", lambda e: e.tensor_tensor(out=yaT[:, 2 * hp + hd, tk], in0=acc[0:64, hd, tk], in1=rl[:, :], op=ALU.mult),
                                 reads=[rlB, accB], writes=[yaB[2 * hp + hd]])
                T.barrier()

        def gdn(hT, hB, ybT, ybB):
            with contextlib.ExitStack() as gs:
                def gsb(name, shape, dt):
                    return gs.enter_context(nc.sbuf_tensor("g_" + name, list(shape), dt))
                gcB = Buf()
                triu = gsb("triu", [128, 128], F32)
                smask = gsb("smask", [128, 128], F32)
                odiv = gsb("odiv", [128, 128], BF16)
                convw = gsb("convw", [128, 8, 3, 4], F32)
                alog = gsb("alog", [128, 8], F32)
                dtb = gsb("dtb", [128, 8], F32)
                onorm = gsb("onorm", [128, 1], F32)
                wbd = gsb("wbd", [128, NCH, 16], BF16)
                T.dma("sp", triu[:], dt_in["triu"], writes=[gcB])
                T.dma("sp", smask[:], dt_in["smask"], writes=[gcB])
                T.dma("pool", odiv[:], dt_in["odiv"], writes=[gcB])
                T.dma("sp", convw[:].rearrange("p h w t -> p (h w t)"), dt_in["convw"], writes=[gcB])
                T.dma("sp", alog[:], dt_in["alog"], writes=[gcB])
                T.dma("sp", dtb[:], dt_in["dtb"], writes=[gcB])
                T.dma("sp", onorm[:], dt_in["onorm"], writes=[gcB])
                T.dma("pool", wbd[:].rearrange("p c f -> p (c f)"), dt_in["wbd"], writes=[gcB])
                braw = gsb("braw", [128, 16, 16], F32)
                beta = gsb("beta", [128, 16, 8], F32)
                negb = gsb("negb", [128, 16, 8], F32)
                gt = gsb("gt", [128, 16, 8], F32)
                gcum = gsb("gcum", [128, 16, 8], F32)
                glast = gsb("glast", [128, 16, 8], F32)
                kds = gsb("kds", [128, 16, 8], F32)
                sdec = gsb("sdec", [128, 16, 8], F32)
                egc = gsb("egc", [128, 16, 8], F32)
                tmpg = gsb("tmpg", [128, 16, 8], F32)
                negA = gsb("negA", [128, 8], F32)
                gB = Buf()
                b = bank()

                def mmb(e):
                    ins = None
                    for t in range(16):
                        for c in range(NCH):
                            ins = e.matmul(psum[:, b, t * 16:(t + 1) * 16], lhsT=hT[:, c, t * 128:(t + 1) * 128], rhs=wbd[:, c, :],
                                           start=(c == 0), stop=(c == NCH - 1))
                    return ins
                T.op("pe", mmb, reads=hB + [gcB], writes=[pB[b]])
                T.op("act", lambda e: e.activation(out=braw[:].rearrange("p t k -> p (t k)"), in_=psum[:, b, 0:256], func=AF.Copy),
                     reads=[pB[b]], writes=[gB])
                T.op("act", lambda e: e.activation(out=tmpg[:], in_=braw[:, :, 0:8], func=AF.Exp, scale=-1.0), reads=[gB], writes=[gB])
                T.op("dve", lambda e: e.tensor_single_scalar(out=tmpg[:], in_=tmpg[:], scalar=1.0, op=ALU.add), reads=[gB], writes=[gB])
                T.op("dve", lambda e: e.reciprocal(out=beta[:], in_=tmpg[:]), reads=[gB], writes=[gB])
                T.op("dve", lambda e: e.tensor_single_scalar(out=negb[:], in_=beta[:], scalar=-1.0, op=ALU.mult), reads=[gB], writes=[gB])
                T.op("dve", lambda e: e.tensor_tensor(out=tmpg[:], in0=braw[:, :, 8:16], in1=dtb[:].unsqueeze(1).to_broadcast([128, 16, 8]), op=ALU.add),
                     reads=[gB, gcB], writes=[gB])
                T.op("act", lambda e: e.activation(out=tmpg[:], in_=tmpg[:], func=AF.Exp), reads=[gB], writes=[gB])
                T.op("act", lambda e: e.activation(out=tmpg[:], in_=tmpg[:], func=AF.Ln, bias=eps_t[:, 1:2]), reads=[gB, cB], writes=[gB])
                T.op("act", lambda e: e.activation(out=negA[:], in_=alog[:], func=AF.Exp), reads=[gcB], writes=[gB])
                T.op("dve", lambda e: e.tensor_single_scalar(out=negA[:], in_=negA[:], scalar=-1.0, op=ALU.mult), reads=[gB], writes=[gB])
                T.op("dve", lambda e: e.tensor_tensor(out=gt[:], in0=tmpg[:], in1=negA[:].unsqueeze(1).to_broadcast([128, 16, 8]), op=ALU.mult),
                     reads=[gB], writes=[gB])
                b = bank()
                T.op("pe", lambda e: e.matmul(psum[:, b, 0:128], lhsT=triu[:, :], rhs=gt[:].rearrange("p t k -> p (t k)"), start=True, stop=True),
                     reads=[gB, gcB], writes=[pB[b]])
                T.op("act", lambda e: e.activation(out=gcum[:].rearrange("p t k -> p (t k)"), in_=psum[:, b, 0:128], func=AF.Copy),
                     reads=[pB[b]], writes=[gB])
                b = bank()
                T.op("pe", lambda e: e.matmul(psum[:, b, 0:128], lhsT=ones_f[:, :], rhs=gt[:].rearrange("p t k -> p (t k)"), start=True, stop=True),
                     reads=[gB, cB], writes=[pB[b]])
                T.op("act", lambda e: e.activation(out=glast[:].rearrange("p t k -> p (t k)"), in_=psum[:, b, 0:128], func=AF.Copy),
                     reads=[pB[b]], writes=[gB])
                T.op("dve", lambda e: e.tensor_tensor(out=tmpg[:], in0=glast[:], in1=gcum[:], op=ALU.subtract), reads=[gB], writes=[gB])
                T.op("act", lambda e: e.activation(out=kds[:], in_=tmpg[:], func=AF.Exp), reads=[gB], writes=[gB])
                T.op("act", lambda e: e.activation(out=sdec[:], in_=glast[:], func=AF.Exp), reads=[gB], writes=[gB])
                T.op("act", lambda e: e.activation(out=egc[:], in_=gcum[:], func=AF.Exp), reads=[gB], writes=[gB])

                wh = [gsb("wh%d" % i, [128, NCH, 128], BF16) for i in range(4)]
                whB = [Buf() for _ in range(4)]
                Dg = gsb("Dg", [128, 3, 4, 128], BF16)
                DgB = Buf()
                xb = gsb("xb", [128, 4 + S], BF16)
                xbB = [Buf() for _ in range(NB)]
                xb0B = Buf()
                T.op("pool", lambda e: e.memset(xb[:, 0:4], 0.0), writes=[xb0B])
                qkv = [gsb("qkv%d" % i, [128, S], BF16) for i in range(3)]
                qkvB = [[Buf() for _ in range(NB)] for _ in range(3)]
                gsil = gsb("gsil", [128, S], BF16)
                gsilB = [Buf() for _ in range(NB)]
                sqb = gsb("sqb", [128, 512], BF16)
                sqbB = Buf()
                rt = gsb("rt", [128, 512], F32)
                rtB = Buf()
                Fm = [gsb("F%d" % i, [128, 4, 128], F32) for i in range(6)]
                FB = [Buf() for _ in range(6)]
                QT = gsb("QT", [128, 4, 128], BF16)
                QTB = Buf()
                PQ = [gsb("PQ%d" % i, [128, 2, 4, 128], BF16) for i in range(2)]
                PQB = [Buf(), Buf()]
                Wt = gsb("Wt", [128, 4, 128], BF16)
                WtB = Buf()
                ATt = gsb("ATt", [128, 4, 128], BF16)
                ATtB = Buf()
                Aqk = gsb("Aqk", [128, 4, 128], BF16)
                AqkB = Buf()
                kdec = gsb("kdec", [128, 4, 128], BF16)
                kdecB = Buf()
                VK = gsb("VK", [128, 4, 256], BF16)
                VKB = Buf()
                Zt = gsb("Zt", [128, 4, 256], BF16)
                ZtB = Buf()
                Qeff = gsb("Qeff", [128, 4, 128], BF16)
                QeffB = Buf()
                osq = gsb("osq", [128, 512], BF16)
                osqB = Buf()
                Sf = gsb("Sf", [128, 128], F32)
                Sb = gsb("Sb", [128, 128], BF16)
                SfB, SbB = Buf(), Buf()
                QSCALE = 128.0 ** -0.5

                def pv(bk):
                    return psum[:, bk, :].rearrange("p (c i) -> p c i", c=4)

                def fl(tl):
                    return tl[:].rearrange("p c i -> p (c i)")

                def matb(m):
                    return m[:, :].unsqueeze(1).to_broadcast([128, 4, 128])

                for h in range(8):
                    def colb(tl, t0, w=128):
                        return tl[:, t0:t0 + 4, h:h + 1].to_broadcast([128, 4, w])
                    for i in range(3):
                        T.dma("pool", wh[i][:].rearrange("p c f -> p (c f)"), dt_in["wgdn"][h, i], writes=[whB[i]])
                    T.dma("pool", wh[3][:].rearrange("p c f -> p (c f)"), dt_in["wgg"][h], writes=[whB[3]])
                    for w3 in range(3):
                        for tp in range(4):
                            T.op("dve", lambda e: e.tensor_scalar_mul(out=Dg[:, w3, tp, :], in0=ident_f[:, :], scalar1=convw[:, h, w3, tp:tp + 1]),
                                 reads=[gcB, cB], writes=[DgB])
                    for w3 in range(3):
                        for n in range(NB):
                            tk = slice(n * 512, (n + 1) * 512)
                            b = bank()

                            def mm(e):
                                ins = None
                                for c in range(NCH):
                                    ins = e.matmul(psum[:, b, :], lhsT=wh[w3][:, c, :], rhs=hT[:, c, tk], start=(c == 0), stop=(c == NCH - 1))
                                return ins
                            T.op("pe", mm, reads=[whB[w3], hB[n]], writes=[pB[b]])
                            T.op("act", lambda e: e.activation(out=xb[:, 4 + n * 512:4 + (n + 1) * 512], in_=psum[:, b, :], func=AF.Copy),
                                 reads=[pB[b]], writes=[xbB[n]])
                        for n in range(NB):
                            tk = slice(n * 512, (n + 1) * 512)
                            b = bank()

                            def mmc(e):
                                ins = None
                                for tp in range(4):
                                    ins = e.matmul(psum[:, b, :], lhsT=Dg[:, w3, tp, :], rhs=xb[:, n * 512 + 1 + tp:n * 512 + 1 + tp + 512],
                                                   start=(tp == 0), stop=(tp == 3))
                                return ins
                            T.op("pe", mmc, reads=[DgB, xbB[n], xb0B] + ([xbB[n - 1]] if n > 0 else []), writes=[pB[b]])
                            T.op("act", lambda e: e.activation(out=qkv[w3][:, tk], in_=psum[:, b, :], func=AF.Silu),
                                 reads=[pB[b]], writes=[qkvB[w3][n]])
                        if w3 < 2:
                            for n in range(NB):
                                tk = slice(n * 512, (n + 1) * 512)
                                T.op("act", lambda e: e.activation(out=sqb[:, :], in_=qkv[w3][:, tk], func=AF.Square), reads=[qkvB[w3][n]], writes=[sqbB])
                                b2 = bank()
                                T.op("pe", lambda e: e.matmul(psum[:, b2, :], lhsT=ones_b[:, :], rhs=sqb[:, :], start=True, stop=True),
                                     reads=[sqbB, cB], writes=[pB[b2]])
                                T.op("act", lambda e: e.activation(out=rt[:, :], in_=psum[:, b2, :], func=AF.Ln, bias=eps_t[:, 0:1]),
                                     reads=[pB[b2], cB], writes=[rtB])
                                T.op("act", lambda e: e.activation(out=rt[:, :], in_=rt[:, :], func=AF.Exp, scale=-0.5), reads=[rtB], writes=[rtB])
                                T.op("dve", lambda e: e.scalar_tensor_tensor(out=qkv[w3][:, tk], in0=qkv[w3][:, tk], scalar=(QSCALE if w3 == 0 else 1.0),
                                                                             in1=rt[:, :], op0=ALU.mult, op1=ALU.mult),
                                     reads=[rtB, qkvB[w3][n]], writes=[qkvB[w3][n]])
                    for n in range(NB):
                        tk = slice(n * 512, (n + 1) * 512)
                        b = bank()

                        def mm(e):
                            ins = None
                            for c in range(NCH):
                                ins = e.matmul(psum[:, b, :], lhsT=wh[3][:, c, :], rhs=hT[:, c, tk], start=(c == 0), stop=(c == NCH - 1))
                            return ins
                        T.op("pe", mm, reads=[whB[3], hB[n]], writes=[pB[b]])
                        T.op("act", lambda e: e.activation(out=rt[:, :], in_=psum[:, b, :], func=AF.Silu), reads=[pB[b]], writes=[rtB])
                        T.op("dve", lambda e: e.tensor_scalar_mul(out=gsil[:, tk], in0=rt[:, :], scalar1=onorm[:, 0:1]),
                             reads=[rtB, gcB], writes=[gsilB[n]])

                    for gi in range(4):
                        t0 = gi * 4
                        gk = slice(gi * 512, (gi + 1) * 512)
                        cks = [slice((t0 + c) * 128, (t0 + c + 1) * 128) for c in range(4)]
                        qB, kB_, vB_ = qkvB[0][gi], qkvB[1][gi], qkvB[2][gi]
                        DT, DTB = Fm[0], FB[0]
                        decF, decFB = Fm[1], FB[1]
                        EG, EGB = Fm[2], FB[2]
                        tmpA, tmpAB = Fm[3], FB[3]
                        bG = bank()

                        def mmg(e):
                            ins = None
                            for c in range(4):
                                ins = e.matmul(psum[:, bG, c * 128:(c + 1) * 128], lhsT=gt[:, t0 + c, h:h + 1].to_broadcast([128, 128]), rhs=triu[:, :],
                                               start=True, stop=True)
                            return ins
                        T.op("pe", mmg, reads=[gB, gcB], writes=[pB[bG]])
                        T.op("dve", lambda e: e.tensor_tensor(out=DT[:], in0=pv(bG), in1=colb(gcum, t0), op=ALU.subtract),
                             reads=[pB[bG], gB], writes=[DTB])
                        T.op("act", lambda e: e.activation(out=fl(EG), in_=psum[:, bG, :], func=AF.Exp), reads=[pB[bG]], writes=[EGB])
                        T.op("dve", lambda e: e.tensor_tensor(out=DT[:], in0=DT[:], in1=matb(smask), op=ALU.add), reads=[DTB, gcB], writes=[DTB])
                        T.op("act", lambda e: e.activation(out=fl(DT), in_=fl(DT), func=AF.Exp), reads=[DTB], writes=[DTB])
                        T.op("pool", lambda e: e.tensor_tensor(out=decF[:], in0=DT[:], in1=matb(ident_f), op=ALU.add), reads=[DTB, cB], writes=[decFB])
                        bK, bQ = bank(), bank()

                        def mmk(e):
                            ins = None
                            for c in range(4):
                                ins = e.matmul(psum[:, bK, c * 128:(c + 1) * 128], lhsT=qkv[1][:, cks[c]], rhs=qkv[1][:, cks[c]], start=True, stop=True)
                            return ins
                        T.op("pe", mmk, reads=[kB_], writes=[pB[bK]])

                        def mmq(e):
                            ins = None
                            for c in range(4):
                                ins = e.matmul(psum[:, bQ, c * 128:(c + 1) * 128], lhsT=qkv[1][:, cks[c]], rhs=qkv[0][:, cks[c]], start=True, stop=True)
                            return ins
                        T.op("pe", mmq, reads=[kB_, qB], writes=[pB[bQ]])
                        T.op("dve", lambda e: e.tensor_tensor(out=tmpA[:], in0=pv(bK), in1=colb(negb, t0), op=ALU.mult), reads=[pB[bK], gB], writes=[tmpAB])
                        T.op("pool", lambda e: e.tensor_tensor(out=QT[:], in0=tmpA[:], in1=DT[:], op=ALU.mult), reads=[tmpAB, DTB], writes=[QTB])
                        T.op("dve", lambda e: e.tensor_tensor(out=Aqk[:], in0=pv(bQ), in1=decF[:], op=ALU.mult), reads=[pB[bQ], decFB], writes=[AqkB])
                        bT, bKt, bVt = bank(), bank(), bank()

                        def mmt(e):
                            ins = None
                            for c in range(4):
                                ins = e.matmul(psum[:, bT, c * 128:(c + 1) * 128], lhsT=QT[:, c, :], rhs=ident_b[:, :], start=True, stop=True)
                            return ins
                        T.op("pe", mmt, reads=[QTB, cB], writes=[pB[bT]])

                        def mmkt(e):
                            ins = None
                            for c in range(4):
                                ins = e.matmul(psum[:, bKt, c * 128:(c + 1) * 128], lhsT=qkv[1][:, cks[c]], rhs=ident_b[:, :], start=True, stop=True)
                            return ins
                        T.op("pe", mmkt, reads=[kB_, cB], writes=[pB[bKt]])

                        def mmvt(e):
                            ins = None
                            for c in range(4):
                                ins = e.matmul(psum[:, bVt, c * 128:(c + 1) * 128], lhsT=qkv[2][:, cks[c]], rhs=ident_b[:, :], start=True, stop=True)
                            return ins
                        T.op("pe", mmvt, reads=[vB_, cB], writes=[pB[bVt]])
                        cur = 0
                        T.op("dve", lambda e: e.tensor_tensor(out=PQ[cur][:, 0], in0=pv(bT), in1=matb(ident_f), op=ALU.add), reads=[pB[bT], cB], writes=[PQB[cur]])
                        T.op("pool", lambda e: e.tensor_tensor(out=PQ[cur][:, 1], in0=QT[:], in1=matb(ident_b), op=ALU.add), reads=[QTB, cB], writes=[PQB[cur]])
                        T.op("pool", lambda e: e.tensor_tensor(out=ATt[:], in0=matb(ident_b), in1=QT[:], op=ALU.subtract), reads=[QTB, cB], writes=[ATtB])
                        T.op("dve", lambda e: e.tensor_tensor(out=kdec[:], in0=pv(bKt), in1=colb(kds, t0), op=ALU.mult), reads=[pB[bKt], gB], writes=[kdecB])
                        T.op("dve", lambda e: e.tensor_tensor(out=VK[:, :, 128:256], in0=pv(bKt), in1=colb(egc, t0), op=ALU.mult), reads=[pB[bKt], gB], writes=[VKB])
                        T.op("act", lambda e: e.activation(out=VK[:, :, 0:128], in_=pv(bVt), func=AF.Copy), reads=[pB[bVt]], writes=[VKB])
                        for k in range(2, 8):
                            nxt = 1 - cur
                            last = (k == 7)
                            bV = bank()

                            def mmv(e):
                                ins = None
                                for c in range(4):
                                    ins = e.matmul(psum[:, bV, c * 128:(c + 1) * 128], lhsT=ATt[:, c, :], rhs=PQ[cur][:, 0, c, :], start=True, stop=True)
                                return ins
                            T.op("pe", mmv, reads=[ATtB, PQB[cur]], writes=[pB[bV]])
                            T.op("dve", lambda e: e.scalar_tensor_tensor(out=Wt[:], in0=pv(bV), scalar=-1.0, in1=matb(ident2_f), op0=ALU.mult, op1=ALU.add),
                                 reads=[pB[bV], cB], writes=[WtB])
                            bX = bank2()

                            def mmn(e):
                                ins = None
                                for c in range(4):
                                    ins = e.matmul(psum[:, bX + 1, c * 128:(c + 1) * 128], lhsT=Wt[:, c, :], rhs=PQ[cur][:, 1, c, :], start=True, stop=True)
                                if not last:
                                    for c in range(4):
                                        ins = e.matmul(psum[:, bX, c * 128:(c + 1) * 128], lhsT=PQ[cur][:, 1, c, :], rhs=Wt[:, c, :], start=True, stop=True)
                                return ins
                            T.op("pe", mmn, reads=[WtB, PQB[cur]], writes=[pB[bX], pB[bX + 1]])
                            if last:
                                T.op("act", lambda e: e.activation(out=fl(QT), in_=psum[:, bX + 1, :], func=AF.Copy), reads=[pB[bX + 1]], writes=[QTB])
                            else:
                                T.op("act", lambda e: e.activation(out=PQ[nxt][:].rearrange("p a c i -> p a (c i)"), in_=psum[:, bX:bX + 2, :], func=AF.Copy),
                                     reads=[pB[bX], pB[bX + 1]], writes=[PQB[nxt]])
                            cur = nxt
                        bZ = bank2()

                        def mmz(e):
                            ins = None
                            for c in range(4):
                                ins = e.matmul(psum[:, bZ + c // 2, (c % 2) * 256:(c % 2) * 256 + 256], lhsT=QT[:, c, :], rhs=VK[:, c, :], start=True, stop=True)
                            return ins
                        T.op("pe", mmz, reads=[QTB, VKB], writes=[pB[bZ], pB[bZ + 1]])
                        T.op("dve", lambda e: e.tensor_tensor(out=Zt[:], in0=psum[:, bZ:bZ + 2, :].rearrange("p a (c w) -> p (a c) w", c=2),
                                                              in1=colb(beta, t0, 256), op=ALU.mult),
                             reads=[pB[bZ], pB[bZ + 1], gB], writes=[ZtB])
                        bH, bGt, bO, bZq = bank(), bank(), bank(), bank()

                        def mm9(e, bk, lf, rf):
                            ins = None
                            for c in range(4):
                                ins = e.matmul(psum[:, bk, c * 128:(c + 1) * 128], lhsT=lf(c), rhs=rf(c), start=True, stop=True)
                            return ins
                        T.op("pe", lambda e: mm9(e, bH, lambda c: kdec[:, c, :], lambda c: Zt[:, c, 0:128]), reads=[kdecB, ZtB], writes=[pB[bH]])
                        T.op("pe", lambda e: mm9(e, bGt, lambda c: Zt[:, c, 128:256], lambda c: kdec[:, c, :]), reads=[kdecB, ZtB], writes=[pB[bGt]])
                        T.op("pe", lambda e: mm9(e, bO, lambda c: Zt[:, c, 0:128], lambda c: Aqk[:, c, :]), reads=[AqkB, ZtB], writes=[pB[bO]])
                        T.op("pe", lambda e: mm9(e, bZq, lambda c: Zt[:, c, 128:256], lambda c: Aqk[:, c, :]), reads=[AqkB, ZtB], writes=[pB[bZq]])
                        sdI, sdIB = Fm[0], FB[0]
                        qd, qdB = Fm[1], FB[1]
                        Hs, HsB = Fm[3], FB[3]
                        ATs, ATsB = Fm[4], FB[4]
                        O0s, O0sB = Fm[5], FB[5]
                        T.op("act", lambda e: e.activation(out=fl(Hs), in_=psum[:, bH, :], func=AF.Copy), reads=[pB[bH]], writes=[HsB])
                        T.op("pool", lambda e: e.tensor_tensor(out=sdI[:], in0=matb(ident_f), in1=colb(sdec, t0), op=ALU.mult), reads=[cB, gB], writes=[sdIB])
                        T.op("dve", lambda e: e.scalar_tensor_tensor(out=ATs[:], in0=pv(bGt), scalar=-1.0, in1=sdI[:], op0=ALU.mult, op1=ALU.add),
                             reads=[pB[bGt], sdIB], writes=[ATsB])
                        T.op("act", lambda e: e.activation(out=fl(O0s), in_=psum[:, bO, :], func=AF.Copy), reads=[pB[bO]], writes=[O0sB])
                        T.op("pool", lambda e: e.tensor_tensor(out=fl(qd), in0=qkv[0][:, gk], in1=fl(EG), op=ALU.mult), reads=[qB, EGB], writes=[qdB])
                        T.op("dve", lambda e: e.tensor_tensor(out=Qeff[:], in0=qd[:], in1=pv(bZq), op=ALU.subtract), reads=[pB[bZq], qdB], writes=[QeffB])
                        bOo = bank()
                        for c in range(4):
                            t = t0 + c
                            if t > 0:
                                T.op("pe", lambda e: e.matmul(psum[:, bOo, c * 128:(c + 1) * 128], lhsT=Sb[:, :], rhs=Qeff[:, c, :], start=True, stop=True),
                                     reads=[SbB, QeffB], writes=[pB[bOo]])
                            if t == 15:
                                continue
                            if t == 0:
                                T.op("dve", lambda e: e.tensor_copy(out=Sf[:, :], in_=Hs[:, 0, :]), reads=[HsB], writes=[SfB])
                            else:
                                bS = bank()
                                T.op("pe", lambda e: e.matmul(psum[:, bS, 0:128], lhsT=ATs[:, c, :], rhs=Sf[:, :], start=True, stop=True),
                                     reads=[ATsB, SfB], writes=[pB[bS]])
                                T.op("dve", lambda e: e.tensor_tensor(out=Sf[:, :], in0=psum[:, bS, 0:128], in1=Hs[:, c, :], op=ALU.add),
                                     reads=[pB[bS], HsB], writes=[SfB])
                            T.op("act", lambda e: e.activation(out=Sb[:, :], in_=Sf[:, :], func=AF.Copy), reads=[SfB], writes=[SbB])
                        osum, osumB = Fm[0], FB[0]
                        otmp, otmpB = Fm[1], FB[1]
                        rr, rrB = Fm[2], FB[2]
                        if gi == 0:
                            T.op("dve", lambda e: e.tensor_copy(out=osum[:, 0, :], in_=O0s[:, 0, :]), reads=[O0sB], writes=[osumB])
                            T.op("dve", lambda e: e.tensor_tensor(out=osum[:, 1:4, :], in0=pv(bOo)[:, 1:4, :], in1=O0s[:, 1:4, :], op=ALU.add),
                                 reads=[pB[bOo], O0sB], writes=[osumB])
                        else:
                            T.op("dve", lambda e: e.tensor_tensor(out=osum[:], in0=pv(bOo), in1=O0s[:], op=ALU.add), reads=[pB[bOo], O0sB], writes=[osumB])
                        T.op("act", lambda e: e.activation(out=osq[:, :], in_=fl(osum), func=AF.Square), reads=[osumB], writes=[osqB])
                        b2 = bank()
                        T.op("pe", lambda e: e.matmul(psum[:, b2, :], lhsT=odiv[:, :], rhs=osq[:, :], start=True, stop=True), reads=[osqB, gcB], writes=[pB[b2]])
                        T.op("act", lambda e: e.activation(out=fl(rr), in_=psum[:, b2, :], func=AF.Ln, bias=eps_t[:, 0:1]), reads=[pB[b2], cB], writes=[rrB])
                        T.op("act", lambda e: e.activation(out=fl(rr), in_=fl(rr), func=AF.Exp, scale=-0.5), reads=[rrB], writes=[rrB])
                        T.op("dve", lambda e: e.tensor_tensor(out=otmp[:], in0=osum[:], in1=rr[:], op=ALU.mult), reads=[osumB, rrB], writes=[otmpB])
                        T.op("pool", lambda e: e.tensor_tensor(out=ybT[:, h, gk], in0=fl(otmp), in1=gsil[:, gk], op=ALU.mult),
                             reads=[otmpB, gsilB[gi]], writes=[ybB[h]])
                T.barrier()

        def merge(hT, hB, yaT, yaB, ybT, ybB):
            with contextlib.ExitStack() as gs:
                def gsb(name, shape, dt):
                    return gs.enter_context(nc.sbuf_tensor("x_" + name, list(shape), dt))
                mg = gsb("mg", [128, 8, S], BF16)
                mgB = [[Buf() for _ in range(NB)] for _ in range(8)]
                wga = [gsb("wga%d" % i, [128, 2, NCH, 128], BF16) for i in range(2)]
                wba = [gsb("wba%d" % i, [64, 4, 128], BF16) for i in range(2)]
                wbb = [gsb("wbb%d" % i, [128, 8, 128], BF16) for i in range(2)]
                wB = [Buf(), Buf()]
                wo = [gsb("wo%d" % i, [128, 8, 128], BF16) for i in range(2)]
                woB = [Buf(), Buf()]
                one_c = eps_t[:, 1:2]
                sa = gsb("sa", [128, 512], F32)
                sb_ = gsb("sb", [128, 512], F32)
                ta = gsb("ta", [128, 512], F32)
                tb = gsb("tb", [128, 512], F32)
                saB, sbB, taB, tbB = Buf(), Buf(), Buf(), Buf()

                def loadw(dt):
                    i = dt % 2
                    T.dma("pool", wga[i][:].rearrange("p w c f -> p (w c f)"), dt_in["wgab"][dt], writes=[wB[i]])
                    T.dma("pool", wba[i][:].rearrange("p s f -> p (s f)"), dt_in["wba"][dt], writes=[wB[i]])
                    T.dma("pool", wbb[i][:].rearrange("p s f -> p (s f)"), dt_in["wbb"][dt], writes=[wB[i]])
                loadw(0)
                loadw(1)
                for dt in range(8):
                    wi = dt % 2
                    for n in range(NB):
                        tk = slice(n * 512, (n + 1) * 512)
                        bga, bgb, ba, bb = bank(), bank(), bank(), bank()

                        def mmg(e, which, b):
                            ins = None
                            for c in range(NCH):
                                ins = e.matmul(psum[:, b, :], lhsT=wga[wi][:, which, c, :], rhs=hT[:, c, tk], start=(c == 0), stop=(c == NCH - 1))
                            return ins
                        T.op("pe", lambda e: mmg(e, 0, bga), reads=[wB[wi], hB[n]], writes=[pB[bga]])
                        T.op("pe", lambda e: mmg(e, 1, bgb), reads=[wB[wi], hB[n]], writes=[pB[bgb]])

                        def mma(e):
                            ins = None
                            for s4 in range(4):
                                ins = e.matmul(psum[:, ba, :], lhsT=wba[wi][:, s4, :], rhs=yaT[:, s4, tk], start=(s4 == 0), stop=(s4 == 3))
                            return ins
                        T.op("pe", mma, reads=[wB[wi]] + yaB, writes=[pB[ba]])

                        def mmb(e):
                            ins = None
                            for h in range(8):
                                ins = e.matmul(psum[:, bb, :], lhsT=wbb[wi][:, h, :], rhs=ybT[:, h, tk], start=(h == 0), stop=(h == 7))
                            return ins
                        T.op("pe", mmb, reads=[wB[wi]] + ybB, writes=[pB[bb]])
                        T.op("act", lambda e: e.activation(out=sa[:, :], in_=psum[:, bga, :], func=AF.Sigmoid), reads=[pB[bga]], writes=[saB])
                        T.op("act", lambda e: e.activation(out=sb_[:, :], in_=psum[:, bgb, :], func=AF.Sigmoid), reads=[pB[bgb]], writes=[sbB])
                        T.op("dve", lambda e: e.tensor_tensor(out=ta[:, :], in0=psum[:, ba, :], in1=sa[:, :], op=ALU.mult), reads=[pB[ba], saB], writes=[taB])
                        T.op("dve", lambda e: e.tensor_tensor(out=tb[:, :], in0=psum[:, bb, :], in1=sb_[:, :], op=ALU.mult), reads=[pB[bb], sbB], writes=[tbB])
                        T.op("pool", lambda e: e.tensor_tensor(out=mg[:, dt, tk], in0=ta[:, :], in1=tb[:, :], op=ALU.add), reads=[taB, tbB], writes=[mgB[dt][n]])
                    if dt + 2 < 8:
                        loadw(dt + 2)

                def loado(dt):
                    T.dma("pool", wo[dt % 2][:].rearrange("p s f -> p (s f)"), dt_in["wout"][dt], writes=[woB[dt % 2]])
                loado(0)
                loado(1)
                for dt in range(8):
                    wi = dt % 2
                    for n in range(NB):
                        tk = slice(n * 512, (n + 1) * 512)
                        b = bank()

                        def mmo(e):
                            ins = None
                            for k in range(8):
                                ins = e.matmul(psum[:, b, :], lhsT=wo[wi][:, k, :], rhs=mg[:, k, tk], start=(k == 0), stop=(k == 7))
                            return ins
                        T.op("pe", mmo, reads=[woB[wi]] + [mgB[k][n] for k in range(8)], writes=[pB[b]])
                        T.op("dve", lambda e: e.tensor_tensor(out=xT[:, dt, tk], in0=psum[:, b, :], in1=xT[:, dt, tk], op=ALU.add),
                             reads=[pB[b]], writes=[xB[n]])
                    if dt + 2 < 8:
                        loado(dt + 2)
                T.barrier()

        def dbg_dump(slabs):
            with contextlib.ExitStack() as ds:
                dtile = [ds.enter_context(nc.sbuf_tensor("dbg%d" % i, [128, S], F32)) for i in range(2)]
                dB = [Buf(), Buf()]
                for k, (ap, bufs, row0, P) in enumerate(slabs):
                    i = k % 2
                    T.op("act", lambda e: e.activation(out=dtile[i][0:P, :], in_=ap, func=AF.Copy), reads=bufs, writes=[dB[i]])
                    T.dma("sp", out_d[row0:row0 + P, :], dtile[i][0:P, :], reads=[dB[i]])
                T.barrier()

        ffn(0, 0)
        done = False
        if stop_after not in ("ffn1_raw", "ffn1"):
            r = mixer()
            if r is not None:
                done = True
        if stop_after in ("all", "x3_raw"):
            ffn(1, 2)
        if not done:
          with contextlib.ExitStack() as fs:
            yT = [fs.enter_context(nc.sbuf_tensor("yT%d" % i, [128, NCH, 512], F32)) for i in range(2)]
            yB = [Buf(), Buf()]
            sq_t = fs.enter_context(nc.sbuf_tensor("fsq", [128, NCH, 512], BF16))
            sq_b = [Buf() for _ in range(NCH)]
            rs_t = fs.enter_context(nc.sbuf_tensor("frs", [128, 512], F32))
            rs_b = Buf()
            out_dv = out_d.rearrange("(c p) t -> p c t", p=128)
            for n in range(NB):
                i = n % 2
                if stop_after.endswith("_raw"):
                    for c in range(NCH):
                        T.op("dve", lambda e: e.tensor_copy(out=yT[i][:, c, :], in_=xT[:, c, n * 512:(n + 1) * 512]),
                             reads=[xB[n]], writes=[yB[i]])
                else:
                    rmsnorm_block(n, 3, lambda c: yT[i][:, c, :], yB[i], sq_t, sq_b, rs_t, rs_b)
                T.dma("sp", out_dv[:, :, n * 512:(n + 1) * 512], yT[i][:], reads=[yB[i]])
            T.barrier()
    return nc, set(dt_in.keys())


def _consts():
    c = {}
    c["ident_f"] = np.eye(128, dtype=np.float32)
    c["ones_f"] = np.ones((128, 128), dtype=np.float32)
    half = 32
    inv_freq = (10000.0 ** (-np.arange(half, dtype=np.float32) / half)).astype(np.float32)
    pos = np.arange(S, dtype=np.float32)
    ang = (pos[None, :] * inv_freq[:, None]).astype(np.float32)
    cosv, sinv = np.cos(ang).astype(np.float32), np.sin(ang).astype(np.float32)
    cos_t = np.zeros((128, S), np.float32)
    sin_t = np.zeros((128, S), np.float32)
    rm = np.zeros((128, 128), np.float32)
    for p in range(128):
        d = p % 64
        cos_t[p] = cosv[d % 32]
        sin_t[p] = -sinv[d % 32] if d < 32 else sinv[d % 32]
        partner = p + 32 if d < 32 else p - 32
        rm[partner, p] = 1.0
    c["rope_cos"], c["rope_sin"], c["rm"] = cos_t, sin_t, rm
    j = np.arange(128)[:, None]
    i = np.arange(128)[None, :]
    mprev = (j >= i).astype(np.float32)
    mcur = (j <= i).astype(np.float32)
    c["amask"] = np.concatenate([mprev, mcur, mprev, mcur], axis=1)
    em = np.zeros((128, 64), np.float32)
    em[64, :] = 1.0
    c["em"] = em
    c["triu"] = (j <= i).astype(np.float32)
    c["smask"] = np.where(i > j, 0.0, -30000.0).astype(np.float32)
    c["odiv"] = np.full((128, 128), 1.0 / 128.0, np.float32)
    return c


def _prep_inputs(inp):
    f = lambda a: np.ascontiguousarray(a, dtype=np.float32)
    shared = {}

    def ncol(v):
        return f(np.asarray(v).reshape(NCH, 128).T)
    shared["ffn1_norm"] = ncol(inp["ffn1_norm"][0])
    shared["ffn2_norm"] = ncol(inp["ffn2_norm"][0])
    shared["mix_norm"] = ncol(inp["mix_norm"][0])
    shared["final_norm"] = ncol(inp["final_norm"])
    ffn_in = {1: (inp["ffn1_w_gate"], inp["ffn1_w_up"], inp["ffn1_w_down"]),
              2: (inp["ffn2_w_gate"], inp["ffn2_w_up"], inp["ffn2_w_down"])}
    for i in (1, 2):
        wg = np.asarray(ffn_in[i][0][0])
        wu = np.asarray(ffn_in[i][1][0])
        wd = np.asarray(ffn_in[i][2][0])
        shared["wg%d" % i] = f(wg.reshape(NCH, 128, 11, 256).transpose(2, 1, 0, 3).reshape(11, 128, NCH * 256))
        shared["wu%d" % i] = f(wu.reshape(NCH, 128, 11, 256).transpose(2, 1, 0, 3).reshape(11, 128, NCH * 256))
        shared["wd%d" % i] = f(wd.reshape(NF, 128, 8, 128).transpose(2, 1, 0, 3).reshape(8, 128, NF * 128))
    w_in = np.asarray(inp["w_in"][0])

    def cols(c0, n):
        return w_in[:, c0:c0 + n].reshape(NCH, 128, n).transpose(1, 0, 2).reshape(128, NCH * n)
    wa = np.zeros((6, 3, 128, NCH * 128), np.float32)
    for g in range(3):
        for hp in range(2):
            for i in range(3):
                wa[g * 2 + hp, i] = cols(i * 768 + (4 * g + 2 * hp) * 64, 128)
    shared["wqkv_a"] = wa
    wg_ = np.zeros((8, 3, 128, NCH * 128), np.float32)
    wgg = np.zeros((8, 128, NCH * 128), np.float32)
    for h in range(8):
        for i in range(3):
            wg_[h, i] = cols(2304 + i * 1024 + h * 128, 128)
        wgg[h] = cols(5392 + h * 128, 128)
    shared["wgdn"] = wg_
    shared["wgg"] = wgg
    shared["wbd"] = f(cols(5376, 16))
    wgab = np.zeros((8, 128, 2 * NCH * 128), np.float32)
    for dt in range(8):
        a = cols(6416 + dt * 128, 128).reshape(128, 1, NCH * 128)
        b = cols(7440 + dt * 128, 128).reshape(128, 1, NCH * 128)
        wgab[dt] = np.concatenate([a, b], axis=1).reshape(128, 2 * NCH * 128)
    shared["wgab"] = wgab
    wba = np.asarray(inp["w_branch_a"][0])
    shared["wba"] = f(wba.reshape(4, 64, 8, 128).transpose(2, 1, 0, 3).reshape(8, 64, 4 * 128))
    wbb = np.asarray(inp["w_branch_b"][0])
    shared["wbb"] = f(wbb.reshape(8, 128, 8, 128).transpose(2, 1, 0, 3).reshape(8, 128, 8 * 128))
    wo = np.asarray(inp["w_out"][0])
    shared["wout"] = f(wo.reshape(8, 128, 8, 128).transpose(2, 1, 0, 3).reshape(8, 128, 8 * 128))
    cw = np.asarray(inp["gdn_conv_w"][0])
    shared["convw"] = f(cw.reshape(4, 3, 8, 128).transpose(3, 2, 1, 0).reshape(128, 96))
    shared["alog"] = f(np.broadcast_to(np.asarray(inp["gdn_a_log"][0])[None, :], (128, 8)))
    shared["dtb"] = f(np.broadcast_to(np.asarray(inp["gdn_dt_bias"][0])[None, :], (128, 8)))
    shared["onorm"] = f(np.asarray(inp["gdn_out_norm"][0]).reshape(128, 1))
    shared.update(_consts())
    x = np.asarray(inp["x"])
    maps = []
    for b in range(8):
        m = dict(shared)
        m["xT"] = f(x[b].T)
        maps.append(m)
    return maps


_NC_CACHE = {}


def kernel(**inputs):
    stop_after = inputs.pop("_stop_after", "all")
    if stop_after not in _NC_CACHE:
        _NC_CACHE[stop_after] = build_program(stop_after)
    nc, names = _NC_CACHE[stop_after]
    maps = _prep_inputs(inputs)
    maps = [{k: v for k, v in m.items() if k in names} for m in maps]
    res = run_bass_kernel_spmd(nc, maps, core_ids=list(range(8)))
    out = np.stack([np.ascontiguousarray(r["outT"].T) for r in res.results], axis=0)
    return out.astype(np.float32)
```

```python
import contextlib
import numpy as np
import concourse.bass as bass
import concourse.mybir as mybir
from concourse.bass_utils import run_bass_kernel_spmd

F32 = mybir.dt.float32
BF16 = mybir.dt.bfloat16
AF = mybir.ActivationFunctionType
ALU = mybir.AluOpType
AX = mybir.AxisListType

S = 2048
D = 1024
NCH = 8
DFF = 2816
NF = 22
NB = 4
EPS = 1e-6
D_IN = 8464
NDS = 16
import os
ROPE_ADD_ENG = os.environ.get("ROPE_ADD_ENG", "pool")


class _Stop(Exception):
    pass


class Buf:
    __slots__ = ("w", "r", "excl")

    def __init__(self, excl=False):
        self.w = None
        self.r = []
        self.excl = excl


class Trk:
    def __init__(self, nc, es):
        self.nc = nc
        self.eng = {"pe": nc.tensor, "act": nc.scalar, "dve": nc.vector, "pool": nc.gpsimd, "sp": nc.sync}
        self.sem = {k: es.enter_context(nc.semaphore("sem_" + k)) for k in self.eng}
        self.cnt = {k: 0 for k in self.eng}
        self.seen = {k: {} for k in self.eng}
        self.dsem = [es.enter_context(nc.semaphore("dsem%d" % i)) for i in range(NDS)]
        self.dcnt = [0] * NDS
        self.dnext = {"pool": 0, "sp": NDS // 2, "act": NDS // 2}
        self.same = {"pe": False, "act": True, "dve": True, "pool": True, "sp": True}

    def _semof(self, k):
        return self.sem[k] if isinstance(k, str) else self.dsem[k[1]]

    def _wait(self, e, deps):
        best = {}
        for (k, v) in deps:
            if v > best.get(k, 0):
                best[k] = v
        for k, v in best.items():
            if k == e and not self.same[e]:
                continue
            if self.seen[e].get(k, 0) >= v:
                continue
            self.eng[e].wait_ge(self._semof(k), v)
            self.seen[e][k] = v

    @staticmethod
    def _deps(reads, writes):
        deps = []
        for b in reads:
            if b.w is not None:
                deps.append(b.w)
        for b in writes:
            if b.w is not None:
                deps.append(b.w)
            deps.extend(b.r)
        return deps

    @staticmethod
    def _upd(d, reads, writes):
        for b in reads:
            b.r.append(d)
            if len(b.r) > 64:
                best = {}
                for (k, v) in b.r:
                    if v > best.get(k, 0):
                        best[k] = v
                b.r = list(best.items())
        for b in writes:
            b.w = d
            b.r = []

    def op(self, e, fn, reads=(), writes=()):
        if any(b.excl for b in reads):
            writes = list(writes) + [b for b in reads if b.excl]
            reads = [b for b in reads if not b.excl]
        self._wait(e, self._deps(reads, writes))
        ins = fn(self.eng[e])
        self.cnt[e] += 1
        ins.then_inc(self.sem[e], 1)
        self._upd((e, self.cnt[e]), reads, writes)

    def dma(self, q, out, in_, reads=(), writes=()):
        deps = self._deps(reads, writes)
        i = self.dnext[q]
        lo = 0 if q == "pool" else NDS // 2
        self.dnext[q] = lo + (i - lo + 1) % (NDS // 2)
        if self.dcnt[i] > 0:
            deps.append((("d", i), self.dcnt[i]))
        self._wait(q, deps)
        self.eng[q].dma_start(out=out, in_=in_).then_inc(self.dsem[i], 16)
        self.dcnt[i] += 16
        self._upd((("d", i), self.dcnt[i]), reads, writes)

    def barrier(self):
        deps = [(k, v) for k, v in self.cnt.items() if v > 0]
        deps += [(("d", i), c) for i, c in enumerate(self.dcnt) if c > 0]
        for e in self.eng:
            self._wait(e, [d for d in deps if d[0] != e])


def build_program(stop_after="all", debug=False):
    nc = bass.Bass("TRN2", target_bir_lowering=False)
    dt_in = {}

    def din(name, shape):
        dt_in[name] = nc.dram_tensor(name, list(shape), F32, kind="ExternalInput").ap()
        return dt_in[name]

    xT_d = din("xT", [D, S])
    n1_d = din("ffn1_norm", [128, NCH])
    n2_d = din("ffn2_norm", [128, NCH])
    nm_d = din("mix_norm", [128, NCH])
    nf_d = din("final_norm", [128, NCH])
    ffn_w = []
    for i in (1, 2):
        ffn_w.append((din("wg%d" % i, [11, 128, NCH * 256]), din("wu%d" % i, [11, 128, NCH * 256]),
                      din("wd%d" % i, [8, 128, NF * 128])))
    din("wqkv_a", [6, 3, 128, NCH * 128])
    din("rope_cos", [128, S]); din("rope_sin", [128, S]); din("rm", [128, 128]); din("amask", [128, 512]); din("em", [128, 64])
    din("wgdn", [8, 3, 128, NCH * 128]); din("wgg", [8, 128, NCH * 128]); din("wbd", [128, NCH * 16])
    din("convw", [128, 96]); din("alog", [128, 8]); din("dtb", [128, 8]); din("onorm", [128, 1])
    din("triu", [128, 128]); din("smask", [128, 128]); din("odiv", [128, 128])
    din("wgab", [8, 128, 2 * NCH * 128]); din("wba", [8, 64, 4 * 128]); din("wbb", [8, 128, 8 * 128]); din("wout", [8, 128, 8 * 128])
    ident_d = din("ident_f", [128, 128])
    ones_d = din("ones_f", [128, 128])
    out_d = nc.dram_tensor("outT", [D, S], F32, kind="ExternalOutput").ap()
    dbg = {}

    with contextlib.ExitStack() as es:
        T = Trk(nc, es)

        def sb(name, shape, dt):
            return es.enter_context(nc.sbuf_tensor(name, list(shape), dt))

        xT = sb("xT_sb", [128, NCH, S], F32)
        xB = [Buf() for _ in range(NB)]
        psum = es.enter_context(nc.psum_tensor("psum", [128, 8, 512], F32))
        pB = [Buf(True) for _ in range(8)]
        pstate = {"n": 0}

        def bank():
            b = pstate["n"]
            pstate["n"] = (b + 1) % 8
            return b

        def bank2():
            b = pstate["n"]
            if b % 2:
                b = (b + 1) % 8
            pstate["n"] = (b + 2) % 8
            return b

        ident_f = sb("ident_f_sb", [128, 128], F32)
        ident_b = sb("ident_b_sb", [128, 128], BF16)
        ones_b = sb("ones_b_sb", [128, 128], BF16)
        ones_f = sb("ones_f_sb", [128, 128], F32)
        ident2_f = sb("ident2_f_sb", [128, 128], F32)
        cB = Buf()
        norms = sb("norms_sb", [128, 4, NCH], F32)
        eps_t = sb("eps_sb", [128, 2], F32)
        T.op("dve", lambda e: e.memset(eps_t[:, 0:1], EPS), writes=[cB])
        T.op("dve", lambda e: e.memset(eps_t[:, 1:2], 1.0), writes=[cB])
        T.dma("sp", ident_f[:], ident_d, writes=[cB])
        T.dma("sp", ones_f[:], ones_d, writes=[cB])
        T.op("dve", lambda e: e.tensor_single_scalar(out=ident2_f[:, :], in_=ident_f[:, :], scalar=2.0, op=ALU.mult), reads=[cB], writes=[cB])
        T.dma("pool", ident_b[:], ident_d, writes=[cB])
        T.dma("pool", ones_b[:], ones_d, writes=[cB])
        for i, nd in enumerate((n1_d, nm_d, n2_d, nf_d)):
            T.dma("sp", norms[:, i, :], nd, writes=[cB])
        xT_dv = xT_d.rearrange("(c p) t -> p c t", p=128)
        for n in range(NB):
            T.dma("sp", xT[:, :, n * 512:(n + 1) * 512], xT_dv[:, :, n * 512:(n + 1) * 512], writes=[xB[n]])

        def rmsnorm_block(n, gi, out_ap_fn, out_buf, sq_t, sq_b, rs_t, rs_b):
            tok = slice(n * 512, (n + 1) * 512)
            b = bank()
            for c in range(NCH):
                T.op("act", lambda e, c=c: e.activation(out=sq_t[:, c, :], in_=xT[:, c, tok], func=AF.Square),
                     reads=[xB[n]], writes=[sq_b[c]])
            def mm(e):
                ins = None
                for c in range(NCH):
                    ins = e.matmul(psum[:, b, :], lhsT=ones_b[:, :], rhs=sq_t[:, c, :], start=(c == 0), stop=(c == NCH - 1))
                return ins
            T.op("pe", mm, reads=sq_b + [cB], writes=[pB[b]])
            T.op("act", lambda e: e.activation(out=rs_t[:, :], in_=psum[:, b, :], func=AF.Ln, scale=1.0 / D, bias=eps_t[:, 0:1]),
                 reads=[pB[b], cB], writes=[rs_b])
            T.op("act", lambda e: e.activation(out=rs_t[:, :], in_=rs_t[:, :], func=AF.Exp, scale=-0.5), reads=[rs_b], writes=[rs_b])
            for c in range(NCH):
                T.op("dve", lambda e, c=c: e.scalar_tensor_tensor(out=out_ap_fn(c), in0=xT[:, c, tok],
                                                                  scalar=norms[:, gi, c:c + 1], in1=rs_t[:, :],
                                                                  op0=ALU.mult, op1=ALU.mult),
                     reads=[xB[n], rs_b, cB], writes=[out_buf])

        def ffn(which, gi):
            wg_d, wu_d, wd_d = ffn_w[which]
            with contextlib.ExitStack() as fs:
                def fsb(name, shape, dt):
                    return fs.enter_context(nc.sbuf_tensor("f%d_%s" % (which, name), list(shape), dt))
                hT = fsb("hT", [128, NCH, 1024], BF16)
                hB = [Buf(), Buf()]
                actT = fsb("actT", [128, NF, 1024], BF16)
                aB = [[Buf(), Buf()] for _ in range(NF)]
                sq_t = fsb("sq", [128, NCH, 512], BF16)
                sq_b = [Buf() for _ in range(NCH)]
                rs_t = fsb("rs", [128, 512], F32)
                rs_b = Buf()
                NWB = 3
                wgb = [fsb("wg%d" % i, [128, NCH, 256], BF16) for i in range(NWB)]
                wub = [fsb("wu%d" % i, [128, NCH, 256], BF16) for i in range(NWB)]
                wgB = [Buf() for _ in range(NWB)]
                wuB = [Buf() for _ in range(NWB)]
                wdb = [fsb("wd%d" % i, [128, NF, 128], BF16) for i in range(2)]
                wdB = [Buf() for _ in range(2)]
                sg = [fsb("sg%d" % i, [128, 512], F32) for i in range(3)]
                sgB = [Buf() for _ in range(3)]
                sgn = 0

                def load_gu(g):
                    i = g % NWB
                    T.dma("pool", wgb[i][:].rearrange("p c f -> p (c f)"), wg_d[g], writes=[wgB[i]])
                    T.dma("pool", wub[i][:].rearrange("p c f -> p (c f)"), wu_d[g], writes=[wuB[i]])

                def load_d(dt):
                    i = dt % 2
                    T.dma("pool", wdb[i][:].rearrange("p f j -> p (f j)"), wd_d[dt], writes=[wdB[i]])

                for half in range(2):
                    load_gu(0)
                    load_gu(1)
                    for nn in range(2):
                        n = half * 2 + nn
                        rmsnorm_block(n, gi, lambda c, nn=nn: hT[:, c, nn * 512:(nn + 1) * 512], hB[nn],
                                      sq_t, sq_b, rs_t, rs_b)
                    for g in range(11):
                        if g + 2 < 11:
                            load_gu(g + 2)
                        if g == 9:
                            load_d(0)
                        if g == 10:
                            load_d(1)
                        wi = g % NWB
                        for j in range(2):
                            f = 2 * g + j
                            for nn in range(2):
                                tk = slice(nn * 512, (nn + 1) * 512)
                                bg = bank()
                                bu = bank()

                                def mmg(e, w=wgb[wi], b=bg, j=j, tk=tk):
                                    ins = None
                                    for c in range(NCH):
                                        ins = e.matmul(psum[:, b, :], lhsT=w[:, c, j * 128:(j + 1) * 128],
                                                       rhs=hT[:, c, tk], start=(c == 0), stop=(c == NCH - 1))
                                    return ins
                                T.op("pe", mmg, reads=[wgB[wi], hB[nn]], writes=[pB[bg]])
                                T.op("pe", lambda e, b=bu, j=j, tk=tk: mmg(e, wub[wi], b, j, tk),
                                     reads=[wuB[wi], hB[nn]], writes=[pB[bu]])
                                si = sgn % 3
                                sgn += 1
                                T.op("act", lambda e, si=si, b=bg: e.activation(out=sg[si][:, :], in_=psum[:, b, :],
                                                                               func=AF.Silu),
                                     reads=[pB[bg]], writes=[sgB[si]])
                                T.op("dve", lambda e, si=si, b=bu, f=f, tk=tk: e.tensor_tensor(
                                    out=actT[:, f, tk], in0=psum[:, b, :], in1=sg[si][:, :], op=ALU.mult),
                                     reads=[pB[bu], sgB[si]], writes=[aB[f][nn]])
                    for dt in range(8):
                        if dt + 2 < 8:
                            pass
                        wi = dt % 2
                        for nn in range(2):
                            n = half * 2 + nn
                            tk = slice(nn * 512, (nn + 1) * 512)
                            b = bank()

                            def mmd(e, b=b, wi=wi, tk=tk):
                                ins = None
                                for f in range(NF):
                                    ins = e.matmul(psum[:, b, :], lhsT=wdb[wi][:, f, :], rhs=actT[:, f, tk],
                                                   start=(f == 0), stop=(f == NF - 1))
                                return ins
                            T.op("pe", mmd, reads=[wdB[wi]] + [aB[f][nn] for f in range(NF)], writes=[pB[b]])
                            T.op("dve", lambda e, b=b, dt=dt, n=n: e.scalar_tensor_tensor(
                                out=xT[:, dt, n * 512:(n + 1) * 512], in0=psum[:, b, :], scalar=0.5,
                                in1=xT[:, dt, n * 512:(n + 1) * 512], op0=ALU.mult, op1=ALU.add),
                                 reads=[pB[b]], writes=[xB[n]])
                        if dt + 2 < 8:
                            load_d(dt + 2)
                T.barrier()

        def mixer():
            with contextlib.ExitStack() as ms:
                def msb(name, shape, dt):
                    return ms.enter_context(nc.sbuf_tensor("m_" + name, list(shape), dt))
                hT = msb("hT", [128, NCH, S], BF16)
                hB = [Buf() for _ in range(NB)]
                with contextlib.ExitStack() as ns:
                    sq_t = ns.enter_context(nc.sbuf_tensor("m_sq", [128, NCH, 512], BF16))
                    sq_b = [Buf() for _ in range(NCH)]
                    rs_t = ns.enter_context(nc.sbuf_tensor("m_rs", [128, 512], F32))
                    rs_b = Buf()
                    for n in range(NB):
                        rmsnorm_block(n, 1, lambda c, n=n: hT[:, c, n * 512:(n + 1) * 512], hB[n], sq_t, sq_b, rs_t, rs_b)
                    T.barrier()
                if stop_after == "dbg_h":
                    dbg_dump([(hT[:, c, :], hB, c * 128, 128) for c in range(NCH)])
                    return True
                ybT = msb("ybT", [128, 8, S], BF16)
                ybB = [Buf() for _ in range(8)]
                gdn(hT, hB, ybT, ybB)
                T.barrier()
                if stop_after == "dbg_yb":
                    dbg_dump([(ybT[:, c, :], ybB, c * 128, 128) for c in range(8)])
                    return True
                yaT = msb("yaT", [64, 4, S], BF16)
                yaB = [Buf() for _ in range(4)]
                if attention(hT, hB, yaT, yaB):
                    return True
                T.barrier()
                if stop_after == "dbg_ya":
                    dbg_dump([(yaT[:, c, :], yaB, c * 64, 64) for c in range(4)])
                    return True
                merge(hT, hB, yaT, yaB, ybT, ybB)
                T.barrier()
            return None

        def nset(lo, hi):
            return list(range(lo // 512, hi // 512 + 1))

        def attention(hT, hB, yaT, yaB):
            wqkv_d = dt_in["wqkv_a"]
            with contextlib.ExitStack() as as_:
                def asb(name, shape, dt):
                    return as_.enter_context(nc.sbuf_tensor("a_" + name, list(shape), dt))
                cos_t = asb("cos", [128, S], BF16)
                sin_t = asb("sin", [128, S], BF16)
                rm_t = asb("rm", [128, 128], BF16)
                am_t = asb("am", [128, 512], BF16)
                em_t = asb("em", [128, 64], F32)
                acB = Buf()
                T.dma("pool", cos_t[:], dt_in["rope_cos"], writes=[acB])
                T.dma("pool", sin_t[:], dt_in["rope_sin"], writes=[acB])
                T.dma("pool", rm_t[:], dt_in["rm"], writes=[acB])
                T.dma("pool", am_t[:], dt_in["amask"], writes=[acB])
                T.dma("sp", em_t[:], dt_in["em"], writes=[acB])
                wq = [asb("w%d" % i, [128, NCH, 128], BF16) for i in range(3)]
                wB = [Buf() for _ in range(3)]
                qk = [asb("qk%d" % i, [128, S], BF16) for i in range(2)]
                qkB = [[Buf() for _ in range(NB)] for _ in range(2)]
                Vt = asb("Vt", [128, 16, 2, 65], BF16)
                VtB = [Buf() for _ in range(16)]
                VoneB = Buf()
                T.op(ROPE_ADD_ENG, lambda e: e.memset(Vt[:, :, :, 64:65], 1.0), writes=[VoneB])
                acc = asb("acc", [128, 2, S], F32)
                accB = Buf()
                qb = asb("qb", [128, 512], BF16)
                qbB = Buf()
                t1 = asb("t1", [128, 512], F32)
                t2 = asb("t2", [128, 512], F32)
                t1B, t2B = Buf(), Buf()
                ex = [asb("ex%d" % i, [128, 512], BF16) for i in range(3)]
                exB = [Buf(), Buf(), Buf()]
                exm = [asb("exm%d" % i, [128, 512], BF16) for i in range(3)]
                exmB = [Buf(), Buf(), Buf()]
                rl = asb("rl", [64, 512], F32)
                rlB = Buf()
                exn = 0
                if stop_after == "dbg_att0":
                    dbg_dump([(cos_t[:, :], [acB], 0, 128), (sin_t[:, :], [acB, VoneB], 128, 128)])
                    return True
                for hp in range(2):
                    for g in range(3):
                        dil = (1, 4, 16)[g]
                        nbk = 16 // dil
                        pi = g * 2 + hp
                        for i in range(3):
                            T.dma("pool", wq[i][:].rearrange("p c f -> p (c f)"), wqkv_d[pi, i], writes=[wB[i]])
                        for i in range(2):
                            for n in range(NB):
                                tk = slice(n * 512, (n + 1) * 512)
                                b = bank()

                                def mm(e):
                                    ins = None
                                    for c in range(NCH):
                                        ins = e.matmul(psum[:, b, :], lhsT=wq[i][:, c, :], rhs=hT[:, c, tk],
                                                       start=(c == 0), stop=(c == NCH - 1))
                                    return ins
                                CUT = int(os.environ.get("CUT", "9"))
                                if CUT >= 1:
                                    T.op("pe", mm, reads=[wB[i], hB[n]], writes=[pB[b]])
                                if CUT >= 2:
                                    T.op("act", lambda e: e.activation(out=qb[:, :], in_=psum[:, b, :], func=AF.Copy),
                                         reads=[pB[b]], writes=[qbB])
                                b2 = bank()
                                if CUT >= 3:
                                    T.op("pe", lambda e: e.matmul(psum[:, b2, :], lhsT=rm_t[:, :], rhs=qb[:, :], start=True, stop=True),
                                         reads=[qbB, acB], writes=[pB[b2]])
                                if CUT >= 4:
                                    T.op("dve", lambda e: e.tensor_tensor(out=t1[:, :], in0=psum[:, b, :], in1=cos_t[:, tk], op=ALU.mult),
                                         reads=[pB[b], acB], writes=[t1B])
                                if CUT >= 5:
                                    T.op("dve", lambda e: e.tensor_tensor(out=t2[:, :], in0=psum[:, b2, :], in1=sin_t[:, tk], op=ALU.mult),
                                         reads=[pB[b2], acB], writes=[t2B])
                                if CUT >= 6:
                                    T.op(ROPE_ADD_ENG, lambda e: e.tensor_tensor(out=qk[i][:, tk], in0=t1[:, :], in1=t2[:, :], op=ALU.add),
                                         reads=[t1B, t2B], writes=[qkB[i][n]])
                                if stop_after == "dbg_att1a":
                                    T.barrier()
                                    with contextlib.ExitStack() as ds:
                                        dtl = ds.enter_context(nc.sbuf_tensor("dbgx", [128, 512], F32))
                                        dxB = Buf()
                                        T.op("act", lambda e: e.activation(out=dtl[:, :], in_=qk[0][:, 0:512], func=AF.Copy), reads=[qkB[0][0]], writes=[dxB])
                                        T.dma("sp", out_d[0:128, 0:512], t1[:, :], reads=[t1B])
                                        T.dma("sp", out_d[128:256, 0:512], t2[:, :], reads=[t2B])
                                        T.dma("sp", out_d[256:384, 0:512], dtl[:, :], reads=[dxB])
                                        T.barrier()
                                    return True
                        if stop_after == "dbg_att1":
                            dbg_dump([(qk[0][:, :], qkB[0], 0, 128), (qk[1][:, :], qkB[1], 128, 128)])
                            return True
                        def tokslice(bi):
                            r, nk = bi // nbk, bi % nbk
                            st = r + dil * 128 * nk
                            return st, slice(st, st + 127 * dil + 1, dil)
                        for b4 in range(4):
                            b = bank()

                            def mmv(e):
                                ins = None
                                for k in range(4):
                                    st, sl = tokslice(b4 * 4 + k)
                                    for c in range(NCH):
                                        ins = e.matmul(psum[:, b, k * 128:(k + 1) * 128], lhsT=hT[:, c, sl], rhs=wq[2][:, c, :],
                                                       start=(c == 0), stop=(c == NCH - 1))
                                return ins
                            T.op("pe", mmv, reads=[wB[2]] + hB, writes=[pB[b]])
                            T.op("act", lambda e: e.activation(
                                out=Vt[:, b4 * 4:(b4 + 1) * 4, :, 0:64],
                                in_=psum[:, b, :].rearrange("p (k e d) -> p k e d", k=4, e=2), func=AF.Copy),
                                 reads=[pB[b]], writes=[VtB[b4 * 4 + k] for k in range(4)])
                        if stop_after == "dbg_att2":
                            dbg_dump([(Vt[:, :, :, :].rearrange("p a b c -> p (a b c)")[:, 0:2048], VtB + [VoneB], 0, 128)])
                            return True
                        def stage1(bi):
                            nonlocal exn
                            r, nk = bi // nbk, bi % nbk
                            st, qsl = tokslice(bi)
                            hi_tok = st + 127 * dil
                            qn_set = nset(st, hi_tok)
                            b = bank2()
                            if nk > 0:
                                stp, ksl_p = tokslice(bi - 1)
                                kn_set = nset(stp, hi_tok)
                            else:
                                kn_set = qn_set

                            def mms(e):
                                ins = None
                                for hd in range(2):
                                    hs = slice(hd * 64, (hd + 1) * 64)
                                    if nk > 0:
                                        ins = e.matmul(psum[:, b + hd, 0:128], lhsT=qk[1][hs, ksl_p], rhs=qk[0][hs, qsl],
                                                       start=True, stop=True)
                                    ins = e.matmul(psum[:, b + hd, 128:256], lhsT=qk[1][hs, qsl], rhs=qk[0][hs, qsl],
                                                   start=True, stop=True)
                                return ins
                            T.op("pe", mms, reads=[qkB[0][n] for n in qn_set] + [qkB[1][n] for n in kn_set], writes=[pB[b], pB[b + 1]])
                            xi = exn % 3
                            exn += 1
                            lo = 0 if nk > 0 else 128
                            src = psum[:, b:b + 2, lo:256]
                            dst = ex[xi][:, :].rearrange("p (e w) -> p e w", e=2)[:, :, lo:256]
                            dstm = exm[xi][:, :].rearrange("p (e w) -> p e w", e=2)[:, :, lo:256]
                            msk = am_t[:, :].rearrange("p (e w) -> p e w", e=2)[:, :, lo:256]
                            T.op("act", lambda e: e.activation(out=dst, in_=src, func=AF.Exp, scale=0.125),
                                 reads=[pB[b], pB[b + 1]], writes=[exB[xi]])
                            T.op("pool", lambda e: e.tensor_tensor(out=dstm, in0=dst, in1=msk, op=ALU.mult),
                                 reads=[exB[xi], acB], writes=[exmB[xi]])
                            return xi

                        def stage2(bi, xi):
                            r, nk = bi // nbk, bi % nbk
                            st, qsl = tokslice(bi)
                            b2 = bank()

                            def mmu(e):
                                ins = None
                                for hd in range(2):
                                    o = psum[0:65, b2, hd * 128:(hd + 1) * 128]
                                    if nk > 0:
                                        e.matmul(o, lhsT=Vt[:, bi - 1, hd, :], rhs=exm[xi][:, hd * 256:hd * 256 + 128], start=True, stop=False)
                                    ins = e.matmul(o, lhsT=Vt[:, bi, hd, :], rhs=exm[xi][:, hd * 256 + 128:hd * 256 + 256],
                                                   start=(nk == 0), stop=True)
                                return ins
                            T.op("pe", mmu, reads=[exmB[xi], VtB[bi], VoneB] + ([VtB[bi - 1]] if nk > 0 else []), writes=[pB[b2]])
                            src2 = psum[0:65, b2, 0:256].rearrange("p (e q) -> p e q", e=2)
                            if g == 0:
                                T.op("dve", lambda e: e.tensor_copy(out=acc[0:65, :, qsl], in_=src2), reads=[pB[b2]], writes=[accB])
                            else:
                                T.op("dve", lambda e: e.tensor_tensor(out=acc[0:65, :, qsl], in0=src2, in1=acc[0:65, :, qsl], op=ALU.add),
                                     reads=[pB[b2]], writes=[accB])
                        pend = []
                        for bi in range(16):
                            pend.append((bi, stage1(bi)))
                            if len(pend) > 1:
                                stage2(*pend.pop(0))
                        while pend:
                            stage2(*pend.pop(0))
                        if stop_after == "dbg_att3":
                            dbg_dump([(acc[:, 0, :], [accB], 0, 128), (acc[:, 1, :], [accB], 128, 128)])
                            return True
                    for hd in range(2):
                        for n in range(NB):
                            tk = slice(n * 512, (n + 1) * 512)
                            b = bank()
                            T.op("pe", lambda e: e.matmul(psum[0:64, b, :], lhsT=em_t[0:65, :], rhs=acc[0:65, hd, tk], start=True, stop=True),
                                 reads=[accB, acB], writes=[pB[b]])
                            T.op("dve", lambda e: e.reciprocal(out=rl[:, :], in_=psum[0:64, b, :]), reads=[pB[b]], writes=[rlB])
                            T.op("dve", lambda e: e.tensor_tensor(out=yaT[:, 2 * hp + hd, tk], in0=acc[0:64, hd, tk], in1=rl[:, :], op=ALU.mult),
                                 reads=[rlB, accB], writes=[yaB[2 * hp + hd]])
                T.barrier()

        def gdn(hT, hB, ybT, ybB):
            with contextlib.ExitStack() as gs:
                def gsb(name, shape, dt):
                    return gs.enter_context(nc.sbuf_tensor("g_" + name, list(shape), dt))
                gcB = Buf()
                triu = gsb("triu", [128, 128], F32)
                smask = gsb("smask", [128, 128], F32)
                odiv = gsb("odiv", [128, 128], BF16)
                convw = gsb("convw", [128, 8, 3, 4], F32)
                alog = gsb("alog", [128, 8], F32)
                dtb = gsb("dtb", [128, 8], F32)
                onorm = gsb("onorm", [128, 1], F32)
                wbd = gsb("wbd", [128, NCH, 16], BF16)
                T.dma("sp", triu[:], dt_in["triu"], writes=[gcB])
                T.dma("sp", smask[:], dt_in["smask"], writes=[gcB])
                T.dma("pool", odiv[:], dt_in["odiv"], writes=[gcB])
                T.dma("sp", convw[:].rearrange("p h w t -> p (h w t)"), dt_in["convw"], writes=[gcB])
                T.dma("sp", alog[:], dt_in["alog"], writes=[gcB])
                T.dma("sp", dtb[:], dt_in["dtb"], writes=[gcB])
                T.dma("sp", onorm[:], dt_in["onorm"], writes=[gcB])
                T.dma("pool", wbd[:].rearrange("p c f -> p (c f)"), dt_in["wbd"], writes=[gcB])
                braw = gsb("braw", [128, 16, 16], F32)
                beta = gsb("beta", [128, 16, 8], F32)
                negb = gsb("negb", [128, 16, 8], F32)
                gt = gsb("gt", [128, 16, 8], F32)
                gcum = gsb("gcum", [128, 16, 8], F32)
                glast = gsb("glast", [128, 16, 8], F32)
                kds = gsb("kds", [128, 16, 8], F32)
                sdec = gsb("sdec", [128, 16, 8], F32)
                egc = gsb("egc", [128, 16, 8], F32)
                tmpg = gsb("tmpg", [128, 16, 8], F32)
                negA = gsb("negA", [128, 8], F32)
                gB = Buf()
                b = bank()

                def mmb(e):
                    ins = None
                    for t in range(16):
                        for c in range(NCH):
                            ins = e.matmul(psum[:, b, t * 16:(t + 1) * 16], lhsT=hT[:, c, t * 128:(t + 1) * 128], rhs=wbd[:, c, :],
                                           start=(c == 0), stop=(c == NCH - 1))
                    return ins
                T.op("pe", mmb, reads=hB + [gcB], writes=[pB[b]])
                T.op("act", lambda e: e.activation(out=braw[:].rearrange("p t k -> p (t k)"), in_=psum[:, b, 0:256], func=AF.Copy),
                     reads=[pB[b]], writes=[gB])
                T.op("act", lambda e: e.activation(out=tmpg[:], in_=braw[:, :, 0:8], func=AF.Exp, scale=-1.0), reads=[gB], writes=[gB])
                T.op("dve", lambda e: e.tensor_single_scalar(out=tmpg[:], in_=tmpg[:], scalar=1.0, op=ALU.add), reads=[gB], writes=[gB])
                T.op("dve", lambda e: e.reciprocal(out=beta[:], in_=tmpg[:]), reads=[gB], writes=[gB])
                T.op("dve", lambda e: e.tensor_single_scalar(out=negb[:], in_=beta[:], scalar=-1.0, op=ALU.mult), reads=[gB], writes=[gB])
                T.op("dve", lambda e: e.tensor_tensor(out=tmpg[:], in0=braw[:, :, 8:16], in1=dtb[:].unsqueeze(1).to_broadcast([128, 16, 8]), op=ALU.add),
                     reads=[gB, gcB], writes=[gB])
                T.op("act", lambda e: e.activation(out=tmpg[:], in_=tmpg[:], func=AF.Exp), reads=[gB], writes=[gB])
                T.op("act", lambda e: e.activation(out=tmpg[:], in_=tmpg[:], func=AF.Ln, bias=eps_t[:, 1:2]), reads=[gB, cB], writes=[gB])
                T.op("act", lambda e: e.activation(out=negA[:], in_=alog[:], func=AF.Exp), reads=[gcB], writes=[gB])
                T.op("dve", lambda e: e.tensor_single_scalar(out=negA[:], in_=negA[:], scalar=-1.0, op=ALU.mult), reads=[gB], writes=[gB])
                T.op("dve", lambda e: e.tensor_tensor(out=gt[:], in0=tmpg[:], in1=negA[:].unsqueeze(1).to_broadcast([128, 16, 8]), op=ALU.mult),
                     reads=[gB], writes=[gB])
                b = bank()
                T.op("pe", lambda e: e.matmul(psum[:, b, 0:128], lhsT=triu[:, :], rhs=gt[:].rearrange("p t k -> p (t k)"), start=True, stop=True),
                     reads=[gB, gcB], writes=[pB[b]])
                T.op("act", lambda e: e.activation(out=gcum[:].rearrange("p t k -> p (t k)"), in_=psum[:, b, 0:128], func=AF.Copy),
                     reads=[pB[b]], writes=[gB])
                b = bank()
                T.op("pe", lambda e: e.matmul(psum[:, b, 0:128], lhsT=ones_f[:, :], rhs=gt[:].rearrange("p t k -> p (t k)"), start=True, stop=True),
                     reads=[gB, cB], writes=[pB[b]])
                T.op("act", lambda e: e.activation(out=glast[:].rearrange("p t k -> p (t k)"), in_=psum[:, b, 0:128], func=AF.Copy),
                     reads=[pB[b]], writes=[gB])
                T.op("dve", lambda e: e.tensor_tensor(out=tmpg[:], in0=glast[:], in1=gcum[:], op=ALU.subtract), reads=[gB], writes=[gB])
                T.op("act", lambda e: e.activation(out=kds[:], in_=tmpg[:], func=AF.Exp), reads=[gB], writes=[gB])
                T.op("act", lambda e: e.activation(out=sdec[:], in_=glast[:], func=AF.Exp), reads=[gB], writes=[gB])
                T.op("act", lambda e: e.activation(out=egc[:], in_=gcum[:], func=AF.Exp), reads=[gB], writes=[gB])

                wh = [gsb("wh%d" % i, [128, NCH, 128], BF16) for i in range(4)]
                whB = [Buf() for _ in range(4)]
                Dg = gsb("Dg", [128, 3, 4, 128], BF16)
                DgB = Buf()
                xb = gsb("xb", [128, 4 + S], BF16)
                xbB = [Buf() for _ in range(NB)]
                xb0B = Buf()
                T.op("pool", lambda e: e.memset(xb[:, 0:4], 0.0), writes=[xb0B])
                qkv = [gsb("qkv%d" % i, [128, S], BF16) for i in range(3)]
                qkvB = [[Buf() for _ in range(NB)] for _ in range(3)]
                gsil = gsb("gsil", [128, S], BF16)
                gsilB = [Buf() for _ in range(NB)]
                sqb = gsb("sqb", [128, 512], BF16)
                sqbB = Buf()
                rt = gsb("rt", [128, 512], F32)
                rtB = Buf()
                Fm = [gsb("F%d" % i, [128, 4, 128], F32) for i in range(6)]
                FB = [[Buf() for _ in range(4)] for _ in range(6)]
                QT = gsb("QT", [128, 4, 128], BF16)
                QTB = [Buf() for _ in range(4)]
                PQ = [gsb("PQ%d" % i, [128, 2, 4, 128], BF16) for i in range(2)]
                PQB = [[Buf() for _ in range(4)] for _ in range(2)]
                Wt = gsb("Wt", [128, 4, 128], BF16)
                WtB = [Buf() for _ in range(4)]
                ATt = gsb("ATt", [128, 4, 128], BF16)
                ATtB = [Buf() for _ in range(4)]
                Aqk = gsb("Aqk", [128, 4, 128], BF16)
                AqkB = [Buf() for _ in range(4)]
                kdec = gsb("kdec", [128, 4, 128], BF16)
                kdecB = [Buf() for _ in range(4)]
                VK = gsb("VK", [128, 4, 256], BF16)
                VKB = [Buf() for _ in range(4)]
                Zt = gsb("Zt", [128, 4, 256], BF16)
                ZtB = [Buf() for _ in range(4)]
                Qeff = gsb("Qeff", [128, 4, 128], BF16)
                QeffB = [Buf() for _ in range(4)]
                osq = gsb("osq", [128, 512], BF16)
                osqB = Buf()
                Sf = gsb("Sf", [128, 128], F32)
                Sb = gsb("Sb", [128, 128], BF16)
                SfB, SbB = Buf(), Buf()
                QSCALE = 128.0 ** -0.5

                def pv(bk):
                    return psum[:, bk, :].rearrange("p (c i) -> p c i", c=4)

                def fl(tl):
                    return tl[:].rearrange("p c i -> p (c i)")

                def matb(m):
                    return m[:, :].unsqueeze(1).to_broadcast([128, 4, 128])

                for h in range(8):
                    def colb(tl, t0, w=128):
                        return tl[:, t0:t0 + 4, h:h + 1].to_broadcast([128, 4, w])
                    for i in range(3):
                        T.dma("pool", wh[i][:].rearrange("p c f -> p (c f)"), dt_in["wgdn"][h, i], writes=[whB[i]])
                    T.dma("pool", wh[3][:].rearrange("p c f -> p (c f)"), dt_in["wgg"][h], writes=[whB[3]])
                    for w3 in range(3):
                        for tp in range(4):
                            T.op("dve", lambda e: e.tensor_scalar_mul(out=Dg[:, w3, tp, :], in0=ident_f[:, :], scalar1=convw[:, h, w3, tp:tp + 1]),
                                 reads=[gcB, cB], writes=[DgB])
                    for w3 in range(3):
                        for n in range(NB):
                            tk = slice(n * 512, (n + 1) * 512)
                            b = bank()

                            def mm(e):
                                ins = None
                                for c in range(NCH):
                                    ins = e.matmul(psum[:, b, :], lhsT=wh[w3][:, c, :], rhs=hT[:, c, tk], start=(c == 0), stop=(c == NCH - 1))
                                return ins
                            T.op("pe", mm, reads=[whB[w3], hB[n]], writes=[pB[b]])
                            T.op("act", lambda e: e.activation(out=xb[:, 4 + n * 512:4 + (n + 1) * 512], in_=psum[:, b, :], func=AF.Copy),
                                 reads=[pB[b]], writes=[xbB[n]])
                        for n in range(NB):
                            tk = slice(n * 512, (n + 1) * 512)
                            b = bank()

                            def mmc(e):
                                ins = None
                                for tp in range(4):
                                    ins = e.matmul(psum[:, b, :], lhsT=Dg[:, w3, tp, :], rhs=xb[:, n * 512 + 1 + tp:n * 512 + 1 + tp + 512],
                                                   start=(tp == 0), stop=(tp == 3))
                                return ins
                            T.op("pe", mmc, reads=[DgB, xbB[n], xb0B] + ([xbB[n - 1]] if n > 0 else []), writes=[pB[b]])
                            T.op("act", lambda e: e.activation(out=qkv[w3][:, tk], in_=psum[:, b, :], func=AF.Silu),
                                 reads=[pB[b]], writes=[qkvB[w3][n]])
                        if w3 < 2:
                            for n in range(NB):
                                tk = slice(n * 512, (n + 1) * 512)
                                T.op("act", lambda e: e.activation(out=sqb[:, :], in_=qkv[w3][:, tk], func=AF.Square), reads=[qkvB[w3][n]], writes=[sqbB])
                                b2 = bank()
                                T.op("pe", lambda e: e.matmul(psum[:, b2, :], lhsT=ones_b[:, :], rhs=sqb[:, :], start=True, stop=True),
                                     reads=[sqbB, cB], writes=[pB[b2]])
                                T.op("act", lambda e: e.activation(out=rt[:, :], in_=psum[:, b2, :], func=AF.Ln, bias=eps_t[:, 0:1]),
                                     reads=[pB[b2], cB], writes=[rtB])
                                T.op("act", lambda e: e.activation(out=rt[:, :], in_=rt[:, :], func=AF.Exp, scale=-0.5), reads=[rtB], writes=[rtB])
                                T.op("dve", lambda e: e.scalar_tensor_tensor(out=qkv[w3][:, tk], in0=qkv[w3][:, tk], scalar=(QSCALE if w3 == 0 else 1.0),
                                                                             in1=rt[:, :], op0=ALU.mult, op1=ALU.mult),
                                     reads=[rtB, qkvB[w3][n]], writes=[qkvB[w3][n]])
                    for n in range(NB):
                        tk = slice(n * 512, (n + 1) * 512)
                        b = bank()

                        def mm(e):
                            ins = None
                            for c in range(NCH):
                                ins = e.matmul(psum[:, b, :], lhsT=wh[3][:, c, :], rhs=hT[:, c, tk], start=(c == 0), stop=(c == NCH - 1))
                            return ins
                        T.op("pe", mm, reads=[whB[3], hB[n]], writes=[pB[b]])
                        T.op("act", lambda e: e.activation(out=rt[:, :], in_=psum[:, b, :], func=AF.Silu), reads=[pB[b]], writes=[rtB])
                        T.op("dve", lambda e: e.tensor_scalar_mul(out=gsil[:, tk], in0=rt[:, :], scalar1=onorm[:, 0:1]),
                             reads=[rtB, gcB], writes=[gsilB[n]])

                    for gi in range(4):
                        t0 = gi * 4
                        gk = slice(gi * 512, (gi + 1) * 512)
                        cks = [slice((t0 + c) * 128, (t0 + c + 1) * 128) for c in range(4)]
                        qB, kB_, vB_ = qkvB[0][gi], qkvB[1][gi], qkvB[2][gi]

                        def prep(c):
                            t = t0 + c
                            ck = cks[c]
                            ba, bb = 2 * c, 2 * c + 1
                            DT, decF, EG = Fm[0][:, c, :], Fm[1][:, c, :], Fm[2][:, c, :]
                            gc = gcum[:, t, h:h + 1]
                            T.op("pe", lambda e: e.matmul(psum[:, ba, 0:128], lhsT=gt[:, t, h:h + 1].to_broadcast([128, 128]), rhs=triu[:, :],
                                                          start=True, stop=True), reads=[gB, gcB], writes=[pB[ba]])
                            yield
                            T.op("dve", lambda e: e.scalar_tensor_tensor(out=DT, in0=psum[:, ba, 0:128], scalar=gc, in1=smask[:, :],
                                                                         op0=ALU.subtract, op1=ALU.add), reads=[pB[ba], gB, gcB], writes=[FB[0][c]])
                            yield
                            T.op("act", lambda e: e.activation(out=EG, in_=psum[:, ba, 0:128], func=AF.Exp), reads=[pB[ba]], writes=[FB[2][c]])
                            yield
                            T.op("act", lambda e: e.activation(out=DT, in_=DT, func=AF.Exp), reads=[FB[0][c]], writes=[FB[0][c]])
                            yield
                            T.op("pool", lambda e: e.tensor_tensor(out=decF, in0=DT, in1=ident_f[:, :], op=ALU.add), reads=[FB[0][c], cB], writes=[FB[1][c]])
                            yield
                            T.op("pe", lambda e: e.matmul(psum[:, bb, 0:128], lhsT=qkv[1][:, ck], rhs=qkv[1][:, ck], start=True, stop=True),
                                 reads=[kB_], writes=[pB[bb]])
                            yield
                            T.op("dve", lambda e: e.scalar_tensor_tensor(out=QT[:, c, :], in0=psum[:, bb, 0:128], scalar=negb[:, t, h:h + 1], in1=DT,
                                                                         op0=ALU.mult, op1=ALU.mult), reads=[pB[bb], gB, FB[0][c]], writes=[QTB[c]])
                            yield
                            T.op("pe", lambda e: e.matmul(psum[:, ba, 0:128], lhsT=qkv[1][:, ck], rhs=qkv[0][:, ck], start=True, stop=True),
                                 reads=[kB_, qB], writes=[pB[ba]])
                            yield
                            T.op("dve", lambda e: e.tensor_tensor(out=Aqk[:, c, :], in0=psum[:, ba, 0:128], in1=decF, op=ALU.mult),
                                 reads=[pB[ba], FB[1][c]], writes=[AqkB[c]])
                            yield
                            T.op("pe", lambda e: e.matmul(psum[:, bb, 0:128], lhsT=QT[:, c, :], rhs=ident_b[:, :], start=True, stop=True),
                                 reads=[QTB[c], cB], writes=[pB[bb]])
                            yield
                            T.op("dve", lambda e: e.tensor_tensor(out=PQ[0][:, 0, c, :], in0=psum[:, bb, 0:128], in1=ident_f[:, :], op=ALU.add),
                                 reads=[pB[bb], cB], writes=[PQB[0][c]])
                            yield
                            T.op("pool", lambda e: e.tensor_tensor(out=PQ[0][:, 1, c, :], in0=QT[:, c, :], in1=ident_b[:, :], op=ALU.add),
                                 reads=[QTB[c], cB], writes=[PQB[0][c]])
                            yield
                            T.op("pool", lambda e: e.tensor_tensor(out=ATt[:, c, :], in0=ident_b[:, :], in1=QT[:, c, :], op=ALU.subtract),
                                 reads=[QTB[c], cB], writes=[ATtB[c]])
                            yield
                            T.op("pe", lambda e: e.matmul(psum[:, ba, 0:128], lhsT=qkv[1][:, ck], rhs=ident_b[:, :], start=True, stop=True),
                                 reads=[kB_, cB], writes=[pB[ba]])
                            yield
                            T.op("dve", lambda e: e.tensor_scalar_mul(out=kdec[:, c, :], in0=psum[:, ba, 0:128], scalar1=kds[:, t, h:h + 1]),
                                 reads=[pB[ba], gB], writes=[kdecB[c]])
                            yield
                            T.op("dve", lambda e: e.tensor_scalar_mul(out=VK[:, c, 128:256], in0=psum[:, ba, 0:128], scalar1=egc[:, t, h:h + 1]),
                                 reads=[pB[ba], gB], writes=[VKB[c]])
                            yield
                            T.op("pe", lambda e: e.matmul(psum[:, bb, 0:128], lhsT=qkv[2][:, ck], rhs=ident_b[:, :], start=True, stop=True),
                                 reads=[vB_, cB], writes=[pB[bb]])
                            yield
                            T.op("act", lambda e: e.activation(out=VK[:, c, 0:128], in_=psum[:, bb, 0:128], func=AF.Copy), reads=[pB[bb]], writes=[VKB[c]])
                            yield
                            cur = 0
                            for k in range(2, 8):
                                nxt = 1 - cur
                                last = (k == 7)
                                T.op("pe", lambda e: e.matmul(psum[:, ba, 0:128], lhsT=ATt[:, c, :], rhs=PQ[cur][:, 0, c, :], start=True, stop=True),
                                     reads=[ATtB[c], PQB[cur][c]], writes=[pB[ba]])
                                yield
                                T.op("dve", lambda e: e.scalar_tensor_tensor(out=Wt[:, c, :], in0=psum[:, ba, 0:128], scalar=-1.0, in1=ident2_f[:, :],
                                                                             op0=ALU.mult, op1=ALU.add), reads=[pB[ba], cB], writes=[WtB[c]])
                                yield

                                def mmn(e):
                                    ins = e.matmul(psum[:, bb, 0:128], lhsT=Wt[:, c, :], rhs=PQ[cur][:, 1, c, :], start=True, stop=True)
                                    if not last:
                                        ins = e.matmul(psum[:, ba, 0:128], lhsT=PQ[cur][:, 1, c, :], rhs=Wt[:, c, :], start=True, stop=True)
                                    return ins
                                T.op("pe", mmn, reads=[WtB[c], PQB[cur][c]], writes=[pB[ba], pB[bb]])
                                yield
                                if last:
                                    T.op("act", lambda e: e.activation(out=QT[:, c, :], in_=psum[:, bb, 0:128], func=AF.Copy), reads=[pB[bb]], writes=[QTB[c]])
                                else:
                                    T.op("act", lambda e: e.activation(out=PQ[nxt][:, :, c, :], in_=psum[:, ba:bb + 1, 0:128], func=AF.Copy),
                                         reads=[pB[ba], pB[bb]], writes=[PQB[nxt][c]])
                                yield
                                cur = nxt
                            T.op("pe", lambda e: e.matmul(psum[:, ba, 0:256], lhsT=QT[:, c, :], rhs=VK[:, c, :], start=True, stop=True),
                                 reads=[QTB[c], VKB[c]], writes=[pB[ba]])
                            yield
                            T.op("dve", lambda e: e.tensor_scalar_mul(out=Zt[:, c, :], in0=psum[:, ba, 0:256], scalar1=beta[:, t, h:h + 1]),
                                 reads=[pB[ba], gB], writes=[ZtB[c]])
                            yield
                            T.op("pe", lambda e: e.matmul(psum[:, bb, 0:128], lhsT=kdec[:, c, :], rhs=Zt[:, c, 0:128], start=True, stop=True),
                                 reads=[kdecB[c], ZtB[c]], writes=[pB[bb]])
                            yield
                            T.op("act", lambda e: e.activation(out=Fm[3][:, c, :], in_=psum[:, bb, 0:128], func=AF.Copy), reads=[pB[bb]], writes=[FB[3][c]])
                            yield
                            T.op("pe", lambda e: e.matmul(psum[:, ba, 0:128], lhsT=Zt[:, c, 128:256], rhs=kdec[:, c, :], start=True, stop=True),
                                 reads=[kdecB[c], ZtB[c]], writes=[pB[ba]])
                            yield
                            T.op("dve", lambda e: e.scalar_tensor_tensor(out=Fm[4][:, c, :], in0=ident_f[:, :], scalar=sdec[:, t, h:h + 1], in1=psum[:, ba, 0:128],
                                                                         op0=ALU.mult, op1=ALU.subtract), reads=[pB[ba], gB, cB], writes=[FB[4][c]])
                            yield
                            T.op("pe", lambda e: e.matmul(psum[:, bb, 0:128], lhsT=Zt[:, c, 0:128], rhs=Aqk[:, c, :], start=True, stop=True),
                                 reads=[AqkB[c], ZtB[c]], writes=[pB[bb]])
                            yield
                            T.op("act", lambda e: e.activation(out=Fm[5][:, c, :], in_=psum[:, bb, 0:128], func=AF.Copy), reads=[pB[bb]], writes=[FB[5][c]])
                            yield
                            T.op("pe", lambda e: e.matmul(psum[:, ba, 0:128], lhsT=Zt[:, c, 128:256], rhs=Aqk[:, c, :], start=True, stop=True),
                                 reads=[AqkB[c], ZtB[c]], writes=[pB[ba]])
                            yield
                            T.op("pool", lambda e: e.tensor_tensor(out=decF, in0=qkv[0][:, ck], in1=EG, op=ALU.mult), reads=[qB, FB[2][c], FB[1][c]], writes=[FB[1][c]])
                            yield
                            T.op("dve", lambda e: e.tensor_tensor(out=Qeff[:, c, :], in0=decF, in1=psum[:, ba, 0:128], op=ALU.subtract),
                                 reads=[pB[ba], FB[1][c]], writes=[QeffB[c]])
                            yield

                        gens = [prep(c) for c in range(4)]
                        while gens:
                            for g_ in list(gens):
                                try:
                                    next(g_)
                                except StopIteration:
                                    gens.remove(g_)
                        pstate["n"] = 0
                        Hs, ATs, O0s = Fm[3], Fm[4], Fm[5]
                        bOo = bank()
                        for c in range(4):
                            t = t0 + c
                            if t > 0:
                                T.op("pe", lambda e: e.matmul(psum[:, bOo, c * 128:(c + 1) * 128], lhsT=Sb[:, :], rhs=Qeff[:, c, :], start=True, stop=True),
                                     reads=[SbB, QeffB[c]], writes=[pB[bOo]])
                            if t == 15:
                                continue
                            if t == 0:
                                T.op("dve", lambda e: e.tensor_copy(out=Sf[:, :], in_=Hs[:, 0, :]), reads=[FB[3][0]], writes=[SfB])
                            else:
                                bS = bank()
                                T.op("pe", lambda e: e.matmul(psum[:, bS, 0:128], lhsT=ATs[:, c, :], rhs=Sf[:, :], start=True, stop=True),
                                     reads=[FB[4][c], SfB], writes=[pB[bS]])
                                T.op("dve", lambda e: e.tensor_tensor(out=Sf[:, :], in0=psum[:, bS, 0:128], in1=Hs[:, c, :], op=ALU.add),
                                     reads=[pB[bS], FB[3][c]], writes=[SfB])
                            T.op("act", lambda e: e.activation(out=Sb[:, :], in_=Sf[:, :], func=AF.Copy), reads=[SfB], writes=[SbB])
                        osum, otmp, rr = Fm[0], Fm[1], Fm[2]
                        if gi == 0:
                            T.op("dve", lambda e: e.tensor_copy(out=osum[:, 0, :], in_=O0s[:, 0, :]), reads=[FB[5][0]], writes=[FB[0][0]])
                            T.op("dve", lambda e: e.tensor_tensor(out=osum[:, 1:4, :], in0=pv(bOo)[:, 1:4, :], in1=O0s[:, 1:4, :], op=ALU.add),
                                 reads=[pB[bOo]] + FB[5][1:4], writes=FB[0][1:4])
                        else:
                            T.op("dve", lambda e: e.tensor_tensor(out=osum[:], in0=pv(bOo), in1=O0s[:], op=ALU.add), reads=[pB[bOo]] + FB[5], writes=FB[0])
                        T.op("act", lambda e: e.activation(out=osq[:, :], in_=fl(osum), func=AF.Square), reads=FB[0], writes=[osqB])
                        b2 = bank()
                        T.op("pe", lambda e: e.matmul(psum[:, b2, :], lhsT=odiv[:, :], rhs=osq[:, :], start=True, stop=True), reads=[osqB, gcB], writes=[pB[b2]])
                        T.op("act", lambda e: e.activation(out=fl(rr), in_=psum[:, b2, :], func=AF.Ln, bias=eps_t[:, 0:1]), reads=[pB[b2], cB], writes=FB[2])
                        T.op("act", lambda e: e.activation(out=fl(rr), in_=fl(rr), func=AF.Exp, scale=-0.5), reads=FB[2], writes=FB[2])
                        T.op("dve", lambda e: e.tensor_tensor(out=otmp[:], in0=osum[:], in1=rr[:], op=ALU.mult), reads=FB[0] + FB[2], writes=FB[1])
                        T.op("pool", lambda e: e.tensor_tensor(out=ybT[:, h, gk], in0=fl(otmp), in1=gsil[:, gk], op=ALU.mult),
                             reads=FB[1] + [gsilB[gi]], writes=[ybB[h]])
                T.barrier()

        def merge(hT, hB, yaT, yaB, ybT, ybB):
            with contextlib.ExitStack() as gs:
                def gsb(name, shape, dt):
                    return gs.enter_context(nc.sbuf_tensor("x_" + name, list(shape), dt))
                mg = gsb("mg", [128, 8, S], BF16)
                mgB = [[Buf() for _ in range(NB)] for _ in range(8)]
                wga = [gsb("wga%d" % i, [128, 2, NCH, 128], BF16) for i in range(2)]
                wba = [gsb("wba%d" % i, [64, 4, 128], BF16) for i in range(2)]
                wbb = [gsb("wbb%d" % i, [128, 8, 128], BF16) for i in range(2)]
                wB = [Buf(), Buf()]
                wo = [gsb("wo%d" % i, [128, 8, 128], BF16) for i in range(2)]
                woB = [Buf(), Buf()]
                one_c = eps_t[:, 1:2]
                sa = gsb("sa", [128, 512], F32)
                sb_ = gsb("sb", [128, 512], F32)
                ta = gsb("ta", [128, 512], F32)
                tb = gsb("tb", [128, 512], F32)
                saB, sbB, taB, tbB = Buf(), Buf(), Buf(), Buf()

                def loadw(dt):
                    i = dt % 2
                    T.dma("pool", wga[i][:].rearrange("p w c f -> p (w c f)"), dt_in["wgab"][dt], writes=[wB[i]])
                    T.dma("pool", wba[i][:].rearrange("p s f -> p (s f)"), dt_in["wba"][dt], writes=[wB[i]])
                    T.dma("pool", wbb[i][:].rearrange("p s f -> p (s f)"), dt_in["wbb"][dt], writes=[wB[i]])
                loadw(0)
                loadw(1)
                for dt in range(8):
                    wi = dt % 2
                    for n in range(NB):
                        tk = slice(n * 512, (n + 1) * 512)
                        bga, bgb, ba, bb = bank(), bank(), bank(), bank()

                        def mmg(e, which, b):
                            ins = None
                            for c in range(NCH):
                                ins = e.matmul(psum[:, b, :], lhsT=wga[wi][:, which, c, :], rhs=hT[:, c, tk], start=(c == 0), stop=(c == NCH - 1))
                            return ins
                        T.op("pe", lambda e: mmg(e, 0, bga), reads=[wB[wi], hB[n]], writes=[pB[bga]])
                        T.op("pe", lambda e: mmg(e, 1, bgb), reads=[wB[wi], hB[n]], writes=[pB[bgb]])

                        def mma(e):
                            ins = None
                            for s4 in range(4):
                                ins = e.matmul(psum[:, ba, :], lhsT=wba[wi][:, s4, :], rhs=yaT[:, s4, tk], start=(s4 == 0), stop=(s4 == 3))
                            return ins
                        T.op("pe", mma, reads=[wB[wi]] + yaB, writes=[pB[ba]])

                        def mmb(e):
                            ins = None
                            for h in range(8):
                                ins = e.matmul(psum[:, bb, :], lhsT=wbb[wi][:, h, :], rhs=ybT[:, h, tk], start=(h == 0), stop=(h == 7))
                            return ins
                        T.op("pe", mmb, reads=[wB[wi]] + ybB, writes=[pB[bb]])
                        T.op("act", lambda e: e.activation(out=sa[:, :], in_=psum[:, bga, :], func=AF.Sigmoid), reads=[pB[bga]], writes=[saB])
                        T.op("act", lambda e: e.activation(out=sb_[:, :], in_=psum[:, bgb, :], func=AF.Sigmoid), reads=[pB[bgb]], writes=[sbB])
                        T.op("dve", lambda e: e.tensor_tensor(out=ta[:, :], in0=psum[:, ba, :], in1=sa[:, :], op=ALU.mult), reads=[pB[ba], saB], writes=[taB])
                        T.op("dve", lambda e: e.tensor_tensor(out=tb[:, :], in0=psum[:, bb, :], in1=sb_[:, :], op=ALU.mult), reads=[pB[bb], sbB], writes=[tbB])
                        T.op("pool", lambda e: e.tensor_tensor(out=mg[:, dt, tk], in0=ta[:, :], in1=tb[:, :], op=ALU.add), reads=[taB, tbB], writes=[mgB[dt][n]])
                    if dt + 2 < 8:
                        loadw(dt + 2)

                def loado(dt):
                    T.dma("pool", wo[dt % 2][:].rearrange("p s f -> p (s f)"), dt_in["wout"][dt], writes=[woB[dt % 2]])
                loado(0)
                loado(1)
                for dt in range(8):
                    wi = dt % 2
                    for n in range(NB):
                        tk = slice(n * 512, (n + 1) * 512)
                        b = bank()

                        def mmo(e):
                            ins = None
                            for k in range(8):
                                ins = e.matmul(psum[:, b, :], lhsT=wo[wi][:, k, :], rhs=mg[:, k, tk], start=(k == 0), stop=(k == 7))
                            return ins
                        T.op("pe", mmo, reads=[woB[wi]] + [mgB[k][n] for k in range(8)], writes=[pB[b]])
                        T.op("dve", lambda e: e.tensor_tensor(out=xT[:, dt, tk], in0=psum[:, b, :], in1=xT[:, dt, tk], op=ALU.add),
                             reads=[pB[b]], writes=[xB[n]])
                    if dt + 2 < 8:
                        loado(dt + 2)
                T.barrier()

        def dbg_dump(slabs):
            with contextlib.ExitStack() as ds:
                dtile = [ds.enter_context(nc.sbuf_tensor("dbg%d" % i, [128, S], F32)) for i in range(2)]
                dB = [Buf(), Buf()]
                for k, (ap, bufs, row0, P) in enumerate(slabs):
                    i = k % 2
                    T.op("act", lambda e: e.activation(out=dtile[i][0:P, :], in_=ap, func=AF.Copy), reads=bufs, writes=[dB[i]])
                    T.dma("sp", out_d[row0:row0 + P, :], dtile[i][0:P, :], reads=[dB[i]])
                T.barrier()

        ffn(0, 0)
        done = False
        if stop_after not in ("ffn1_raw", "ffn1"):
            r = mixer()
            if r is not None:
                done = True
        if stop_after in ("all", "x3_raw"):
            ffn(1, 2)
        if not done:
          with contextlib.ExitStack() as fs:
            yT = [fs.enter_context(nc.sbuf_tensor("yT%d" % i, [128, NCH, 512], F32)) for i in range(2)]
            yB = [Buf(), Buf()]
            sq_t = fs.enter_context(nc.sbuf_tensor("fsq", [128, NCH, 512], BF16))
            sq_b = [Buf() for _ in range(NCH)]
            rs_t = fs.enter_context(nc.sbuf_tensor("frs", [128, 512], F32))
            rs_b = Buf()
            out_dv = out_d.rearrange("(c p) t -> p c t", p=128)
            for n in range(NB):
                i = n % 2
                if stop_after.endswith("_raw"):
                    for c in range(NCH):
                        T.op("dve", lambda e: e.tensor_copy(out=yT[i][:, c, :], in_=xT[:, c, n * 512:(n + 1) * 512]),
                             reads=[xB[n]], writes=[yB[i]])
                else:
                    rmsnorm_block(n, 3, lambda c: yT[i][:, c, :], yB[i], sq_t, sq_b, rs_t, rs_b)
                T.dma("sp", out_dv[:, :, n * 512:(n + 1) * 512], yT[i][:], reads=[yB[i]])
            T.barrier()
    return nc, set(dt_in.keys())


def _consts():
    c = {}
    c["ident_f"] = np.eye(128, dtype=np.float32)
    c["ones_f"] = np.ones((128, 128), dtype=np.float32)
    half = 32
    inv_freq = (10000.0 ** (-np.arange(half, dtype=np.float32) / half)).astype(np.float32)
    pos = np.arange(S, dtype=np.float32)
    ang = (pos[None, :] * inv_freq[:, None]).astype(np.float32)
    cosv, sinv = np.cos(ang).astype(np.float32), np.sin(ang).astype(np.float32)
    cos_t = np.zeros((128, S), np.float32)
    sin_t = np.zeros((128, S), np.float32)
    rm = np.zeros((128, 128), np.float32)
    for p in range(128):
        d = p % 64
        cos_t[p] = cosv[d % 32]
        sin_t[p] = -sinv[d % 32] if d < 32 else sinv[d % 32]
        partner = p + 32 if d < 32 else p - 32
        rm[partner, p] = 1.0
    c["rope_cos"], c["rope_sin"], c["rm"] = cos_t, sin_t, rm
    j = np.arange(128)[:, None]
    i = np.arange(128)[None, :]
    mprev = (j >= i).astype(np.float32)
    mcur = (j <= i).astype(np.float32)
    c["amask"] = np.concatenate([mprev, mcur, mprev, mcur], axis=1)
    em = np.zeros((128, 64), np.float32)
    em[64, :] = 1.0
    c["em"] = em
    c["triu"] = (j <= i).astype(np.float32)
    c["smask"] = np.where(i > j, 0.0, -30000.0).astype(np.float32)
    c["odiv"] = np.full((128, 128), 1.0 / 128.0, np.float32)
    return c


def _prep_inputs(inp):
    f = lambda a: np.ascontiguousarray(a, dtype=np.float32)
    shared = {}

    def ncol(v):
        return f(np.asarray(v).reshape(NCH, 128).T)
    shared["ffn1_norm"] = ncol(inp["ffn1_norm"][0])
    shared["ffn2_norm"] = ncol(inp["ffn2_norm"][0])
    shared["mix_norm"] = ncol(inp["mix_norm"][0])
    shared["final_norm"] = ncol(inp["final_norm"])
    ffn_in = {1: (inp["ffn1_w_gate"], inp["ffn1_w_up"], inp["ffn1_w_down"]),
              2: (inp["ffn2_w_gate"], inp["ffn2_w_up"], inp["ffn2_w_down"])}
    for i in (1, 2):
        wg = np.asarray(ffn_in[i][0][0])
        wu = np.asarray(ffn_in[i][1][0])
        wd = np.asarray(ffn_in[i][2][0])
        shared["wg%d" % i] = f(wg.reshape(NCH, 128, 11, 256).transpose(2, 1, 0, 3).reshape(11, 128, NCH * 256))
        shared["wu%d" % i] = f(wu.reshape(NCH, 128, 11, 256).transpose(2, 1, 0, 3).reshape(11, 128, NCH * 256))
        shared["wd%d" % i] = f(wd.reshape(NF, 128, 8, 128).transpose(2, 1, 0, 3).reshape(8, 128, NF * 128))
    w_in = np.asarray(inp["w_in"][0])

    def cols(c0, n):
        return w_in[:, c0:c0 + n].reshape(NCH, 128, n).transpose(1, 0, 2).reshape(128, NCH * n)
    wa = np.zeros((6, 3, 128, NCH * 128), np.float32)
    for g in range(3):
        for hp in range(2):
            for i in range(3):
                wa[g * 2 + hp, i] = cols(i * 768 + (4 * g + 2 * hp) * 64, 128)
    shared["wqkv_a"] = wa
    wg_ = np.zeros((8, 3, 128, NCH * 128), np.float32)
    wgg = np.zeros((8, 128, NCH * 128), np.float32)
    for h in range(8):
        for i in range(3):
            wg_[h, i] = cols(2304 + i * 1024 + h * 128, 128)
        wgg[h] = cols(5392 + h * 128, 128)
    shared["wgdn"] = wg_
    shared["wgg"] = wgg
    shared["wbd"] = f(cols(5376, 16))
    wgab = np.zeros((8, 128, 2 * NCH * 128), np.float32)
    for dt in range(8):
        a = cols(6416 + dt * 128, 128).reshape(128, 1, NCH * 128)
        b = cols(7440 + dt * 128, 128).reshape(128, 1, NCH * 128)
        wgab[dt] = np.concatenate([a, b], axis=1).reshape(128, 2 * NCH * 128)
    shared["wgab"] = wgab
    wba = np.asarray(inp["w_branch_a"][0])
    shared["wba"] = f(wba.reshape(4, 64, 8, 128).transpose(2, 1, 0, 3).reshape(8, 64, 4 * 128))
    wbb = np.asarray(inp["w_branch_b"][0])
    shared["wbb"] = f(wbb.reshape(8, 128, 8, 128).transpose(2, 1, 0, 3).reshape(8, 128, 8 * 128))
    wo = np.asarray(inp["w_out"][0])
    shared["wout"] = f(wo.reshape(8, 128, 8, 128).transpose(2, 1, 0, 3).reshape(8, 128, 8 * 128))
    cw = np.asarray(inp["gdn_conv_w"][0])
    shared["convw"] = f(cw.reshape(4, 3, 8, 128).transpose(3, 2, 1, 0).reshape(128, 96))
    shared["alog"] = f(np.broadcast_to(np.asarray(inp["gdn_a_log"][0])[None, :], (128, 8)))
    shared["dtb"] = f(np.broadcast_to(np.asarray(inp["gdn_dt_bias"][0])[None, :], (128, 8)))
    shared["onorm"] = f(np.asarray(inp["gdn_out_norm"][0]).reshape(128, 1))
    shared.update(_consts())
    x = np.asarray(inp["x"])
    maps = []
    for b in range(8):
        m = dict(shared)
        m["xT"] = f(x[b].T)
        maps.append(m)
    return maps


_NC_CACHE = {}


def kernel(**inputs):
    stop_after = inputs.pop("_stop_after", "all")
    if stop_after not in _NC_CACHE:
        _NC_CACHE[stop_after] = build_program(stop_after)
    nc, names = _NC_CACHE[stop_after]
    maps = _prep_inputs(inputs)
    maps = [{k: v for k, v in m.items() if k in names} for m in maps]
    res = run_bass_kernel_spmd(nc, maps, core_ids=list(range(8)))
    out = np.stack([np.ascontiguousarray(r["outT"].T) for r in res.results], axis=0)
    return out.astype(np.float32)
```

```python
import contextlib
import numpy as np
import concourse.bass as bass
import concourse.mybir as mybir
from concourse.bass_utils import run_bass_kernel_spmd

F32 = mybir.dt.float32
BF16 = mybir.dt.bfloat16
AF = mybir.ActivationFunctionType
ALU = mybir.AluOpType
AX = mybir.AxisListType

S = 2048
D = 1024
NCH = 8
DFF = 2816
NF = 22
NB = 4
EPS = 1e-6
D_IN = 8464
NDS = 16
import os
ROPE_ADD_ENG = os.environ.get("ROPE_ADD_ENG", "pool")


class _Stop(Exception):
    pass


class Buf:
    __slots__ = ("w", "r", "excl")

    def __init__(self, excl=False):
        self.w = None
        self.r = []
        self.excl = excl


class Trk:
    def __init__(self, nc, es):
        self.nc = nc
        self.eng = {"pe": nc.tensor, "act": nc.scalar, "dve": nc.vector, "pool": nc.gpsimd, "sp": nc.sync}
        self.sem = {k: es.enter_context(nc.semaphore("sem_" + k)) for k in self.eng}
        self.cnt = {k: 0 for k in self.eng}
        self.seen = {k: {} for k in self.eng}
        self.dsem = [es.enter_context(nc.semaphore("dsem%d" % i)) for i in range(NDS)]
        self.dcnt = [0] * NDS
        self.dnext = {"pool": 0, "sp": NDS // 2, "act": NDS // 2}
        self.same = {"pe": False, "act": True, "dve": True, "pool": True, "sp": True}

    def _semof(self, k):
        return self.sem[k] if isinstance(k, str) else self.dsem[k[1]]

    def _wait(self, e, deps):
        best = {}
        for (k, v) in deps:
            if v > best.get(k, 0):
                best[k] = v
        for k, v in best.items():
            if k == e and not self.same[e]:
                continue
            if self.seen[e].get(k, 0) >= v:
                continue
            self.eng[e].wait_ge(self._semof(k), v)
            self.seen[e][k] = v

    @staticmethod
    def _deps(reads, writes):
        deps = []
        for b in reads:
            if b.w is not None:
                deps.append(b.w)
        for b in writes:
            if b.w is not None:
                deps.append(b.w)
            deps.extend(b.r)
        return deps

    @staticmethod
    def _upd(d, reads, writes):
        for b in reads:
            b.r.append(d)
            if len(b.r) > 64:
                best = {}
                for (k, v) in b.r:
                    if v > best.get(k, 0):
                        best[k] = v
                b.r = list(best.items())
        for b in writes:
            b.w = d
            b.r = []

    def op(self, e, fn, reads=(), writes=()):
        if any(b.excl for b in reads):
            writes = list(writes) + [b for b in reads if b.excl]
            reads = [b for b in reads if not b.excl]
        self._wait(e, self._deps(reads, writes))
        ins = fn(self.eng[e])
        self.cnt[e] += 1
        ins.then_inc(self.sem[e], 1)
        self._upd((e, self.cnt[e]), reads, writes)

    def dma(self, q, out, in_, reads=(), writes=()):
        deps = self._deps(reads, writes)
        i = self.dnext[q]
        lo = 0 if q == "pool" else NDS // 2
        self.dnext[q] = lo + (i - lo + 1) % (NDS // 2)
        if self.dcnt[i] > 0:
            deps.append((("d", i), self.dcnt[i]))
        self._wait(q, deps)
        self.eng[q].dma_start(out=out, in_=in_).then_inc(self.dsem[i], 16)
        self.dcnt[i] += 16
        self._upd((("d", i), self.dcnt[i]), reads, writes)

    def barrier(self):
        deps = [(k, v) for k, v in self.cnt.items() if v > 0]
        deps += [(("d", i), c) for i, c in enumerate(self.dcnt) if c > 0]
        for e in self.eng:
            self._wait(e, [d for d in deps if d[0] != e])


def build_program(stop_after="all", debug=False):
    nc = bass.Bass("TRN2", target_bir_lowering=False)
    dt_in = {}

    def din(name, shape):
        dt_in[name] = nc.dram_tensor(name, list(shape), F32, kind="ExternalInput").ap()
        return dt_in[name]

    xT_d = din("xT", [D, S])
    n1_d = din("ffn1_norm", [128, NCH])
    n2_d = din("ffn2_norm", [128, NCH])
    nm_d = din("mix_norm", [128, NCH])
    nf_d = din("final_norm", [128, NCH])
    ffn_w = []
    for i in (1, 2):
        ffn_w.append((din("wg%d" % i, [11, 128, NCH * 256]), din("wu%d" % i, [11, 128, NCH * 256]),
                      din("wd%d" % i, [8, 128, NF * 128])))
    din("wqkv_a", [6, 3, 128, NCH * 128])
    din("rope_cos", [128, S]); din("rope_sin", [128, S]); din("rm", [128, 128]); din("amask", [128, 512]); din("em", [128, 64])
    din("wgdn", [8, 3, 128, NCH * 128]); din("wgg", [8, 128, NCH * 128]); din("wbd", [128, NCH * 16])
    din("convw", [128, 96]); din("alog", [128, 8]); din("dtb", [128, 8]); din("onorm", [128, 1])
    din("triu", [128, 128]); din("smask", [128, 128]); din("odiv", [128, 128])
    din("wgab", [8, 128, 2 * NCH * 128]); din("wba", [8, 64, 4 * 128]); din("wbb", [8, 128, 8 * 128]); din("wout", [8, 128, 8 * 128])
    ident_d = din("ident_f", [128, 128])
    ones_d = din("ones_f", [128, 128])
    out_d = nc.dram_tensor("outT", [D, S], F32, kind="ExternalOutput").ap()
    dbg = {}

    with contextlib.ExitStack() as es:
        T = Trk(nc, es)

        def sb(name, shape, dt):
            return es.enter_context(nc.sbuf_tensor(name, list(shape), dt))

        xT = sb("xT_sb", [128, NCH, S], F32)
        xB = [Buf() for _ in range(NB)]
        psum = es.enter_context(nc.psum_tensor("psum", [128, 8, 512], F32))
        pB = [Buf(True) for _ in range(8)]
        pstate = {"n": 0}

        def bank():
            b = pstate["n"]
            pstate["n"] = (b + 1) % 8
            return b

        def bank2():
            b = pstate["n"]
            if b % 2:
                b = (b + 1) % 8
            pstate["n"] = (b + 2) % 8
            return b

        ident_f = sb("ident_f_sb", [128, 128], F32)
        ident_b = sb("ident_b_sb", [128, 128], BF16)
        ones_b = sb("ones_b_sb", [128, 128], BF16)
        ones_f = sb("ones_f_sb", [128, 128], F32)
        ident2_f = sb("ident2_f_sb", [128, 128], F32)
        cB = Buf()
        norms = sb("norms_sb", [128, 4, NCH], F32)
        eps_t = sb("eps_sb", [128, 2], F32)
        T.op("dve", lambda e: e.memset(eps_t[:, 0:1], EPS), writes=[cB])
        T.op("dve", lambda e: e.memset(eps_t[:, 1:2], 1.0), writes=[cB])
        T.dma("sp", ident_f[:], ident_d, writes=[cB])
        T.dma("sp", ones_f[:], ones_d, writes=[cB])
        T.op("dve", lambda e: e.tensor_single_scalar(out=ident2_f[:, :], in_=ident_f[:, :], scalar=2.0, op=ALU.mult), reads=[cB], writes=[cB])
        T.dma("pool", ident_b[:], ident_d, writes=[cB])
        T.dma("pool", ones_b[:], ones_d, writes=[cB])
        for i, nd in enumerate((n1_d, nm_d, n2_d, nf_d)):
            T.dma("sp", norms[:, i, :], nd, writes=[cB])
        xT_dv = xT_d.rearrange("(c p) t -> p c t", p=128)
        for n in range(NB):
            T.dma("sp", xT[:, :, n * 512:(n + 1) * 512], xT_dv[:, :, n * 512:(n + 1) * 512], writes=[xB[n]])

        def rmsnorm_block(n, gi, out_ap_fn, out_buf, sq_t, sq_b, rs_t, rs_b):
            tok = slice(n * 512, (n + 1) * 512)
            b = bank()
            for c in range(NCH):
                T.op("act", lambda e, c=c: e.activation(out=sq_t[:, c, :], in_=xT[:, c, tok], func=AF.Square),
                     reads=[xB[n]], writes=[sq_b[c]])
            def mm(e):
                ins = None
                for c in range(NCH):
                    ins = e.matmul(psum[:, b, :], lhsT=ones_b[:, :], rhs=sq_t[:, c, :], start=(c == 0), stop=(c == NCH - 1))
                return ins
            T.op("pe", mm, reads=sq_b + [cB], writes=[pB[b]])
            T.op("act", lambda e: e.activation(out=rs_t[:, :], in_=psum[:, b, :], func=AF.Ln, scale=1.0 / D, bias=eps_t[:, 0:1]),
                 reads=[pB[b], cB], writes=[rs_b])
            T.op("act", lambda e: e.activation(out=rs_t[:, :], in_=rs_t[:, :], func=AF.Exp, scale=-0.5), reads=[rs_b], writes=[rs_b])
            for c in range(NCH):
                T.op("dve", lambda e, c=c: e.scalar_tensor_tensor(out=out_ap_fn(c), in0=xT[:, c, tok],
                                                                  scalar=norms[:, gi, c:c + 1], in1=rs_t[:, :],
                                                                  op0=ALU.mult, op1=ALU.mult),
                     reads=[xB[n], rs_b, cB], writes=[out_buf])

        def ffn(which, gi):
            wg_d, wu_d, wd_d = ffn_w[which]
            with contextlib.ExitStack() as fs:
                def fsb(name, shape, dt):
                    return fs.enter_context(nc.sbuf_tensor("f%d_%s" % (which, name), list(shape), dt))
                hT = fsb("hT", [128, NCH, 1024], BF16)
                hB = [Buf(), Buf()]
                actT = fsb("actT", [128, NF, 1024], BF16)
                aB = [[Buf(), Buf()] for _ in range(NF)]
                sq_t = fsb("sq", [128, NCH, 512], BF16)
                sq_b = [Buf() for _ in range(NCH)]
                rs_t = fsb("rs", [128, 512], F32)
                rs_b = Buf()
                NWB = 3
                wgb = [fsb("wg%d" % i, [128, NCH, 256], BF16) for i in range(NWB)]
                wub = [fsb("wu%d" % i, [128, NCH, 256], BF16) for i in range(NWB)]
                wgB = [Buf() for _ in range(NWB)]
                wuB = [Buf() for _ in range(NWB)]
                wdb = [fsb("wd%d" % i, [128, NF, 128], BF16) for i in range(2)]
                wdB = [Buf() for _ in range(2)]
                sg = [fsb("sg%d" % i, [128, 512], F32) for i in range(3)]
                sgB = [Buf() for _ in range(3)]
                sgn = 0

                def load_gu(g):
                    i = g % NWB
                    T.dma("pool", wgb[i][:].rearrange("p c f -> p (c f)"), wg_d[g], writes=[wgB[i]])
                    T.dma("pool", wub[i][:].rearrange("p c f -> p (c f)"), wu_d[g], writes=[wuB[i]])

                def load_d(dt):
                    i = dt % 2
                    T.dma("pool", wdb[i][:].rearrange("p f j -> p (f j)"), wd_d[dt], writes=[wdB[i]])

                for half in range(2):
                    load_gu(0)
                    load_gu(1)
                    for nn in range(2):
                        n = half * 2 + nn
                        rmsnorm_block(n, gi, lambda c, nn=nn: hT[:, c, nn * 512:(nn + 1) * 512], hB[nn],
                                      sq_t, sq_b, rs_t, rs_b)
                    for g in range(11):
                        if g + 2 < 11:
                            load_gu(g + 2)
                        if g == 9:
                            load_d(0)
                        if g == 10:
                            load_d(1)
                        wi = g % NWB
                        for j in range(2):
                            f = 2 * g + j
                            for nn in range(2):
                                tk = slice(nn * 512, (nn + 1) * 512)
                                bg = bank()
                                bu = bank()

                                def mmg(e, w=wgb[wi], b=bg, j=j, tk=tk):
                                    ins = None
                                    for c in range(NCH):
                                        ins = e.matmul(psum[:, b, :], lhsT=w[:, c, j * 128:(j + 1) * 128],
                                                       rhs=hT[:, c, tk], start=(c == 0), stop=(c == NCH - 1))
                                    return ins
                                T.op("pe", mmg, reads=[wgB[wi], hB[nn]], writes=[pB[bg]])
                                T.op("pe", lambda e, b=bu, j=j, tk=tk: mmg(e, wub[wi], b, j, tk),
                                     reads=[wuB[wi], hB[nn]], writes=[pB[bu]])
                                si = sgn % 3
                                sgn += 1
                                T.op("act", lambda e, si=si, b=bg: e.activation(out=sg[si][:, :], in_=psum[:, b, :],
                                                                               func=AF.Silu),
                                     reads=[pB[bg]], writes=[sgB[si]])
                                T.op("dve", lambda e, si=si, b=bu, f=f, tk=tk: e.tensor_tensor(
                                    out=actT[:, f, tk], in0=psum[:, b, :], in1=sg[si][:, :], op=ALU.mult),
                                     reads=[pB[bu], sgB[si]], writes=[aB[f][nn]])
                    for dt in range(8):
                        if dt + 2 < 8:
                            pass
                        wi = dt % 2
                        for nn in range(2):
                            n = half * 2 + nn
                            tk = slice(nn * 512, (nn + 1) * 512)
                            b = bank()

                            def mmd(e, b=b, wi=wi, tk=tk):
                                ins = None
                                for f in range(NF):
                                    ins = e.matmul(psum[:, b, :], lhsT=wdb[wi][:, f, :], rhs=actT[:, f, tk],
                                                   start=(f == 0), stop=(f == NF - 1))
                                return ins
                            T.op("pe", mmd, reads=[wdB[wi]] + [aB[f][nn] for f in range(NF)], writes=[pB[b]])
                            T.op("dve", lambda e, b=b, dt=dt, n=n: e.scalar_tensor_tensor(
                                out=xT[:, dt, n * 512:(n + 1) * 512], in0=psum[:, b, :], scalar=0.5,
                                in1=xT[:, dt, n * 512:(n + 1) * 512], op0=ALU.mult, op1=ALU.add),
                                 reads=[pB[b]], writes=[xB[n]])
                        if dt + 2 < 8:
                            load_d(dt + 2)
                T.barrier()

        def mixer():
            with contextlib.ExitStack() as ms:
                def msb(name, shape, dt):
                    return ms.enter_context(nc.sbuf_tensor("m_" + name, list(shape), dt))
                hT = msb("hT", [128, NCH, S], BF16)
                hB = [Buf() for _ in range(NB)]
                with contextlib.ExitStack() as ns:
                    sq_t = ns.enter_context(nc.sbuf_tensor("m_sq", [128, NCH, 512], BF16))
                    sq_b = [Buf() for _ in range(NCH)]
                    rs_t = ns.enter_context(nc.sbuf_tensor("m_rs", [128, 512], F32))
                    rs_b = Buf()
                    for n in range(NB):
                        rmsnorm_block(n, 1, lambda c, n=n: hT[:, c, n * 512:(n + 1) * 512], hB[n], sq_t, sq_b, rs_t, rs_b)
                    T.barrier()
                if stop_after == "dbg_h":
                    dbg_dump([(hT[:, c, :], hB, c * 128, 128) for c in range(NCH)])
                    return True
                ybT = msb("ybT", [128, 8, S], BF16)
                ybB = [Buf() for _ in range(8)]
                gdn(hT, hB, ybT, ybB)
                T.barrier()
                if stop_after == "dbg_yb":
                    dbg_dump([(ybT[:, c, :], ybB, c * 128, 128) for c in range(8)])
                    return True
                yaT = msb("yaT", [64, 4, S], BF16)
                yaB = [Buf() for _ in range(4)]
                if attention(hT, hB, yaT, yaB):
                    return True
                T.barrier()
                if stop_after == "dbg_ya":
                    dbg_dump([(yaT[:, c, :], yaB, c * 64, 64) for c in range(4)])
                    return True
                merge(hT, hB, yaT, yaB, ybT, ybB)
                T.barrier()
            return None

        def nset(lo, hi):
            return list(range(lo // 512, hi // 512 + 1))

        def attention(hT, hB, yaT, yaB):
            wqkv_d = dt_in["wqkv_a"]
            with contextlib.ExitStack() as as_:
                def asb(name, shape, dt):
                    return as_.enter_context(nc.sbuf_tensor("a_" + name, list(shape), dt))
                cos_t = asb("cos", [128, S], BF16)
                sin_t = asb("sin", [128, S], BF16)
                rm_t = asb("rm", [128, 128], BF16)
                am_t = asb("am", [128, 512], BF16)
                em_t = asb("em", [128, 64], F32)
                acB = Buf()
                T.dma("pool", cos_t[:], dt_in["rope_cos"], writes=[acB])
                T.dma("pool", sin_t[:], dt_in["rope_sin"], writes=[acB])
                T.dma("pool", rm_t[:], dt_in["rm"], writes=[acB])
                T.dma("pool", am_t[:], dt_in["amask"], writes=[acB])
                T.dma("sp", em_t[:], dt_in["em"], writes=[acB])
                wq = [asb("w%d" % i, [128, NCH, 128], BF16) for i in range(3)]
                wB = [Buf() for _ in range(3)]
                qk = [asb("qk%d" % i, [128, S], BF16) for i in range(2)]
                qkB = [[Buf() for _ in range(NB)] for _ in range(2)]
                vT = asb("vT", [128, S], BF16)
                vTB = [Buf() for _ in range(NB)]
                Vt = asb("Vt", [128, 16, 2, 65], BF16)
                VtB = [Buf() for _ in range(16)]
                VoneB = Buf()
                T.op(ROPE_ADD_ENG, lambda e: e.memset(Vt[:, :, :, 64:65], 1.0), writes=[VoneB])
                acc = asb("acc", [128, 2, S], F32)
                accB = Buf()
                qb = asb("qb", [128, 512], BF16)
                qbB = Buf()
                t1 = asb("t1", [128, 512], F32)
                t2 = asb("t2", [128, 512], F32)
                t1B, t2B = Buf(), Buf()
                ex = [asb("ex%d" % i, [128, 512], BF16) for i in range(3)]
                exB = [Buf(), Buf(), Buf()]
                exm = [asb("exm%d" % i, [128, 512], BF16) for i in range(3)]
                exmB = [Buf(), Buf(), Buf()]
                rl = asb("rl", [64, 512], F32)
                rlB = Buf()
                exn = 0
                if stop_after == "dbg_att0":
                    dbg_dump([(cos_t[:, :], [acB], 0, 128), (sin_t[:, :], [acB, VoneB], 128, 128)])
                    return True
                for hp in range(2):
                    for g in range(3):
                        dil = (1, 4, 16)[g]
                        nbk = 16 // dil
                        pi = g * 2 + hp
                        for i in range(3):
                            T.dma("pool", wq[i][:].rearrange("p c f -> p (c f)"), wqkv_d[pi, i], writes=[wB[i]])
                        for i in range(2):
                            for n in range(NB):
                                tk = slice(n * 512, (n + 1) * 512)
                                b = bank()

                                def mm(e):
                                    ins = None
                                    for c in range(NCH):
                                        ins = e.matmul(psum[:, b, :], lhsT=wq[i][:, c, :], rhs=hT[:, c, tk],
                                                       start=(c == 0), stop=(c == NCH - 1))
                                    return ins
                                CUT = int(os.environ.get("CUT", "9"))
                                if CUT >= 1:
                                    T.op("pe", mm, reads=[wB[i], hB[n]], writes=[pB[b]])
                                if CUT >= 2:
                                    T.op("act", lambda e: e.activation(out=qb[:, :], in_=psum[:, b, :], func=AF.Copy),
                                         reads=[pB[b]], writes=[qbB])
                                b2 = bank()
                                if CUT >= 3:
                                    T.op("pe", lambda e: e.matmul(psum[:, b2, :], lhsT=rm_t[:, :], rhs=qb[:, :], start=True, stop=True),
                                         reads=[qbB, acB], writes=[pB[b2]])
                                if CUT >= 4:
                                    T.op("dve", lambda e: e.tensor_tensor(out=t1[:, :], in0=psum[:, b, :], in1=cos_t[:, tk], op=ALU.mult),
                                         reads=[pB[b], acB], writes=[t1B])
                                if CUT >= 5:
                                    T.op("dve", lambda e: e.tensor_tensor(out=t2[:, :], in0=psum[:, b2, :], in1=sin_t[:, tk], op=ALU.mult),
                                         reads=[pB[b2], acB], writes=[t2B])
                                if CUT >= 6:
                                    T.op(ROPE_ADD_ENG, lambda e: e.tensor_tensor(out=qk[i][:, tk], in0=t1[:, :], in1=t2[:, :], op=ALU.add),
                                         reads=[t1B, t2B], writes=[qkB[i][n]])
                                if stop_after == "dbg_att1a":
                                    T.barrier()
                                    with contextlib.ExitStack() as ds:
                                        dtl = ds.enter_context(nc.sbuf_tensor("dbgx", [128, 512], F32))
                                        dxB = Buf()
                                        T.op("act", lambda e: e.activation(out=dtl[:, :], in_=qk[0][:, 0:512], func=AF.Copy), reads=[qkB[0][0]], writes=[dxB])
                                        T.dma("sp", out_d[0:128, 0:512], t1[:, :], reads=[t1B])
                                        T.dma("sp", out_d[128:256, 0:512], t2[:, :], reads=[t2B])
                                        T.dma("sp", out_d[256:384, 0:512], dtl[:, :], reads=[dxB])
                                        T.barrier()
                                    return True
                        if stop_after == "dbg_att1":
                            dbg_dump([(qk[0][:, :], qkB[0], 0, 128), (qk[1][:, :], qkB[1], 128, 128)])
                            return True
                        def tokslice(bi):
                            r, nk = bi // nbk, bi % nbk
                            st = r + dil * 128 * nk
                            return st, slice(st, st + 127 * dil + 1, dil)
                        for n in range(NB):
                            tk = slice(n * 512, (n + 1) * 512)
                            b = bank()

                            def mmvf(e):
                                ins = None
                                for c in range(NCH):
                                    ins = e.matmul(psum[:, b, :], lhsT=wq[2][:, c, :], rhs=hT[:, c, tk], start=(c == 0), stop=(c == NCH - 1))
                                return ins
                            T.op("pe", mmvf, reads=[wB[2], hB[n]], writes=[pB[b]])
                            T.op("act", lambda e: e.activation(out=vT[:, tk], in_=psum[:, b, :], func=AF.Copy), reads=[pB[b]], writes=[vTB[n]])
                        for b4 in range(4):
                            b = bank()

                            def mmv(e):
                                ins = None
                                for k in range(4):
                                    st, sl = tokslice(b4 * 4 + k)
                                    ins = e.matmul(psum[:, b, k * 128:(k + 1) * 128], lhsT=vT[:, sl], rhs=ident_b[:, :], start=True, stop=True)
                                return ins
                            T.op("pe", mmv, reads=vTB + [cB], writes=[pB[b]])
                            T.op("act", lambda e: e.activation(
                                out=Vt[:, b4 * 4:(b4 + 1) * 4, :, 0:64],
                                in_=psum[:, b, :].rearrange("p (k e d) -> p k e d", k=4, e=2), func=AF.Copy),
                                 reads=[pB[b]], writes=[VtB[b4 * 4 + k] for k in range(4)])
                        if stop_after == "dbg_att2":
                            dbg_dump([(Vt[:, :, :, :].rearrange("p a b c -> p (a b c)")[:, 0:2048], VtB + [VoneB], 0, 128)])
                            return True
                        def stage1(bi):
                            nonlocal exn
                            r, nk = bi // nbk, bi % nbk
                            st, qsl = tokslice(bi)
                            hi_tok = st + 127 * dil
                            qn_set = nset(st, hi_tok)
                            b = bank2()
                            if nk > 0:
                                stp, ksl_p = tokslice(bi - 1)
                                kn_set = nset(stp, hi_tok)
                            else:
                                kn_set = qn_set

                            def mms(e):
                                ins = None
                                for hd in range(2):
                                    hs = slice(hd * 64, (hd + 1) * 64)
                                    if nk > 0:
                                        ins = e.matmul(psum[:, b + hd, 0:128], lhsT=qk[1][hs, ksl_p], rhs=qk[0][hs, qsl],
                                                       start=True, stop=True)
                                    ins = e.matmul(psum[:, b + hd, 128:256], lhsT=qk[1][hs, qsl], rhs=qk[0][hs, qsl],
                                                   start=True, stop=True)
                                return ins
                            T.op("pe", mms, reads=[qkB[0][n] for n in qn_set] + [qkB[1][n] for n in kn_set], writes=[pB[b], pB[b + 1]])
                            xi = exn % 3
                            exn += 1
                            lo = 0 if nk > 0 else 128
                            src = psum[:, b:b + 2, lo:256]
                            dst = ex[xi][:, :].rearrange("p (e w) -> p e w", e=2)[:, :, lo:256]
                            dstm = exm[xi][:, :].rearrange("p (e w) -> p e w", e=2)[:, :, lo:256]
                            msk = am_t[:, :].rearrange("p (e w) -> p e w", e=2)[:, :, lo:256]
                            T.op("act", lambda e: e.activation(out=dst, in_=src, func=AF.Exp, scale=0.125),
                                 reads=[pB[b], pB[b + 1]], writes=[exB[xi]])
                            T.op("pool", lambda e: e.tensor_tensor(out=dstm, in0=dst, in1=msk, op=ALU.mult),
                                 reads=[exB[xi], acB], writes=[exmB[xi]])
                            return xi

                        def stage2(bi, xi):
                            r, nk = bi // nbk, bi % nbk
                            st, qsl = tokslice(bi)
                            b2 = bank()

                            def mmu(e):
                                ins = None
                                for hd in range(2):
                                    o = psum[0:65, b2, hd * 128:(hd + 1) * 128]
                                    if nk > 0:
                                        e.matmul(o, lhsT=Vt[:, bi - 1, hd, :], rhs=exm[xi][:, hd * 256:hd * 256 + 128], start=True, stop=False)
                                    ins = e.matmul(o, lhsT=Vt[:, bi, hd, :], rhs=exm[xi][:, hd * 256 + 128:hd * 256 + 256],
                                                   start=(nk == 0), stop=True)
                                return ins
                            T.op("pe", mmu, reads=[exmB[xi], VtB[bi], VoneB] + ([VtB[bi - 1]] if nk > 0 else []), writes=[pB[b2]])
                            src2 = psum[0:65, b2, 0:256].rearrange("p (e q) -> p e q", e=2)
                            if g == 0:
                                T.op("dve", lambda e: e.tensor_copy(out=acc[0:65, :, qsl], in_=src2), reads=[pB[b2]], writes=[accB])
                            else:
                                T.op("dve", lambda e: e.tensor_tensor(out=acc[0:65, :, qsl], in0=src2, in1=acc[0:65, :, qsl], op=ALU.add),
                                     reads=[pB[b2]], writes=[accB])
                        pend = []
                        for bi in range(16):
                            pend.append((bi, stage1(bi)))
                            if len(pend) > 1:
                                stage2(*pend.pop(0))
                        while pend:
                            stage2(*pend.pop(0))
                        if stop_after == "dbg_att3":
                            dbg_dump([(acc[:, 0, :], [accB], 0, 128), (acc[:, 1, :], [accB], 128, 128)])
                            return True
                    for hd in range(2):
                        for n in range(NB):
                            tk = slice(n * 512, (n + 1) * 512)
                            b = bank()
                            T.op("pe", lambda e: e.matmul(psum[0:64, b, :], lhsT=em_t[0:65, :], rhs=acc[0:65, hd, tk], start=True, stop=True),
                                 reads=[accB, acB], writes=[pB[b]])
                            T.op("dve", lambda e: e.reciprocal(out=rl[:, :], in_=psum[0:64, b, :]), reads=[pB[b]], writes=[rlB])
                            T.op("dve", lambda e: e.tensor_tensor(out=yaT[:, 2 * hp + hd, tk], in0=acc[0:64, hd, tk], in1=rl[:, :], op=ALU.mult),
                                 reads=[rlB, accB], writes=[yaB[2 * hp + hd]])
                T.barrier()

        def gdn(hT, hB, ybT, ybB):
            with contextlib.ExitStack() as gs:
                def gsb(name, shape, dt):
                    return gs.enter_context(nc.sbuf_tensor("g_" + name, list(shape), dt))
                gcB = Buf()
                triu = gsb("triu", [128, 128], F32)
                smask = gsb("smask", [128, 128], F32)
                odiv = gsb("odiv", [128, 128], BF16)
                convw = gsb("convw", [128, 8, 3, 4], F32)
                alog = gsb("alog", [128, 8], F32)
                dtb = gsb("dtb", [128, 8], F32)
                onorm = gsb("onorm", [128, 1], F32)
                wbd = gsb("wbd", [128, NCH, 16], BF16)
                T.dma("sp", triu[:], dt_in["triu"], writes=[gcB])
                T.dma("sp", smask[:], dt_in["smask"], writes=[gcB])
                T.dma("pool", odiv[:], dt_in["odiv"], writes=[gcB])
                T.dma("sp", convw[:].rearrange("p h w t -> p (h w t)"), dt_in["convw"], writes=[gcB])
                T.dma("sp", alog[:], dt_in["alog"], writes=[gcB])
                T.dma("sp", dtb[:], dt_in["dtb"], writes=[gcB])
                T.dma("sp", onorm[:], dt_in["onorm"], writes=[gcB])
                T.dma("pool", wbd[:].rearrange("p c f -> p (c f)"), dt_in["wbd"], writes=[gcB])
                braw = gsb("braw", [128, 16, 16], F32)
                beta = gsb("beta", [128, 16, 8], F32)
                negb = gsb("negb", [128, 16, 8], F32)
                gt = gsb("gt", [128, 16, 8], F32)
                gcum = gsb("gcum", [128, 16, 8], F32)
                glast = gsb("glast", [128, 16, 8], F32)
                kds = gsb("kds", [128, 16, 8], F32)
                sdec = gsb("sdec", [128, 16, 8], F32)
                egc = gsb("egc", [128, 16, 8], F32)
                tmpg = gsb("tmpg", [128, 16, 8], F32)
                negA = gsb("negA", [128, 8], F32)
                gB = Buf()
                b = bank()

                def mmb(e):
                    ins = None
                    for t in range(16):
                        for c in range(NCH):
                            ins = e.matmul(psum[:, b, t * 16:(t + 1) * 16], lhsT=hT[:, c, t * 128:(t + 1) * 128], rhs=wbd[:, c, :],
                                           start=(c == 0), stop=(c == NCH - 1))
                    return ins
                T.op("pe", mmb, reads=hB + [gcB], writes=[pB[b]])
                T.op("act", lambda e: e.activation(out=braw[:].rearrange("p t k -> p (t k)"), in_=psum[:, b, 0:256], func=AF.Copy),
                     reads=[pB[b]], writes=[gB])
                T.op("act", lambda e: e.activation(out=tmpg[:], in_=braw[:, :, 0:8], func=AF.Exp, scale=-1.0), reads=[gB], writes=[gB])
                T.op("dve", lambda e: e.tensor_single_scalar(out=tmpg[:], in_=tmpg[:], scalar=1.0, op=ALU.add), reads=[gB], writes=[gB])
                T.op("dve", lambda e: e.reciprocal(out=beta[:], in_=tmpg[:]), reads=[gB], writes=[gB])
                T.op("dve", lambda e: e.tensor_single_scalar(out=negb[:], in_=beta[:], scalar=-1.0, op=ALU.mult), reads=[gB], writes=[gB])
                T.op("dve", lambda e: e.tensor_tensor(out=tmpg[:], in0=braw[:, :, 8:16], in1=dtb[:].unsqueeze(1).to_broadcast([128, 16, 8]), op=ALU.add),
                     reads=[gB, gcB], writes=[gB])
                T.op("act", lambda e: e.activation(out=tmpg[:], in_=tmpg[:], func=AF.Exp), reads=[gB], writes=[gB])
                T.op("act", lambda e: e.activation(out=tmpg[:], in_=tmpg[:], func=AF.Ln, bias=eps_t[:, 1:2]), reads=[gB, cB], writes=[gB])
                T.op("act", lambda e: e.activation(out=negA[:], in_=alog[:], func=AF.Exp), reads=[gcB], writes=[gB])
                T.op("dve", lambda e: e.tensor_single_scalar(out=negA[:], in_=negA[:], scalar=-1.0, op=ALU.mult), reads=[gB], writes=[gB])
                T.op("dve", lambda e: e.tensor_tensor(out=gt[:], in0=tmpg[:], in1=negA[:].unsqueeze(1).to_broadcast([128, 16, 8]), op=ALU.mult),
                     reads=[gB], writes=[gB])
                b = bank()
                T.op("pe", lambda e: e.matmul(psum[:, b, 0:128], lhsT=triu[:, :], rhs=gt[:].rearrange("p t k -> p (t k)"), start=True, stop=True),
                     reads=[gB, gcB], writes=[pB[b]])
                T.op("act", lambda e: e.activation(out=gcum[:].rearrange("p t k -> p (t k)"), in_=psum[:, b, 0:128], func=AF.Copy),
                     reads=[pB[b]], writes=[gB])
                b = bank()
                T.op("pe", lambda e: e.matmul(psum[:, b, 0:128], lhsT=ones_f[:, :], rhs=gt[:].rearrange("p t k -> p (t k)"), start=True, stop=True),
                     reads=[gB, cB], writes=[pB[b]])
                T.op("act", lambda e: e.activation(out=glast[:].rearrange("p t k -> p (t k)"), in_=psum[:, b, 0:128], func=AF.Copy),
                     reads=[pB[b]], writes=[gB])
                T.op("dve", lambda e: e.tensor_tensor(out=tmpg[:], in0=glast[:], in1=gcum[:], op=ALU.subtract), reads=[gB], writes=[gB])
                T.op("act", lambda e: e.activation(out=kds[:], in_=tmpg[:], func=AF.Exp), reads=[gB], writes=[gB])
                T.op("act", lambda e: e.activation(out=sdec[:], in_=glast[:], func=AF.Exp), reads=[gB], writes=[gB])
                T.op("act", lambda e: e.activation(out=egc[:], in_=gcum[:], func=AF.Exp), reads=[gB], writes=[gB])

                wh = [gsb("wh%d" % i, [128, NCH, 128], BF16) for i in range(4)]
                whB = [Buf() for _ in range(4)]
                cv = gsb("cv", [128, 512], F32)
                cvB = Buf()
                xb = [gsb("xb%d" % i, [128, 4 + 512], BF16) for i in range(3)]
                xbB = [Buf() for _ in range(3)]
                kqt = [gsb("kq%d" % st_, [128, 2, 512], BF16) for st_ in range(2)]
                vts = [gsb("vt%d" % st_, [128, 512], BF16) for st_ in range(2)]
                qkvs = [[kqt[st_][:, 1, :], kqt[st_][:, 0, :], vts[st_][:, :]] for st_ in range(2)]
                qkvsB = [[Buf() for _ in range(3)] for _ in range(2)]
                gsils = [gsb("gsil%d" % st_, [128, 512], BF16) for st_ in range(2)]
                gsilsB = [Buf(), Buf()]
                sqb = gsb("sqb", [128, 512], BF16)
                sqbB = Buf()
                rt = gsb("rt", [128, 512], F32)
                rtB = Buf()
                Fm = [gsb("F%d" % i, [128, 4, 128], F32) for i in range(6)]
                FB = [[Buf() for _ in range(4)] for _ in range(6)]
                QT = gsb("QT", [128, 4, 128], BF16)
                QTB = [Buf() for _ in range(4)]
                PQ = [gsb("PQ%d" % i, [128, 2, 4, 128], BF16) for i in range(2)]
                PQB = [[Buf() for _ in range(4)] for _ in range(2)]
                Wt = gsb("Wt", [128, 4, 128], BF16)
                WtB = [Buf() for _ in range(4)]
                ATt = gsb("ATt", [128, 4, 128], BF16)
                ATtB = [Buf() for _ in range(4)]
                Aqk = gsb("Aqk", [128, 4, 128], BF16)
                AqkB = [Buf() for _ in range(4)]
                kdec = gsb("kdec", [128, 4, 128], BF16)
                kdecB = [Buf() for _ in range(4)]
                VK = gsb("VK", [128, 4, 256], BF16)
                VKB = [Buf() for _ in range(4)]
                Zt = gsb("Zt", [128, 4, 256], BF16)
                ZtB = [Buf() for _ in range(4)]
                Qeff = gsb("Qeff", [128, 4, 128], BF16)
                QeffB = [Buf() for _ in range(4)]
                osq = gsb("osq", [128, 512], BF16)
                osqB = Buf()
                Sf = gsb("Sf", [128, 128], F32)
                Sb = gsb("Sb", [128, 128], BF16)
                SfB, SbB = Buf(), Buf()
                QSCALE = 128.0 ** -0.5
                hbk = {"n": 0}

                def hbank():
                    b_ = 4 + hbk["n"]
                    hbk["n"] = (hbk["n"] + 1) % 3
                    return b_

                def pv(bk):
                    return psum[:, bk, :].rearrange("p (c i) -> p c i", c=4)

                def fl(tl):
                    return tl[:].rearrange("p c i -> p (c i)")

                def g1b(h, n, st):
                    tk = slice(n * 512, (n + 1) * 512)
                    if n == 0:
                        for i in range(3):
                            T.dma("pool", wh[i][:].rearrange("p c f -> p (c f)"), dt_in["wgdn"][h, i], writes=[whB[i]])
                        T.dma("pool", wh[3][:].rearrange("p c f -> p (c f)"), dt_in["wgg"][h], writes=[whB[3]])
                        yield
                    for w3 in range(3):
                        if n == 0:
                            T.op("pool", lambda e: e.memset(xb[w3][:, 0:4], 0.0), writes=[xbB[w3]])
                        else:
                            T.op("pool", lambda e: e.tensor_copy(out=xb[w3][:, 0:4], in_=xb[w3][:, 512:516]), reads=[xbB[w3]], writes=[xbB[w3]])
                        yield
                        b = hbank()

                        def mm(e):
                            ins = None
                            for c in range(NCH):
                                ins = e.matmul(psum[:, b, :], lhsT=wh[w3][:, c, :], rhs=hT[:, c, tk], start=(c == 0), stop=(c == NCH - 1))
                            return ins
                        T.op("pe", mm, reads=[whB[w3], hB[n]], writes=[pB[b]])
                        yield
                        T.op("act", lambda e: e.activation(out=xb[w3][:, 4:516], in_=psum[:, b, :], func=AF.Copy), reads=[pB[b]], writes=[xbB[w3]])
                        yield
                        T.op("dve", lambda e: e.tensor_scalar_mul(out=cv[:, :], in0=xb[w3][:, 1:513], scalar1=convw[:, h, w3, 0:1]),
                             reads=[xbB[w3], gcB], writes=[cvB])
                        yield
                        for tp in range(1, 4):
                            T.op("dve", lambda e: e.scalar_tensor_tensor(out=cv[:, :], in0=xb[w3][:, 1 + tp:513 + tp], scalar=convw[:, h, w3, tp:tp + 1],
                                                                         in1=cv[:, :], op0=ALU.mult, op1=ALU.add), reads=[xbB[w3], gcB, cvB], writes=[cvB])
                            yield
                        T.op("act", lambda e: e.activation(out=qkvs[st][w3][:, :], in_=cv[:, :], func=AF.Silu), reads=[cvB], writes=[qkvsB[st][w3]])
                        yield
                        if w3 < 2:
                            T.op("act", lambda e: e.activation(out=sqb[:, :], in_=qkvs[st][w3][:, :], func=AF.Square), reads=[qkvsB[st][w3]], writes=[sqbB])
                            yield
                            b2 = hbank()
                            T.op("pe", lambda e: e.matmul(psum[:, b2, :], lhsT=ones_b[:, :], rhs=sqb[:, :], start=True, stop=True),
                                 reads=[sqbB, cB], writes=[pB[b2]])
                            yield
                            T.op("act", lambda e: e.activation(out=rt[:, :], in_=psum[:, b2, :], func=AF.Ln, bias=eps_t[:, 0:1]), reads=[pB[b2], cB], writes=[rtB])
                            yield
                            T.op("act", lambda e: e.activation(out=rt[:, :], in_=rt[:, :], func=AF.Exp, scale=-0.5), reads=[rtB], writes=[rtB])
                            yield
                            T.op("dve", lambda e: e.scalar_tensor_tensor(out=qkvs[st][w3][:, :], in0=qkvs[st][w3][:, :], scalar=(QSCALE if w3 == 0 else 1.0),
                                                                         in1=rt[:, :], op0=ALU.mult, op1=ALU.mult),
                                 reads=[rtB, qkvsB[st][w3]], writes=[qkvsB[st][w3]])
                            yield
                    b = hbank()

                    def mm(e):
                        ins = None
                        for c in range(NCH):
                            ins = e.matmul(psum[:, b, :], lhsT=wh[3][:, c, :], rhs=hT[:, c, tk], start=(c == 0), stop=(c == NCH - 1))
                        return ins
                    T.op("pe", mm, reads=[whB[3], hB[n]], writes=[pB[b]])
                    yield
                    T.op("act", lambda e: e.activation(out=rt[:, :], in_=psum[:, b, :], func=AF.Silu), reads=[pB[b]], writes=[rtB])
                    yield
                    T.op("dve", lambda e: e.tensor_scalar_mul(out=gsils[st][:, :], in0=rt[:, :], scalar1=onorm[:, 0:1]), reads=[rtB, gcB], writes=[gsilsB[st]])
                    yield

                def run_streams(gens):
                    gens = list(gens)
                    while gens:
                        for g_ in list(gens):
                            try:
                                next(g_)
                            except StopIteration:
                                gens.remove(g_)

                blocks = [(h_, n_) for h_ in range(8) for n_ in range(4)]
                run_streams([g1b(0, 0, 0)])
                for bi_, (h, gi) in enumerate(blocks):
                    st = bi_ % 2
                    if True:
                        t0 = gi * 4
                        gk = slice(gi * 512, (gi + 1) * 512)
                        cks = [slice(c * 128, (c + 1) * 128) for c in range(4)]
                        qn_, kn_, vn_ = qkvs[st][0], qkvs[st][1], qkvs[st][2]
                        kq_ = kqt[st]
                        qB, kB_, vB_ = qkvsB[st][0], qkvsB[st][1], qkvsB[st][2]

                        def prep(c):
                            t = t0 + c
                            ck = cks[c]
                            ba = bb = c
                            DT, decF, EG = Fm[0][:, c, :], Fm[1][:, c, :], Fm[2][:, c, :]
                            gc = gcum[:, t, h:h + 1]
                            T.op("pe", lambda e: e.matmul(psum[:, ba, 0:128], lhsT=gt[:, t, h:h + 1].to_broadcast([128, 128]), rhs=triu[:, :],
                                                          start=True, stop=True), reads=[gB, gcB], writes=[pB[ba]])
                            yield
                            T.op("dve", lambda e: e.scalar_tensor_tensor(out=DT, in0=psum[:, ba, 0:128], scalar=gc, in1=smask[:, :],
                                                                         op0=ALU.subtract, op1=ALU.add), reads=[pB[ba], gB, gcB], writes=[FB[0][c]])
                            yield
                            T.op("act", lambda e: e.activation(out=EG, in_=psum[:, ba, 0:128], func=AF.Exp), reads=[pB[ba]], writes=[FB[2][c]])
                            yield
                            T.op("act", lambda e: e.activation(out=DT, in_=DT, func=AF.Exp), reads=[FB[0][c]], writes=[FB[0][c]])
                            yield
                            T.op("pool", lambda e: e.tensor_tensor(out=decF, in0=DT, in1=ident_f[:, :], op=ALU.add), reads=[FB[0][c], cB], writes=[FB[1][c]])
                            yield
                            T.op("pe", lambda e: e.matmul(psum[:, ba, 0:256], lhsT=kn_[:, ck], rhs=kq_[:, :, ck], start=True, stop=True),
                                 reads=[kB_, qB], writes=[pB[ba]])
                            yield
                            T.op("dve", lambda e: e.scalar_tensor_tensor(out=QT[:, c, :], in0=psum[:, ba, 0:128], scalar=negb[:, t, h:h + 1], in1=DT,
                                                                         op0=ALU.mult, op1=ALU.mult), reads=[pB[ba], gB, FB[0][c]], writes=[QTB[c]])
                            yield
                            T.op("dve", lambda e: e.tensor_tensor(out=Aqk[:, c, :], in0=psum[:, ba, 128:256], in1=decF, op=ALU.mult),
                                 reads=[pB[ba], FB[1][c]], writes=[AqkB[c]])
                            yield
                            T.op("pe", lambda e: e.matmul(psum[:, bb, 128:256], lhsT=QT[:, c, :], rhs=ident_b[:, :], start=True, stop=True),
                                 reads=[QTB[c], cB], writes=[pB[bb]])
                            yield
                            T.op("dve", lambda e: e.tensor_tensor(out=PQ[0][:, 0, c, :], in0=psum[:, bb, 128:256], in1=ident_f[:, :], op=ALU.add),
                                 reads=[pB[bb], cB], writes=[PQB[0][c]])
                            yield
                            T.op("pool", lambda e: e.tensor_tensor(out=PQ[0][:, 1, c, :], in0=QT[:, c, :], in1=ident_b[:, :], op=ALU.add),
                                 reads=[QTB[c], cB], writes=[PQB[0][c]])
                            yield
                            T.op("pool", lambda e: e.tensor_tensor(out=ATt[:, c, :], in0=ident_b[:, :], in1=QT[:, c, :], op=ALU.subtract),
                                 reads=[QTB[c], cB], writes=[ATtB[c]])
                            yield
                            T.op("pe", lambda e: e.matmul(psum[:, ba, 0:128], lhsT=kn_[:, ck], rhs=ident_b[:, :], start=True, stop=True),
                                 reads=[kB_, cB], writes=[pB[ba]])
                            yield
                            T.op("dve", lambda e: e.tensor_scalar_mul(out=kdec[:, c, :], in0=psum[:, ba, 0:128], scalar1=kds[:, t, h:h + 1]),
                                 reads=[pB[ba], gB], writes=[kdecB[c]])
                            yield
                            T.op("dve", lambda e: e.tensor_scalar_mul(out=VK[:, c, 128:256], in0=psum[:, ba, 0:128], scalar1=egc[:, t, h:h + 1]),
                                 reads=[pB[ba], gB], writes=[VKB[c]])
                            yield
                            T.op("pe", lambda e: e.matmul(psum[:, bb, 128:256], lhsT=vn_[:, ck], rhs=ident_b[:, :], start=True, stop=True),
                                 reads=[vB_, cB], writes=[pB[bb]])
                            yield
                            T.op("act", lambda e: e.activation(out=VK[:, c, 0:128], in_=psum[:, bb, 128:256], func=AF.Copy), reads=[pB[bb]], writes=[VKB[c]])
                            yield
                            cur = 0
                            for k in range(2, 8):
                                nxt = 1 - cur
                                last = (k == 7)
                                T.op("pe", lambda e: e.matmul(psum[:, ba, 0:128], lhsT=ATt[:, c, :], rhs=PQ[cur][:, 0, c, :], start=True, stop=True),
                                     reads=[ATtB[c], PQB[cur][c]], writes=[pB[ba]])
                                yield
                                T.op("dve", lambda e: e.scalar_tensor_tensor(out=Wt[:, c, :], in0=psum[:, ba, 0:128], scalar=-1.0, in1=ident2_f[:, :],
                                                                             op0=ALU.mult, op1=ALU.add), reads=[pB[ba], cB], writes=[WtB[c]])
                                yield

                                def mmn(e):
                                    ins = e.matmul(psum[:, bb, 128:256], lhsT=Wt[:, c, :], rhs=PQ[cur][:, 1, c, :], start=True, stop=True)
                                    if not last:
                                        ins = e.matmul(psum[:, ba, 0:128], lhsT=PQ[cur][:, 1, c, :], rhs=Wt[:, c, :], start=True, stop=True)
                                    return ins
                                T.op("pe", mmn, reads=[WtB[c], PQB[cur][c]], writes=[pB[ba]])
                                yield
                                if last:
                                    T.op("act", lambda e: e.activation(out=QT[:, c, :], in_=psum[:, bb, 128:256], func=AF.Copy), reads=[pB[bb]], writes=[QTB[c]])
                                else:
                                    T.op("act", lambda e: e.activation(out=PQ[nxt][:, :, c, :], in_=psum[:, ba, 0:256].rearrange("p (a i) -> p a i", a=2), func=AF.Copy),
                                         reads=[pB[ba]], writes=[PQB[nxt][c]])
                                yield
                                cur = nxt
                            T.op("pe", lambda e: e.matmul(psum[:, ba, 0:256], lhsT=QT[:, c, :], rhs=VK[:, c, :], start=True, stop=True),
                                 reads=[QTB[c], VKB[c]], writes=[pB[ba]])
                            yield
                            T.op("dve", lambda e: e.tensor_scalar_mul(out=Zt[:, c, :], in0=psum[:, ba, 0:256], scalar1=beta[:, t, h:h + 1]),
                                 reads=[pB[ba], gB], writes=[ZtB[c]])
                            yield
                            T.op("pe", lambda e: e.matmul(psum[:, bb, 128:256], lhsT=kdec[:, c, :], rhs=Zt[:, c, 0:128], start=True, stop=True),
                                 reads=[kdecB[c], ZtB[c]], writes=[pB[bb]])
                            yield
                            T.op("act", lambda e: e.activation(out=Fm[3][:, c, :], in_=psum[:, bb, 128:256], func=AF.Copy), reads=[pB[bb]], writes=[FB[3][c]])
                            yield
                            T.op("pe", lambda e: e.matmul(psum[:, ba, 0:128], lhsT=Zt[:, c, 128:256], rhs=kdec[:, c, :], start=True, stop=True),
                                 reads=[kdecB[c], ZtB[c]], writes=[pB[ba]])
                            yield
                            T.op("dve", lambda e: e.scalar_tensor_tensor(out=Fm[4][:, c, :], in0=ident_f[:, :], scalar=sdec[:, t, h:h + 1], in1=psum[:, ba, 0:128],
                                                                         op0=ALU.mult, op1=ALU.subtract), reads=[pB[ba], gB, cB], writes=[FB[4][c]])
                            yield
                            T.op("pe", lambda e: e.matmul(psum[:, bb, 128:256], lhsT=Zt[:, c, 0:128], rhs=Aqk[:, c, :], start=True, stop=True),
                                 reads=[AqkB[c], ZtB[c]], writes=[pB[bb]])
                            yield
                            T.op("act", lambda e: e.activation(out=Fm[5][:, c, :], in_=psum[:, bb, 128:256], func=AF.Copy), reads=[pB[bb]], writes=[FB[5][c]])
                            yield
                            T.op("pe", lambda e: e.matmul(psum[:, ba, 0:128], lhsT=Zt[:, c, 128:256], rhs=Aqk[:, c, :], start=True, stop=True),
                                 reads=[AqkB[c], ZtB[c]], writes=[pB[ba]])
                            yield
                            T.op("pool", lambda e: e.tensor_tensor(out=decF, in0=qn_[:, ck], in1=EG, op=ALU.mult), reads=[qB, FB[2][c], FB[1][c]], writes=[FB[1][c]])
                            yield
                            T.op("dve", lambda e: e.tensor_tensor(out=Qeff[:, c, :], in0=decF, in1=psum[:, ba, 0:128], op=ALU.subtract),
                                 reads=[pB[ba], FB[1][c]], writes=[QeffB[c]])
                            yield

                        gens = [prep(c) for c in range(4)]
                        if bi_ + 1 < len(blocks):
                            hn, nn = blocks[bi_ + 1]
                            gens.append(g1b(hn, nn, 1 - st))
                        run_streams(gens)
                        Hs, ATs, O0s = Fm[3], Fm[4], Fm[5]
                        bOo = 7
                        for c in range(4):
                            t = t0 + c
                            if t > 0:
                                T.op("pe", lambda e: e.matmul(psum[:, bOo, c * 128:(c + 1) * 128], lhsT=Sb[:, :], rhs=Qeff[:, c, :], start=True, stop=True),
                                     reads=[SbB, QeffB[c]], writes=[pB[bOo]])
                            if t == 15:
                                continue
                            if t == 0:
                                T.op("dve", lambda e: e.tensor_copy(out=Sf[:, :], in_=Hs[:, 0, :]), reads=[FB[3][0]], writes=[SfB])
                            else:
                                bS = hbank()
                                T.op("pe", lambda e: e.matmul(psum[:, bS, 0:128], lhsT=ATs[:, c, :], rhs=Sf[:, :], start=True, stop=True),
                                     reads=[FB[4][c], SfB], writes=[pB[bS]])
                                T.op("dve", lambda e: e.tensor_tensor(out=Sf[:, :], in0=psum[:, bS, 0:128], in1=Hs[:, c, :], op=ALU.add),
                                     reads=[pB[bS], FB[3][c]], writes=[SfB])
                            T.op("act", lambda e: e.activation(out=Sb[:, :], in_=Sf[:, :], func=AF.Copy), reads=[SfB], writes=[SbB])
                        osum, otmp, rr = Fm[0], Fm[1], Fm[2]
                        if gi == 0:
                            T.op("dve", lambda e: e.tensor_copy(out=osum[:, 0, :], in_=O0s[:, 0, :]), reads=[FB[5][0]], writes=[FB[0][0]])
                            T.op("dve", lambda e: e.tensor_tensor(out=osum[:, 1:4, :], in0=pv(bOo)[:, 1:4, :], in1=O0s[:, 1:4, :], op=ALU.add),
                                 reads=[pB[bOo]] + FB[5][1:4], writes=FB[0][1:4])
                        else:
                            T.op("dve", lambda e: e.tensor_tensor(out=osum[:], in0=pv(bOo), in1=O0s[:], op=ALU.add), reads=[pB[bOo]] + FB[5], writes=FB[0])
                        T.op("act", lambda e: e.activation(out=osq[:, :], in_=fl(osum), func=AF.Square), reads=FB[0], writes=[osqB])
                        b2 = hbank()
                        T.op("pe", lambda e: e.matmul(psum[:, b2, :], lhsT=odiv[:, :], rhs=osq[:, :], start=True, stop=True), reads=[osqB, gcB], writes=[pB[b2]])
                        T.op("act", lambda e: e.activation(out=fl(rr), in_=psum[:, b2, :], func=AF.Ln, bias=eps_t[:, 0:1]), reads=[pB[b2], cB], writes=FB[2])
                        T.op("act", lambda e: e.activation(out=fl(rr), in_=fl(rr), func=AF.Exp, scale=-0.5), reads=FB[2], writes=FB[2])
                        T.op("dve", lambda e: e.tensor_tensor(out=otmp[:], in0=osum[:], in1=rr[:], op=ALU.mult), reads=FB[0] + FB[2], writes=FB[1])
                        T.op("pool", lambda e: e.tensor_tensor(out=ybT[:, h, gk], in0=fl(otmp), in1=gsils[st][:, :], op=ALU.mult),
                             reads=FB[1] + [gsilsB[st]], writes=[ybB[h]])
                T.barrier()

        def merge(hT, hB, yaT, yaB, ybT, ybB):
            with contextlib.ExitStack() as gs:
                def gsb(name, shape, dt):
                    return gs.enter_context(nc.sbuf_tensor("x_" + name, list(shape), dt))
                mg = gsb("mg", [128, 8, S], BF16)
                mgB = [[Buf() for _ in range(NB)] for _ in range(8)]
                wga = [gsb("wga%d" % i, [128, 2, NCH, 128], BF16) for i in range(2)]
                wba = [gsb("wba%d" % i, [64, 4, 128], BF16) for i in range(2)]
                wbb = [gsb("wbb%d" % i, [128, 8, 128], BF16) for i in range(2)]
                wB = [Buf(), Buf()]
                wo = [gsb("wo%d" % i, [128, 8, 128], BF16) for i in range(2)]
                woB = [Buf(), Buf()]
                one_c = eps_t[:, 1:2]
                sa = gsb("sa", [128, 512], F32)
                sb_ = gsb("sb", [128, 512], F32)
                ta = gsb("ta", [128, 512], F32)
                tb = gsb("tb", [128, 512], F32)
                saB, sbB, taB, tbB = Buf(), Buf(), Buf(), Buf()

                def loadw(dt):
                    i = dt % 2
                    T.dma("pool", wga[i][:].rearrange("p w c f -> p (w c f)"), dt_in["wgab"][dt], writes=[wB[i]])
                    T.dma("pool", wba[i][:].rearrange("p s f -> p (s f)"), dt_in["wba"][dt], writes=[wB[i]])
                    T.dma("pool", wbb[i][:].rearrange("p s f -> p (s f)"), dt_in["wbb"][dt], writes=[wB[i]])
                loadw(0)
                loadw(1)
                for dt in range(8):
                    wi = dt % 2
                    for n in range(NB):
                        tk = slice(n * 512, (n + 1) * 512)
                        bga, bgb, ba, bb = bank(), bank(), bank(), bank()

                        def mmg(e, which, b):
                            ins = None
                            for c in range(NCH):
                                ins = e.matmul(psum[:, b, :], lhsT=wga[wi][:, which, c, :], rhs=hT[:, c, tk], start=(c == 0), stop=(c == NCH - 1))
                            return ins
                        T.op("pe", lambda e: mmg(e, 0, bga), reads=[wB[wi], hB[n]], writes=[pB[bga]])
                        T.op("pe", lambda e: mmg(e, 1, bgb), reads=[wB[wi], hB[n]], writes=[pB[bgb]])

                        def mma(e):
                            ins = None
                            for s4 in range(4):
                                ins = e.matmul(psum[:, ba, :], lhsT=wba[wi][:, s4, :], rhs=yaT[:, s4, tk], start=(s4 == 0), stop=(s4 == 3))
                            return ins
                        T.op("pe", mma, reads=[wB[wi]] + yaB, writes=[pB[ba]])

                        def mmb(e):
                            ins = None
                            for h in range(8):
                                ins = e.matmul(psum[:, bb, :], lhsT=wbb[wi][:, h, :], rhs=ybT[:, h, tk], start=(h == 0), stop=(h == 7))
                            return ins
                        T.op("pe", mmb, reads=[wB[wi]] + ybB, writes=[pB[bb]])
                        T.op("act", lambda e: e.activation(out=sa[:, :], in_=psum[:, bga, :], func=AF.Sigmoid), reads=[pB[bga]], writes=[saB])
                        T.op("act", lambda e: e.activation(out=sb_[:, :], in_=psum[:, bgb, :], func=AF.Sigmoid), reads=[pB[bgb]], writes=[sbB])
                        T.op("dve", lambda e: e.tensor_tensor(out=ta[:, :], in0=psum[:, ba, :], in1=sa[:, :], op=ALU.mult), reads=[pB[ba], saB], writes=[taB])
                        T.op("dve", lambda e: e.tensor_tensor(out=tb[:, :], in0=psum[:, bb, :], in1=sb_[:, :], op=ALU.mult), reads=[pB[bb], sbB], writes=[tbB])
                        T.op("pool", lambda e: e.tensor_tensor(out=mg[:, dt, tk], in0=ta[:, :], in1=tb[:, :], op=ALU.add), reads=[taB, tbB], writes=[mgB[dt][n]])
                    if dt + 2 < 8:
                        loadw(dt + 2)

                def loado(dt):
                    T.dma("pool", wo[dt % 2][:].rearrange("p s f -> p (s f)"), dt_in["wout"][dt], writes=[woB[dt % 2]])
                loado(0)
                loado(1)
                for dt in range(8):
                    wi = dt % 2
                    for n in range(NB):
                        tk = slice(n * 512, (n + 1) * 512)
                        b = bank()

                        def mmo(e):
                            ins = None
                            for k in range(8):
                                ins = e.matmul(psum[:, b, :], lhsT=wo[wi][:, k, :], rhs=mg[:, k, tk], start=(k == 0), stop=(k == 7))
                            return ins
                        T.op("pe", mmo, reads=[woB[wi]] + [mgB[k][n] for k in range(8)], writes=[pB[b]])
                        T.op("dve", lambda e: e.tensor_tensor(out=xT[:, dt, tk], in0=psum[:, b, :], in1=xT[:, dt, tk], op=ALU.add),
                             reads=[pB[b]], writes=[xB[n]])
                    if dt + 2 < 8:
                        loado(dt + 2)
                T.barrier()

        def dbg_dump(slabs):
            with contextlib.ExitStack() as ds:
                dtile = [ds.enter_context(nc.sbuf_tensor("dbg%d" % i, [128, S], F32)) for i in range(2)]
                dB = [Buf(), Buf()]
                for k, (ap, bufs, row0, P) in enumerate(slabs):
                    i = k % 2
                    T.op("act", lambda e: e.activation(out=dtile[i][0:P, :], in_=ap, func=AF.Copy), reads=bufs, writes=[dB[i]])
                    T.dma("sp", out_d[row0:row0 + P, :], dtile[i][0:P, :], reads=[dB[i]])
                T.barrier()

        ffn(0, 0)
        done = False
        if stop_after not in ("ffn1_raw", "ffn1"):
            r = mixer()
            if r is not None:
                done = True
        if stop_after in ("all", "x3_raw"):
            ffn(1, 2)
        if not done:
          with contextlib.ExitStack() as fs:
            yT = [fs.enter_context(nc.sbuf_tensor("yT%d" % i, [128, NCH, 512], F32)) for i in range(2)]
            yB = [Buf(), Buf()]
            sq_t = fs.enter_context(nc.sbuf_tensor("fsq", [128, NCH, 512], BF16))
            sq_b = [Buf() for _ in range(NCH)]
            rs_t = fs.enter_context(nc.sbuf_tensor("frs", [128, 512], F32))
            rs_b = Buf()
            out_dv = out_d.rearrange("(c p) t -> p c t", p=128)
            for n in range(NB):
                i = n % 2
                if stop_after.endswith("_raw"):
                    for c in range(NCH):
                        T.op("dve", lambda e: e.tensor_copy(out=yT[i][:, c, :], in_=xT[:, c, n * 512:(n + 1) * 512]),
                             reads=[xB[n]], writes=[yB[i]])
                else:
                    rmsnorm_block(n, 3, lambda c: yT[i][:, c, :], yB[i], sq_t, sq_b, rs_t, rs_b)
                T.dma("sp", out_dv[:, :, n * 512:(n + 1) * 512], yT[i][:], reads=[yB[i]])
            T.barrier()
    return nc, set(dt_in.keys())


def _consts():
    c = {}
    c["ident_f"] = np.eye(128, dtype=np.float32)
    c["ones_f"] = np.ones((128, 128), dtype=np.float32)
    half = 32
    inv_freq = 10000.0 ** (-np.arange(half, dtype=np.float64) / half)
    pos = np.arange(S, dtype=np.float64)
    ang = pos[None, :] * inv_freq[:, None]
    cosv, sinv = np.cos(ang).astype(np.float32), np.sin(ang).astype(np.float32)
    cos_t = np.zeros((128, S), np.float32)
    sin_t = np.zeros((128, S), np.float32)
    rm = np.zeros((128, 128), np.float32)
    for p in range(128):
        d = p % 64
        cos_t[p] = cosv[d % 32]
        sin_t[p] = -sinv[d % 32] if d < 32 else sinv[d % 32]
        partner = p + 32 if d < 32 else p - 32
        rm[partner, p] = 1.0
    c["rope_cos"], c["rope_sin"], c["rm"] = cos_t, sin_t, rm
    j = np.arange(128)[:, None]
    i = np.arange(128)[None, :]
    mprev = (j >= i).astype(np.float32)
    mcur = (j <= i).astype(np.float32)
    c["amask"] = np.concatenate([mprev, mcur, mprev, mcur], axis=1)
    em = np.zeros((128, 64), np.float32)
    em[64, :] = 1.0
    c["em"] = em
    c["triu"] = (j <= i).astype(np.float32)
    c["smask"] = np.where(i > j, 0.0, -30000.0).astype(np.float32)
    c["odiv"] = np.full((128, 128), 1.0 / 128.0, np.float32)
    return c


def _prep_inputs(inp):
    f = lambda a: np.ascontiguousarray(a, dtype=np.float32)
    shared = {}

    def ncol(v):
        return f(np.asarray(v).reshape(NCH, 128).T)
    shared["ffn1_norm"] = ncol(inp["ffn1_norm"][0])
    shared["ffn2_norm"] = ncol(inp["ffn2_norm"][0])
    shared["mix_norm"] = ncol(inp["mix_norm"][0])
    shared["final_norm"] = ncol(inp["final_norm"])
    ffn_in = {1: (inp["ffn1_w_gate"], inp["ffn1_w_up"], inp["ffn1_w_down"]),
              2: (inp["ffn2_w_gate"], inp["ffn2_w_up"], inp["ffn2_w_down"])}
    for i in (1, 2):
        wg = np.asarray(ffn_in[i][0][0])
        wu = np.asarray(ffn_in[i][1][0])
        wd = np.asarray(ffn_in[i][2][0])
        shared["wg%d" % i] = f(wg.reshape(NCH, 128, 11, 256).transpose(2, 1, 0, 3).reshape(11, 128, NCH * 256))
        shared["wu%d" % i] = f(wu.reshape(NCH, 128, 11, 256).transpose(2, 1, 0, 3).reshape(11, 128, NCH * 256))
        shared["wd%d" % i] = f(wd.reshape(NF, 128, 8, 128).transpose(2, 1, 0, 3).reshape(8, 128, NF * 128))
    w_in = np.asarray(inp["w_in"][0])

    def cols(c0, n):
        return w_in[:, c0:c0 + n].reshape(NCH, 128, n).transpose(1, 0, 2).reshape(128, NCH * n)
    wa = np.zeros((6, 3, 128, NCH * 128), np.float32)
    for g in range(3):
        for hp in range(2):
            for i in range(3):
                wa[g * 2 + hp, i] = cols(i * 768 + (4 * g + 2 * hp) * 64, 128)
    shared["wqkv_a"] = wa
    wg_ = np.zeros((8, 3, 128, NCH * 128), np.float32)
    wgg = np.zeros((8, 128, NCH * 128), np.float32)
    for h in range(8):
        for i in range(3):
            wg_[h, i] = cols(2304 + i * 1024 + h * 128, 128)
        wgg[h] = cols(5392 + h * 128, 128)
    shared["wgdn"] = wg_
    shared["wgg"] = wgg
    shared["wbd"] = f(cols(5376, 16))
    wgab = np.zeros((8, 128, 2 * NCH * 128), np.float32)
    for dt in range(8):
        a = cols(6416 + dt * 128, 128).reshape(128, 1, NCH * 128)
        b = cols(7440 + dt * 128, 128).reshape(128, 1, NCH * 128)
        wgab[dt] = np.concatenate([a, b], axis=1).reshape(128, 2 * NCH * 128)
    shared["wgab"] = wgab
    wba = np.asarray(inp["w_branch_a"][0])
    shared["wba"] = f(wba.reshape(4, 64, 8, 128).transpose(2, 1, 0, 3).reshape(8, 64, 4 * 128))
    wbb = np.asarray(inp["w_branch_b"][0])
    shared["wbb"] = f(wbb.reshape(8, 128, 8, 128).transpose(2, 1, 0, 3).reshape(8, 128, 8 * 128))
    wo = np.asarray(inp["w_out"][0])
    shared["wout"] = f(wo.reshape(8, 128, 8, 128).transpose(2, 1, 0, 3).reshape(8, 128, 8 * 128))
    cw = np.asarray(inp["gdn_conv_w"][0])
    shared["convw"] = f(cw.reshape(4, 3, 8, 128).transpose(3, 2, 1, 0).reshape(128, 96))
    shared["alog"] = f(np.broadcast_to(np.asarray(inp["gdn_a_log"][0])[None, :], (128, 8)))
    shared["dtb"] = f(np.broadcast_to(np.asarray(inp["gdn_dt_bias"][0])[None, :], (128, 8)))
    shared["onorm"] = f(np.asarray(inp["gdn_out_norm"][0]).reshape(128, 1))
    shared.update(_consts())
    x = np.asarray(inp["x"])
    maps = []
    for b in range(8):
        m = dict(shared)
        m["xT"] = f(x[b].T)
        maps.append(m)
    return maps


_NC_CACHE = {}


def kernel(**inputs):
    stop_after = inputs.pop("_stop_after", "all")
    if stop_after not in _NC_CACHE:
        _NC_CACHE[stop_after] = build_program(stop_after)
    nc, names = _NC_CACHE[stop_after]
    maps = _prep_inputs(inputs)
    maps = [{k: v for k, v in m.items() if k in names} for m in maps]
    res = run_bass_kernel_spmd(nc, maps, core_ids=list(range(8)))
    out = np.stack([np.ascontiguousarray(r["outT"].T) for r in res.results], axis=0)
    return out.astype(np.float32)
```

```python
import contextlib
import numpy as np
import concourse.bass as bass
import concourse.mybir as mybir
from concourse.bass_utils import run_bass_kernel_spmd

F32 = mybir.dt.float32
BF16 = mybir.dt.bfloat16
AF = mybir.ActivationFunctionType
ALU = mybir.AluOpType
AX = mybir.AxisListType

S = 2048
D = 1024
NCH = 8
DFF = 2816
NF = 22
NB = 4
EPS = 1e-6
D_IN = 8464
NDS = 16
import os
ROPE_ADD_ENG = os.environ.get("ROPE_ADD_ENG", "pool")


class _Stop(Exception):
    pass


class Buf:
    __slots__ = ("w", "r", "excl")

    def __init__(self, excl=False):
        self.w = None
        self.r = []
        self.excl = excl


class Desc:
    __slots__ = ("eng", "cost", "run")

    def __init__(self, eng, cost, run):
        self.eng, self.cost, self.run = eng, cost, run


def run_streams(gens, lat=0.3):
    st = []
    for g in gens:
        try:
            st.append([g, next(g), 0.0])
        except StopIteration:
            pass
    efree = {}
    while st:
        best = None
        for s_ in st:
            start = max(s_[2], efree.get(s_[1].eng, 0.0))
            if best is None or start < best[0]:
                best = (start, s_)
        start, s_ = best
        d = s_[1]
        d.run()
        fin = start + d.cost
        efree[d.eng] = fin
        s_[2] = fin + lat
        try:
            s_[1] = next(s_[0])
        except StopIteration:
            st.remove(s_)


class Trk:
    def __init__(self, nc, es):
        self.nc = nc
        self.eng = {"pe": nc.tensor, "act": nc.scalar, "dve": nc.vector, "pool": nc.gpsimd, "sp": nc.sync}
        self.sem = {k: es.enter_context(nc.semaphore("sem_" + k)) for k in self.eng}
        self.cnt = {k: 0 for k in self.eng}
        self.seen = {k: {} for k in self.eng}
        self.dsem = [es.enter_context(nc.semaphore("dsem%d" % i)) for i in range(NDS)]
        self.dcnt = [0] * NDS
        self.dnext = {"pool": 0, "sp": NDS // 2, "act": NDS // 2}
        self.same = {"pe": False, "act": True, "dve": True, "pool": True, "sp": True}

    def _semof(self, k):
        return self.sem[k] if isinstance(k, str) else self.dsem[k[1]]

    def _wait(self, e, deps):
        best = {}
        for (k, v) in deps:
            if v > best.get(k, 0):
                best[k] = v
        for k, v in best.items():
            if k == e and not self.same[e]:
                continue
            if self.seen[e].get(k, 0) >= v:
                continue
            self.eng[e].wait_ge(self._semof(k), v)
            self.seen[e][k] = v

    @staticmethod
    def _deps(reads, writes):
        deps = []
        for b in reads:
            if b.w is not None:
                deps.append(b.w)
        for b in writes:
            if b.w is not None:
                deps.append(b.w)
            deps.extend(b.r)
        return deps

    @staticmethod
    def _upd(d, reads, writes):
        for b in reads:
            b.r.append(d)
            if len(b.r) > 64:
                best = {}
                for (k, v) in b.r:
                    if v > best.get(k, 0):
                        best[k] = v
                b.r = list(best.items())
        for b in writes:
            b.w = d
            b.r = []

    def op(self, e, fn, reads=(), writes=()):
        if any(b.excl for b in reads):
            writes = list(writes) + [b for b in reads if b.excl]
            reads = [b for b in reads if not b.excl]
        self._wait(e, self._deps(reads, writes))
        ins = fn(self.eng[e])
        self.cnt[e] += 1
        ins.then_inc(self.sem[e], 1)
        self._upd((e, self.cnt[e]), reads, writes)

    def dma(self, q, out, in_, reads=(), writes=()):
        deps = self._deps(reads, writes)
        i = self.dnext[q]
        lo = 0 if q == "pool" else NDS // 2
        self.dnext[q] = lo + (i - lo + 1) % (NDS // 2)
        if self.dcnt[i] > 0:
            deps.append((("d", i), self.dcnt[i]))
        self._wait(q, deps)
        self.eng[q].dma_start(out=out, in_=in_).then_inc(self.dsem[i], 16)
        self.dcnt[i] += 16
        self._upd((("d", i), self.dcnt[i]), reads, writes)

    def defer(self, e, fn, reads=(), writes=(), cost=None):
        return Desc(e, cost if cost is not None else {"pe": 0.3, "dve": 0.3, "act": 0.3, "pool": 0.6, "sp": 0.1}[e],
                    lambda: self.op(e, fn, reads, writes))

    def defer5(self, e, fn, reads=(), writes=(), cost=None):
        return Desc(e, cost if cost is not None else {"pe": 0.3, "dve": 0.6, "act": 0.6, "pool": 1.1, "sp": 0.1}[e],
                    lambda: self.op(e, fn, reads, writes))

    def defer_dma(self, q, out, in_, reads=(), writes=(), cost=0.2):
        return Desc(q, cost, lambda: self.dma(q, out, in_, reads, writes))

    def barrier(self):
        deps = [(k, v) for k, v in self.cnt.items() if v > 0]
        deps += [(("d", i), c) for i, c in enumerate(self.dcnt) if c > 0]
        for e in self.eng:
            self._wait(e, [d for d in deps if d[0] != e])


def build_program(stop_after="all", debug=False):
    nc = bass.Bass("TRN2", target_bir_lowering=False)
    dt_in = {}

    def din(name, shape):
        dt_in[name] = nc.dram_tensor(name, list(shape), F32, kind="ExternalInput").ap()
        return dt_in[name]

    xT_d = din("xT", [D, S])
    n1_d = din("ffn1_norm", [128, NCH])
    n2_d = din("ffn2_norm", [128, NCH])
    nm_d = din("mix_norm", [128, NCH])
    nf_d = din("final_norm", [128, NCH])
    ffn_w = []
    for i in (1, 2):
        ffn_w.append((din("wg%d" % i, [11, 128, NCH * 256]), din("wu%d" % i, [11, 128, NCH * 256]),
                      din("wd%d" % i, [8, 128, NF * 128])))
    din("wqkv_a", [6, 3, 128, NCH * 128])
    din("rope_cos", [128, S]); din("rope_sin", [128, S]); din("rm", [128, 128]); din("amask", [128, 512]); din("em", [128, 64])
    din("wgdn", [8, 3, 128, NCH * 128]); din("wgg", [8, 128, NCH * 128]); din("wbd", [128, NCH * 16])
    din("convw", [128, 96]); din("alog", [128, 8]); din("dtb", [128, 8]); din("onorm", [128, 1])
    din("triu", [128, 128]); din("smask", [128, 128]); din("odiv", [128, 128])
    din("wgab", [8, 128, 2 * NCH * 128]); din("wba", [8, 64, 4 * 128]); din("wbb", [8, 128, 8 * 128]); din("wout", [8, 128, 8 * 128])
    ident_d = din("ident_f", [128, 128])
    ones_d = din("ones_f", [128, 128])
    out_d = nc.dram_tensor("outT", [D, S], F32, kind="ExternalOutput").ap()
    dbg = {}

    with contextlib.ExitStack() as es:
        T = Trk(nc, es)

        def sb(name, shape, dt):
            return es.enter_context(nc.sbuf_tensor(name, list(shape), dt))

        xT = sb("xT_sb", [128, NCH, S], F32)
        xB = [Buf() for _ in range(NB)]
        psum = es.enter_context(nc.psum_tensor("psum", [128, 8, 512], F32))
        pB = [Buf(True) for _ in range(8)]
        pstate = {"n": 0}

        def bank():
            b = pstate["n"]
            pstate["n"] = (b + 1) % 8
            return b

        def bank2():
            b = pstate["n"]
            if b % 2:
                b = (b + 1) % 8
            pstate["n"] = (b + 2) % 8
            return b

        ident_f = sb("ident_f_sb", [128, 128], F32)
        ident_b = sb("ident_b_sb", [128, 128], BF16)
        ones_b = sb("ones_b_sb", [128, 128], BF16)
        ones_f = sb("ones_f_sb", [128, 128], F32)
        ident2_f = sb("ident2_f_sb", [128, 128], F32)
        cB = Buf()
        norms = sb("norms_sb", [128, 4, NCH], F32)
        eps_t = sb("eps_sb", [128, 2], F32)
        T.op("dve", lambda e: e.memset(eps_t[:, 0:1], EPS), writes=[cB])
        T.op("dve", lambda e: e.memset(eps_t[:, 1:2], 1.0), writes=[cB])
        T.dma("sp", ident_f[:], ident_d, writes=[cB])
        T.dma("sp", ones_f[:], ones_d, writes=[cB])
        T.op("dve", lambda e: e.tensor_single_scalar(out=ident2_f[:, :], in_=ident_f[:, :], scalar=2.0, op=ALU.mult), reads=[cB], writes=[cB])
        T.dma("pool", ident_b[:], ident_d, writes=[cB])
        T.dma("pool", ones_b[:], ones_d, writes=[cB])
        for i, nd in enumerate((n1_d, nm_d, n2_d, nf_d)):
            T.dma("sp", norms[:, i, :], nd, writes=[cB])
        xT_dv = xT_d.rearrange("(c p) t -> p c t", p=128)
        for n in range(NB):
            T.dma("sp", xT[:, :, n * 512:(n + 1) * 512], xT_dv[:, :, n * 512:(n + 1) * 512], writes=[xB[n]])

        def rmsnorm_block(n, gi, out_ap_fn, out_buf, sq_t, sq_b, rs_t, rs_b):
            tok = slice(n * 512, (n + 1) * 512)
            b = bank()
            for c in range(NCH):
                T.op("act", lambda e, c=c: e.activation(out=sq_t[:, c, :], in_=xT[:, c, tok], func=AF.Square),
                     reads=[xB[n]], writes=[sq_b[c]])
            def mm(e):
                ins = None
                for c in range(NCH):
                    ins = e.matmul(psum[:, b, :], lhsT=ones_b[:, :], rhs=sq_t[:, c, :], start=(c == 0), stop=(c == NCH - 1))
                return ins
            T.op("pe", mm, reads=sq_b + [cB], writes=[pB[b]])
            T.op("act", lambda e: e.activation(out=rs_t[:, :], in_=psum[:, b, :], func=AF.Ln, scale=1.0 / D, bias=eps_t[:, 0:1]),
                 reads=[pB[b], cB], writes=[rs_b])
            T.op("act", lambda e: e.activation(out=rs_t[:, :], in_=rs_t[:, :], func=AF.Exp, scale=-0.5), reads=[rs_b], writes=[rs_b])
            for c in range(NCH):
                T.op("dve", lambda e, c=c: e.scalar_tensor_tensor(out=out_ap_fn(c), in0=xT[:, c, tok],
                                                                  scalar=norms[:, gi, c:c + 1], in1=rs_t[:, :],
                                                                  op0=ALU.mult, op1=ALU.mult),
                     reads=[xB[n], rs_b, cB], writes=[out_buf])

        def ffn(which, gi):
            wg_d, wu_d, wd_d = ffn_w[which]
            with contextlib.ExitStack() as fs:
                def fsb(name, shape, dt):
                    return fs.enter_context(nc.sbuf_tensor("f%d_%s" % (which, name), list(shape), dt))
                hT = fsb("hT", [128, NCH, 1024], BF16)
                hB = [Buf(), Buf()]
                actT = fsb("actT", [128, NF, 1024], BF16)
                aB = [[Buf(), Buf()] for _ in range(NF)]
                sq_t = fsb("sq", [128, NCH, 512], BF16)
                sq_b = [Buf() for _ in range(NCH)]
                rs_t = fsb("rs", [128, 512], F32)
                rs_b = Buf()
                NWB = 3
                wgb = [fsb("wg%d" % i, [128, NCH, 256], BF16) for i in range(NWB)]
                wub = [fsb("wu%d" % i, [128, NCH, 256], BF16) for i in range(NWB)]
                wgB = [Buf() for _ in range(NWB)]
                wuB = [Buf() for _ in range(NWB)]
                wdb = [fsb("wd%d" % i, [128, NF, 128], BF16) for i in range(2)]
                wdB = [Buf() for _ in range(2)]
                sg = [fsb("sg%d" % i, [128, 512], F32) for i in range(3)]
                sgB = [Buf() for _ in range(3)]
                sgn = 0

                def load_gu(g):
                    i = g % NWB
                    T.dma("pool", wgb[i][:].rearrange("p c f -> p (c f)"), wg_d[g], writes=[wgB[i]])
                    T.dma("pool", wub[i][:].rearrange("p c f -> p (c f)"), wu_d[g], writes=[wuB[i]])

                def load_d(dt):
                    i = dt % 2
                    T.dma("pool", wdb[i][:].rearrange("p f j -> p (f j)"), wd_d[dt], writes=[wdB[i]])

                for half in range(2):
                    load_gu(0)
                    load_gu(1)
                    for nn in range(2):
                        n = half * 2 + nn
                        rmsnorm_block(n, gi, lambda c, nn=nn: hT[:, c, nn * 512:(nn + 1) * 512], hB[nn],
                                      sq_t, sq_b, rs_t, rs_b)
                    for g in range(11):
                        if g + 2 < 11:
                            load_gu(g + 2)
                        if g == 9:
                            load_d(0)
                        if g == 10:
                            load_d(1)
                        wi = g % NWB
                        for j in range(2):
                            f = 2 * g + j
                            for nn in range(2):
                                tk = slice(nn * 512, (nn + 1) * 512)
                                bg = bank()
                                bu = bank()

                                def mmg(e, w=wgb[wi], b=bg, j=j, tk=tk):
                                    ins = None
                                    for c in range(NCH):
                                        ins = e.matmul(psum[:, b, :], lhsT=w[:, c, j * 128:(j + 1) * 128],
                                                       rhs=hT[:, c, tk], start=(c == 0), stop=(c == NCH - 1))
                                    return ins
                                T.op("pe", mmg, reads=[wgB[wi], hB[nn]], writes=[pB[bg]])
                                T.op("pe", lambda e, b=bu, j=j, tk=tk: mmg(e, wub[wi], b, j, tk),
                                     reads=[wuB[wi], hB[nn]], writes=[pB[bu]])
                                si = sgn % 3
                                sgn += 1
                                T.op("act", lambda e, si=si, b=bg: e.activation(out=sg[si][:, :], in_=psum[:, b, :],
                                                                               func=AF.Silu),
                                     reads=[pB[bg]], writes=[sgB[si]])
                                T.op("dve", lambda e, si=si, b=bu, f=f, tk=tk: e.tensor_tensor(
                                    out=actT[:, f, tk], in0=psum[:, b, :], in1=sg[si][:, :], op=ALU.mult),
                                     reads=[pB[bu], sgB[si]], writes=[aB[f][nn]])
                    for dt in range(8):
                        if dt + 2 < 8:
                            pass
                        wi = dt % 2
                        for nn in range(2):
                            n = half * 2 + nn
                            tk = slice(nn * 512, (nn + 1) * 512)
                            b = bank()

                            def mmd(e, b=b, wi=wi, tk=tk):
                                ins = None
                                for f in range(NF):
                                    ins = e.matmul(psum[:, b, :], lhsT=wdb[wi][:, f, :], rhs=actT[:, f, tk],
                                                   start=(f == 0), stop=(f == NF - 1))
                                return ins
                            T.op("pe", mmd, reads=[wdB[wi]] + [aB[f][nn] for f in range(NF)], writes=[pB[b]])
                            T.op("dve", lambda e, b=b, dt=dt, n=n: e.scalar_tensor_tensor(
                                out=xT[:, dt, n * 512:(n + 1) * 512], in0=psum[:, b, :], scalar=0.5,
                                in1=xT[:, dt, n * 512:(n + 1) * 512], op0=ALU.mult, op1=ALU.add),
                                 reads=[pB[b]], writes=[xB[n]])
                        if dt + 2 < 8:
                            load_d(dt + 2)
                T.barrier()

        def mixer():
            with contextlib.ExitStack() as ms:
                def msb(name, shape, dt):
                    return ms.enter_context(nc.sbuf_tensor("m_" + name, list(shape), dt))
                hT = msb("hT", [128, NCH, S], BF16)
                hB = [Buf() for _ in range(NB)]
                with contextlib.ExitStack() as ns:
                    sq_t = ns.enter_context(nc.sbuf_tensor("m_sq", [128, NCH, 512], BF16))
                    sq_b = [Buf() for _ in range(NCH)]
                    rs_t = ns.enter_context(nc.sbuf_tensor("m_rs", [128, 512], F32))
                    rs_b = Buf()
                    for n in range(NB):
                        rmsnorm_block(n, 1, lambda c, n=n: hT[:, c, n * 512:(n + 1) * 512], hB[n], sq_t, sq_b, rs_t, rs_b)
                    T.barrier()
                if stop_after == "dbg_h":
                    dbg_dump([(hT[:, c, :], hB, c * 128, 128) for c in range(NCH)])
                    return True
                ybT = msb("ybT", [128, 8, S], BF16)
                ybB = [Buf() for _ in range(8)]
                gdn(hT, hB, ybT, ybB)
                T.barrier()
                if stop_after == "dbg_yb":
                    dbg_dump([(ybT[:, c, :], ybB, c * 128, 128) for c in range(8)])
                    return True
                yaT = msb("yaT", [64, 4, S], BF16)
                yaB = [Buf() for _ in range(4)]
                if attention(hT, hB, yaT, yaB):
                    return True
                T.barrier()
                if stop_after == "dbg_ya":
                    dbg_dump([(yaT[:, c, :], yaB, c * 64, 64) for c in range(4)])
                    return True
                merge(hT, hB, yaT, yaB, ybT, ybB)
                T.barrier()
            return None

        def nset(lo, hi):
            return list(range(lo // 512, hi // 512 + 1))

        def attention(hT, hB, yaT, yaB):
            wqkv_d = dt_in["wqkv_a"]
            with contextlib.ExitStack() as as_:
                def asb(name, shape, dt):
                    return as_.enter_context(nc.sbuf_tensor("a_" + name, list(shape), dt))
                cos_t = asb("cos", [128, S], BF16)
                sin_t = asb("sin", [128, S], BF16)
                rm_t = asb("rm", [128, 128], BF16)
                am_t = asb("am", [128, 512], BF16)
                em_t = asb("em", [128, 64], F32)
                acB = Buf()
                T.dma("pool", cos_t[:], dt_in["rope_cos"], writes=[acB])
                T.dma("pool", sin_t[:], dt_in["rope_sin"], writes=[acB])
                T.dma("pool", rm_t[:], dt_in["rm"], writes=[acB])
                T.dma("pool", am_t[:], dt_in["amask"], writes=[acB])
                T.dma("sp", em_t[:], dt_in["em"], writes=[acB])
                wq = [asb("w%d" % i, [128, NCH, 128], BF16) for i in range(3)]
                wB = [Buf() for _ in range(3)]
                qk = [asb("qk%d" % i, [128, S], BF16) for i in range(2)]
                qkB = [[Buf() for _ in range(NB)] for _ in range(2)]
                vT = asb("vT", [128, S], BF16)
                vTB = [Buf() for _ in range(NB)]
                Vt = asb("Vt", [128, 16, 2, 65], BF16)
                VtB = [Buf() for _ in range(16)]
                VoneB = Buf()
                T.op(ROPE_ADD_ENG, lambda e: e.memset(Vt[:, :, :, 64:65], 1.0), writes=[VoneB])
                acc = asb("acc", [128, 2, S], F32)
                accB = Buf()
                qb = asb("qb", [128, 512], BF16)
                qbB = Buf()
                t1 = asb("t1", [128, 512], F32)
                t2 = asb("t2", [128, 512], F32)
                t1B, t2B = Buf(), Buf()
                ex = [asb("ex%d" % i, [128, 512], BF16) for i in range(3)]
                exB = [Buf(), Buf(), Buf()]
                exm = [asb("exm%d" % i, [128, 512], BF16) for i in range(3)]
                exmB = [Buf(), Buf(), Buf()]
                rl = asb("rl", [64, 512], F32)
                rlB = Buf()
                exn = 0
                if stop_after == "dbg_att0":
                    dbg_dump([(cos_t[:, :], [acB], 0, 128), (sin_t[:, :], [acB, VoneB], 128, 128)])
                    return True
                for hp in range(2):
                    for g in range(3):
                        dil = (1, 4, 16)[g]
                        nbk = 16 // dil
                        pi = g * 2 + hp
                        for i in range(3):
                            T.dma("pool", wq[i][:].rearrange("p c f -> p (c f)"), wqkv_d[pi, i], writes=[wB[i]])
                        for i in range(2):
                            for n in range(NB):
                                tk = slice(n * 512, (n + 1) * 512)
                                b = bank()

                                def mm(e):
                                    ins = None
                                    for c in range(NCH):
                                        ins = e.matmul(psum[:, b, :], lhsT=wq[i][:, c, :], rhs=hT[:, c, tk],
                                                       start=(c == 0), stop=(c == NCH - 1))
                                    return ins
                                CUT = int(os.environ.get("CUT", "9"))
                                if CUT >= 1:
                                    T.op("pe", mm, reads=[wB[i], hB[n]], writes=[pB[b]])
                                if CUT >= 2:
                                    T.op("act", lambda e: e.activation(out=qb[:, :], in_=psum[:, b, :], func=AF.Copy),
                                         reads=[pB[b]], writes=[qbB])
                                b2 = bank()
                                if CUT >= 3:
                                    T.op("pe", lambda e: e.matmul(psum[:, b2, :], lhsT=rm_t[:, :], rhs=qb[:, :], start=True, stop=True),
                                         reads=[qbB, acB], writes=[pB[b2]])
                                if CUT >= 4:
                                    T.op("dve", lambda e: e.tensor_tensor(out=t1[:, :], in0=psum[:, b, :], in1=cos_t[:, tk], op=ALU.mult),
                                         reads=[pB[b], acB], writes=[t1B])
                                if CUT >= 5:
                                    T.op("dve", lambda e: e.tensor_tensor(out=t2[:, :], in0=psum[:, b2, :], in1=sin_t[:, tk], op=ALU.mult),
                                         reads=[pB[b2], acB], writes=[t2B])
                                if CUT >= 6:
                                    T.op(ROPE_ADD_ENG, lambda e: e.tensor_tensor(out=qk[i][:, tk], in0=t1[:, :], in1=t2[:, :], op=ALU.add),
                                         reads=[t1B, t2B], writes=[qkB[i][n]])
                                if stop_after == "dbg_att1a":
                                    T.barrier()
                                    with contextlib.ExitStack() as ds:
                                        dtl = ds.enter_context(nc.sbuf_tensor("dbgx", [128, 512], F32))
                                        dxB = Buf()
                                        T.op("act", lambda e: e.activation(out=dtl[:, :], in_=qk[0][:, 0:512], func=AF.Copy), reads=[qkB[0][0]], writes=[dxB])
                                        T.dma("sp", out_d[0:128, 0:512], t1[:, :], reads=[t1B])
                                        T.dma("sp", out_d[128:256, 0:512], t2[:, :], reads=[t2B])
                                        T.dma("sp", out_d[256:384, 0:512], dtl[:, :], reads=[dxB])
                                        T.barrier()
                                    return True
                        if stop_after == "dbg_att1":
                            dbg_dump([(qk[0][:, :], qkB[0], 0, 128), (qk[1][:, :], qkB[1], 128, 128)])
                            return True
                        def tokslice(bi):
                            r, nk = bi // nbk, bi % nbk
                            st = r + dil * 128 * nk
                            return st, slice(st, st + 127 * dil + 1, dil)
                        for n in range(NB):
                            tk = slice(n * 512, (n + 1) * 512)
                            b = bank()

                            def mmvf(e):
                                ins = None
                                for c in range(NCH):
                                    ins = e.matmul(psum[:, b, :], lhsT=wq[2][:, c, :], rhs=hT[:, c, tk], start=(c == 0), stop=(c == NCH - 1))
                                return ins
                            T.op("pe", mmvf, reads=[wB[2], hB[n]], writes=[pB[b]])
                            T.op("act", lambda e: e.activation(out=vT[:, tk], in_=psum[:, b, :], func=AF.Copy), reads=[pB[b]], writes=[vTB[n]])
                        for b4 in range(4):
                            b = bank()

                            def mmv(e):
                                ins = None
                                for k in range(4):
                                    st, sl = tokslice(b4 * 4 + k)
                                    ins = e.matmul(psum[:, b, k * 128:(k + 1) * 128], lhsT=vT[:, sl], rhs=ident_b[:, :], start=True, stop=True)
                                return ins
                            T.op("pe", mmv, reads=vTB + [cB], writes=[pB[b]])
                            T.op("act", lambda e: e.activation(
                                out=Vt[:, b4 * 4:(b4 + 1) * 4, :, 0:64],
                                in_=psum[:, b, :].rearrange("p (k e d) -> p k e d", k=4, e=2), func=AF.Copy),
                                 reads=[pB[b]], writes=[VtB[b4 * 4 + k] for k in range(4)])
                        if stop_after == "dbg_att2":
                            dbg_dump([(Vt[:, :, :, :].rearrange("p a b c -> p (a b c)")[:, 0:2048], VtB + [VoneB], 0, 128)])
                            return True
                        def stage1(bi):
                            nonlocal exn
                            r, nk = bi // nbk, bi % nbk
                            st, qsl = tokslice(bi)
                            hi_tok = st + 127 * dil
                            qn_set = nset(st, hi_tok)
                            b = bank2()
                            if nk > 0:
                                stp, ksl_p = tokslice(bi - 1)
                                kn_set = nset(stp, hi_tok)
                            else:
                                kn_set = qn_set

                            def mms(e):
                                ins = None
                                for hd in range(2):
                                    hs = slice(hd * 64, (hd + 1) * 64)
                                    if nk > 0:
                                        ins = e.matmul(psum[:, b + hd, 0:128], lhsT=qk[1][hs, ksl_p], rhs=qk[0][hs, qsl],
                                                       start=True, stop=True)
                                    ins = e.matmul(psum[:, b + hd, 128:256], lhsT=qk[1][hs, qsl], rhs=qk[0][hs, qsl],
                                                   start=True, stop=True)
                                return ins
                            T.op("pe", mms, reads=[qkB[0][n] for n in qn_set] + [qkB[1][n] for n in kn_set], writes=[pB[b], pB[b + 1]])
                            xi = exn % 3
                            exn += 1
                            lo = 0 if nk > 0 else 128
                            src = psum[:, b:b + 2, lo:256]
                            dst = ex[xi][:, :].rearrange("p (e w) -> p e w", e=2)[:, :, lo:256]
                            dstm = exm[xi][:, :].rearrange("p (e w) -> p e w", e=2)[:, :, lo:256]
                            msk = am_t[:, :].rearrange("p (e w) -> p e w", e=2)[:, :, lo:256]
                            T.op("act", lambda e: e.activation(out=dst, in_=src, func=AF.Exp, scale=0.125),
                                 reads=[pB[b], pB[b + 1]], writes=[exB[xi]])
                            T.op("pool", lambda e: e.tensor_tensor(out=dstm, in0=dst, in1=msk, op=ALU.mult),
                                 reads=[exB[xi], acB], writes=[exmB[xi]])
                            return xi

                        def stage2(bi, xi):
                            r, nk = bi // nbk, bi % nbk
                            st, qsl = tokslice(bi)
                            b2 = bank()

                            def mmu(e):
                                ins = None
                                for hd in range(2):
                                    o = psum[0:65, b2, hd * 128:(hd + 1) * 128]
                                    if nk > 0:
                                        e.matmul(o, lhsT=Vt[:, bi - 1, hd, :], rhs=exm[xi][:, hd * 256:hd * 256 + 128], start=True, stop=False)
                                    ins = e.matmul(o, lhsT=Vt[:, bi, hd, :], rhs=exm[xi][:, hd * 256 + 128:hd * 256 + 256],
                                                   start=(nk == 0), stop=True)
                                return ins
                            T.op("pe", mmu, reads=[exmB[xi], VtB[bi], VoneB] + ([VtB[bi - 1]] if nk > 0 else []), writes=[pB[b2]])
                            src2 = psum[0:65, b2, 0:256].rearrange("p (e q) -> p e q", e=2)
                            if g == 0:
                                T.op("dve", lambda e: e.tensor_copy(out=acc[0:65, :, qsl], in_=src2), reads=[pB[b2]], writes=[accB])
                            else:
                                T.op("dve", lambda e: e.tensor_tensor(out=acc[0:65, :, qsl], in0=src2, in1=acc[0:65, :, qsl], op=ALU.add),
                                     reads=[pB[b2]], writes=[accB])
                        pend = []
                        for bi in range(16):
                            pend.append((bi, stage1(bi)))
                            if len(pend) > 1:
                                stage2(*pend.pop(0))
                        while pend:
                            stage2(*pend.pop(0))
                        if stop_after == "dbg_att3":
                            dbg_dump([(acc[:, 0, :], [accB], 0, 128), (acc[:, 1, :], [accB], 128, 128)])
                            return True
                    for hd in range(2):
                        for n in range(NB):
                            tk = slice(n * 512, (n + 1) * 512)
                            b = bank()
                            T.op("pe", lambda e: e.matmul(psum[0:64, b, :], lhsT=em_t[0:65, :], rhs=acc[0:65, hd, tk], start=True, stop=True),
                                 reads=[accB, acB], writes=[pB[b]])
                            T.op("dve", lambda e: e.reciprocal(out=rl[:, :], in_=psum[0:64, b, :]), reads=[pB[b]], writes=[rlB])
                            T.op("dve", lambda e: e.tensor_tensor(out=yaT[:, 2 * hp + hd, tk], in0=acc[0:64, hd, tk], in1=rl[:, :], op=ALU.mult),
                                 reads=[rlB, accB], writes=[yaB[2 * hp + hd]])
                T.barrier()

        def gdn(hT, hB, ybT, ybB):
            with contextlib.ExitStack() as gs:
                def gsb(name, shape, dt):
                    return gs.enter_context(nc.sbuf_tensor("g_" + name, list(shape), dt))
                gcB = Buf()
                triu = gsb("triu", [128, 128], F32)
                smask = gsb("smask", [128, 128], F32)
                odiv = gsb("odiv", [128, 128], BF16)
                convw = gsb("convw", [128, 8, 3, 4], F32)
                alog = gsb("alog", [128, 8], F32)
                dtb = gsb("dtb", [128, 8], F32)
                onorm = gsb("onorm", [128, 1], F32)
                wbd = gsb("wbd", [128, NCH, 16], BF16)
                T.dma("sp", triu[:], dt_in["triu"], writes=[gcB])
                T.dma("sp", smask[:], dt_in["smask"], writes=[gcB])
                T.dma("pool", odiv[:], dt_in["odiv"], writes=[gcB])
                T.dma("sp", convw[:].rearrange("p h w t -> p (h w t)"), dt_in["convw"], writes=[gcB])
                T.dma("sp", alog[:], dt_in["alog"], writes=[gcB])
                T.dma("sp", dtb[:], dt_in["dtb"], writes=[gcB])
                T.dma("sp", onorm[:], dt_in["onorm"], writes=[gcB])
                T.dma("pool", wbd[:].rearrange("p c f -> p (c f)"), dt_in["wbd"], writes=[gcB])
                braw = gsb("braw", [128, 16, 16], F32)
                beta = gsb("beta", [128, 16, 8], F32)
                negb = gsb("negb", [128, 16, 8], F32)
                gt = gsb("gt", [128, 16, 8], F32)
                gcum = gsb("gcum", [128, 16, 8], F32)
                glast = gsb("glast", [128, 16, 8], F32)
                kds = gsb("kds", [128, 16, 8], F32)
                sdec = gsb("sdec", [128, 16, 8], F32)
                egc = gsb("egc", [128, 16, 8], F32)
                tmpg = gsb("tmpg", [128, 16, 8], F32)
                negA = gsb("negA", [128, 8], F32)
                gB = Buf()
                b = bank()

                def mmb(e):
                    ins = None
                    for t in range(16):
                        for c in range(NCH):
                            ins = e.matmul(psum[:, b, t * 16:(t + 1) * 16], lhsT=hT[:, c, t * 128:(t + 1) * 128], rhs=wbd[:, c, :],
                                           start=(c == 0), stop=(c == NCH - 1))
                    return ins
                T.op("pe", mmb, reads=hB + [gcB], writes=[pB[b]])
                T.op("act", lambda e: e.activation(out=braw[:].rearrange("p t k -> p (t k)"), in_=psum[:, b, 0:256], func=AF.Copy),
                     reads=[pB[b]], writes=[gB])
                T.op("act", lambda e: e.activation(out=tmpg[:], in_=braw[:, :, 0:8], func=AF.Exp, scale=-1.0), reads=[gB], writes=[gB])
                T.op("dve", lambda e: e.tensor_single_scalar(out=tmpg[:], in_=tmpg[:], scalar=1.0, op=ALU.add), reads=[gB], writes=[gB])
                T.op("dve", lambda e: e.reciprocal(out=beta[:], in_=tmpg[:]), reads=[gB], writes=[gB])
                T.op("dve", lambda e: e.tensor_single_scalar(out=negb[:], in_=beta[:], scalar=-1.0, op=ALU.mult), reads=[gB], writes=[gB])
                T.op("dve", lambda e: e.tensor_tensor(out=tmpg[:], in0=braw[:, :, 8:16], in1=dtb[:].unsqueeze(1).to_broadcast([128, 16, 8]), op=ALU.add),
                     reads=[gB, gcB], writes=[gB])
                T.op("act", lambda e: e.activation(out=tmpg[:], in_=tmpg[:], func=AF.Exp), reads=[gB], writes=[gB])
                T.op("act", lambda e: e.activation(out=tmpg[:], in_=tmpg[:], func=AF.Ln, bias=eps_t[:, 1:2]), reads=[gB, cB], writes=[gB])
                T.op("act", lambda e: e.activation(out=negA[:], in_=alog[:], func=AF.Exp), reads=[gcB], writes=[gB])
                T.op("dve", lambda e: e.tensor_single_scalar(out=negA[:], in_=negA[:], scalar=-1.0, op=ALU.mult), reads=[gB], writes=[gB])
                T.op("dve", lambda e: e.tensor_tensor(out=gt[:], in0=tmpg[:], in1=negA[:].unsqueeze(1).to_broadcast([128, 16, 8]), op=ALU.mult),
                     reads=[gB], writes=[gB])
                b = bank()
                T.op("pe", lambda e: e.matmul(psum[:, b, 0:128], lhsT=triu[:, :], rhs=gt[:].rearrange("p t k -> p (t k)"), start=True, stop=True),
                     reads=[gB, gcB], writes=[pB[b]])
                T.op("act", lambda e: e.activation(out=gcum[:].rearrange("p t k -> p (t k)"), in_=psum[:, b, 0:128], func=AF.Copy),
                     reads=[pB[b]], writes=[gB])
                b = bank()
                T.op("pe", lambda e: e.matmul(psum[:, b, 0:128], lhsT=ones_f[:, :], rhs=gt[:].rearrange("p t k -> p (t k)"), start=True, stop=True),
                     reads=[gB, cB], writes=[pB[b]])
                T.op("act", lambda e: e.activation(out=glast[:].rearrange("p t k -> p (t k)"), in_=psum[:, b, 0:128], func=AF.Copy),
                     reads=[pB[b]], writes=[gB])
                T.op("dve", lambda e: e.tensor_tensor(out=tmpg[:], in0=glast[:], in1=gcum[:], op=ALU.subtract), reads=[gB], writes=[gB])
                T.op("act", lambda e: e.activation(out=kds[:], in_=tmpg[:], func=AF.Exp), reads=[gB], writes=[gB])
                T.op("act", lambda e: e.activation(out=sdec[:], in_=glast[:], func=AF.Exp), reads=[gB], writes=[gB])
                T.op("act", lambda e: e.activation(out=egc[:], in_=gcum[:], func=AF.Exp), reads=[gB], writes=[gB])

                wh = [gsb("wh%d" % i, [128, NCH, 128], BF16) for i in range(4)]
                whB = [Buf() for _ in range(4)]
                cv = gsb("cv", [128, 512], F32)
                cvB = Buf()
                xb = [gsb("xb%d" % i, [128, 4 + 512], BF16) for i in range(3)]
                xbB = [Buf() for _ in range(3)]
                kqt = [gsb("kq%d" % st_, [128, 2, 512], BF16) for st_ in range(2)]
                vts = [gsb("vt%d" % st_, [128, 512], BF16) for st_ in range(2)]
                qkvs = [[kqt[st_][:, 1, :], kqt[st_][:, 0, :], vts[st_][:, :]] for st_ in range(2)]
                qkvsB = [[Buf() for _ in range(3)] for _ in range(2)]
                gsils = [gsb("gsil%d" % st_, [128, 512], BF16) for st_ in range(3)]
                gsilsB = [Buf(), Buf(), Buf()]
                HsS = [gsb("HsS%d" % i, [128, 4, 128], F32) for i in range(2)]
                ATsS = [gsb("ATsS%d" % i, [128, 4, 128], F32) for i in range(2)]
                O0sS = [gsb("O0sS%d" % i, [128, 4, 128], F32) for i in range(2)]
                QeffS = [gsb("QeffS%d" % i, [128, 4, 128], BF16) for i in range(2)]
                HsSB = [[Buf() for _ in range(4)] for _ in range(2)]
                ATsSB = [[Buf() for _ in range(4)] for _ in range(2)]
                O0sSB = [[Buf() for _ in range(4)] for _ in range(2)]
                QeffSB = [[Buf() for _ in range(4)] for _ in range(2)]
                osum = gsb("osum", [128, 4, 128], F32)
                osumB = Buf()
                rr2 = gsb("rr2", [128, 4, 128], F32)
                rr2B = Buf()
                g1bk = {"n": 0}

                def g1bank():
                    g1bk["n"] = 1 - g1bk["n"]
                    return 4 + g1bk["n"]
                sqb = gsb("sqb", [128, 512], BF16)
                sqbB = Buf()
                rt = gsb("rt", [128, 512], F32)
                rtB = Buf()
                Fm = [gsb("F%d" % i, [128, 4, 128], F32) for i in range(3)]
                FB = [[Buf() for _ in range(4)] for _ in range(3)]
                QT = gsb("QT", [128, 4, 128], BF16)
                QTB = [Buf() for _ in range(4)]
                PQ = [gsb("PQ%d" % i, [128, 2, 4, 128], BF16) for i in range(2)]
                PQB = [[Buf() for _ in range(4)] for _ in range(2)]
                Wt = gsb("Wt", [128, 4, 128], BF16)
                WtB = [Buf() for _ in range(4)]
                ATt = gsb("ATt", [128, 4, 128], BF16)
                ATtB = [Buf() for _ in range(4)]
                Aqk = gsb("Aqk", [128, 4, 128], BF16)
                AqkB = [Buf() for _ in range(4)]
                kdec = gsb("kdec", [128, 4, 128], BF16)
                kdecB = [Buf() for _ in range(4)]
                VK = gsb("VK", [128, 4, 256], BF16)
                VKB = [Buf() for _ in range(4)]
                Zt = gsb("Zt", [128, 4, 256], BF16)
                ZtB = [Buf() for _ in range(4)]
                osq = gsb("osq", [128, 512], BF16)
                osqB = Buf()
                Sf = gsb("Sf", [128, 128], F32)
                Sb = gsb("Sb", [128, 128], BF16)
                SfB, SbB = Buf(), Buf()
                QSCALE = 128.0 ** -0.5
                hbk = {"n": 0}

                def hbank():
                    b_ = 4 + hbk["n"]
                    hbk["n"] = (hbk["n"] + 1) % 3
                    return b_

                def pv(bk):
                    return psum[:, bk, :].rearrange("p (c i) -> p c i", c=4)

                def fl(tl):
                    return tl[:].rearrange("p c i -> p (c i)")

                def g1b(h, n, st, sg):
                    tk = slice(n * 512, (n + 1) * 512)
                    if n == 0:
                        for i in range(3):
                            yield T.defer_dma("pool", wh[i][:].rearrange("p c f -> p (c f)"), dt_in["wgdn"][h, i], writes=[whB[i]])
                        yield T.defer_dma("pool", wh[3][:].rearrange("p c f -> p (c f)"), dt_in["wgg"][h], writes=[whB[3]])
                    for w3 in range(3):
                        if n == 0:
                            yield T.defer5("pool", lambda e: e.memset(xb[w3][:, 0:4], 0.0), writes=[xbB[w3]])
                        else:
                            yield T.defer5("pool", lambda e: e.tensor_copy(out=xb[w3][:, 0:4], in_=xb[w3][:, 512:516]), reads=[xbB[w3]], writes=[xbB[w3]])
                        b = g1bank()

                        def mm(e):
                            ins = None
                            for c in range(NCH):
                                ins = e.matmul(psum[:, b, :], lhsT=wh[w3][:, c, :], rhs=hT[:, c, tk], start=(c == 0), stop=(c == NCH - 1))
                            return ins
                        yield T.defer5("pe", mm, reads=[whB[w3], hB[n]], writes=[pB[b]], cost=2.1)
                        yield T.defer5("act", lambda e: e.activation(out=xb[w3][:, 4:516], in_=psum[:, b, :], func=AF.Copy), reads=[pB[b]], writes=[xbB[w3]])
                        yield T.defer5("dve", lambda e: e.tensor_scalar_mul(out=cv[:, :], in0=xb[w3][:, 1:513], scalar1=convw[:, h, w3, 0:1]),
                             reads=[xbB[w3], gcB], writes=[cvB])
                        for tp in range(1, 4):
                            yield T.defer5("dve", lambda e: e.scalar_tensor_tensor(out=cv[:, :], in0=xb[w3][:, 1 + tp:513 + tp], scalar=convw[:, h, w3, tp:tp + 1],
                                                                         in1=cv[:, :], op0=ALU.mult, op1=ALU.add), reads=[xbB[w3], gcB, cvB], writes=[cvB])
                        yield T.defer5("act", lambda e: e.activation(out=qkvs[st][w3][:, :], in_=cv[:, :], func=AF.Silu), reads=[cvB], writes=[qkvsB[st][w3]])
                        if w3 < 2:
                            yield T.defer5("act", lambda e: e.activation(out=sqb[:, :], in_=qkvs[st][w3][:, :], func=AF.Square), reads=[qkvsB[st][w3]], writes=[sqbB])
                            b2 = g1bank()
                            yield T.defer5("pe", lambda e: e.matmul(psum[:, b2, :], lhsT=ones_b[:, :], rhs=sqb[:, :], start=True, stop=True),
                                 reads=[sqbB, cB], writes=[pB[b2]])
                            yield T.defer5("act", lambda e: e.activation(out=rt[:, :], in_=psum[:, b2, :], func=AF.Ln, bias=eps_t[:, 0:1]), reads=[pB[b2], cB], writes=[rtB])
                            yield T.defer5("act", lambda e: e.activation(out=rt[:, :], in_=rt[:, :], func=AF.Exp, scale=-0.5), reads=[rtB], writes=[rtB])
                            yield T.defer5("dve", lambda e: e.scalar_tensor_tensor(out=qkvs[st][w3][:, :], in0=qkvs[st][w3][:, :], scalar=(QSCALE if w3 == 0 else 1.0),
                                                                         in1=rt[:, :], op0=ALU.mult, op1=ALU.mult),
                                 reads=[rtB, qkvsB[st][w3]], writes=[qkvsB[st][w3]])
                    b = g1bank()

                    def mm(e):
                        ins = None
                        for c in range(NCH):
                            ins = e.matmul(psum[:, b, :], lhsT=wh[3][:, c, :], rhs=hT[:, c, tk], start=(c == 0), stop=(c == NCH - 1))
                        return ins
                    yield T.defer5("pe", mm, reads=[whB[3], hB[n]], writes=[pB[b]], cost=2.1)
                    yield T.defer5("act", lambda e: e.activation(out=rt[:, :], in_=psum[:, b, :], func=AF.Silu), reads=[pB[b]], writes=[rtB])
                    yield T.defer5("dve", lambda e: e.tensor_scalar_mul(out=gsils[sg][:, :], in0=rt[:, :], scalar1=onorm[:, 0:1]), reads=[rtB, gcB], writes=[gsilsB[sg]])

                def scanpost(h, gi, pg, sg):
                    t0 = gi * 4
                    gk = slice(gi * 512, (gi + 1) * 512)
                    Hs, ATs, O0s, Qf = HsS[pg], ATsS[pg], O0sS[pg], QeffS[pg]
                    bOo, bS = 7, 6
                    for c in range(4):
                        t = t0 + c
                        if t > 0:
                            yield T.defer("pe", lambda e: e.matmul(psum[:, bOo, c * 128:(c + 1) * 128], lhsT=Sb[:, :], rhs=Qf[:, c, :], start=True, stop=True),
                                          reads=[SbB, QeffSB[pg][c]], writes=[pB[bOo]])
                        if t == 15:
                            continue
                        if t == 0:
                            yield T.defer("dve", lambda e: e.tensor_copy(out=Sf[:, :], in_=Hs[:, 0, :]), reads=[HsSB[pg][0]], writes=[SfB])
                        else:
                            yield T.defer("pe", lambda e: e.matmul(psum[:, bS, 0:128], lhsT=ATs[:, c, :], rhs=Sf[:, :], start=True, stop=True),
                                          reads=[ATsSB[pg][c], SfB], writes=[pB[bS]], cost=0.5)
                            yield T.defer("dve", lambda e: e.tensor_tensor(out=Sf[:, :], in0=psum[:, bS, 0:128], in1=Hs[:, c, :], op=ALU.add),
                                          reads=[pB[bS], HsSB[pg][c]], writes=[SfB])
                        yield T.defer("act", lambda e: e.activation(out=Sb[:, :], in_=Sf[:, :], func=AF.Copy), reads=[SfB], writes=[SbB])
                    if gi == 0:
                        yield T.defer("dve", lambda e: e.tensor_copy(out=osum[:, 0, :], in_=O0s[:, 0, :]), reads=[O0sSB[pg][0]], writes=[osumB])
                        yield T.defer("dve", lambda e: e.tensor_tensor(out=osum[:, 1:4, :], in0=pv(bOo)[:, 1:4, :], in1=O0s[:, 1:4, :], op=ALU.add),
                                      reads=[pB[bOo]] + O0sSB[pg][1:4], writes=[osumB], cost=0.5)
                    else:
                        yield T.defer("dve", lambda e: e.tensor_tensor(out=osum[:], in0=pv(bOo), in1=O0s[:], op=ALU.add), reads=[pB[bOo]] + O0sSB[pg], writes=[osumB], cost=0.6)
                    yield T.defer("act", lambda e: e.activation(out=osq[:, :], in_=fl(osum), func=AF.Square), reads=[osumB], writes=[osqB], cost=0.6)
                    yield T.defer("pe", lambda e: e.matmul(psum[:, bS, :], lhsT=odiv[:, :], rhs=osq[:, :], start=True, stop=True), reads=[osqB, gcB], writes=[pB[bS]])
                    yield T.defer("act", lambda e: e.activation(out=fl(rr2), in_=psum[:, bS, :], func=AF.Ln, bias=eps_t[:, 0:1]), reads=[pB[bS], cB], writes=[rr2B], cost=0.6)
                    yield T.defer("act", lambda e: e.activation(out=fl(rr2), in_=fl(rr2), func=AF.Exp, scale=-0.5), reads=[rr2B], writes=[rr2B], cost=0.6)
                    yield T.defer("dve", lambda e: e.tensor_tensor(out=osum[:], in0=osum[:], in1=rr2[:], op=ALU.mult), reads=[osumB, rr2B], writes=[osumB], cost=0.6)
                    yield T.defer("pool", lambda e: e.tensor_tensor(out=ybT[:, h, gk], in0=fl(osum), in1=gsils[sg][:, :], op=ALU.mult),
                                  reads=[osumB, gsilsB[sg]], writes=[ybB[h]], cost=1.1)

                blocks = [(h_, n_) for h_ in range(8) for n_ in range(4)]
                NBK = len(blocks)
                run_streams([g1b(0, 0, 0, 0)])
                for r in range(NBK + 1):
                    gens = []
                    if r < NBK:
                        h, gi = blocks[r]
                        st = r % 2
                        pg = r % 2
                        t0 = gi * 4
                        cks = [slice(c * 128, (c + 1) * 128) for c in range(4)]
                        qn_, kn_, vn_ = qkvs[st][0], qkvs[st][1], qkvs[st][2]
                        kq_ = kqt[st]
                        qB, kB_, vB_ = qkvsB[st][0], qkvsB[st][1], qkvsB[st][2]

                        def prep(c):
                            t = t0 + c
                            ck = cks[c]
                            ba = bb = c
                            DT, decF, EG = Fm[0][:, c, :], Fm[1][:, c, :], Fm[2][:, c, :]
                            gc = gcum[:, t, h:h + 1]
                            yield T.defer("pe", lambda e: e.matmul(psum[:, ba, 0:128], lhsT=gt[:, t, h:h + 1].to_broadcast([128, 128]), rhs=triu[:, :],
                                                          start=True, stop=True), reads=[gB, gcB], writes=[pB[ba]])
                            yield T.defer("dve", lambda e: e.scalar_tensor_tensor(out=DT, in0=psum[:, ba, 0:128], scalar=gc, in1=smask[:, :],
                                                                         op0=ALU.subtract, op1=ALU.add), reads=[pB[ba], gB, gcB], writes=[FB[0][c]])
                            yield T.defer("act", lambda e: e.activation(out=EG, in_=psum[:, ba, 0:128], func=AF.Exp), reads=[pB[ba]], writes=[FB[2][c]])
                            yield T.defer("act", lambda e: e.activation(out=DT, in_=DT, func=AF.Exp), reads=[FB[0][c]], writes=[FB[0][c]])
                            yield T.defer("pool", lambda e: e.tensor_tensor(out=decF, in0=DT, in1=ident_f[:, :], op=ALU.add), reads=[FB[0][c], cB], writes=[FB[1][c]])
                            yield T.defer("pe", lambda e: e.matmul(psum[:, ba, 0:256], lhsT=kn_[:, ck], rhs=kq_[:, :, ck], start=True, stop=True),
                                 reads=[kB_, qB], writes=[pB[ba]])
                            yield T.defer("dve", lambda e: e.scalar_tensor_tensor(out=QT[:, c, :], in0=psum[:, ba, 0:128], scalar=negb[:, t, h:h + 1], in1=DT,
                                                                         op0=ALU.mult, op1=ALU.mult), reads=[pB[ba], gB, FB[0][c]], writes=[QTB[c]])
                            yield T.defer("dve", lambda e: e.tensor_tensor(out=Aqk[:, c, :], in0=psum[:, ba, 128:256], in1=decF, op=ALU.mult),
                                 reads=[pB[ba], FB[1][c]], writes=[AqkB[c]])
                            yield T.defer("pe", lambda e: e.matmul(psum[:, bb, 128:256], lhsT=QT[:, c, :], rhs=ident_b[:, :], start=True, stop=True),
                                 reads=[QTB[c], cB], writes=[pB[bb]])
                            yield T.defer("dve", lambda e: e.tensor_tensor(out=PQ[0][:, 0, c, :], in0=psum[:, bb, 128:256], in1=ident_f[:, :], op=ALU.add),
                                 reads=[pB[bb], cB], writes=[PQB[0][c]])
                            yield T.defer("pool", lambda e: e.tensor_tensor(out=PQ[0][:, 1, c, :], in0=QT[:, c, :], in1=ident_b[:, :], op=ALU.add),
                                 reads=[QTB[c], cB], writes=[PQB[0][c]])
                            yield T.defer("pool", lambda e: e.tensor_tensor(out=ATt[:, c, :], in0=ident_b[:, :], in1=QT[:, c, :], op=ALU.subtract),
                                 reads=[QTB[c], cB], writes=[ATtB[c]])
                            yield T.defer("pe", lambda e: e.matmul(psum[:, ba, 0:128], lhsT=kn_[:, ck], rhs=ident_b[:, :], start=True, stop=True),
                                 reads=[kB_, cB], writes=[pB[ba]])
                            yield T.defer("dve", lambda e: e.tensor_scalar_mul(out=kdec[:, c, :], in0=psum[:, ba, 0:128], scalar1=kds[:, t, h:h + 1]),
                                 reads=[pB[ba], gB], writes=[kdecB[c]])
                            yield T.defer("dve", lambda e: e.tensor_scalar_mul(out=VK[:, c, 128:256], in0=psum[:, ba, 0:128], scalar1=egc[:, t, h:h + 1]),
                                 reads=[pB[ba], gB], writes=[VKB[c]])
                            yield T.defer("pe", lambda e: e.matmul(psum[:, bb, 128:256], lhsT=vn_[:, ck], rhs=ident_b[:, :], start=True, stop=True),
                                 reads=[vB_, cB], writes=[pB[bb]])
                            yield T.defer("act", lambda e: e.activation(out=VK[:, c, 0:128], in_=psum[:, bb, 128:256], func=AF.Copy), reads=[pB[bb]], writes=[VKB[c]])
                            cur = 0
                            for k in range(2, 8):
                                nxt = 1 - cur
                                last = (k == 7)
                                yield T.defer("pe", lambda e: e.matmul(psum[:, ba, 0:128], lhsT=ATt[:, c, :], rhs=PQ[cur][:, 0, c, :], start=True, stop=True),
                                     reads=[ATtB[c], PQB[cur][c]], writes=[pB[ba]])
                                yield T.defer("dve", lambda e: e.scalar_tensor_tensor(out=Wt[:, c, :], in0=psum[:, ba, 0:128], scalar=-1.0, in1=ident2_f[:, :],
                                                                             op0=ALU.mult, op1=ALU.add), reads=[pB[ba], cB], writes=[WtB[c]])

                                def mmn(e):
                                    ins = e.matmul(psum[:, bb, 128:256], lhsT=Wt[:, c, :], rhs=PQ[cur][:, 1, c, :], start=True, stop=True)
                                    if not last:
                                        ins = e.matmul(psum[:, ba, 0:128], lhsT=PQ[cur][:, 1, c, :], rhs=Wt[:, c, :], start=True, stop=True)
                                    return ins
                                yield T.defer("pe", mmn, reads=[WtB[c], PQB[cur][c]], writes=[pB[ba]])
                                if last:
                                    yield T.defer("act", lambda e: e.activation(out=QT[:, c, :], in_=psum[:, bb, 128:256], func=AF.Copy), reads=[pB[bb]], writes=[QTB[c]])
                                else:
                                    yield T.defer("act", lambda e: e.activation(out=PQ[nxt][:, :, c, :], in_=psum[:, ba, 0:256].rearrange("p (a i) -> p a i", a=2), func=AF.Copy),
                                         reads=[pB[ba]], writes=[PQB[nxt][c]])
                                cur = nxt
                            yield T.defer("pe", lambda e: e.matmul(psum[:, ba, 0:256], lhsT=QT[:, c, :], rhs=VK[:, c, :], start=True, stop=True),
                                 reads=[QTB[c], VKB[c]], writes=[pB[ba]])
                            yield T.defer("dve", lambda e: e.tensor_scalar_mul(out=Zt[:, c, :], in0=psum[:, ba, 0:256], scalar1=beta[:, t, h:h + 1]),
                                 reads=[pB[ba], gB], writes=[ZtB[c]])
                            yield T.defer("pe", lambda e: e.matmul(psum[:, bb, 128:256], lhsT=kdec[:, c, :], rhs=Zt[:, c, 0:128], start=True, stop=True),
                                 reads=[kdecB[c], ZtB[c]], writes=[pB[bb]])
                            yield T.defer("act", lambda e: e.activation(out=HsS[pg][:, c, :], in_=psum[:, bb, 128:256], func=AF.Copy), reads=[pB[bb]], writes=[HsSB[pg][c]])
                            yield T.defer("pe", lambda e: e.matmul(psum[:, ba, 0:128], lhsT=Zt[:, c, 128:256], rhs=kdec[:, c, :], start=True, stop=True),
                                 reads=[kdecB[c], ZtB[c]], writes=[pB[ba]])
                            yield T.defer("dve", lambda e: e.scalar_tensor_tensor(out=ATsS[pg][:, c, :], in0=ident_f[:, :], scalar=sdec[:, t, h:h + 1], in1=psum[:, ba, 0:128],
                                                                         op0=ALU.mult, op1=ALU.subtract), reads=[pB[ba], gB, cB], writes=[ATsSB[pg][c]])
                            yield T.defer("pe", lambda e: e.matmul(psum[:, bb, 128:256], lhsT=Zt[:, c, 0:128], rhs=Aqk[:, c, :], start=True, stop=True),
                                 reads=[AqkB[c], ZtB[c]], writes=[pB[bb]])
                            yield T.defer("act", lambda e: e.activation(out=O0sS[pg][:, c, :], in_=psum[:, bb, 128:256], func=AF.Copy), reads=[pB[bb]], writes=[O0sSB[pg][c]])
                            yield T.defer("pe", lambda e: e.matmul(psum[:, ba, 0:128], lhsT=Zt[:, c, 128:256], rhs=Aqk[:, c, :], start=True, stop=True),
                                 reads=[AqkB[c], ZtB[c]], writes=[pB[ba]])
                            yield T.defer("pool", lambda e: e.tensor_tensor(out=decF, in0=qn_[:, ck], in1=EG, op=ALU.mult), reads=[qB, FB[2][c], FB[1][c]], writes=[FB[1][c]])
                            yield T.defer("dve", lambda e: e.tensor_tensor(out=QeffS[pg][:, c, :], in0=decF, in1=psum[:, ba, 0:128], op=ALU.subtract),
                                 reads=[pB[ba], FB[1][c]], writes=[QeffSB[pg][c]])

                        gens += [prep(c) for c in range(4)]
                    if r + 1 < NBK:
                        hn, nn = blocks[r + 1]
                        gens.append(g1b(hn, nn, (r + 1) % 2, (r + 1) % 3))
                    if r >= 1:
                        hp, gp = blocks[r - 1]
                        gens.append(scanpost(hp, gp, (r - 1) % 2, (r - 1) % 3))
                    run_streams(gens)
                T.barrier()

        def merge(hT, hB, yaT, yaB, ybT, ybB):
            with contextlib.ExitStack() as gs:
                def gsb(name, shape, dt):
                    return gs.enter_context(nc.sbuf_tensor("x_" + name, list(shape), dt))
                mg = gsb("mg", [128, 8, S], BF16)
                mgB = [[Buf() for _ in range(NB)] for _ in range(8)]
                wga = [gsb("wga%d" % i, [128, 2, NCH, 128], BF16) for i in range(2)]
                wba = [gsb("wba%d" % i, [64, 4, 128], BF16) for i in range(2)]
                wbb = [gsb("wbb%d" % i, [128, 8, 128], BF16) for i in range(2)]
                wB = [Buf(), Buf()]
                wo = [gsb("wo%d" % i, [128, 8, 128], BF16) for i in range(2)]
                woB = [Buf(), Buf()]
                one_c = eps_t[:, 1:2]
                sa = gsb("sa", [128, 512], F32)
                sb_ = gsb("sb", [128, 512], F32)
                ta = gsb("ta", [128, 512], F32)
                tb = gsb("tb", [128, 512], F32)
                saB, sbB, taB, tbB = Buf(), Buf(), Buf(), Buf()

                def loadw(dt):
                    i = dt % 2
                    T.dma("pool", wga[i][:].rearrange("p w c f -> p (w c f)"), dt_in["wgab"][dt], writes=[wB[i]])
                    T.dma("pool", wba[i][:].rearrange("p s f -> p (s f)"), dt_in["wba"][dt], writes=[wB[i]])
                    T.dma("pool", wbb[i][:].rearrange("p s f -> p (s f)"), dt_in["wbb"][dt], writes=[wB[i]])
                loadw(0)
                loadw(1)
                for dt in range(8):
                    wi = dt % 2
                    for n in range(NB):
                        tk = slice(n * 512, (n + 1) * 512)
                        bga, bgb, ba, bb = bank(), bank(), bank(), bank()

                        def mmg(e, which, b):
                            ins = None
                            for c in range(NCH):
                                ins = e.matmul(psum[:, b, :], lhsT=wga[wi][:, which, c, :], rhs=hT[:, c, tk], start=(c == 0), stop=(c == NCH - 1))
                            return ins
                        T.op("pe", lambda e: mmg(e, 0, bga), reads=[wB[wi], hB[n]], writes=[pB[bga]])
                        T.op("pe", lambda e: mmg(e, 1, bgb), reads=[wB[wi], hB[n]], writes=[pB[bgb]])

                        def mma(e):
                            ins = None
                            for s4 in range(4):
                                ins = e.matmul(psum[:, ba, :], lhsT=wba[wi][:, s4, :], rhs=yaT[:, s4, tk], start=(s4 == 0), stop=(s4 == 3))
                            return ins
                        T.op("pe", mma, reads=[wB[wi]] + yaB, writes=[pB[ba]])

                        def mmb(e):
                            ins = None
                            for h in range(8):
                                ins = e.matmul(psum[:, bb, :], lhsT=wbb[wi][:, h, :], rhs=ybT[:, h, tk], start=(h == 0), stop=(h == 7))
                            return ins
                        T.op("pe", mmb, reads=[wB[wi]] + ybB, writes=[pB[bb]])
                        T.op("act", lambda e: e.activation(out=sa[:, :], in_=psum[:, bga, :], func=AF.Sigmoid), reads=[pB[bga]], writes=[saB])
                        T.op("act", lambda e: e.activation(out=sb_[:, :], in_=psum[:, bgb, :], func=AF.Sigmoid), reads=[pB[bgb]], writes=[sbB])
                        T.op("dve", lambda e: e.tensor_tensor(out=ta[:, :], in0=psum[:, ba, :], in1=sa[:, :], op=ALU.mult), reads=[pB[ba], saB], writes=[taB])
                        T.op("dve", lambda e: e.tensor_tensor(out=tb[:, :], in0=psum[:, bb, :], in1=sb_[:, :], op=ALU.mult), reads=[pB[bb], sbB], writes=[tbB])
                        T.op("pool", lambda e: e.tensor_tensor(out=mg[:, dt, tk], in0=ta[:, :], in1=tb[:, :], op=ALU.add), reads=[taB, tbB], writes=[mgB[dt][n]])
                    if dt + 2 < 8:
                        loadw(dt + 2)

                def loado(dt):
                    T.dma("pool", wo[dt % 2][:].rearrange("p s f -> p (s f)"), dt_in["wout"][dt], writes=[woB[dt % 2]])
                loado(0)
                loado(1)
                for dt in range(8):
                    wi = dt % 2
                    for n in range(NB):
                        tk = slice(n * 512, (n + 1) * 512)
                        b = bank()

                        def mmo(e):
                            ins = None
                            for k in range(8):
                                ins = e.matmul(psum[:, b, :], lhsT=wo[wi][:, k, :], rhs=mg[:, k, tk], start=(k == 0), stop=(k == 7))
                            return ins
                        T.op("pe", mmo, reads=[woB[wi]] + [mgB[k][n] for k in range(8)], writes=[pB[b]])
                        T.op("dve", lambda e: e.tensor_tensor(out=xT[:, dt, tk], in0=psum[:, b, :], in1=xT[:, dt, tk], op=ALU.add),
                             reads=[pB[b]], writes=[xB[n]])
                    if dt + 2 < 8:
                        loado(dt + 2)
                T.barrier()

        def dbg_dump(slabs):
            with contextlib.ExitStack() as ds:
                dtile = [ds.enter_context(nc.sbuf_tensor("dbg%d" % i, [128, S], F32)) for i in range(2)]
                dB = [Buf(), Buf()]
                for k, (ap, bufs, row0, P) in enumerate(slabs):
                    i = k % 2
                    T.op("act", lambda e: e.activation(out=dtile[i][0:P, :], in_=ap, func=AF.Copy), reads=bufs, writes=[dB[i]])
                    T.dma("sp", out_d[row0:row0 + P, :], dtile[i][0:P, :], reads=[dB[i]])
                T.barrier()

        ffn(0, 0)
        done = False
        if stop_after not in ("ffn1_raw", "ffn1"):
            r = mixer()
            if r is not None:
                done = True
        if stop_after in ("all", "x3_raw"):
            ffn(1, 2)
        if not done:
          with contextlib.ExitStack() as fs:
            yT = [fs.enter_context(nc.sbuf_tensor("yT%d" % i, [128, NCH, 512], F32)) for i in range(2)]
            yB = [Buf(), Buf()]
            sq_t = fs.enter_context(nc.sbuf_tensor("fsq", [128, NCH, 512], BF16))
            sq_b = [Buf() for _ in range(NCH)]
            rs_t = fs.enter_context(nc.sbuf_tensor("frs", [128, 512], F32))
            rs_b = Buf()
            out_dv = out_d.rearrange("(c p) t -> p c t", p=128)
            for n in range(NB):
                i = n % 2
                if stop_after.endswith("_raw"):
                    for c in range(NCH):
                        T.op("dve", lambda e: e.tensor_copy(out=yT[i][:, c, :], in_=xT[:, c, n * 512:(n + 1) * 512]),
                             reads=[xB[n]], writes=[yB[i]])
                else:
                    rmsnorm_block(n, 3, lambda c: yT[i][:, c, :], yB[i], sq_t, sq_b, rs_t, rs_b)
                T.dma("sp", out_dv[:, :, n * 512:(n + 1) * 512], yT[i][:], reads=[yB[i]])
            T.barrier()
    return nc, set(dt_in.keys())


def _consts():
    c = {}
    c["ident_f"] = np.eye(128, dtype=np.float32)
    c["ones_f"] = np.ones((128, 128), dtype=np.float32)
    half = 32
    inv_freq = 10000.0 ** (-np.arange(half, dtype=np.float64) / half)
    pos = np.arange(S, dtype=np.float64)
    ang = pos[None, :] * inv_freq[:, None]
    cosv, sinv = np.cos(ang).astype(np.float32), np.sin(ang).astype(np.float32)
    cos_t = np.zeros((128, S), np.float32)
    sin_t = np.zeros((128, S), np.float32)
    rm = np.zeros((128, 128), np.float32)
    for p in range(128):
        d = p % 64
        cos_t[p] = cosv[d % 32]
        sin_t[p] = -sinv[d % 32] if d < 32 else sinv[d % 32]
        partner = p + 32 if d < 32 else p - 32
        rm[partner, p] = 1.0
    c["rope_cos"], c["rope_sin"], c["rm"] = cos_t, sin_t, rm
    j = np.arange(128)[:, None]
    i = np.arange(128)[None, :]
    mprev = (j >= i).astype(np.float32)
    mcur = (j <= i).astype(np.float32)
    c["amask"] = np.concatenate([mprev, mcur, mprev, mcur], axis=1)
    em = np.zeros((128, 64), np.float32)
    em[64, :] = 1.0
    c["em"] = em
    c["triu"] = (j <= i).astype(np.float32)
    c["smask"] = np.where(i > j, 0.0, -30000.0).astype(np.float32)
    c["odiv"] = np.full((128, 128), 1.0 / 128.0, np.float32)
    return c


def _prep_inputs(inp):
    f = lambda a: np.ascontiguousarray(a, dtype=np.float32)
    shared = {}

    def ncol(v):
        return f(np.asarray(v).reshape(NCH, 128).T)
    shared["ffn1_norm"] = ncol(inp["ffn1_norm"][0])
    shared["ffn2_norm"] = ncol(inp["ffn2_norm"][0])
    shared["mix_norm"] = ncol(inp["mix_norm"][0])
    shared["final_norm"] = ncol(inp["final_norm"])
    ffn_in = {1: (inp["ffn1_w_gate"], inp["ffn1_w_up"], inp["ffn1_w_down"]),
              2: (inp["ffn2_w_gate"], inp["ffn2_w_up"], inp["ffn2_w_down"])}
    for i in (1, 2):
        wg = np.asarray(ffn_in[i][0][0])
        wu = np.asarray(ffn_in[i][1][0])
        wd = np.asarray(ffn_in[i][2][0])
        shared["wg%d" % i] = f(wg.reshape(NCH, 128, 11, 256).transpose(2, 1, 0, 3).reshape(11, 128, NCH * 256))
        shared["wu%d" % i] = f(wu.reshape(NCH, 128, 11, 256).transpose(2, 1, 0, 3).reshape(11, 128, NCH * 256))
        shared["wd%d" % i] = f(wd.reshape(NF, 128, 8, 128).transpose(2, 1, 0, 3).reshape(8, 128, NF * 128))
    w_in = np.asarray(inp["w_in"][0])

    def cols(c0, n):
        return w_in[:, c0:c0 + n].reshape(NCH, 128, n).transpose(1, 0, 2).reshape(128, NCH * n)
    wa = np.zeros((6, 3, 128, NCH * 128), np.float32)
    for g in range(3):
        for hp in range(2):
            for i in range(3):
                wa[g * 2 + hp, i] = cols(i * 768 + (4 * g + 2 * hp) * 64, 128)
    shared["wqkv_a"] = wa
    wg_ = np.zeros((8, 3, 128, NCH * 128), np.float32)
    wgg = np.zeros((8, 128, NCH * 128), np.float32)
    for h in range(8):
        for i in range(3):
            wg_[h, i] = cols(2304 + i * 1024 + h * 128, 128)
        wgg[h] = cols(5392 + h * 128, 128)
    shared["wgdn"] = wg_
    shared["wgg"] = wgg
    shared["wbd"] = f(cols(5376, 16))
    wgab = np.zeros((8, 128, 2 * NCH * 128), np.float32)
    for dt in range(8):
        a = cols(6416 + dt * 128, 128).reshape(128, 1, NCH * 128)
        b = cols(7440 + dt * 128, 128).reshape(128, 1, NCH * 128)
        wgab[dt] = np.concatenate([a, b], axis=1).reshape(128, 2 * NCH * 128)
    shared["wgab"] = wgab
    wba = np.asarray(inp["w_branch_a"][0])
    shared["wba"] = f(wba.reshape(4, 64, 8, 128).transpose(2, 1, 0, 3).reshape(8, 64, 4 * 128))
    wbb = np.asarray(inp["w_branch_b"][0])
    shared["wbb"] = f(wbb.reshape(8, 128, 8, 128).transpose(2, 1, 0, 3).reshape(8, 128, 8 * 128))
    wo = np.asarray(inp["w_out"][0])
    shared["wout"] = f(wo.reshape(8, 128, 8, 128).transpose(2, 1, 0, 3).reshape(8, 128, 8 * 128))
    cw = np.asarray(inp["gdn_conv_w"][0])
    shared["convw"] = f(cw.reshape(4, 3, 8, 128).transpose(3, 2, 1, 0).reshape(128, 96))
    shared["alog"] = f(np.broadcast_to(np.asarray(inp["gdn_a_log"][0])[None, :], (128, 8)))
    shared["dtb"] = f(np.broadcast_to(np.asarray(inp["gdn_dt_bias"][0])[None, :], (128, 8)))
    shared["onorm"] = f(np.asarray(inp["gdn_out_norm"][0]).reshape(128, 1))
    shared.update(_consts())
    x = np.asarray(inp["x"])
    maps = []
    for b in range(8):
        m = dict(shared)
        m["xT"] = f(x[b].T)
        maps.append(m)
    return maps


_NC_CACHE = {}


def kernel(**inputs):
    stop_after = inputs.pop("_stop_after", "all")
    if stop_after not in _NC_CACHE:
        _NC_CACHE[stop_after] = build_program(stop_after)
    nc, names = _NC_CACHE[stop_after]
    maps = _prep_inputs(inputs)
    maps = [{k: v for k, v in m.items() if k in names} for m in maps]
    res = run_bass_kernel_spmd(nc, maps, core_ids=list(range(8)))
    out = np.stack([np.ascontiguousarray(r["outT"].T) for r in res.results], axis=0)
    return out.astype(np.float32)
```

```python
import contextlib
import numpy as np
import concourse.bass as bass
import concourse.mybir as mybir
from concourse.bass_utils import run_bass_kernel_spmd

F32 = mybir.dt.float32
BF16 = mybir.dt.bfloat16
AF = mybir.ActivationFunctionType
ALU = mybir.AluOpType
AX = mybir.AxisListType

S = 2048
D = 1024
NCH = 8
DFF = 2816
NF = 22
NB = 4
EPS = 1e-6
D_IN = 8464
NDS = 16
import os
ROPE_ADD_ENG = os.environ.get("ROPE_ADD_ENG", "pool")
FILLN = int(os.environ.get("FILLN", "1"))


class _Stop(Exception):
    pass


class Buf:
    __slots__ = ("w", "r", "excl")

    def __init__(self, excl=False):
        self.w = None
        self.r = []
        self.excl = excl


class Desc:
    __slots__ = ("eng", "cost", "run")

    def __init__(self, eng, cost, run):
        self.eng, self.cost, self.run = eng, cost, run


def run_streams(gens, lat=0.3):
    st = []
    for g in gens:
        try:
            st.append([g, next(g), 0.0])
        except StopIteration:
            pass
    efree = {}
    while st:
        best = None
        for s_ in st:
            start = max(s_[2], efree.get(s_[1].eng, 0.0))
            if best is None or start < best[0]:
                best = (start, s_)
        start, s_ = best
        d = s_[1]
        d.run()
        fin = start + d.cost
        efree[d.eng] = fin
        s_[2] = fin + lat
        try:
            s_[1] = next(s_[0])
        except StopIteration:
            st.remove(s_)


class Trk:
    def __init__(self, nc, es):
        self.nc = nc
        self.eng = {"pe": nc.tensor, "act": nc.scalar, "dve": nc.vector, "pool": nc.gpsimd, "sp": nc.sync}
        self.sem = {k: es.enter_context(nc.semaphore("sem_" + k)) for k in self.eng}
        self.cnt = {k: 0 for k in self.eng}
        self.seen = {k: {} for k in self.eng}
        self.dsem = [es.enter_context(nc.semaphore("dsem%d" % i)) for i in range(NDS)]
        self.dcnt = [0] * NDS
        self.dnext = {"pool": 0, "sp": NDS // 2, "act": NDS // 2}
        self.same = {"pe": False, "act": True, "dve": True, "pool": True, "sp": True}

    def _semof(self, k):
        return self.sem[k] if isinstance(k, str) else self.dsem[k[1]]

    def _wait(self, e, deps):
        best = {}
        for (k, v) in deps:
            if v > best.get(k, 0):
                best[k] = v
        for k, v in best.items():
            if k == e and not self.same[e]:
                continue
            if self.seen[e].get(k, 0) >= v:
                continue
            self.eng[e].wait_ge(self._semof(k), v)
            self.seen[e][k] = v

    @staticmethod
    def _deps(reads, writes):
        deps = []
        for b in reads:
            if b.w is not None:
                deps.append(b.w)
        for b in writes:
            if b.w is not None:
                deps.append(b.w)
            deps.extend(b.r)
        return deps

    @staticmethod
    def _upd(d, reads, writes):
        for b in reads:
            b.r.append(d)
            if len(b.r) > 64:
                best = {}
                for (k, v) in b.r:
                    if v > best.get(k, 0):
                        best[k] = v
                b.r = list(best.items())
        for b in writes:
            b.w = d
            b.r = []

    filler = None

    def op(self, e, fn, reads=(), writes=()):
        if e == "pe" and self.filler is not None:
            self.fcnt = getattr(self, "fcnt", 0) + 1
            if self.fcnt % FILLN == 0:
                self.filler(self.eng["pe"])
        if any(b.excl for b in reads):
            writes = list(writes) + [b for b in reads if b.excl]
            reads = [b for b in reads if not b.excl]
        self._wait(e, self._deps(reads, writes))
        ins = fn(self.eng[e])
        self.cnt[e] += 1
        ins.then_inc(self.sem[e], 1)
        self._upd((e, self.cnt[e]), reads, writes)

    def dma(self, q, out, in_, reads=(), writes=()):
        deps = self._deps(reads, writes)
        i = self.dnext[q]
        lo = 0 if q == "pool" else NDS // 2
        self.dnext[q] = lo + (i - lo + 1) % (NDS // 2)
        if self.dcnt[i] > 0:
            deps.append((("d", i), self.dcnt[i]))
        self._wait(q, deps)
        self.eng[q].dma_start(out=out, in_=in_).then_inc(self.dsem[i], 16)
        self.dcnt[i] += 16
        self._upd((("d", i), self.dcnt[i]), reads, writes)

    def defer(self, e, fn, reads=(), writes=(), cost=None):
        return Desc(e, cost if cost is not None else {"pe": 0.3, "dve": 0.3, "act": 0.3, "pool": 0.6, "sp": 0.1}[e],
                    lambda: self.op(e, fn, reads, writes))

    def defer5(self, e, fn, reads=(), writes=(), cost=None):
        return Desc(e, cost if cost is not None else {"pe": 0.3, "dve": 0.6, "act": 0.6, "pool": 1.1, "sp": 0.1}[e],
                    lambda: self.op(e, fn, reads, writes))

    def defer_dma(self, q, out, in_, reads=(), writes=(), cost=0.2):
        return Desc(q, cost, lambda: self.dma(q, out, in_, reads, writes))

    def barrier(self):
        deps = [(k, v) for k, v in self.cnt.items() if v > 0]
        deps += [(("d", i), c) for i, c in enumerate(self.dcnt) if c > 0]
        for e in self.eng:
            self._wait(e, [d for d in deps if d[0] != e])


def build_program(stop_after="all", debug=False):
    nc = bass.Bass("TRN2", target_bir_lowering=False)
    dt_in = {}

    def din(name, shape):
        dt_in[name] = nc.dram_tensor(name, list(shape), F32, kind="ExternalInput").ap()
        return dt_in[name]

    xT_d = din("xT", [D, S])
    n1_d = din("ffn1_norm", [128, NCH])
    n2_d = din("ffn2_norm", [128, NCH])
    nm_d = din("mix_norm", [128, NCH])
    nf_d = din("final_norm", [128, NCH])
    ffn_w = []
    for i in (1, 2):
        ffn_w.append((din("wg%d" % i, [11, 128, NCH * 256]), din("wu%d" % i, [11, 128, NCH * 256]),
                      din("wd%d" % i, [8, 128, NF * 128])))
    din("wqkv_a", [6, 3, 128, NCH * 128])
    din("rope_cos", [128, S]); din("rope_sin", [128, S]); din("rm", [128, 128]); din("amask", [128, 512]); din("em", [128, 64])
    din("wgdn", [8, 3, 128, NCH * 128]); din("wgg", [8, 128, NCH * 128]); din("wbd", [128, NCH * 16])
    din("convw", [128, 96]); din("alog", [128, 8]); din("dtb", [128, 8]); din("onorm", [128, 1])
    din("triu", [128, 128]); din("smask", [128, 128]); din("odiv", [128, 128])
    din("wgab", [8, 128, 2 * NCH * 128]); din("wba", [8, 64, 4 * 128]); din("wbb", [8, 128, 8 * 128]); din("wout", [8, 128, 8 * 128])
    ident_d = din("ident_f", [128, 128])
    ones_d = din("ones_f", [128, 128])
    out_d = nc.dram_tensor("outT", [D, S], F32, kind="ExternalOutput").ap()
    dbg = {}

    with contextlib.ExitStack() as es:
        T = Trk(nc, es)

        def sb(name, shape, dt):
            return es.enter_context(nc.sbuf_tensor(name, list(shape), dt))

        xT = sb("xT_sb", [128, NCH, S], F32)
        xB = [Buf() for _ in range(NB)]
        psum = es.enter_context(nc.psum_tensor("psum", [128, 8, 512], F32))
        pB = [Buf(True) for _ in range(8)]
        pstate = {"n": 0}

        def bank():
            b = pstate["n"]
            pstate["n"] = (b + 1) % 8
            return b

        def bank2():
            b = pstate["n"]
            if b % 2:
                b = (b + 1) % 8
            pstate["n"] = (b + 2) % 8
            return b

        ident_f = sb("ident_f_sb", [128, 128], F32)
        ident_b = sb("ident_b_sb", [128, 128], BF16)
        ones_b = sb("ones_b_sb", [128, 128], BF16)
        ones_f = sb("ones_f_sb", [128, 128], F32)
        ident2_f = sb("ident2_f_sb", [128, 128], F32)
        cB = Buf()
        norms = sb("norms_sb", [128, 4, NCH], F32)
        eps_t = sb("eps_sb", [128, 2], F32)
        T.op("dve", lambda e: e.memset(eps_t[:, 0:1], EPS), writes=[cB])
        T.op("dve", lambda e: e.memset(eps_t[:, 1:2], 1.0), writes=[cB])
        T.dma("sp", ident_f[:], ident_d, writes=[cB])
        T.dma("sp", ones_f[:], ones_d, writes=[cB])
        T.op("dve", lambda e: e.tensor_single_scalar(out=ident2_f[:, :], in_=ident_f[:, :], scalar=2.0, op=ALU.mult), reads=[cB], writes=[cB])
        T.dma("pool", ident_b[:], ident_d, writes=[cB])
        T.dma("pool", ones_b[:], ones_d, writes=[cB])
        for i, nd in enumerate((n1_d, nm_d, n2_d, nf_d)):
            T.dma("sp", norms[:, i, :], nd, writes=[cB])
        xT_dv = xT_d.rearrange("(c p) t -> p c t", p=128)
        for n in range(NB):
            T.dma("sp", xT[:, :, n * 512:(n + 1) * 512], xT_dv[:, :, n * 512:(n + 1) * 512], writes=[xB[n]])

        def rmsnorm_block(n, gi, out_ap_fn, out_buf, sq_t, sq_b, rs_t, rs_b):
            tok = slice(n * 512, (n + 1) * 512)
            b = bank()
            for c in range(NCH):
                T.op("act", lambda e, c=c: e.activation(out=sq_t[:, c, :], in_=xT[:, c, tok], func=AF.Square),
                     reads=[xB[n]], writes=[sq_b[c]])
            def mm(e):
                ins = None
                for c in range(NCH):
                    ins = e.matmul(psum[:, b, :], lhsT=ones_b[:, :], rhs=sq_t[:, c, :], start=(c == 0), stop=(c == NCH - 1))
                return ins
            T.op("pe", mm, reads=sq_b + [cB], writes=[pB[b]])
            T.op("act", lambda e: e.activation(out=rs_t[:, :], in_=psum[:, b, :], func=AF.Ln, scale=1.0 / D, bias=eps_t[:, 0:1]),
                 reads=[pB[b], cB], writes=[rs_b])
            T.op("act", lambda e: e.activation(out=rs_t[:, :], in_=rs_t[:, :], func=AF.Exp, scale=-0.5), reads=[rs_b], writes=[rs_b])
            for c in range(NCH):
                T.op("dve", lambda e, c=c: e.scalar_tensor_tensor(out=out_ap_fn(c), in0=xT[:, c, tok],
                                                                  scalar=norms[:, gi, c:c + 1], in1=rs_t[:, :],
                                                                  op0=ALU.mult, op1=ALU.mult),
                     reads=[xB[n], rs_b, cB], writes=[out_buf])

        def ffn(which, gi):
            wg_d, wu_d, wd_d = ffn_w[which]
            with contextlib.ExitStack() as fs:
                def fsb(name, shape, dt):
                    return fs.enter_context(nc.sbuf_tensor("f%d_%s" % (which, name), list(shape), dt))
                hT = fsb("hT", [128, NCH, 1024], BF16)
                hB = [Buf(), Buf()]
                actT = fsb("actT", [128, NF, 1024], BF16)
                aB = [[Buf(), Buf()] for _ in range(NF)]
                sq_t = fsb("sq", [128, NCH, 512], BF16)
                sq_b = [Buf() for _ in range(NCH)]
                rs_t = fsb("rs", [128, 512], F32)
                rs_b = Buf()
                NWB = 3
                wgb = [fsb("wg%d" % i, [128, NCH, 256], BF16) for i in range(NWB)]
                wub = [fsb("wu%d" % i, [128, NCH, 256], BF16) for i in range(NWB)]
                wgB = [Buf() for _ in range(NWB)]
                wuB = [Buf() for _ in range(NWB)]
                wdb = [fsb("wd%d" % i, [128, NF, 128], BF16) for i in range(2)]
                wdB = [Buf() for _ in range(2)]
                sg = [fsb("sg%d" % i, [128, 512], F32) for i in range(3)]
                sgB = [Buf() for _ in range(3)]
                sgn = 0

                def load_gu(g):
                    i = g % NWB
                    T.dma("pool", wgb[i][:].rearrange("p c f -> p (c f)"), wg_d[g], writes=[wgB[i]])
                    T.dma("pool", wub[i][:].rearrange("p c f -> p (c f)"), wu_d[g], writes=[wuB[i]])

                def load_d(dt):
                    i = dt % 2
                    T.dma("pool", wdb[i][:].rearrange("p f j -> p (f j)"), wd_d[dt], writes=[wdB[i]])

                for half in range(2):
                    load_gu(0)
                    load_gu(1)
                    for nn in range(2):
                        n = half * 2 + nn
                        rmsnorm_block(n, gi, lambda c, nn=nn: hT[:, c, nn * 512:(nn + 1) * 512], hB[nn],
                                      sq_t, sq_b, rs_t, rs_b)
                    for g in range(11):
                        if g + 2 < 11:
                            load_gu(g + 2)
                        if g == 9:
                            load_d(0)
                        if g == 10:
                            load_d(1)
                        wi = g % NWB
                        for j in range(2):
                            f = 2 * g + j
                            for nn in range(2):
                                tk = slice(nn * 512, (nn + 1) * 512)
                                bg = bank()
                                bu = bank()

                                def mmg(e, w=wgb[wi], b=bg, j=j, tk=tk):
                                    ins = None
                                    for c in range(NCH):
                                        ins = e.matmul(psum[:, b, :], lhsT=w[:, c, j * 128:(j + 1) * 128],
                                                       rhs=hT[:, c, tk], start=(c == 0), stop=(c == NCH - 1))
                                    return ins
                                T.op("pe", mmg, reads=[wgB[wi], hB[nn]], writes=[pB[bg]])
                                T.op("pe", lambda e, b=bu, j=j, tk=tk: mmg(e, wub[wi], b, j, tk),
                                     reads=[wuB[wi], hB[nn]], writes=[pB[bu]])
                                si = sgn % 3
                                sgn += 1
                                T.op("act", lambda e, si=si, b=bg: e.activation(out=sg[si][:, :], in_=psum[:, b, :],
                                                                               func=AF.Silu),
                                     reads=[pB[bg]], writes=[sgB[si]])
                                T.op("dve", lambda e, si=si, b=bu, f=f, tk=tk: e.tensor_tensor(
                                    out=actT[:, f, tk], in0=psum[:, b, :], in1=sg[si][:, :], op=ALU.mult),
                                     reads=[pB[bu], sgB[si]], writes=[aB[f][nn]])
                    for dt in range(8):
                        if dt + 2 < 8:
                            pass
                        wi = dt % 2
                        for nn in range(2):
                            n = half * 2 + nn
                            tk = slice(nn * 512, (nn + 1) * 512)
                            b = bank()

                            def mmd(e, b=b, wi=wi, tk=tk):
                                ins = None
                                for f in range(NF):
                                    ins = e.matmul(psum[:, b, :], lhsT=wdb[wi][:, f, :], rhs=actT[:, f, tk],
                                                   start=(f == 0), stop=(f == NF - 1))
                                return ins
                            T.op("pe", mmd, reads=[wdB[wi]] + [aB[f][nn] for f in range(NF)], writes=[pB[b]])
                            T.op("dve", lambda e, b=b, dt=dt, n=n: e.scalar_tensor_tensor(
                                out=xT[:, dt, n * 512:(n + 1) * 512], in0=psum[:, b, :], scalar=0.5,
                                in1=xT[:, dt, n * 512:(n + 1) * 512], op0=ALU.mult, op1=ALU.add),
                                 reads=[pB[b]], writes=[xB[n]])
                        if dt + 2 < 8:
                            load_d(dt + 2)
                T.barrier()

        def mixer():
            with contextlib.ExitStack() as ms:
                def msb(name, shape, dt):
                    return ms.enter_context(nc.sbuf_tensor("m_" + name, list(shape), dt))
                hT = msb("hT", [128, NCH, S], BF16)
                hB = [Buf() for _ in range(NB)]
                with contextlib.ExitStack() as ns:
                    sq_t = ns.enter_context(nc.sbuf_tensor("m_sq", [128, NCH, 512], BF16))
                    sq_b = [Buf() for _ in range(NCH)]
                    rs_t = ns.enter_context(nc.sbuf_tensor("m_rs", [128, 512], F32))
                    rs_b = Buf()
                    for n in range(NB):
                        rmsnorm_block(n, 1, lambda c, n=n: hT[:, c, n * 512:(n + 1) * 512], hB[n], sq_t, sq_b, rs_t, rs_b)
                    T.barrier()
                if stop_after == "dbg_h":
                    dbg_dump([(hT[:, c, :], hB, c * 128, 128) for c in range(NCH)])
                    return True
                ybT = msb("ybT", [128, 8, S], BF16)
                ybB = [Buf() for _ in range(8)]
                gdn(hT, hB, ybT, ybB)
                T.barrier()
                if stop_after == "dbg_yb":
                    dbg_dump([(ybT[:, c, :], ybB, c * 128, 128) for c in range(8)])
                    return True
                yaT = msb("yaT", [64, 4, S], BF16)
                yaB = [Buf() for _ in range(4)]
                if attention(hT, hB, yaT, yaB):
                    return True
                T.barrier()
                if stop_after == "dbg_ya":
                    dbg_dump([(yaT[:, c, :], yaB, c * 64, 64) for c in range(4)])
                    return True
                merge(hT, hB, yaT, yaB, ybT, ybB)
                T.barrier()
            return None

        def nset(lo, hi):
            return list(range(lo // 512, hi // 512 + 1))

        def attention(hT, hB, yaT, yaB):
            wqkv_d = dt_in["wqkv_a"]
            with contextlib.ExitStack() as as_:
                def asb(name, shape, dt):
                    return as_.enter_context(nc.sbuf_tensor("a_" + name, list(shape), dt))
                cos_t = asb("cos", [128, S], BF16)
                sin_t = asb("sin", [128, S], BF16)
                rm_t = asb("rm", [128, 128], BF16)
                am_t = asb("am", [128, 512], BF16)
                em_t = asb("em", [128, 64], F32)
                acB = Buf()
                T.dma("pool", cos_t[:], dt_in["rope_cos"], writes=[acB])
                T.dma("pool", sin_t[:], dt_in["rope_sin"], writes=[acB])
                T.dma("pool", rm_t[:], dt_in["rm"], writes=[acB])
                T.dma("pool", am_t[:], dt_in["amask"], writes=[acB])
                T.dma("sp", em_t[:], dt_in["em"], writes=[acB])
                wq = [asb("w%d" % i, [128, NCH, 128], BF16) for i in range(3)]
                wB = [Buf() for _ in range(3)]
                qk = [asb("qk%d" % i, [128, S], BF16) for i in range(2)]
                qkB = [[Buf() for _ in range(NB)] for _ in range(2)]
                vT = asb("vT", [128, S], BF16)
                vTB = [Buf() for _ in range(NB)]
                Vt = asb("Vt", [128, 16, 2, 65], BF16)
                VtB = [Buf() for _ in range(16)]
                VoneB = Buf()
                T.op(ROPE_ADD_ENG, lambda e: e.memset(Vt[:, :, :, 64:65], 1.0), writes=[VoneB])
                acc = asb("acc", [128, 2, S], F32)
                accB = Buf()
                qb = asb("qb", [128, 512], BF16)
                qbB = Buf()
                t1 = asb("t1", [128, 512], F32)
                t2 = asb("t2", [128, 512], F32)
                t1B, t2B = Buf(), Buf()
                ex = [asb("ex%d" % i, [128, 512], BF16) for i in range(3)]
                exB = [Buf(), Buf(), Buf()]
                exm = [asb("exm%d" % i, [128, 512], BF16) for i in range(3)]
                exmB = [Buf(), Buf(), Buf()]
                rl = asb("rl", [64, 512], F32)
                rlB = Buf()
                exn = 0
                if stop_after == "dbg_att0":
                    dbg_dump([(cos_t[:, :], [acB], 0, 128), (sin_t[:, :], [acB, VoneB], 128, 128)])
                    return True
                for hp in range(2):
                    for g in range(3):
                        dil = (1, 4, 16)[g]
                        nbk = 16 // dil
                        pi = g * 2 + hp
                        for i in range(3):
                            T.dma("pool", wq[i][:].rearrange("p c f -> p (c f)"), wqkv_d[pi, i], writes=[wB[i]])
                        for i in range(2):
                            for n in range(NB):
                                tk = slice(n * 512, (n + 1) * 512)
                                b = bank()

                                def mm(e):
                                    ins = None
                                    for c in range(NCH):
                                        ins = e.matmul(psum[:, b, :], lhsT=wq[i][:, c, :], rhs=hT[:, c, tk],
                                                       start=(c == 0), stop=(c == NCH - 1))
                                    return ins
                                CUT = int(os.environ.get("CUT", "9"))
                                if CUT >= 1:
                                    T.op("pe", mm, reads=[wB[i], hB[n]], writes=[pB[b]])
                                if CUT >= 2:
                                    T.op("act", lambda e: e.activation(out=qb[:, :], in_=psum[:, b, :], func=AF.Copy),
                                         reads=[pB[b]], writes=[qbB])
                                b2 = bank()
                                if CUT >= 3:
                                    T.op("pe", lambda e: e.matmul(psum[:, b2, :], lhsT=rm_t[:, :], rhs=qb[:, :], start=True, stop=True),
                                         reads=[qbB, acB], writes=[pB[b2]])
                                if CUT >= 4:
                                    T.op("dve", lambda e: e.tensor_tensor(out=t1[:, :], in0=psum[:, b, :], in1=cos_t[:, tk], op=ALU.mult),
                                         reads=[pB[b], acB], writes=[t1B])
                                if CUT >= 5:
                                    T.op("dve", lambda e: e.tensor_tensor(out=t2[:, :], in0=psum[:, b2, :], in1=sin_t[:, tk], op=ALU.mult),
                                         reads=[pB[b2], acB], writes=[t2B])
                                if CUT >= 6:
                                    T.op(ROPE_ADD_ENG, lambda e: e.tensor_tensor(out=qk[i][:, tk], in0=t1[:, :], in1=t2[:, :], op=ALU.add),
                                         reads=[t1B, t2B], writes=[qkB[i][n]])
                                if stop_after == "dbg_att1a":
                                    T.barrier()
                                    with contextlib.ExitStack() as ds:
                                        dtl = ds.enter_context(nc.sbuf_tensor("dbgx", [128, 512], F32))
                                        dxB = Buf()
                                        T.op("act", lambda e: e.activation(out=dtl[:, :], in_=qk[0][:, 0:512], func=AF.Copy), reads=[qkB[0][0]], writes=[dxB])
                                        T.dma("sp", out_d[0:128, 0:512], t1[:, :], reads=[t1B])
                                        T.dma("sp", out_d[128:256, 0:512], t2[:, :], reads=[t2B])
                                        T.dma("sp", out_d[256:384, 0:512], dtl[:, :], reads=[dxB])
                                        T.barrier()
                                    return True
                        if stop_after == "dbg_att1":
                            dbg_dump([(qk[0][:, :], qkB[0], 0, 128), (qk[1][:, :], qkB[1], 128, 128)])
                            return True
                        def tokslice(bi):
                            r, nk = bi // nbk, bi % nbk
                            st = r + dil * 128 * nk
                            return st, slice(st, st + 127 * dil + 1, dil)
                        for n in range(NB):
                            tk = slice(n * 512, (n + 1) * 512)
                            b = bank()

                            def mmvf(e):
                                ins = None
                                for c in range(NCH):
                                    ins = e.matmul(psum[:, b, :], lhsT=wq[2][:, c, :], rhs=hT[:, c, tk], start=(c == 0), stop=(c == NCH - 1))
                                return ins
                            T.op("pe", mmvf, reads=[wB[2], hB[n]], writes=[pB[b]])
                            T.op("act", lambda e: e.activation(out=vT[:, tk], in_=psum[:, b, :], func=AF.Copy), reads=[pB[b]], writes=[vTB[n]])
                        for b4 in range(4):
                            b = bank()

                            def mmv(e):
                                ins = None
                                for k in range(4):
                                    st, sl = tokslice(b4 * 4 + k)
                                    ins = e.matmul(psum[:, b, k * 128:(k + 1) * 128], lhsT=vT[:, sl], rhs=ident_b[:, :], start=True, stop=True)
                                return ins
                            T.op("pe", mmv, reads=vTB + [cB], writes=[pB[b]])
                            T.op("act", lambda e: e.activation(
                                out=Vt[:, b4 * 4:(b4 + 1) * 4, :, 0:64],
                                in_=psum[:, b, :].rearrange("p (k e d) -> p k e d", k=4, e=2), func=AF.Copy),
                                 reads=[pB[b]], writes=[VtB[b4 * 4 + k] for k in range(4)])
                        if stop_after == "dbg_att2":
                            dbg_dump([(Vt[:, :, :, :].rearrange("p a b c -> p (a b c)")[:, 0:2048], VtB + [VoneB], 0, 128)])
                            return True
                        def stage1(bi):
                            nonlocal exn
                            r, nk = bi // nbk, bi % nbk
                            st, qsl = tokslice(bi)
                            hi_tok = st + 127 * dil
                            qn_set = nset(st, hi_tok)
                            b = bank2()
                            if nk > 0:
                                stp, ksl_p = tokslice(bi - 1)
                                kn_set = nset(stp, hi_tok)
                            else:
                                kn_set = qn_set

                            def mms(e):
                                ins = None
                                for hd in range(2):
                                    hs = slice(hd * 64, (hd + 1) * 64)
                                    if nk > 0:
                                        ins = e.matmul(psum[:, b + hd, 0:128], lhsT=qk[1][hs, ksl_p], rhs=qk[0][hs, qsl],
                                                       start=True, stop=True)
                                    ins = e.matmul(psum[:, b + hd, 128:256], lhsT=qk[1][hs, qsl], rhs=qk[0][hs, qsl],
                                                   start=True, stop=True)
                                return ins
                            T.op("pe", mms, reads=[qkB[0][n] for n in qn_set] + [qkB[1][n] for n in kn_set], writes=[pB[b], pB[b + 1]])
                            xi = exn % 3
                            exn += 1
                            lo = 0 if nk > 0 else 128
                            src = psum[:, b:b + 2, lo:256]
                            dst = ex[xi][:, :].rearrange("p (e w) -> p e w", e=2)[:, :, lo:256]
                            dstm = exm[xi][:, :].rearrange("p (e w) -> p e w", e=2)[:, :, lo:256]
                            msk = am_t[:, :].rearrange("p (e w) -> p e w", e=2)[:, :, lo:256]
                            T.op("act", lambda e: e.activation(out=dst, in_=src, func=AF.Exp, scale=0.125),
                                 reads=[pB[b], pB[b + 1]], writes=[exB[xi]])
                            T.op("pool", lambda e: e.tensor_tensor(out=dstm, in0=dst, in1=msk, op=ALU.mult),
                                 reads=[exB[xi], acB], writes=[exmB[xi]])
                            return xi

                        def stage2(bi, xi):
                            r, nk = bi // nbk, bi % nbk
                            st, qsl = tokslice(bi)
                            b2 = bank()

                            def mmu(e):
                                ins = None
                                for hd in range(2):
                                    o = psum[0:65, b2, hd * 128:(hd + 1) * 128]
                                    if nk > 0:
                                        e.matmul(o, lhsT=Vt[:, bi - 1, hd, :], rhs=exm[xi][:, hd * 256:hd * 256 + 128], start=True, stop=False)
                                    ins = e.matmul(o, lhsT=Vt[:, bi, hd, :], rhs=exm[xi][:, hd * 256 + 128:hd * 256 + 256],
                                                   start=(nk == 0), stop=True)
                                return ins
                            T.op("pe", mmu, reads=[exmB[xi], VtB[bi], VoneB] + ([VtB[bi - 1]] if nk > 0 else []), writes=[pB[b2]])
                            src2 = psum[0:65, b2, 0:256].rearrange("p (e q) -> p e q", e=2)
                            if g == 0:
                                T.op("dve", lambda e: e.tensor_copy(out=acc[0:65, :, qsl], in_=src2), reads=[pB[b2]], writes=[accB])
                            else:
                                T.op("dve", lambda e: e.tensor_tensor(out=acc[0:65, :, qsl], in0=src2, in1=acc[0:65, :, qsl], op=ALU.add),
                                     reads=[pB[b2]], writes=[accB])
                        pend = []
                        for bi in range(16):
                            pend.append((bi, stage1(bi)))
                            if len(pend) > 1:
                                stage2(*pend.pop(0))
                        while pend:
                            stage2(*pend.pop(0))
                        if stop_after == "dbg_att3":
                            dbg_dump([(acc[:, 0, :], [accB], 0, 128), (acc[:, 1, :], [accB], 128, 128)])
                            return True
                    for hd in range(2):
                        for n in range(NB):
                            tk = slice(n * 512, (n + 1) * 512)
                            b = bank()
                            T.op("pe", lambda e: e.matmul(psum[0:64, b, :], lhsT=em_t[0:65, :], rhs=acc[0:65, hd, tk], start=True, stop=True),
                                 reads=[accB, acB], writes=[pB[b]])
                            T.op("dve", lambda e: e.reciprocal(out=rl[:, :], in_=psum[0:64, b, :]), reads=[pB[b]], writes=[rlB])
                            T.op("dve", lambda e: e.tensor_tensor(out=yaT[:, 2 * hp + hd, tk], in0=acc[0:64, hd, tk], in1=rl[:, :], op=ALU.mult),
                                 reads=[rlB, accB], writes=[yaB[2 * hp + hd]])
                T.barrier()

        def gdn(hT, hB, ybT, ybB):
            with contextlib.ExitStack() as gs:
                def gsb(name, shape, dt):
                    return gs.enter_context(nc.sbuf_tensor("g_" + name, list(shape), dt))
                gcB = Buf()
                triu = gsb("triu", [128, 128], F32)
                smask = gsb("smask", [128, 128], F32)
                odiv = gsb("odiv", [128, 128], BF16)
                convw = gsb("convw", [128, 8, 3, 4], F32)
                alog = gsb("alog", [128, 8], F32)
                dtb = gsb("dtb", [128, 8], F32)
                onorm = gsb("onorm", [128, 1], F32)
                wbd = gsb("wbd", [128, NCH, 16], BF16)
                T.dma("sp", triu[:], dt_in["triu"], writes=[gcB])
                T.dma("sp", smask[:], dt_in["smask"], writes=[gcB])
                T.dma("pool", odiv[:], dt_in["odiv"], writes=[gcB])
                T.dma("sp", convw[:].rearrange("p h w t -> p (h w t)"), dt_in["convw"], writes=[gcB])
                T.dma("sp", alog[:], dt_in["alog"], writes=[gcB])
                T.dma("sp", dtb[:], dt_in["dtb"], writes=[gcB])
                T.dma("sp", onorm[:], dt_in["onorm"], writes=[gcB])
                T.dma("pool", wbd[:].rearrange("p c f -> p (c f)"), dt_in["wbd"], writes=[gcB])
                braw = gsb("braw", [128, 16, 16], F32)
                beta = gsb("beta", [128, 16, 8], F32)
                negb = gsb("negb", [128, 16, 8], F32)
                gt = gsb("gt", [128, 16, 8], F32)
                gcum = gsb("gcum", [128, 16, 8], F32)
                glast = gsb("glast", [128, 16, 8], F32)
                kds = gsb("kds", [128, 16, 8], F32)
                sdec = gsb("sdec", [128, 16, 8], F32)
                egc = gsb("egc", [128, 16, 8], F32)
                tmpg = gsb("tmpg", [128, 16, 8], F32)
                negA = gsb("negA", [128, 8], F32)
                gB = Buf()
                b = bank()

                def mmb(e):
                    ins = None
                    for t in range(16):
                        for c in range(NCH):
                            ins = e.matmul(psum[:, b, t * 16:(t + 1) * 16], lhsT=hT[:, c, t * 128:(t + 1) * 128], rhs=wbd[:, c, :],
                                           start=(c == 0), stop=(c == NCH - 1))
                    return ins
                T.op("pe", mmb, reads=hB + [gcB], writes=[pB[b]])
                T.op("act", lambda e: e.activation(out=braw[:].rearrange("p t k -> p (t k)"), in_=psum[:, b, 0:256], func=AF.Copy),
                     reads=[pB[b]], writes=[gB])
                T.op("act", lambda e: e.activation(out=tmpg[:], in_=braw[:, :, 0:8], func=AF.Exp, scale=-1.0), reads=[gB], writes=[gB])
                T.op("dve", lambda e: e.tensor_single_scalar(out=tmpg[:], in_=tmpg[:], scalar=1.0, op=ALU.add), reads=[gB], writes=[gB])
                T.op("dve", lambda e: e.reciprocal(out=beta[:], in_=tmpg[:]), reads=[gB], writes=[gB])
                T.op("dve", lambda e: e.tensor_single_scalar(out=negb[:], in_=beta[:], scalar=-1.0, op=ALU.mult), reads=[gB], writes=[gB])
                T.op("dve", lambda e: e.tensor_tensor(out=tmpg[:], in0=braw[:, :, 8:16], in1=dtb[:].unsqueeze(1).to_broadcast([128, 16, 8]), op=ALU.add),
                     reads=[gB, gcB], writes=[gB])
                T.op("act", lambda e: e.activation(out=tmpg[:], in_=tmpg[:], func=AF.Exp), reads=[gB], writes=[gB])
                T.op("act", lambda e: e.activation(out=tmpg[:], in_=tmpg[:], func=AF.Ln, bias=eps_t[:, 1:2]), reads=[gB, cB], writes=[gB])
                T.op("act", lambda e: e.activation(out=negA[:], in_=alog[:], func=AF.Exp), reads=[gcB], writes=[gB])
                T.op("dve", lambda e: e.tensor_single_scalar(out=negA[:], in_=negA[:], scalar=-1.0, op=ALU.mult), reads=[gB], writes=[gB])
                T.op("dve", lambda e: e.tensor_tensor(out=gt[:], in0=tmpg[:], in1=negA[:].unsqueeze(1).to_broadcast([128, 16, 8]), op=ALU.mult),
                     reads=[gB], writes=[gB])
                b = bank()
                T.op("pe", lambda e: e.matmul(psum[:, b, 0:128], lhsT=triu[:, :], rhs=gt[:].rearrange("p t k -> p (t k)"), start=True, stop=True),
                     reads=[gB, gcB], writes=[pB[b]])
                T.op("act", lambda e: e.activation(out=gcum[:].rearrange("p t k -> p (t k)"), in_=psum[:, b, 0:128], func=AF.Copy),
                     reads=[pB[b]], writes=[gB])
                b = bank()
                T.op("pe", lambda e: e.matmul(psum[:, b, 0:128], lhsT=ones_f[:, :], rhs=gt[:].rearrange("p t k -> p (t k)"), start=True, stop=True),
                     reads=[gB, cB], writes=[pB[b]])
                T.op("act", lambda e: e.activation(out=glast[:].rearrange("p t k -> p (t k)"), in_=psum[:, b, 0:128], func=AF.Copy),
                     reads=[pB[b]], writes=[gB])
                T.op("dve", lambda e: e.tensor_tensor(out=tmpg[:], in0=glast[:], in1=gcum[:], op=ALU.subtract), reads=[gB], writes=[gB])
                T.op("act", lambda e: e.activation(out=kds[:], in_=tmpg[:], func=AF.Exp), reads=[gB], writes=[gB])
                T.op("act", lambda e: e.activation(out=sdec[:], in_=glast[:], func=AF.Exp), reads=[gB], writes=[gB])
                T.op("act", lambda e: e.activation(out=egc[:], in_=gcum[:], func=AF.Exp), reads=[gB], writes=[gB])

                wh = [gsb("wh%d" % i, [128, NCH, 128], BF16) for i in range(4)]
                whB = [Buf() for _ in range(4)]
                cv = gsb("cv", [128, 512], F32)
                cvB = Buf()
                CONV_PE = int(os.environ.get("CONV_PE", "1"))
                Dg = gsb("Dg", [128, 3, 4, 128], BF16)
                DgB = Buf()
                xb = [gsb("xb%d" % i, [128, 4 + 512], BF16) for i in range(3)]
                xbB = [Buf() for _ in range(3)]
                kqt = [gsb("kq%d" % st_, [128, 2, 512], BF16) for st_ in range(2)]
                vts = [gsb("vt%d" % st_, [128, 512], BF16) for st_ in range(2)]
                qkvs = [[kqt[st_][:, 1, :], kqt[st_][:, 0, :], vts[st_][:, :]] for st_ in range(2)]
                qkvsB = [[Buf() for _ in range(3)] for _ in range(2)]
                gsils = [gsb("gsil%d" % st_, [128, 512], BF16) for st_ in range(3)]
                gsilsB = [Buf(), Buf(), Buf()]
                HsS = [gsb("HsS%d" % i, [128, 4, 128], F32) for i in range(2)]
                ATsS = [gsb("ATsS%d" % i, [128, 4, 128], F32) for i in range(2)]
                O0sS = [gsb("O0sS%d" % i, [128, 4, 128], F32) for i in range(2)]
                QeffS = [gsb("QeffS%d" % i, [128, 4, 128], BF16) for i in range(2)]
                HsSB = [[Buf() for _ in range(4)] for _ in range(2)]
                ATsSB = [[Buf() for _ in range(4)] for _ in range(2)]
                O0sSB = [[Buf() for _ in range(4)] for _ in range(2)]
                QeffSB = [[Buf() for _ in range(4)] for _ in range(2)]
                osum = gsb("osum", [128, 4, 128], F32)
                osumB = Buf()
                rr2 = gsb("rr2", [128, 4, 128], F32)
                rr2B = Buf()
                g1bk = {"n": 0}

                FILL = int(os.environ.get("FILL", "0"))

                def g1bank():
                    if FILL:
                        return 4
                    g1bk["n"] = 1 - g1bk["n"]
                    return 4 + g1bk["n"]
                if FILL:
                    T.filler = lambda e: e.matmul(psum[:, 5, 0:FILL], lhsT=ident_b[:, :], rhs=ones_b[:, :].unsqueeze(1).to_broadcast([128, FILL // 128, 128]) if FILL > 128 else ident_b[:, :], start=True, stop=True)
                sqb = gsb("sqb", [128, 512], BF16)
                sqbB = Buf()
                rt = gsb("rt", [128, 512], F32)
                rtB = Buf()
                Fm = [gsb("F%d" % i, [128, 4, 128], F32) for i in range(3)]
                FB = [[Buf() for _ in range(4)] for _ in range(3)]
                QT = gsb("QT", [128, 4, 128], BF16)
                QTB = [Buf() for _ in range(4)]
                PQ = [gsb("PQ%d" % i, [128, 2, 4, 128], BF16) for i in range(2)]
                PQB = [[Buf() for _ in range(4)] for _ in range(2)]
                Wt = gsb("Wt", [128, 4, 128], BF16)
                WtB = [Buf() for _ in range(4)]
                ATt = gsb("ATt", [128, 4, 128], BF16)
                ATtB = [Buf() for _ in range(4)]
                Aqk = gsb("Aqk", [128, 4, 128], BF16)
                AqkB = [Buf() for _ in range(4)]
                kdec = gsb("kdec", [128, 4, 128], BF16)
                kdecB = [Buf() for _ in range(4)]
                VK = gsb("VK", [128, 4, 256], BF16)
                VKB = [Buf() for _ in range(4)]
                Zt = gsb("Zt", [128, 4, 256], BF16)
                ZtB = [Buf() for _ in range(4)]
                osq = gsb("osq", [128, 512], BF16)
                osqB = Buf()
                Sf = gsb("Sf", [128, 128], F32)
                Sb = gsb("Sb", [128, 128], BF16)
                SfB, SbB = Buf(), Buf()
                QSCALE = 128.0 ** -0.5
                hbk = {"n": 0}

                def hbank():
                    b_ = 4 + hbk["n"]
                    hbk["n"] = (hbk["n"] + 1) % 3
                    return b_

                def pv(bk):
                    return psum[:, bk, :].rearrange("p (c i) -> p c i", c=4)

                def fl(tl):
                    return tl[:].rearrange("p c i -> p (c i)")

                def g1b(h, n, st, sg):
                    tk = slice(n * 512, (n + 1) * 512)
                    if n == 0:
                        for i in range(3):
                            yield T.defer_dma("pool", wh[i][:].rearrange("p c f -> p (c f)"), dt_in["wgdn"][h, i], writes=[whB[i]])
                        yield T.defer_dma("pool", wh[3][:].rearrange("p c f -> p (c f)"), dt_in["wgg"][h], writes=[whB[3]])
                        if CONV_PE:
                            for w3 in range(3):
                                for tp in range(4):
                                    yield T.defer("dve", lambda e: e.tensor_scalar_mul(out=Dg[:, w3, tp, :], in0=ident_f[:, :], scalar1=convw[:, h, w3, tp:tp + 1]),
                                                  reads=[gcB, cB], writes=[DgB])
                    for w3 in range(3):
                        if n == 0:
                            yield T.defer5("pool", lambda e: e.memset(xb[w3][:, 0:4], 0.0), writes=[xbB[w3]])
                        else:
                            yield T.defer5("pool", lambda e: e.tensor_copy(out=xb[w3][:, 0:4], in_=xb[w3][:, 512:516]), reads=[xbB[w3]], writes=[xbB[w3]])
                        b = g1bank()

                        def mm(e):
                            ins = None
                            for c in range(NCH):
                                ins = e.matmul(psum[:, b, :], lhsT=wh[w3][:, c, :], rhs=hT[:, c, tk], start=(c == 0), stop=(c == NCH - 1))
                            return ins
                        yield T.defer5("pe", mm, reads=[whB[w3], hB[n]], writes=[pB[b]], cost=2.1)
                        yield T.defer5("act", lambda e: e.activation(out=xb[w3][:, 4:516], in_=psum[:, b, :], func=AF.Copy), reads=[pB[b]], writes=[xbB[w3]])
                        if CONV_PE:
                            b = g1bank()

                            def mmc(e):
                                ins = None
                                for tp in range(4):
                                    ins = e.matmul(psum[:, b, :], lhsT=Dg[:, w3, tp, :], rhs=xb[w3][:, 1 + tp:1 + tp + 512], start=(tp == 0), stop=(tp == 3))
                                return ins
                            yield T.defer5("pe", mmc, reads=[DgB, xbB[w3]], writes=[pB[b]], cost=1.0)
                            yield T.defer5("act", lambda e: e.activation(out=qkvs[st][w3][:, :], in_=psum[:, b, :], func=AF.Silu), reads=[pB[b]], writes=[qkvsB[st][w3]])
                        else:
                            yield T.defer5("dve", lambda e: e.tensor_scalar_mul(out=cv[:, :], in0=xb[w3][:, 1:513], scalar1=convw[:, h, w3, 0:1]),
                                 reads=[xbB[w3], gcB], writes=[cvB])
                            for tp in range(1, 4):
                                yield T.defer5("dve", lambda e: e.scalar_tensor_tensor(out=cv[:, :], in0=xb[w3][:, 1 + tp:513 + tp], scalar=convw[:, h, w3, tp:tp + 1],
                                                                             in1=cv[:, :], op0=ALU.mult, op1=ALU.add), reads=[xbB[w3], gcB, cvB], writes=[cvB])
                            yield T.defer5("act", lambda e: e.activation(out=qkvs[st][w3][:, :], in_=cv[:, :], func=AF.Silu), reads=[cvB], writes=[qkvsB[st][w3]])
                        if w3 < 2:
                            yield T.defer5("act", lambda e: e.activation(out=sqb[:, :], in_=qkvs[st][w3][:, :], func=AF.Square), reads=[qkvsB[st][w3]], writes=[sqbB])
                            b2 = g1bank()
                            yield T.defer5("pe", lambda e: e.matmul(psum[:, b2, :], lhsT=ones_b[:, :], rhs=sqb[:, :], start=True, stop=True),
                                 reads=[sqbB, cB], writes=[pB[b2]])
                            yield T.defer5("act", lambda e: e.activation(out=rt[:, :], in_=psum[:, b2, :], func=AF.Ln, bias=eps_t[:, 0:1]), reads=[pB[b2], cB], writes=[rtB])
                            yield T.defer5("act", lambda e: e.activation(out=rt[:, :], in_=rt[:, :], func=AF.Exp, scale=-0.5), reads=[rtB], writes=[rtB])
                            yield T.defer5("dve", lambda e: e.scalar_tensor_tensor(out=qkvs[st][w3][:, :], in0=qkvs[st][w3][:, :], scalar=(QSCALE if w3 == 0 else 1.0),
                                                                         in1=rt[:, :], op0=ALU.mult, op1=ALU.mult),
                                 reads=[rtB, qkvsB[st][w3]], writes=[qkvsB[st][w3]])
                    b = g1bank()

                    def mm(e):
                        ins = None
                        for c in range(NCH):
                            ins = e.matmul(psum[:, b, :], lhsT=wh[3][:, c, :], rhs=hT[:, c, tk], start=(c == 0), stop=(c == NCH - 1))
                        return ins
                    yield T.defer5("pe", mm, reads=[whB[3], hB[n]], writes=[pB[b]], cost=2.1)
                    yield T.defer5("act", lambda e: e.activation(out=rt[:, :], in_=psum[:, b, :], func=AF.Silu), reads=[pB[b]], writes=[rtB])
                    yield T.defer5("dve", lambda e: e.tensor_scalar_mul(out=gsils[sg][:, :], in0=rt[:, :], scalar1=onorm[:, 0:1]), reads=[rtB, gcB], writes=[gsilsB[sg]])

                def scanpost(h, gi, pg, sg):
                    t0 = gi * 4
                    gk = slice(gi * 512, (gi + 1) * 512)
                    Hs, ATs, O0s, Qf = HsS[pg], ATsS[pg], O0sS[pg], QeffS[pg]
                    bOo, bS = 7, 6
                    for c in range(4):
                        t = t0 + c
                        if t > 0:
                            yield T.defer("pe", lambda e: e.matmul(psum[:, bOo, c * 128:(c + 1) * 128], lhsT=Sb[:, :], rhs=Qf[:, c, :], start=True, stop=True),
                                          reads=[SbB, QeffSB[pg][c]], writes=[pB[bOo]])
                        if t == 15:
                            continue
                        if t == 0:
                            yield T.defer("dve", lambda e: e.tensor_copy(out=Sf[:, :], in_=Hs[:, 0, :]), reads=[HsSB[pg][0]], writes=[SfB])
                        else:
                            yield T.defer("pe", lambda e: e.matmul(psum[:, bS, 0:128], lhsT=ATs[:, c, :], rhs=Sf[:, :], start=True, stop=True),
                                          reads=[ATsSB[pg][c], SfB], writes=[pB[bS]], cost=0.5)
                            yield T.defer("dve", lambda e: e.tensor_tensor(out=Sf[:, :], in0=psum[:, bS, 0:128], in1=Hs[:, c, :], op=ALU.add),
                                          reads=[pB[bS], HsSB[pg][c]], writes=[SfB])
                        yield T.defer("act", lambda e: e.activation(out=Sb[:, :], in_=Sf[:, :], func=AF.Copy), reads=[SfB], writes=[SbB])
                    if gi == 0:
                        yield T.defer("dve", lambda e: e.tensor_copy(out=osum[:, 0, :], in_=O0s[:, 0, :]), reads=[O0sSB[pg][0]], writes=[osumB])
                        yield T.defer("dve", lambda e: e.tensor_tensor(out=osum[:, 1:4, :], in0=pv(bOo)[:, 1:4, :], in1=O0s[:, 1:4, :], op=ALU.add),
                                      reads=[pB[bOo]] + O0sSB[pg][1:4], writes=[osumB], cost=0.5)
                    else:
                        yield T.defer("dve", lambda e: e.tensor_tensor(out=osum[:], in0=pv(bOo), in1=O0s[:], op=ALU.add), reads=[pB[bOo]] + O0sSB[pg], writes=[osumB], cost=0.6)
                    yield T.defer("act", lambda e: e.activation(out=osq[:, :], in_=fl(osum), func=AF.Square), reads=[osumB], writes=[osqB], cost=0.6)
                    yield T.defer("pe", lambda e: e.matmul(psum[:, bS, :], lhsT=odiv[:, :], rhs=osq[:, :], start=True, stop=True), reads=[osqB, gcB], writes=[pB[bS]])
                    yield T.defer("act", lambda e: e.activation(out=fl(rr2), in_=psum[:, bS, :], func=AF.Ln, bias=eps_t[:, 0:1]), reads=[pB[bS], cB], writes=[rr2B], cost=0.6)
                    yield T.defer("act", lambda e: e.activation(out=fl(rr2), in_=fl(rr2), func=AF.Exp, scale=-0.5), reads=[rr2B], writes=[rr2B], cost=0.6)
                    yield T.defer("dve", lambda e: e.tensor_tensor(out=osum[:], in0=osum[:], in1=rr2[:], op=ALU.mult), reads=[osumB, rr2B], writes=[osumB], cost=0.6)
                    yield T.defer("pool", lambda e: e.tensor_tensor(out=ybT[:, h, gk], in0=fl(osum), in1=gsils[sg][:, :], op=ALU.mult),
                                  reads=[osumB, gsilsB[sg]], writes=[ybB[h]], cost=1.1)

                blocks = [(h_, n_) for h_ in range(8) for n_ in range(4)]
                NBK = len(blocks)
                run_streams([g1b(0, 0, 0, 0)])
                for r in range(NBK + 1):
                    gens = []
                    if r < NBK:
                        h, gi = blocks[r]
                        st = r % 2
                        pg = r % 2
                        t0 = gi * 4
                        cks = [slice(c * 128, (c + 1) * 128) for c in range(4)]
                        qn_, kn_, vn_ = qkvs[st][0], qkvs[st][1], qkvs[st][2]
                        kq_ = kqt[st]
                        qB, kB_, vB_ = qkvsB[st][0], qkvsB[st][1], qkvsB[st][2]

                        def prep(c):
                            t = t0 + c
                            ck = cks[c]
                            ba = bb = c
                            DT, decF, EG = Fm[0][:, c, :], Fm[1][:, c, :], Fm[2][:, c, :]
                            gc = gcum[:, t, h:h + 1]
                            yield T.defer("pe", lambda e: e.matmul(psum[:, ba, 0:128], lhsT=gt[:, t, h:h + 1].to_broadcast([128, 128]), rhs=triu[:, :],
                                                          start=True, stop=True), reads=[gB, gcB], writes=[pB[ba]])
                            yield T.defer("dve", lambda e: e.scalar_tensor_tensor(out=DT, in0=psum[:, ba, 0:128], scalar=gc, in1=smask[:, :],
                                                                         op0=ALU.subtract, op1=ALU.add), reads=[pB[ba], gB, gcB], writes=[FB[0][c]])
                            yield T.defer("act", lambda e: e.activation(out=EG, in_=psum[:, ba, 0:128], func=AF.Exp), reads=[pB[ba]], writes=[FB[2][c]])
                            yield T.defer("act", lambda e: e.activation(out=DT, in_=DT, func=AF.Exp), reads=[FB[0][c]], writes=[FB[0][c]])
                            yield T.defer("pool", lambda e: e.tensor_tensor(out=decF, in0=DT, in1=ident_f[:, :], op=ALU.add), reads=[FB[0][c], cB], writes=[FB[1][c]])
                            yield T.defer("pe", lambda e: e.matmul(psum[:, ba, 0:256], lhsT=kn_[:, ck], rhs=kq_[:, :, ck], start=True, stop=True),
                                 reads=[kB_, qB], writes=[pB[ba]])
                            yield T.defer("dve", lambda e: e.scalar_tensor_tensor(out=QT[:, c, :], in0=psum[:, ba, 0:128], scalar=negb[:, t, h:h + 1], in1=DT,
                                                                         op0=ALU.mult, op1=ALU.mult), reads=[pB[ba], gB, FB[0][c]], writes=[QTB[c]])
                            yield T.defer("dve", lambda e: e.tensor_tensor(out=Aqk[:, c, :], in0=psum[:, ba, 128:256], in1=decF, op=ALU.mult),
                                 reads=[pB[ba], FB[1][c]], writes=[AqkB[c]])
                            yield T.defer("pe", lambda e: e.matmul(psum[:, bb, 128:256], lhsT=QT[:, c, :], rhs=ident_b[:, :], start=True, stop=True),
                                 reads=[QTB[c], cB], writes=[pB[bb]])
                            yield T.defer("dve", lambda e: e.tensor_tensor(out=PQ[0][:, 0, c, :], in0=psum[:, bb, 128:256], in1=ident_f[:, :], op=ALU.add),
                                 reads=[pB[bb], cB], writes=[PQB[0][c]])
                            yield T.defer("pool", lambda e: e.tensor_tensor(out=PQ[0][:, 1, c, :], in0=QT[:, c, :], in1=ident_b[:, :], op=ALU.add),
                                 reads=[QTB[c], cB], writes=[PQB[0][c]])
                            yield T.defer("pool", lambda e: e.tensor_tensor(out=ATt[:, c, :], in0=ident_b[:, :], in1=QT[:, c, :], op=ALU.subtract),
                                 reads=[QTB[c], cB], writes=[ATtB[c]])
                            yield T.defer("pe", lambda e: e.matmul(psum[:, ba, 0:128], lhsT=kn_[:, ck], rhs=ident_b[:, :], start=True, stop=True),
                                 reads=[kB_, cB], writes=[pB[ba]])
                            yield T.defer("dve", lambda e: e.tensor_scalar_mul(out=kdec[:, c, :], in0=psum[:, ba, 0:128], scalar1=kds[:, t, h:h + 1]),
                                 reads=[pB[ba], gB], writes=[kdecB[c]])
                            yield T.defer("dve", lambda e: e.tensor_scalar_mul(out=VK[:, c, 128:256], in0=psum[:, ba, 0:128], scalar1=egc[:, t, h:h + 1]),
                                 reads=[pB[ba], gB], writes=[VKB[c]])
                            yield T.defer("pe", lambda e: e.matmul(psum[:, bb, 128:256], lhsT=vn_[:, ck], rhs=ident_b[:, :], start=True, stop=True),
                                 reads=[vB_, cB], writes=[pB[bb]])
                            yield T.defer("act", lambda e: e.activation(out=VK[:, c, 0:128], in_=psum[:, bb, 128:256], func=AF.Copy), reads=[pB[bb]], writes=[VKB[c]])
                            cur = 0
                            for k in range(2, 8):
                                nxt = 1 - cur
                                last = (k == 7)
                                yield T.defer("pe", lambda e: e.matmul(psum[:, ba, 0:128], lhsT=ATt[:, c, :], rhs=PQ[cur][:, 0, c, :], start=True, stop=True),
                                     reads=[ATtB[c], PQB[cur][c]], writes=[pB[ba]])
                                yield T.defer("dve", lambda e: e.scalar_tensor_tensor(out=Wt[:, c, :], in0=psum[:, ba, 0:128], scalar=-1.0, in1=ident2_f[:, :],
                                                                             op0=ALU.mult, op1=ALU.add), reads=[pB[ba], cB], writes=[WtB[c]])

                                def mmn(e):
                                    ins = e.matmul(psum[:, bb, 128:256], lhsT=Wt[:, c, :], rhs=PQ[cur][:, 1, c, :], start=True, stop=True)
                                    if not last:
                                        ins = e.matmul(psum[:, ba, 0:128], lhsT=PQ[cur][:, 1, c, :], rhs=Wt[:, c, :], start=True, stop=True)
                                    return ins
                                yield T.defer("pe", mmn, reads=[WtB[c], PQB[cur][c]], writes=[pB[ba]])
                                if last:
                                    yield T.defer("act", lambda e: e.activation(out=QT[:, c, :], in_=psum[:, bb, 128:256], func=AF.Copy), reads=[pB[bb]], writes=[QTB[c]])
                                else:
                                    yield T.defer("act", lambda e: e.activation(out=PQ[nxt][:, :, c, :], in_=psum[:, ba, 0:256].rearrange("p (a i) -> p a i", a=2), func=AF.Copy),
                                         reads=[pB[ba]], writes=[PQB[nxt][c]])
                                cur = nxt
                            yield T.defer("pe", lambda e: e.matmul(psum[:, ba, 0:256], lhsT=QT[:, c, :], rhs=VK[:, c, :], start=True, stop=True),
                                 reads=[QTB[c], VKB[c]], writes=[pB[ba]])
                            yield T.defer("dve", lambda e: e.tensor_scalar_mul(out=Zt[:, c, :], in0=psum[:, ba, 0:256], scalar1=beta[:, t, h:h + 1]),
                                 reads=[pB[ba], gB], writes=[ZtB[c]])
                            yield T.defer("pe", lambda e: e.matmul(psum[:, bb, 128:256], lhsT=kdec[:, c, :], rhs=Zt[:, c, 0:128], start=True, stop=True),
                                 reads=[kdecB[c], ZtB[c]], writes=[pB[bb]])
                            yield T.defer("act", lambda e: e.activation(out=HsS[pg][:, c, :], in_=psum[:, bb, 128:256], func=AF.Copy), reads=[pB[bb]], writes=[HsSB[pg][c]])
                            yield T.defer("pe", lambda e: e.matmul(psum[:, ba, 0:128], lhsT=Zt[:, c, 128:256], rhs=kdec[:, c, :], start=True, stop=True),
                                 reads=[kdecB[c], ZtB[c]], writes=[pB[ba]])
                            yield T.defer("dve", lambda e: e.scalar_tensor_tensor(out=ATsS[pg][:, c, :], in0=ident_f[:, :], scalar=sdec[:, t, h:h + 1], in1=psum[:, ba, 0:128],
                                                                         op0=ALU.mult, op1=ALU.subtract), reads=[pB[ba], gB, cB], writes=[ATsSB[pg][c]])
                            yield T.defer("pe", lambda e: e.matmul(psum[:, bb, 128:256], lhsT=Zt[:, c, 0:128], rhs=Aqk[:, c, :], start=True, stop=True),
                                 reads=[AqkB[c], ZtB[c]], writes=[pB[bb]])
                            yield T.defer("act", lambda e: e.activation(out=O0sS[pg][:, c, :], in_=psum[:, bb, 128:256], func=AF.Copy), reads=[pB[bb]], writes=[O0sSB[pg][c]])
                            yield T.defer("pe", lambda e: e.matmul(psum[:, ba, 0:128], lhsT=Zt[:, c, 128:256], rhs=Aqk[:, c, :], start=True, stop=True),
                                 reads=[AqkB[c], ZtB[c]], writes=[pB[ba]])
                            yield T.defer("pool", lambda e: e.tensor_tensor(out=decF, in0=qn_[:, ck], in1=EG, op=ALU.mult), reads=[qB, FB[2][c], FB[1][c]], writes=[FB[1][c]])
                            yield T.defer("dve", lambda e: e.tensor_tensor(out=QeffS[pg][:, c, :], in0=decF, in1=psum[:, ba, 0:128], op=ALU.subtract),
                                 reads=[pB[ba], FB[1][c]], writes=[QeffSB[pg][c]])

                        gens += [prep(c) for c in range(4)]
                    if r + 1 < NBK:
                        hn, nn = blocks[r + 1]
                        gens.append(g1b(hn, nn, (r + 1) % 2, (r + 1) % 3))
                    if r >= 1:
                        hp, gp = blocks[r - 1]
                        gens.append(scanpost(hp, gp, (r - 1) % 2, (r - 1) % 3))
                    run_streams(gens)
                T.filler = None
                T.barrier()

        def merge(hT, hB, yaT, yaB, ybT, ybB):
            with contextlib.ExitStack() as gs:
                def gsb(name, shape, dt):
                    return gs.enter_context(nc.sbuf_tensor("x_" + name, list(shape), dt))
                mg = gsb("mg", [128, 8, S], BF16)
                mgB = [[Buf() for _ in range(NB)] for _ in range(8)]
                wga = [gsb("wga%d" % i, [128, 2, NCH, 128], BF16) for i in range(2)]
                wba = [gsb("wba%d" % i, [64, 4, 128], BF16) for i in range(2)]
                wbb = [gsb("wbb%d" % i, [128, 8, 128], BF16) for i in range(2)]
                wB = [Buf(), Buf()]
                wo = [gsb("wo%d" % i, [128, 8, 128], BF16) for i in range(2)]
                woB = [Buf(), Buf()]
                one_c = eps_t[:, 1:2]
                sa = gsb("sa", [128, 512], F32)
                sb_ = gsb("sb", [128, 512], F32)
                ta = gsb("ta", [128, 512], F32)
                tb = gsb("tb", [128, 512], F32)
                saB, sbB, taB, tbB = Buf(), Buf(), Buf(), Buf()

                def loadw(dt):
                    i = dt % 2
                    T.dma("pool", wga[i][:].rearrange("p w c f -> p (w c f)"), dt_in["wgab"][dt], writes=[wB[i]])
                    T.dma("pool", wba[i][:].rearrange("p s f -> p (s f)"), dt_in["wba"][dt], writes=[wB[i]])
                    T.dma("pool", wbb[i][:].rearrange("p s f -> p (s f)"), dt_in["wbb"][dt], writes=[wB[i]])
                loadw(0)
                loadw(1)
                for dt in range(8):
                    wi = dt % 2
                    for n in range(NB):
                        tk = slice(n * 512, (n + 1) * 512)
                        bga, bgb, ba, bb = bank(), bank(), bank(), bank()

                        def mmg(e, which, b):
                            ins = None
                            for c in range(NCH):
                                ins = e.matmul(psum[:, b, :], lhsT=wga[wi][:, which, c, :], rhs=hT[:, c, tk], start=(c == 0), stop=(c == NCH - 1))
                            return ins
                        T.op("pe", lambda e: mmg(e, 0, bga), reads=[wB[wi], hB[n]], writes=[pB[bga]])
                        T.op("pe", lambda e: mmg(e, 1, bgb), reads=[wB[wi], hB[n]], writes=[pB[bgb]])

                        def mma(e):
                            ins = None
                            for s4 in range(4):
                                ins = e.matmul(psum[:, ba, :], lhsT=wba[wi][:, s4, :], rhs=yaT[:, s4, tk], start=(s4 == 0), stop=(s4 == 3))
                            return ins
                        T.op("pe", mma, reads=[wB[wi]] + yaB, writes=[pB[ba]])

                        def mmb(e):
                            ins = None
                            for h in range(8):
                                ins = e.matmul(psum[:, bb, :], lhsT=wbb[wi][:, h, :], rhs=ybT[:, h, tk], start=(h == 0), stop=(h == 7))
                            return ins
                        T.op("pe", mmb, reads=[wB[wi]] + ybB, writes=[pB[bb]])
                        T.op("act", lambda e: e.activation(out=sa[:, :], in_=psum[:, bga, :], func=AF.Sigmoid), reads=[pB[bga]], writes=[saB])
                        T.op("act", lambda e: e.activation(out=sb_[:, :], in_=psum[:, bgb, :], func=AF.Sigmoid), reads=[pB[bgb]], writes=[sbB])
                        T.op("dve", lambda e: e.tensor_tensor(out=ta[:, :], in0=psum[:, ba, :], in1=sa[:, :], op=ALU.mult), reads=[pB[ba], saB], writes=[taB])
                        T.op("dve", lambda e: e.tensor_tensor(out=tb[:, :], in0=psum[:, bb, :], in1=sb_[:, :], op=ALU.mult), reads=[pB[bb], sbB], writes=[tbB])
                        T.op("pool", lambda e: e.tensor_tensor(out=mg[:, dt, tk], in0=ta[:, :], in1=tb[:, :], op=ALU.add), reads=[taB, tbB], writes=[mgB[dt][n]])
                    if dt + 2 < 8:
                        loadw(dt + 2)

                def loado(dt):
                    T.dma("pool", wo[dt % 2][:].rearrange("p s f -> p (s f)"), dt_in["wout"][dt], writes=[woB[dt % 2]])
                loado(0)
                loado(1)
                for dt in range(8):
                    wi = dt % 2
                    for n in range(NB):
                        tk = slice(n * 512, (n + 1) * 512)
                        b = bank()

                        def mmo(e):
                            ins = None
                            for k in range(8):
                                ins = e.matmul(psum[:, b, :], lhsT=wo[wi][:, k, :], rhs=mg[:, k, tk], start=(k == 0), stop=(k == 7))
                            return ins
                        T.op("pe", mmo, reads=[woB[wi]] + [mgB[k][n] for k in range(8)], writes=[pB[b]])
                        T.op("dve", lambda e: e.tensor_tensor(out=xT[:, dt, tk], in0=psum[:, b, :], in1=xT[:, dt, tk], op=ALU.add),
                             reads=[pB[b]], writes=[xB[n]])
                    if dt + 2 < 8:
                        loado(dt + 2)
                T.barrier()

        def dbg_dump(slabs):
            with contextlib.ExitStack() as ds:
                dtile = [ds.enter_context(nc.sbuf_tensor("dbg%d" % i, [128, S], F32)) for i in range(2)]
                dB = [Buf(), Buf()]
                for k, (ap, bufs, row0, P) in enumerate(slabs):
                    i = k % 2
                    T.op("act", lambda e: e.activation(out=dtile[i][0:P, :], in_=ap, func=AF.Copy), reads=bufs, writes=[dB[i]])
                    T.dma("sp", out_d[row0:row0 + P, :], dtile[i][0:P, :], reads=[dB[i]])
                T.barrier()

        ffn(0, 0)
        done = False
        if stop_after not in ("ffn1_raw", "ffn1"):
            r = mixer()
            if r is not None:
                done = True
        if stop_after in ("all", "x3_raw"):
            ffn(1, 2)
        if not done:
          with contextlib.ExitStack() as fs:
            yT = [fs.enter_context(nc.sbuf_tensor("yT%d" % i, [128, NCH, 512], F32)) for i in range(2)]
            yB = [Buf(), Buf()]
            sq_t = fs.enter_context(nc.sbuf_tensor("fsq", [128, NCH, 512], BF16))
            sq_b = [Buf() for _ in range(NCH)]
            rs_t = fs.enter_context(nc.sbuf_tensor("frs", [128, 512], F32))
            rs_b = Buf()
            out_dv = out_d.rearrange("(c p) t -> p c t", p=128)
            for n in range(NB):
                i = n % 2
                if stop_after.endswith("_raw"):
                    for c in range(NCH):
                        T.op("dve", lambda e: e.tensor_copy(out=yT[i][:, c, :], in_=xT[:, c, n * 512:(n + 1) * 512]),
                             reads=[xB[n]], writes=[yB[i]])
                else:
                    rmsnorm_block(n, 3, lambda c: yT[i][:, c, :], yB[i], sq_t, sq_b, rs_t, rs_b)
                T.dma("sp", out_dv[:, :, n * 512:(n + 1) * 512], yT[i][:], reads=[yB[i]])
            T.barrier()
    return nc, set(dt_in.keys())


def _consts():
    c = {}
    c["ident_f"] = np.eye(128, dtype=np.float32)
    c["ones_f"] = np.ones((128, 128), dtype=np.float32)
    half = 32
    inv_freq = 10000.0 ** (-np.arange(half, dtype=np.float64) / half)
    pos = np.arange(S, dtype=np.float64)
    ang = pos[None, :] * inv_freq[:, None]
    cosv, sinv = np.cos(ang).astype(np.float32), np.sin(ang).astype(np.float32)
    cos_t = np.zeros((128, S), np.float32)
    sin_t = np.zeros((128, S), np.float32)
    rm = np.zeros((128, 128), np.float32)
    for p in range(128):
        d = p % 64
        cos_t[p] = cosv[d % 32]
        sin_t[p] = -sinv[d % 32] if d < 32 else sinv[d % 32]
        partner = p + 32 if d < 32 else p - 32
        rm[partner, p] = 1.0
    c["rope_cos"], c["rope_sin"], c["rm"] = cos_t, sin_t, rm
    j = np.arange(128)[:, None]
    i = np.arange(128)[None, :]
    mprev = (j >= i).astype(np.float32)
    mcur = (j <= i).astype(np.float32)
    c["amask"] = np.concatenate([mprev, mcur, mprev, mcur], axis=1)
    em = np.zeros((128, 64), np.float32)
    em[64, :] = 1.0
    c["em"] = em
    c["triu"] = (j <= i).astype(np.float32)
    c["smask"] = np.where(i > j, 0.0, -30000.0).astype(np.float32)
    c["odiv"] = np.full((128, 128), 1.0 / 128.0, np.float32)
    return c


def _prep_inputs(inp):
    f = lambda a: np.ascontiguousarray(a, dtype=np.float32)
    shared = {}

    def ncol(v):
        return f(np.asarray(v).reshape(NCH, 128).T)
    shared["ffn1_norm"] = ncol(inp["ffn1_norm"][0])
    shared["ffn2_norm"] = ncol(inp["ffn2_norm"][0])
    shared["mix_norm"] = ncol(inp["mix_norm"][0])
    shared["final_norm"] = ncol(inp["final_norm"])
    ffn_in = {1: (inp["ffn1_w_gate"], inp["ffn1_w_up"], inp["ffn1_w_down"]),
              2: (inp["ffn2_w_gate"], inp["ffn2_w_up"], inp["ffn2_w_down"])}
    for i in (1, 2):
        wg = np.asarray(ffn_in[i][0][0])
        wu = np.asarray(ffn_in[i][1][0])
        wd = np.asarray(ffn_in[i][2][0])
        shared["wg%d" % i] = f(wg.reshape(NCH, 128, 11, 256).transpose(2, 1, 0, 3).reshape(11, 128, NCH * 256))
        shared["wu%d" % i] = f(wu.reshape(NCH, 128, 11, 256).transpose(2, 1, 0, 3).reshape(11, 128, NCH * 256))
        shared["wd%d" % i] = f(wd.reshape(NF, 128, 8, 128).transpose(2, 1, 0, 3).reshape(8, 128, NF * 128))
    w_in = np.asarray(inp["w_in"][0])

    def cols(c0, n):
        return w_in[:, c0:c0 + n].reshape(NCH, 128, n).transpose(1, 0, 2).reshape(128, NCH * n)
    wa = np.zeros((6, 3, 128, NCH * 128), np.float32)
    for g in range(3):
        for hp in range(2):
            for i in range(3):
                wa[g * 2 + hp, i] = cols(i * 768 + (4 * g + 2 * hp) * 64, 128)
    shared["wqkv_a"] = wa
    wg_ = np.zeros((8, 3, 128, NCH * 128), np.float32)
    wgg = np.zeros((8, 128, NCH * 128), np.float32)
    for h in range(8):
        for i in range(3):
            wg_[h, i] = cols(2304 + i * 1024 + h * 128, 128)
        wgg[h] = cols(5392 + h * 128, 128)
    shared["wgdn"] = wg_
    shared["wgg"] = wgg
    shared["wbd"] = f(cols(5376, 16))
    wgab = np.zeros((8, 128, 2 * NCH * 128), np.float32)
    for dt in range(8):
        a = cols(6416 + dt * 128, 128).reshape(128, 1, NCH * 128)
        b = cols(7440 + dt * 128, 128).reshape(128, 1, NCH * 128)
        wgab[dt] = np.concatenate([a, b], axis=1).reshape(128, 2 * NCH * 128)
    shared["wgab"] = wgab
    wba = np.asarray(inp["w_branch_a"][0])
    shared["wba"] = f(wba.reshape(4, 64, 8, 128).transpose(2, 1, 0, 3).reshape(8, 64, 4 * 128))
    wbb = np.asarray(inp["w_branch_b"][0])
    shared["wbb"] = f(wbb.reshape(8, 128, 8, 128).transpose(2, 1, 0, 3).reshape(8, 128, 8 * 128))
    wo = np.asarray(inp["w_out"][0])
    shared["wout"] = f(wo.reshape(8, 128, 8, 128).transpose(2, 1, 0, 3).reshape(8, 128, 8 * 128))
    cw = np.asarray(inp["gdn_conv_w"][0])
    shared["convw"] = f(cw.reshape(4, 3, 8, 128).transpose(3, 2, 1, 0).reshape(128, 96))
    shared["alog"] = f(np.broadcast_to(np.asarray(inp["gdn_a_log"][0])[None, :], (128, 8)))
    shared["dtb"] = f(np.broadcast_to(np.asarray(inp["gdn_dt_bias"][0])[None, :], (128, 8)))
    shared["onorm"] = f(np.asarray(inp["gdn_out_norm"][0]).reshape(128, 1))
    shared.update(_consts())
    x = np.asarray(inp["x"])
    maps = []
    for b in range(8):
        m = dict(shared)
        m["xT"] = f(x[b].T)
        maps.append(m)
    return maps


_NC_CACHE = {}


def kernel(**inputs):
    stop_after = inputs.pop("_stop_after", "all")
    if stop_after not in _NC_CACHE:
        _NC_CACHE[stop_after] = build_program(stop_after)
    nc, names = _NC_CACHE[stop_after]
    maps = _prep_inputs(inputs)
    maps = [{k: v for k, v in m.items() if k in names} for m in maps]
    res = run_bass_kernel_spmd(nc, maps, core_ids=list(range(8)))
    out = np.stack([np.ascontiguousarray(r["outT"].T) for r in res.results], axis=0)
    return out.astype(np.float32)
```

```python
import contextlib
import numpy as np
import concourse.bass as bass
import concourse.mybir as mybir
from concourse.bass_utils import run_bass_kernel_spmd

F32 = mybir.dt.float32
BF16 = mybir.dt.bfloat16
AF = mybir.ActivationFunctionType
ALU = mybir.AluOpType
AX = mybir.AxisListType

S = 2048
D = 1024
NCH = 8
DFF = 2816
NF = 22
NB = 4
EPS = 1e-6
D_IN = 8464
NDS = 16
import os
ROPE_ADD_ENG = os.environ.get("ROPE_ADD_ENG", "pool")
FILLN = int(os.environ.get("FILLN", "1"))


class _Stop(Exception):
    pass


class Buf:
    __slots__ = ("w", "r", "excl")

    def __init__(self, excl=False):
        self.w = None
        self.r = []
        self.excl = excl


class Desc:
    __slots__ = ("eng", "cost", "run")

    def __init__(self, eng, cost, run):
        self.eng, self.cost, self.run = eng, cost, run


def run_streams(gens, lat=float(os.environ.get("SLAT", "1.5"))):
    st = []
    for g in gens:
        try:
            st.append([g, next(g), 0.0])
        except StopIteration:
            pass
    efree = {}
    while st:
        best = None
        for s_ in st:
            start = max(s_[2], efree.get(s_[1].eng, 0.0))
            if best is None or start < best[0]:
                best = (start, s_)
        start, s_ = best
        d = s_[1]
        d.run()
        fin = start + d.cost
        efree[d.eng] = fin
        s_[2] = fin + lat
        try:
            s_[1] = next(s_[0])
        except StopIteration:
            st.remove(s_)


class Trk:
    def __init__(self, nc, es):
        self.nc = nc
        self.eng = {"pe": nc.tensor, "act": nc.scalar, "dve": nc.vector, "pool": nc.gpsimd, "sp": nc.sync}
        self.sem = {k: es.enter_context(nc.semaphore("sem_" + k)) for k in self.eng}
        self.cnt = {k: 0 for k in self.eng}
        self.seen = {k: {} for k in self.eng}
        self.dsem = [es.enter_context(nc.semaphore("dsem%d" % i)) for i in range(NDS)]
        self.dcnt = [0] * NDS
        self.dnext = {"pool": 0, "sp": NDS // 2, "act": NDS // 2}
        self.same = {"pe": False, "act": True, "dve": True, "pool": True, "sp": True}

    def _semof(self, k):
        return self.sem[k] if isinstance(k, str) else self.dsem[k[1]]

    def _wait(self, e, deps):
        best = {}
        for (k, v) in deps:
            if v > best.get(k, 0):
                best[k] = v
        for k, v in best.items():
            if k == e and not self.same[e]:
                continue
            if self.seen[e].get(k, 0) >= v:
                continue
            self.eng[e].wait_ge(self._semof(k), v)
            self.seen[e][k] = v

    @staticmethod
    def _deps(reads, writes):
        deps = []
        for b in reads:
            if b.w is not None:
                deps.append(b.w)
        for b in writes:
            if b.w is not None:
                deps.append(b.w)
            deps.extend(b.r)
        return deps

    @staticmethod
    def _upd(d, reads, writes):
        for b in reads:
            b.r.append(d)
            if len(b.r) > 64:
                best = {}
                for (k, v) in b.r:
                    if v > best.get(k, 0):
                        best[k] = v
                b.r = list(best.items())
        for b in writes:
            b.w = d
            b.r = []

    filler = None

    def op(self, e, fn, reads=(), writes=()):
        if e == "pe" and self.filler is not None:
            self.fcnt = getattr(self, "fcnt", 0) + 1
            if self.fcnt % FILLN == 0:
                self.filler(self.eng["pe"])
        if any(b.excl for b in reads):
            writes = list(writes) + [b for b in reads if b.excl]
            reads = [b for b in reads if not b.excl]
        self._wait(e, self._deps(reads, writes))
        ins = fn(self.eng[e])
        self.cnt[e] += 1
        ins.then_inc(self.sem[e], 1)
        self._upd((e, self.cnt[e]), reads, writes)

    def dma(self, q, out, in_, reads=(), writes=()):
        deps = self._deps(reads, writes)
        i = self.dnext[q]
        lo = 0 if q == "pool" else NDS // 2
        self.dnext[q] = lo + (i - lo + 1) % (NDS // 2)
        if self.dcnt[i] > 0:
            deps.append((("d", i), self.dcnt[i]))
        self._wait(q, deps)
        self.eng[q].dma_start(out=out, in_=in_).then_inc(self.dsem[i], 16)
        self.dcnt[i] += 16
        self._upd((("d", i), self.dcnt[i]), reads, writes)

    def defer(self, e, fn, reads=(), writes=(), cost=None):
        return Desc(e, cost if cost is not None else {"pe": 0.3, "dve": 0.3, "act": 0.3, "pool": 0.6, "sp": 0.1}[e],
                    lambda: self.op(e, fn, reads, writes))

    def defer5(self, e, fn, reads=(), writes=(), cost=None):
        return Desc(e, cost if cost is not None else {"pe": 0.3, "dve": 0.6, "act": 0.6, "pool": 1.1, "sp": 0.1}[e],
                    lambda: self.op(e, fn, reads, writes))

    def defer_dma(self, q, out, in_, reads=(), writes=(), cost=0.2):
        return Desc(q, cost, lambda: self.dma(q, out, in_, reads, writes))

    def barrier(self):
        deps = [(k, v) for k, v in self.cnt.items() if v > 0]
        deps += [(("d", i), c) for i, c in enumerate(self.dcnt) if c > 0]
        for e in self.eng:
            self._wait(e, [d for d in deps if d[0] != e])


def build_program(stop_after="all", debug=False):
    nc = bass.Bass("TRN2", target_bir_lowering=False)
    dt_in = {}

    def din(name, shape):
        dt_in[name] = nc.dram_tensor(name, list(shape), F32, kind="ExternalInput").ap()
        return dt_in[name]

    xT_d = din("xT", [D, S])
    n1_d = din("ffn1_norm", [128, NCH])
    n2_d = din("ffn2_norm", [128, NCH])
    nm_d = din("mix_norm", [128, NCH])
    nf_d = din("final_norm", [128, NCH])
    ffn_w = []
    for i in (1, 2):
        ffn_w.append((din("wg%d" % i, [11, 128, NCH * 256]), din("wu%d" % i, [11, 128, NCH * 256]),
                      din("wd%d" % i, [8, 128, NF * 128])))
    din("wqkv_a", [6, 3, 128, NCH * 128])
    din("rope_cos", [128, S]); din("rope_sin", [128, S]); din("rm", [128, 128]); din("amask", [128, 512]); din("em", [128, 64])
    din("wgdn", [8, 3, 128, NCH * 128]); din("wgg", [8, 128, NCH * 128]); din("wbd", [128, NCH * 16])
    din("convw", [128, 96]); din("alog", [128, 8]); din("dtb", [128, 8]); din("onorm", [128, 1])
    din("triu", [128, 128]); din("smask", [128, 128]); din("odiv", [128, 128])
    din("wgab", [8, 128, 2 * NCH * 128]); din("wba", [8, 64, 4 * 128]); din("wbb", [8, 128, 8 * 128]); din("wout", [8, 128, 8 * 128])
    ident_d = din("ident_f", [128, 128])
    ones_d = din("ones_f", [128, 128])
    out_d = nc.dram_tensor("outT", [D, S], F32, kind="ExternalOutput").ap()
    dbg = {}

    with contextlib.ExitStack() as es:
        T = Trk(nc, es)

        def sb(name, shape, dt):
            return es.enter_context(nc.sbuf_tensor(name, list(shape), dt))

        xT = sb("xT_sb", [128, NCH, S], F32)
        xB = [Buf() for _ in range(NB)]
        psum = es.enter_context(nc.psum_tensor("psum", [128, 8, 512], F32))
        pB = [Buf(True) for _ in range(8)]
        pstate = {"n": 0}

        def bank():
            b = pstate["n"]
            pstate["n"] = (b + 1) % 8
            return b

        def bank2():
            b = pstate["n"]
            if b % 2:
                b = (b + 1) % 8
            pstate["n"] = (b + 2) % 8
            return b

        ident_f = sb("ident_f_sb", [128, 128], F32)
        ident_b = sb("ident_b_sb", [128, 128], BF16)
        ones_b = sb("ones_b_sb", [128, 128], BF16)
        ones_f = sb("ones_f_sb", [128, 128], F32)
        ident2_f = sb("ident2_f_sb", [128, 128], F32)
        cB = Buf()
        norms = sb("norms_sb", [128, 4, NCH], F32)
        eps_t = sb("eps_sb", [128, 2], F32)
        T.op("dve", lambda e: e.memset(eps_t[:, 0:1], EPS), writes=[cB])
        T.op("dve", lambda e: e.memset(eps_t[:, 1:2], 1.0), writes=[cB])
        T.dma("sp", ident_f[:], ident_d, writes=[cB])
        T.dma("sp", ones_f[:], ones_d, writes=[cB])
        T.op("dve", lambda e: e.tensor_single_scalar(out=ident2_f[:, :], in_=ident_f[:, :], scalar=2.0, op=ALU.mult), reads=[cB], writes=[cB])
        T.dma("pool", ident_b[:], ident_d, writes=[cB])
        T.dma("pool", ones_b[:], ones_d, writes=[cB])
        for i, nd in enumerate((n1_d, nm_d, n2_d, nf_d)):
            T.dma("sp", norms[:, i, :], nd, writes=[cB])
        xT_dv = xT_d.rearrange("(c p) t -> p c t", p=128)
        for n in range(NB):
            T.dma("sp", xT[:, :, n * 512:(n + 1) * 512], xT_dv[:, :, n * 512:(n + 1) * 512], writes=[xB[n]])

        def rmsnorm_block(n, gi, out_ap_fn, out_buf, sq_t, sq_b, rs_t, rs_b):
            tok = slice(n * 512, (n + 1) * 512)
            b = bank()
            for c in range(NCH):
                T.op("act", lambda e, c=c: e.activation(out=sq_t[:, c, :], in_=xT[:, c, tok], func=AF.Square),
                     reads=[xB[n]], writes=[sq_b[c]])
            def mm(e):
                ins = None
                for c in range(NCH):
                    ins = e.matmul(psum[:, b, :], lhsT=ones_b[:, :], rhs=sq_t[:, c, :], start=(c == 0), stop=(c == NCH - 1))
                return ins
            T.op("pe", mm, reads=sq_b + [cB], writes=[pB[b]])
            T.op("act", lambda e: e.activation(out=rs_t[:, :], in_=psum[:, b, :], func=AF.Ln, scale=1.0 / D, bias=eps_t[:, 0:1]),
                 reads=[pB[b], cB], writes=[rs_b])
            T.op("act", lambda e: e.activation(out=rs_t[:, :], in_=rs_t[:, :], func=AF.Exp, scale=-0.5), reads=[rs_b], writes=[rs_b])
            for c in range(NCH):
                T.op("dve", lambda e, c=c: e.scalar_tensor_tensor(out=out_ap_fn(c), in0=xT[:, c, tok],
                                                                  scalar=norms[:, gi, c:c + 1], in1=rs_t[:, :],
                                                                  op0=ALU.mult, op1=ALU.mult),
                     reads=[xB[n], rs_b, cB], writes=[out_buf])

        def ffn(which, gi):
            wg_d, wu_d, wd_d = ffn_w[which]
            with contextlib.ExitStack() as fs:
                def fsb(name, shape, dt):
                    return fs.enter_context(nc.sbuf_tensor("f%d_%s" % (which, name), list(shape), dt))
                hT = fsb("hT", [128, NCH, 1024], BF16)
                hB = [Buf(), Buf()]
                actT = fsb("actT", [128, NF, 1024], BF16)
                aB = [[Buf(), Buf()] for _ in range(NF)]
                sq_t = fsb("sq", [128, NCH, 512], BF16)
                sq_b = [Buf() for _ in range(NCH)]
                rs_t = fsb("rs", [128, 512], F32)
                rs_b = Buf()
                NWB = 3
                wgb = [fsb("wg%d" % i, [128, NCH, 256], BF16) for i in range(NWB)]
                wub = [fsb("wu%d" % i, [128, NCH, 256], BF16) for i in range(NWB)]
                wgB = [Buf() for _ in range(NWB)]
                wuB = [Buf() for _ in range(NWB)]
                wdb = [fsb("wd%d" % i, [128, NF, 128], BF16) for i in range(2)]
                wdB = [Buf() for _ in range(2)]
                sg = [fsb("sg%d" % i, [128, 512], F32) for i in range(3)]
                sgB = [Buf() for _ in range(3)]
                sgn = 0

                def load_gu(g):
                    i = g % NWB
                    T.dma("pool", wgb[i][:].rearrange("p c f -> p (c f)"), wg_d[g], writes=[wgB[i]])
                    T.dma("pool", wub[i][:].rearrange("p c f -> p (c f)"), wu_d[g], writes=[wuB[i]])

                def load_d(dt):
                    i = dt % 2
                    T.dma("pool", wdb[i][:].rearrange("p f j -> p (f j)"), wd_d[dt], writes=[wdB[i]])

                for half in range(2):
                    load_gu(0)
                    load_gu(1)
                    for nn in range(2):
                        n = half * 2 + nn
                        rmsnorm_block(n, gi, lambda c, nn=nn: hT[:, c, nn * 512:(nn + 1) * 512], hB[nn],
                                      sq_t, sq_b, rs_t, rs_b)
                    for g in range(11):
                        if g + 2 < 11:
                            load_gu(g + 2)
                        if g == 9:
                            load_d(0)
                        if g == 10:
                            load_d(1)
                        wi = g % NWB
                        for j in range(2):
                            f = 2 * g + j
                            for nn in range(2):
                                tk = slice(nn * 512, (nn + 1) * 512)
                                bg = bank()
                                bu = bank()

                                def mmg(e, w=wgb[wi], b=bg, j=j, tk=tk):
                                    ins = None
                                    for c in range(NCH):
                                        ins = e.matmul(psum[:, b, :], lhsT=w[:, c, j * 128:(j + 1) * 128],
                                                       rhs=hT[:, c, tk], start=(c == 0), stop=(c == NCH - 1))
                                    return ins
                                T.op("pe", mmg, reads=[wgB[wi], hB[nn]], writes=[pB[bg]])
                                T.op("pe", lambda e, b=bu, j=j, tk=tk: mmg(e, wub[wi], b, j, tk),
                                     reads=[wuB[wi], hB[nn]], writes=[pB[bu]])
                                si = sgn % 3
                                sgn += 1
                                T.op("act", lambda e, si=si, b=bg: e.activation(out=sg[si][:, :], in_=psum[:, b, :],
                                                                               func=AF.Silu),
                                     reads=[pB[bg]], writes=[sgB[si]])
                                T.op("dve", lambda e, si=si, b=bu, f=f, tk=tk: e.tensor_tensor(
                                    out=actT[:, f, tk], in0=psum[:, b, :], in1=sg[si][:, :], op=ALU.mult),
                                     reads=[pB[bu], sgB[si]], writes=[aB[f][nn]])
                    for dt in range(8):
                        if dt + 2 < 8:
                            pass
                        wi = dt % 2
                        for nn in range(2):
                            n = half * 2 + nn
                            tk = slice(nn * 512, (nn + 1) * 512)
                            b = bank()

                            def mmd(e, b=b, wi=wi, tk=tk):
                                ins = None
                                for f in range(NF):
                                    ins = e.matmul(psum[:, b, :], lhsT=wdb[wi][:, f, :], rhs=actT[:, f, tk],
                                                   start=(f == 0), stop=(f == NF - 1))
                                return ins
                            T.op("pe", mmd, reads=[wdB[wi]] + [aB[f][nn] for f in range(NF)], writes=[pB[b]])
                            T.op("dve", lambda e, b=b, dt=dt, n=n: e.scalar_tensor_tensor(
                                out=xT[:, dt, n * 512:(n + 1) * 512], in0=psum[:, b, :], scalar=0.5,
                                in1=xT[:, dt, n * 512:(n + 1) * 512], op0=ALU.mult, op1=ALU.add),
                                 reads=[pB[b]], writes=[xB[n]])
                        if dt + 2 < 8:
                            load_d(dt + 2)
                T.barrier()

        def mixer():
            with contextlib.ExitStack() as ms:
                def msb(name, shape, dt):
                    return ms.enter_context(nc.sbuf_tensor("m_" + name, list(shape), dt))
                hT = msb("hT", [128, NCH, S], BF16)
                hB = [Buf() for _ in range(NB)]
                with contextlib.ExitStack() as ns:
                    sq_t = ns.enter_context(nc.sbuf_tensor("m_sq", [128, NCH, 512], BF16))
                    sq_b = [Buf() for _ in range(NCH)]
                    rs_t = ns.enter_context(nc.sbuf_tensor("m_rs", [128, 512], F32))
                    rs_b = Buf()
                    for n in range(NB):
                        rmsnorm_block(n, 1, lambda c, n=n: hT[:, c, n * 512:(n + 1) * 512], hB[n], sq_t, sq_b, rs_t, rs_b)
                    T.barrier()
                if stop_after == "dbg_h":
                    dbg_dump([(hT[:, c, :], hB, c * 128, 128) for c in range(NCH)])
                    return True
                ybT = msb("ybT", [128, 8, S], BF16)
                ybB = [Buf() for _ in range(8)]
                gdn(hT, hB, ybT, ybB)
                T.barrier()
                if stop_after == "dbg_yb":
                    dbg_dump([(ybT[:, c, :], ybB, c * 128, 128) for c in range(8)])
                    return True
                yaT = msb("yaT", [64, 4, S], BF16)
                yaB = [Buf() for _ in range(4)]
                if attention(hT, hB, yaT, yaB):
                    return True
                T.barrier()
                if stop_after == "dbg_ya":
                    dbg_dump([(yaT[:, c, :], yaB, c * 64, 64) for c in range(4)])
                    return True
                merge(hT, hB, yaT, yaB, ybT, ybB)
                T.barrier()
            return None

        def nset(lo, hi):
            return list(range(lo // 512, hi // 512 + 1))

        def attention(hT, hB, yaT, yaB):
            wqkv_d = dt_in["wqkv_a"]
            with contextlib.ExitStack() as as_:
                def asb(name, shape, dt):
                    return as_.enter_context(nc.sbuf_tensor("a_" + name, list(shape), dt))
                cos_t = asb("cos", [128, S], BF16)
                sin_t = asb("sin", [128, S], BF16)
                rm_t = asb("rm", [128, 128], BF16)
                am_t = asb("am", [128, 512], BF16)
                em_t = asb("em", [128, 64], F32)
                acB = Buf()
                T.dma("pool", cos_t[:], dt_in["rope_cos"], writes=[acB])
                T.dma("pool", sin_t[:], dt_in["rope_sin"], writes=[acB])
                T.dma("pool", rm_t[:], dt_in["rm"], writes=[acB])
                T.dma("pool", am_t[:], dt_in["amask"], writes=[acB])
                T.dma("sp", em_t[:], dt_in["em"], writes=[acB])
                wq = [asb("w%d" % i, [128, NCH, 128], BF16) for i in range(3)]
                wB = [Buf() for _ in range(3)]
                qk = [asb("qk%d" % i, [128, S], BF16) for i in range(2)]
                qkB = [[Buf() for _ in range(NB)] for _ in range(2)]
                vT = asb("vT", [128, S], BF16)
                vTB = [Buf() for _ in range(NB)]
                Vt = asb("Vt", [128, 16, 2, 65], BF16)
                VtB = [Buf() for _ in range(16)]
                VoneB = Buf()
                T.op(ROPE_ADD_ENG, lambda e: e.memset(Vt[:, :, :, 64:65], 1.0), writes=[VoneB])
                acc = asb("acc", [128, 2, S], F32)
                accB = Buf()
                qb = asb("qb", [128, 512], BF16)
                qbB = Buf()
                t1 = asb("t1", [128, 512], F32)
                t2 = asb("t2", [128, 512], F32)
                t1B, t2B = Buf(), Buf()
                ex = [asb("ex%d" % i, [128, 512], BF16) for i in range(3)]
                exB = [Buf(), Buf(), Buf()]
                exm = [asb("exm%d" % i, [128, 512], BF16) for i in range(3)]
                exmB = [Buf(), Buf(), Buf()]
                rl = asb("rl", [64, 512], F32)
                rlB = Buf()
                exn = 0
                if stop_after == "dbg_att0":
                    dbg_dump([(cos_t[:, :], [acB], 0, 128), (sin_t[:, :], [acB, VoneB], 128, 128)])
                    return True
                for hp in range(2):
                    for g in range(3):
                        dil = (1, 4, 16)[g]
                        nbk = 16 // dil
                        pi = g * 2 + hp
                        for i in range(3):
                            T.dma("pool", wq[i][:].rearrange("p c f -> p (c f)"), wqkv_d[pi, i], writes=[wB[i]])
                        for i in range(2):
                            for n in range(NB):
                                tk = slice(n * 512, (n + 1) * 512)
                                b = bank()

                                def mm(e):
                                    ins = None
                                    for c in range(NCH):
                                        ins = e.matmul(psum[:, b, :], lhsT=wq[i][:, c, :], rhs=hT[:, c, tk],
                                                       start=(c == 0), stop=(c == NCH - 1))
                                    return ins
                                CUT = int(os.environ.get("CUT", "9"))
                                if CUT >= 1:
                                    T.op("pe", mm, reads=[wB[i], hB[n]], writes=[pB[b]])
                                if CUT >= 2:
                                    T.op("act", lambda e: e.activation(out=qb[:, :], in_=psum[:, b, :], func=AF.Copy),
                                         reads=[pB[b]], writes=[qbB])
                                b2 = bank()
                                if CUT >= 3:
                                    T.op("pe", lambda e: e.matmul(psum[:, b2, :], lhsT=rm_t[:, :], rhs=qb[:, :], start=True, stop=True),
                                         reads=[qbB, acB], writes=[pB[b2]])
                                if CUT >= 4:
                                    T.op("dve", lambda e: e.tensor_tensor(out=t1[:, :], in0=psum[:, b, :], in1=cos_t[:, tk], op=ALU.mult),
                                         reads=[pB[b], acB], writes=[t1B])
                                if CUT >= 5:
                                    T.op("dve", lambda e: e.tensor_tensor(out=t2[:, :], in0=psum[:, b2, :], in1=sin_t[:, tk], op=ALU.mult),
                                         reads=[pB[b2], acB], writes=[t2B])
                                if CUT >= 6:
                                    T.op(ROPE_ADD_ENG, lambda e: e.tensor_tensor(out=qk[i][:, tk], in0=t1[:, :], in1=t2[:, :], op=ALU.add),
                                         reads=[t1B, t2B], writes=[qkB[i][n]])
                                if stop_after == "dbg_att1a":
                                    T.barrier()
                                    with contextlib.ExitStack() as ds:
                                        dtl = ds.enter_context(nc.sbuf_tensor("dbgx", [128, 512], F32))
                                        dxB = Buf()
                                        T.op("act", lambda e: e.activation(out=dtl[:, :], in_=qk[0][:, 0:512], func=AF.Copy), reads=[qkB[0][0]], writes=[dxB])
                                        T.dma("sp", out_d[0:128, 0:512], t1[:, :], reads=[t1B])
                                        T.dma("sp", out_d[128:256, 0:512], t2[:, :], reads=[t2B])
                                        T.dma("sp", out_d[256:384, 0:512], dtl[:, :], reads=[dxB])
                                        T.barrier()
                                    return True
                        if stop_after == "dbg_att1":
                            dbg_dump([(qk[0][:, :], qkB[0], 0, 128), (qk[1][:, :], qkB[1], 128, 128)])
                            return True
                        def tokslice(bi):
                            r, nk = bi // nbk, bi % nbk
                            st = r + dil * 128 * nk
                            return st, slice(st, st + 127 * dil + 1, dil)
                        for n in range(NB):
                            tk = slice(n * 512, (n + 1) * 512)
                            b = bank()

                            def mmvf(e):
                                ins = None
                                for c in range(NCH):
                                    ins = e.matmul(psum[:, b, :], lhsT=wq[2][:, c, :], rhs=hT[:, c, tk], start=(c == 0), stop=(c == NCH - 1))
                                return ins
                            T.op("pe", mmvf, reads=[wB[2], hB[n]], writes=[pB[b]])
                            T.op("act", lambda e: e.activation(out=vT[:, tk], in_=psum[:, b, :], func=AF.Copy), reads=[pB[b]], writes=[vTB[n]])
                        for b4 in range(4):
                            b = bank()

                            def mmv(e):
                                ins = None
                                for k in range(4):
                                    st, sl = tokslice(b4 * 4 + k)
                                    ins = e.matmul(psum[:, b, k * 128:(k + 1) * 128], lhsT=vT[:, sl], rhs=ident_b[:, :], start=True, stop=True)
                                return ins
                            T.op("pe", mmv, reads=vTB + [cB], writes=[pB[b]])
                            T.op("act", lambda e: e.activation(
                                out=Vt[:, b4 * 4:(b4 + 1) * 4, :, 0:64],
                                in_=psum[:, b, :].rearrange("p (k e d) -> p k e d", k=4, e=2), func=AF.Copy),
                                 reads=[pB[b]], writes=[VtB[b4 * 4 + k] for k in range(4)])
                        if stop_after == "dbg_att2":
                            dbg_dump([(Vt[:, :, :, :].rearrange("p a b c -> p (a b c)")[:, 0:2048], VtB + [VoneB], 0, 128)])
                            return True
                        def stage1(bi):
                            nonlocal exn
                            r, nk = bi // nbk, bi % nbk
                            st, qsl = tokslice(bi)
                            hi_tok = st + 127 * dil
                            qn_set = nset(st, hi_tok)
                            b = bank2()
                            if nk > 0:
                                stp, ksl_p = tokslice(bi - 1)
                                kn_set = nset(stp, hi_tok)
                            else:
                                kn_set = qn_set

                            def mms(e):
                                ins = None
                                for hd in range(2):
                                    hs = slice(hd * 64, (hd + 1) * 64)
                                    if nk > 0:
                                        ins = e.matmul(psum[:, b + hd, 0:128], lhsT=qk[1][hs, ksl_p], rhs=qk[0][hs, qsl],
                                                       start=True, stop=True)
                                    ins = e.matmul(psum[:, b + hd, 128:256], lhsT=qk[1][hs, qsl], rhs=qk[0][hs, qsl],
                                                   start=True, stop=True)
                                return ins
                            T.op("pe", mms, reads=[qkB[0][n] for n in qn_set] + [qkB[1][n] for n in kn_set], writes=[pB[b], pB[b + 1]])
                            xi = exn % 3
                            exn += 1
                            lo = 0 if nk > 0 else 128
                            src = psum[:, b:b + 2, lo:256]
                            dst = ex[xi][:, :].rearrange("p (e w) -> p e w", e=2)[:, :, lo:256]
                            dstm = exm[xi][:, :].rearrange("p (e w) -> p e w", e=2)[:, :, lo:256]
                            msk = am_t[:, :].rearrange("p (e w) -> p e w", e=2)[:, :, lo:256]
                            T.op("act", lambda e: e.activation(out=dst, in_=src, func=AF.Exp, scale=0.125),
                                 reads=[pB[b], pB[b + 1]], writes=[exB[xi]])
                            T.op("pool", lambda e: e.tensor_tensor(out=dstm, in0=dst, in1=msk, op=ALU.mult),
                                 reads=[exB[xi], acB], writes=[exmB[xi]])
                            return xi

                        def stage2(bi, xi):
                            r, nk = bi // nbk, bi % nbk
                            st, qsl = tokslice(bi)
                            b2 = bank()

                            def mmu(e):
                                ins = None
                                for hd in range(2):
                                    o = psum[0:65, b2, hd * 128:(hd + 1) * 128]
                                    if nk > 0:
                                        e.matmul(o, lhsT=Vt[:, bi - 1, hd, :], rhs=exm[xi][:, hd * 256:hd * 256 + 128], start=True, stop=False)
                                    ins = e.matmul(o, lhsT=Vt[:, bi, hd, :], rhs=exm[xi][:, hd * 256 + 128:hd * 256 + 256],
                                                   start=(nk == 0), stop=True)
                                return ins
                            T.op("pe", mmu, reads=[exmB[xi], VtB[bi], VoneB] + ([VtB[bi - 1]] if nk > 0 else []), writes=[pB[b2]])
                            src2 = psum[0:65, b2, 0:256].rearrange("p (e q) -> p e q", e=2)
                            if g == 0:
                                T.op("dve", lambda e: e.tensor_copy(out=acc[0:65, :, qsl], in_=src2), reads=[pB[b2]], writes=[accB])
                            else:
                                T.op("dve", lambda e: e.tensor_tensor(out=acc[0:65, :, qsl], in0=src2, in1=acc[0:65, :, qsl], op=ALU.add),
                                     reads=[pB[b2]], writes=[accB])
                        pend = []
                        for bi in range(16):
                            pend.append((bi, stage1(bi)))
                            if len(pend) > 1:
                                stage2(*pend.pop(0))
                        while pend:
                            stage2(*pend.pop(0))
                        if stop_after == "dbg_att3":
                            dbg_dump([(acc[:, 0, :], [accB], 0, 128), (acc[:, 1, :], [accB], 128, 128)])
                            return True
                    for hd in range(2):
                        for n in range(NB):
                            tk = slice(n * 512, (n + 1) * 512)
                            b = bank()
                            T.op("pe", lambda e: e.matmul(psum[0:64, b, :], lhsT=em_t[0:65, :], rhs=acc[0:65, hd, tk], start=True, stop=True),
                                 reads=[accB, acB], writes=[pB[b]])
                            T.op("dve", lambda e: e.reciprocal(out=rl[:, :], in_=psum[0:64, b, :]), reads=[pB[b]], writes=[rlB])
                            T.op("dve", lambda e: e.tensor_tensor(out=yaT[:, 2 * hp + hd, tk], in0=acc[0:64, hd, tk], in1=rl[:, :], op=ALU.mult),
                                 reads=[rlB, accB], writes=[yaB[2 * hp + hd]])
                T.barrier()

        def gdn(hT, hB, ybT, ybB):
            with contextlib.ExitStack() as gs:
                def gsb(name, shape, dt):
                    return gs.enter_context(nc.sbuf_tensor("g_" + name, list(shape), dt))
                gcB = Buf()
                triu = gsb("triu", [128, 128], F32)
                smask = gsb("smask", [128, 128], F32)
                odiv = gsb("odiv", [128, 128], BF16)
                convw = gsb("convw", [128, 8, 3, 4], F32)
                alog = gsb("alog", [128, 8], F32)
                dtb = gsb("dtb", [128, 8], F32)
                onorm = gsb("onorm", [128, 1], F32)
                wbd = gsb("wbd", [128, NCH, 16], BF16)
                T.dma("sp", triu[:], dt_in["triu"], writes=[gcB])
                T.dma("sp", smask[:], dt_in["smask"], writes=[gcB])
                T.dma("pool", odiv[:], dt_in["odiv"], writes=[gcB])
                T.dma("sp", convw[:].rearrange("p h w t -> p (h w t)"), dt_in["convw"], writes=[gcB])
                T.dma("sp", alog[:], dt_in["alog"], writes=[gcB])
                T.dma("sp", dtb[:], dt_in["dtb"], writes=[gcB])
                T.dma("sp", onorm[:], dt_in["onorm"], writes=[gcB])
                T.dma("pool", wbd[:].rearrange("p c f -> p (c f)"), dt_in["wbd"], writes=[gcB])
                braw = gsb("braw", [128, 16, 16], F32)
                beta = gsb("beta", [128, 16, 8], F32)
                negb = gsb("negb", [128, 16, 8], F32)
                gt = gsb("gt", [128, 16, 8], F32)
                gcum = gsb("gcum", [128, 16, 8], F32)
                glast = gsb("glast", [128, 16, 8], F32)
                kds = gsb("kds", [128, 16, 8], F32)
                sdec = gsb("sdec", [128, 16, 8], F32)
                egc = gsb("egc", [128, 16, 8], F32)
                tmpg = gsb("tmpg", [128, 16, 8], F32)
                negA = gsb("negA", [128, 8], F32)
                gB = Buf()
                b = bank()

                def mmb(e):
                    ins = None
                    for t in range(16):
                        for c in range(NCH):
                            ins = e.matmul(psum[:, b, t * 16:(t + 1) * 16], lhsT=hT[:, c, t * 128:(t + 1) * 128], rhs=wbd[:, c, :],
                                           start=(c == 0), stop=(c == NCH - 1))
                    return ins
                T.op("pe", mmb, reads=hB + [gcB], writes=[pB[b]])
                T.op("act", lambda e: e.activation(out=braw[:].rearrange("p t k -> p (t k)"), in_=psum[:, b, 0:256], func=AF.Copy),
                     reads=[pB[b]], writes=[gB])
                T.op("act", lambda e: e.activation(out=tmpg[:], in_=braw[:, :, 0:8], func=AF.Exp, scale=-1.0), reads=[gB], writes=[gB])
                T.op("dve", lambda e: e.tensor_single_scalar(out=tmpg[:], in_=tmpg[:], scalar=1.0, op=ALU.add), reads=[gB], writes=[gB])
                T.op("dve", lambda e: e.reciprocal(out=beta[:], in_=tmpg[:]), reads=[gB], writes=[gB])
                T.op("dve", lambda e: e.tensor_single_scalar(out=negb[:], in_=beta[:], scalar=-1.0, op=ALU.mult), reads=[gB], writes=[gB])
                T.op("dve", lambda e: e.tensor_tensor(out=tmpg[:], in0=braw[:, :, 8:16], in1=dtb[:].unsqueeze(1).to_broadcast([128, 16, 8]), op=ALU.add),
                     reads=[gB, gcB], writes=[gB])
                T.op("act", lambda e: e.activation(out=tmpg[:], in_=tmpg[:], func=AF.Exp), reads=[gB], writes=[gB])
                T.op("act", lambda e: e.activation(out=tmpg[:], in_=tmpg[:], func=AF.Ln, bias=eps_t[:, 1:2]), reads=[gB, cB], writes=[gB])
                T.op("act", lambda e: e.activation(out=negA[:], in_=alog[:], func=AF.Exp), reads=[gcB], writes=[gB])
                T.op("dve", lambda e: e.tensor_single_scalar(out=negA[:], in_=negA[:], scalar=-1.0, op=ALU.mult), reads=[gB], writes=[gB])
                T.op("dve", lambda e: e.tensor_tensor(out=gt[:], in0=tmpg[:], in1=negA[:].unsqueeze(1).to_broadcast([128, 16, 8]), op=ALU.mult),
                     reads=[gB], writes=[gB])
                b = bank()
                T.op("pe", lambda e: e.matmul(psum[:, b, 0:128], lhsT=triu[:, :], rhs=gt[:].rearrange("p t k -> p (t k)"), start=True, stop=True),
                     reads=[gB, gcB], writes=[pB[b]])
                T.op("act", lambda e: e.activation(out=gcum[:].rearrange("p t k -> p (t k)"), in_=psum[:, b, 0:128], func=AF.Copy),
                     reads=[pB[b]], writes=[gB])
                b = bank()
                T.op("pe", lambda e: e.matmul(psum[:, b, 0:128], lhsT=ones_f[:, :], rhs=gt[:].rearrange("p t k -> p (t k)"), start=True, stop=True),
                     reads=[gB, cB], writes=[pB[b]])
                T.op("act", lambda e: e.activation(out=glast[:].rearrange("p t k -> p (t k)"), in_=psum[:, b, 0:128], func=AF.Copy),
                     reads=[pB[b]], writes=[gB])
                T.op("dve", lambda e: e.tensor_tensor(out=tmpg[:], in0=glast[:], in1=gcum[:], op=ALU.subtract), reads=[gB], writes=[gB])
                T.op("act", lambda e: e.activation(out=kds[:], in_=tmpg[:], func=AF.Exp), reads=[gB], writes=[gB])
                T.op("act", lambda e: e.activation(out=sdec[:], in_=glast[:], func=AF.Exp), reads=[gB], writes=[gB])
                T.op("act", lambda e: e.activation(out=egc[:], in_=gcum[:], func=AF.Exp), reads=[gB], writes=[gB])

                wh = [gsb("wh%d" % i, [128, NCH, 128], BF16) for i in range(4)]
                whB = [Buf() for _ in range(4)]
                cv = gsb("cv", [128, 512], F32)
                cvB = Buf()
                CONV_PE = int(os.environ.get("CONV_PE", "1"))
                Dg = gsb("Dg", [128, 3, 4, 128], BF16)
                DgB = Buf()
                xb = [gsb("xb%d" % i, [128, 4 + 512], BF16) for i in range(3)]
                xbB = [Buf() for _ in range(3)]
                kqt = [gsb("kq%d" % st_, [128, 2, 512], BF16) for st_ in range(2)]
                vts = [gsb("vt%d" % st_, [128, 512], BF16) for st_ in range(2)]
                qkvs = [[kqt[st_][:, 1, :], kqt[st_][:, 0, :], vts[st_][:, :]] for st_ in range(2)]
                qkvsB = [[Buf() for _ in range(3)] for _ in range(2)]
                gsils = [gsb("gsil%d" % st_, [128, 512], BF16) for st_ in range(3)]
                gsilsB = [Buf(), Buf(), Buf()]
                HsS = [gsb("HsS%d" % i, [128, 4, 128], F32) for i in range(2)]
                ATsS = [gsb("ATsS%d" % i, [128, 4, 128], F32) for i in range(2)]
                O0sS = [gsb("O0sS%d" % i, [128, 4, 128], F32) for i in range(2)]
                QeffS = [gsb("QeffS%d" % i, [128, 4, 128], BF16) for i in range(2)]
                HsSB = [[Buf() for _ in range(4)] for _ in range(2)]
                ATsSB = [[Buf() for _ in range(4)] for _ in range(2)]
                O0sSB = [[Buf() for _ in range(4)] for _ in range(2)]
                QeffSB = [[Buf() for _ in range(4)] for _ in range(2)]
                osum = gsb("osum", [128, 4, 128], F32)
                osumB = Buf()
                rr2 = gsb("rr2", [128, 4, 128], F32)
                rr2B = Buf()
                g1bk = {"n": 0}

                FILL = int(os.environ.get("FILL", "0"))

                def g1bank():
                    if FILL:
                        return 4
                    g1bk["n"] = 1 - g1bk["n"]
                    return 4 + g1bk["n"]
                if FILL:
                    T.filler = lambda e: e.matmul(psum[:, 5, 0:FILL], lhsT=ident_b[:, :], rhs=ones_b[:, :].unsqueeze(1).to_broadcast([128, FILL // 128, 128]) if FILL > 128 else ident_b[:, :], start=True, stop=True)
                sqb = gsb("sqb", [128, 512], BF16)
                sqbB = Buf()
                rt = gsb("rt", [128, 512], F32)
                rtB = Buf()
                Fm = [gsb("F%d" % i, [128, 4, 128], F32) for i in range(3)]
                FB = [[Buf() for _ in range(4)] for _ in range(3)]
                QT = gsb("QT", [128, 4, 128], BF16)
                QTB = [Buf() for _ in range(4)]
                PQ = [gsb("PQ%d" % i, [128, 2, 4, 128], BF16) for i in range(2)]
                PQB = [[Buf() for _ in range(4)] for _ in range(2)]
                Wt = gsb("Wt", [128, 4, 128], BF16)
                WtB = [Buf() for _ in range(4)]
                ATt = gsb("ATt", [128, 4, 128], BF16)
                ATtB = [Buf() for _ in range(4)]
                Aqk = gsb("Aqk", [128, 4, 128], BF16)
                AqkB = [Buf() for _ in range(4)]
                kdec = gsb("kdec", [128, 4, 128], BF16)
                kdecB = [Buf() for _ in range(4)]
                VK = gsb("VK", [128, 4, 256], BF16)
                VKB = [Buf() for _ in range(4)]
                Zt = gsb("Zt", [128, 4, 256], BF16)
                ZtB = [Buf() for _ in range(4)]
                osq = gsb("osq", [128, 512], BF16)
                osqB = Buf()
                Sf = gsb("Sf", [128, 128], F32)
                Sb = gsb("Sb", [128, 128], BF16)
                SfB, SbB = Buf(), Buf()
                QSCALE = 128.0 ** -0.5
                hbk = {"n": 0}

                def hbank():
                    b_ = 4 + hbk["n"]
                    hbk["n"] = (hbk["n"] + 1) % 3
                    return b_

                def pv(bk):
                    return psum[:, bk, :].rearrange("p (c i) -> p c i", c=4)

                def fl(tl):
                    return tl[:].rearrange("p c i -> p (c i)")

                def g1b(h, n, st, sg):
                    tk = slice(n * 512, (n + 1) * 512)
                    if n == 0:
                        for i in range(3):
                            yield T.defer_dma("pool", wh[i][:].rearrange("p c f -> p (c f)"), dt_in["wgdn"][h, i], writes=[whB[i]])
                        yield T.defer_dma("pool", wh[3][:].rearrange("p c f -> p (c f)"), dt_in["wgg"][h], writes=[whB[3]])
                        if CONV_PE:
                            for w3 in range(3):
                                for tp in range(4):
                                    yield T.defer("dve", lambda e: e.tensor_scalar_mul(out=Dg[:, w3, tp, :], in0=ident_f[:, :], scalar1=convw[:, h, w3, tp:tp + 1]),
                                                  reads=[gcB, cB], writes=[DgB])
                    for w3 in range(3):
                        if n == 0:
                            yield T.defer5("pool", lambda e: e.memset(xb[w3][:, 0:4], 0.0), writes=[xbB[w3]])
                        else:
                            yield T.defer5("pool", lambda e: e.tensor_copy(out=xb[w3][:, 0:4], in_=xb[w3][:, 512:516]), reads=[xbB[w3]], writes=[xbB[w3]])
                        b = g1bank()

                        def mm(e):
                            ins = None
                            for c in range(NCH):
                                ins = e.matmul(psum[:, b, :], lhsT=wh[w3][:, c, :], rhs=hT[:, c, tk], start=(c == 0), stop=(c == NCH - 1))
                            return ins
                        yield T.defer5("pe", mm, reads=[whB[w3], hB[n]], writes=[pB[b]], cost=2.1)
                        yield T.defer5("act", lambda e: e.activation(out=xb[w3][:, 4:516], in_=psum[:, b, :], func=AF.Copy), reads=[pB[b]], writes=[xbB[w3]])
                        if CONV_PE:
                            b = g1bank()

                            def mmc(e):
                                ins = None
                                for tp in range(4):
                                    ins = e.matmul(psum[:, b, :], lhsT=Dg[:, w3, tp, :], rhs=xb[w3][:, 1 + tp:1 + tp + 512], start=(tp == 0), stop=(tp == 3))
                                return ins
                            yield T.defer5("pe", mmc, reads=[DgB, xbB[w3]], writes=[pB[b]], cost=1.0)
                            yield T.defer5("act", lambda e: e.activation(out=qkvs[st][w3][:, :], in_=psum[:, b, :], func=AF.Silu), reads=[pB[b]], writes=[qkvsB[st][w3]])
                        else:
                            yield T.defer5("dve", lambda e: e.tensor_scalar_mul(out=cv[:, :], in0=xb[w3][:, 1:513], scalar1=convw[:, h, w3, 0:1]),
                                 reads=[xbB[w3], gcB], writes=[cvB])
                            for tp in range(1, 4):
                                yield T.defer5("dve", lambda e: e.scalar_tensor_tensor(out=cv[:, :], in0=xb[w3][:, 1 + tp:513 + tp], scalar=convw[:, h, w3, tp:tp + 1],
                                                                             in1=cv[:, :], op0=ALU.mult, op1=ALU.add), reads=[xbB[w3], gcB, cvB], writes=[cvB])
                            yield T.defer5("act", lambda e: e.activation(out=qkvs[st][w3][:, :], in_=cv[:, :], func=AF.Silu), reads=[cvB], writes=[qkvsB[st][w3]])
                        if w3 < 2:
                            yield T.defer5("act", lambda e: e.activation(out=sqb[:, :], in_=qkvs[st][w3][:, :], func=AF.Square), reads=[qkvsB[st][w3]], writes=[sqbB])
                            b2 = g1bank()
                            yield T.defer5("pe", lambda e: e.matmul(psum[:, b2, :], lhsT=ones_b[:, :], rhs=sqb[:, :], start=True, stop=True),
                                 reads=[sqbB, cB], writes=[pB[b2]])
                            yield T.defer5("act", lambda e: e.activation(out=rt[:, :], in_=psum[:, b2, :], func=AF.Ln, bias=eps_t[:, 0:1]), reads=[pB[b2], cB], writes=[rtB])
                            yield T.defer5("act", lambda e: e.activation(out=rt[:, :], in_=rt[:, :], func=AF.Exp, scale=-0.5), reads=[rtB], writes=[rtB])
                            yield T.defer5("dve", lambda e: e.scalar_tensor_tensor(out=qkvs[st][w3][:, :], in0=qkvs[st][w3][:, :], scalar=(QSCALE if w3 == 0 else 1.0),
                                                                         in1=rt[:, :], op0=ALU.mult, op1=ALU.mult),
                                 reads=[rtB, qkvsB[st][w3]], writes=[qkvsB[st][w3]])
                    b = g1bank()

                    def mm(e):
                        ins = None
                        for c in range(NCH):
                            ins = e.matmul(psum[:, b, :], lhsT=wh[3][:, c, :], rhs=hT[:, c, tk], start=(c == 0), stop=(c == NCH - 1))
                        return ins
                    yield T.defer5("pe", mm, reads=[whB[3], hB[n]], writes=[pB[b]], cost=2.1)
                    yield T.defer5("act", lambda e: e.activation(out=rt[:, :], in_=psum[:, b, :], func=AF.Silu), reads=[pB[b]], writes=[rtB])
                    yield T.defer5("dve", lambda e: e.tensor_scalar_mul(out=gsils[sg][:, :], in0=rt[:, :], scalar1=onorm[:, 0:1]), reads=[rtB, gcB], writes=[gsilsB[sg]])

                def scanpost(h, gi, pg, sg):
                    t0 = gi * 4
                    gk = slice(gi * 512, (gi + 1) * 512)
                    Hs, ATs, O0s, Qf = HsS[pg], ATsS[pg], O0sS[pg], QeffS[pg]
                    bOo, bS = 7, 6
                    for c in range(4):
                        t = t0 + c
                        if t > 0:
                            yield T.defer("pe", lambda e: e.matmul(psum[:, bOo, c * 128:(c + 1) * 128], lhsT=Sb[:, :], rhs=Qf[:, c, :], start=True, stop=True),
                                          reads=[SbB, QeffSB[pg][c]], writes=[pB[bOo]])
                        if t == 15:
                            continue
                        if t == 0:
                            yield T.defer("dve", lambda e: e.tensor_copy(out=Sf[:, :], in_=Hs[:, 0, :]), reads=[HsSB[pg][0]], writes=[SfB])
                        else:
                            yield T.defer("pe", lambda e: e.matmul(psum[:, bS, 0:128], lhsT=ATs[:, c, :], rhs=Sf[:, :], start=True, stop=True),
                                          reads=[ATsSB[pg][c], SfB], writes=[pB[bS]], cost=0.5)
                            yield T.defer("dve", lambda e: e.tensor_tensor(out=Sf[:, :], in0=psum[:, bS, 0:128], in1=Hs[:, c, :], op=ALU.add),
                                          reads=[pB[bS], HsSB[pg][c]], writes=[SfB])
                        yield T.defer("act", lambda e: e.activation(out=Sb[:, :], in_=Sf[:, :], func=AF.Copy), reads=[SfB], writes=[SbB])
                    if gi == 0:
                        yield T.defer("dve", lambda e: e.tensor_copy(out=osum[:, 0, :], in_=O0s[:, 0, :]), reads=[O0sSB[pg][0]], writes=[osumB])
                        yield T.defer("dve", lambda e: e.tensor_tensor(out=osum[:, 1:4, :], in0=pv(bOo)[:, 1:4, :], in1=O0s[:, 1:4, :], op=ALU.add),
                                      reads=[pB[bOo]] + O0sSB[pg][1:4], writes=[osumB], cost=0.5)
                    else:
                        yield T.defer("dve", lambda e: e.tensor_tensor(out=osum[:], in0=pv(bOo), in1=O0s[:], op=ALU.add), reads=[pB[bOo]] + O0sSB[pg], writes=[osumB], cost=0.6)
                    yield T.defer("act", lambda e: e.activation(out=osq[:, :], in_=fl(osum), func=AF.Square), reads=[osumB], writes=[osqB], cost=0.6)
                    yield T.defer("pe", lambda e: e.matmul(psum[:, bS, :], lhsT=odiv[:, :], rhs=osq[:, :], start=True, stop=True), reads=[osqB, gcB], writes=[pB[bS]])
                    yield T.defer("act", lambda e: e.activation(out=fl(rr2), in_=psum[:, bS, :], func=AF.Ln, bias=eps_t[:, 0:1]), reads=[pB[bS], cB], writes=[rr2B], cost=0.6)
                    yield T.defer("act", lambda e: e.activation(out=fl(rr2), in_=fl(rr2), func=AF.Exp, scale=-0.5), reads=[rr2B], writes=[rr2B], cost=0.6)
                    yield T.defer("dve", lambda e: e.tensor_tensor(out=osum[:], in0=osum[:], in1=rr2[:], op=ALU.mult), reads=[osumB, rr2B], writes=[osumB], cost=0.6)
                    yield T.defer("pool", lambda e: e.tensor_tensor(out=ybT[:, h, gk], in0=fl(osum), in1=gsils[sg][:, :], op=ALU.mult),
                                  reads=[osumB, gsilsB[sg]], writes=[ybB[h]], cost=1.1)

                blocks = [(h_, n_) for h_ in range(8) for n_ in range(4)]
                NBK = len(blocks)
                run_streams([g1b(0, 0, 0, 0)])
                for r in range(NBK + 1):
                    gens = []
                    if r < NBK:
                        h, gi = blocks[r]
                        st = r % 2
                        pg = r % 2
                        t0 = gi * 4
                        cks = [slice(c * 128, (c + 1) * 128) for c in range(4)]
                        qn_, kn_, vn_ = qkvs[st][0], qkvs[st][1], qkvs[st][2]
                        kq_ = kqt[st]
                        qB, kB_, vB_ = qkvsB[st][0], qkvsB[st][1], qkvsB[st][2]

                        def prep(c):
                            t = t0 + c
                            ck = cks[c]
                            ba = bb = c
                            DT, decF, EG = Fm[0][:, c, :], Fm[1][:, c, :], Fm[2][:, c, :]
                            gc = gcum[:, t, h:h + 1]
                            yield T.defer("pe", lambda e: e.matmul(psum[:, ba, 0:128], lhsT=gt[:, t, h:h + 1].to_broadcast([128, 128]), rhs=triu[:, :],
                                                          start=True, stop=True), reads=[gB, gcB], writes=[pB[ba]])
                            yield T.defer("dve", lambda e: e.scalar_tensor_tensor(out=DT, in0=psum[:, ba, 0:128], scalar=gc, in1=smask[:, :],
                                                                         op0=ALU.subtract, op1=ALU.add), reads=[pB[ba], gB, gcB], writes=[FB[0][c]])
                            yield T.defer("act", lambda e: e.activation(out=EG, in_=psum[:, ba, 0:128], func=AF.Exp), reads=[pB[ba]], writes=[FB[2][c]])
                            yield T.defer("act", lambda e: e.activation(out=DT, in_=DT, func=AF.Exp), reads=[FB[0][c]], writes=[FB[0][c]])
                            yield T.defer("pool", lambda e: e.tensor_tensor(out=decF, in0=DT, in1=ident_f[:, :], op=ALU.add), reads=[FB[0][c], cB], writes=[FB[1][c]])
                            yield T.defer("pe", lambda e: e.matmul(psum[:, ba, 0:256], lhsT=kn_[:, ck], rhs=kq_[:, :, ck], start=True, stop=True),
                                 reads=[kB_, qB], writes=[pB[ba]])
                            yield T.defer("dve", lambda e: e.scalar_tensor_tensor(out=QT[:, c, :], in0=psum[:, ba, 0:128], scalar=negb[:, t, h:h + 1], in1=DT,
                                                                         op0=ALU.mult, op1=ALU.mult), reads=[pB[ba], gB, FB[0][c]], writes=[QTB[c]])
                            yield T.defer("dve", lambda e: e.tensor_tensor(out=Aqk[:, c, :], in0=psum[:, ba, 128:256], in1=decF, op=ALU.mult),
                                 reads=[pB[ba], FB[1][c]], writes=[AqkB[c]])
                            yield T.defer("pe", lambda e: e.matmul(psum[:, bb, 128:256], lhsT=QT[:, c, :], rhs=ident_b[:, :], start=True, stop=True),
                                 reads=[QTB[c], cB], writes=[pB[bb]])
                            yield T.defer("dve", lambda e: e.tensor_tensor(out=PQ[0][:, 0, c, :], in0=psum[:, bb, 128:256], in1=ident_f[:, :], op=ALU.add),
                                 reads=[pB[bb], cB], writes=[PQB[0][c]])
                            yield T.defer("pool", lambda e: e.tensor_tensor(out=PQ[0][:, 1, c, :], in0=QT[:, c, :], in1=ident_b[:, :], op=ALU.add),
                                 reads=[QTB[c], cB], writes=[PQB[0][c]])
                            yield T.defer("pool", lambda e: e.tensor_tensor(out=ATt[:, c, :], in0=ident_b[:, :], in1=QT[:, c, :], op=ALU.subtract),
                                 reads=[QTB[c], cB], writes=[ATtB[c]])
                            yield T.defer("pe", lambda e: e.matmul(psum[:, ba, 0:128], lhsT=kn_[:, ck], rhs=ident_b[:, :], start=True, stop=True),
                                 reads=[kB_, cB], writes=[pB[ba]])
                            yield T.defer("dve", lambda e: e.tensor_scalar_mul(out=kdec[:, c, :], in0=psum[:, ba, 0:128], scalar1=kds[:, t, h:h + 1]),
                                 reads=[pB[ba], gB], writes=[kdecB[c]])
                            yield T.defer("dve", lambda e: e.tensor_scalar_mul(out=VK[:, c, 128:256], in0=psum[:, ba, 0:128], scalar1=egc[:, t, h:h + 1]),
                                 reads=[pB[ba], gB], writes=[VKB[c]])
                            yield T.defer("pe", lambda e: e.matmul(psum[:, bb, 128:256], lhsT=vn_[:, ck], rhs=ident_b[:, :], start=True, stop=True),
                                 reads=[vB_, cB], writes=[pB[bb]])
                            yield T.defer("act", lambda e: e.activation(out=VK[:, c, 0:128], in_=psum[:, bb, 128:256], func=AF.Copy), reads=[pB[bb]], writes=[VKB[c]])
                            cur = 0
                            for k in range(2, 8):
                                nxt = 1 - cur
                                last = (k == 7)
                                yield T.defer("pe", lambda e: e.matmul(psum[:, ba, 0:128], lhsT=ATt[:, c, :], rhs=PQ[cur][:, 0, c, :], start=True, stop=True),
                                     reads=[ATtB[c], PQB[cur][c]], writes=[pB[ba]])
                                yield T.defer("dve", lambda e: e.scalar_tensor_tensor(out=Wt[:, c, :], in0=psum[:, ba, 0:128], scalar=-1.0, in1=ident2_f[:, :],
                                                                             op0=ALU.mult, op1=ALU.add), reads=[pB[ba], cB], writes=[WtB[c]])

                                def mmn(e):
                                    ins = e.matmul(psum[:, bb, 128:256], lhsT=Wt[:, c, :], rhs=PQ[cur][:, 1, c, :], start=True, stop=True)
                                    if not last:
                                        ins = e.matmul(psum[:, ba, 0:128], lhsT=PQ[cur][:, 1, c, :], rhs=Wt[:, c, :], start=True, stop=True)
                                    return ins
                                yield T.defer("pe", mmn, reads=[WtB[c], PQB[cur][c]], writes=[pB[ba]])
                                if last:
                                    yield T.defer("act", lambda e: e.activation(out=QT[:, c, :], in_=psum[:, bb, 128:256], func=AF.Copy), reads=[pB[bb]], writes=[QTB[c]])
                                else:
                                    yield T.defer("act", lambda e: e.activation(out=PQ[nxt][:, :, c, :], in_=psum[:, ba, 0:256].rearrange("p (a i) -> p a i", a=2), func=AF.Copy),
                                         reads=[pB[ba]], writes=[PQB[nxt][c]])
                                cur = nxt
                            yield T.defer("pe", lambda e: e.matmul(psum[:, ba, 0:256], lhsT=QT[:, c, :], rhs=VK[:, c, :], start=True, stop=True),
                                 reads=[QTB[c], VKB[c]], writes=[pB[ba]])
                            yield T.defer("dve", lambda e: e.tensor_scalar_mul(out=Zt[:, c, :], in0=psum[:, ba, 0:256], scalar1=beta[:, t, h:h + 1]),
                                 reads=[pB[ba], gB], writes=[ZtB[c]])
                            yield T.defer("pe", lambda e: e.matmul(psum[:, bb, 128:256], lhsT=kdec[:, c, :], rhs=Zt[:, c, 0:128], start=True, stop=True),
                                 reads=[kdecB[c], ZtB[c]], writes=[pB[bb]])
                            yield T.defer("act", lambda e: e.activation(out=HsS[pg][:, c, :], in_=psum[:, bb, 128:256], func=AF.Copy), reads=[pB[bb]], writes=[HsSB[pg][c]])
                            yield T.defer("pe", lambda e: e.matmul(psum[:, ba, 0:128], lhsT=Zt[:, c, 128:256], rhs=kdec[:, c, :], start=True, stop=True),
                                 reads=[kdecB[c], ZtB[c]], writes=[pB[ba]])
                            yield T.defer("dve", lambda e: e.scalar_tensor_tensor(out=ATsS[pg][:, c, :], in0=ident_f[:, :], scalar=sdec[:, t, h:h + 1], in1=psum[:, ba, 0:128],
                                                                         op0=ALU.mult, op1=ALU.subtract), reads=[pB[ba], gB, cB], writes=[ATsSB[pg][c]])
                            yield T.defer("pe", lambda e: e.matmul(psum[:, bb, 128:256], lhsT=Zt[:, c, 0:128], rhs=Aqk[:, c, :], start=True, stop=True),
                                 reads=[AqkB[c], ZtB[c]], writes=[pB[bb]])
                            yield T.defer("act", lambda e: e.activation(out=O0sS[pg][:, c, :], in_=psum[:, bb, 128:256], func=AF.Copy), reads=[pB[bb]], writes=[O0sSB[pg][c]])
                            yield T.defer("pe", lambda e: e.matmul(psum[:, ba, 0:128], lhsT=Zt[:, c, 128:256], rhs=Aqk[:, c, :], start=True, stop=True),
                                 reads=[AqkB[c], ZtB[c]], writes=[pB[ba]])
                            yield T.defer("pool", lambda e: e.tensor_tensor(out=decF, in0=qn_[:, ck], in1=EG, op=ALU.mult), reads=[qB, FB[2][c], FB[1][c]], writes=[FB[1][c]])
                            yield T.defer("dve", lambda e: e.tensor_tensor(out=QeffS[pg][:, c, :], in0=decF, in1=psum[:, ba, 0:128], op=ALU.subtract),
                                 reads=[pB[ba], FB[1][c]], writes=[QeffSB[pg][c]])

                        gens += [prep(c) for c in range(4)]
                    if r + 1 < NBK:
                        hn, nn = blocks[r + 1]
                        gens.append(g1b(hn, nn, (r + 1) % 2, (r + 1) % 3))
                    if r >= 1:
                        hp, gp = blocks[r - 1]
                        gens.append(scanpost(hp, gp, (r - 1) % 2, (r - 1) % 3))
                    run_streams(gens)
                T.filler = None
                T.barrier()

        def merge(hT, hB, yaT, yaB, ybT, ybB):
            with contextlib.ExitStack() as gs:
                def gsb(name, shape, dt):
                    return gs.enter_context(nc.sbuf_tensor("x_" + name, list(shape), dt))
                mg = gsb("mg", [128, 8, S], BF16)
                mgB = [[Buf() for _ in range(NB)] for _ in range(8)]
                wga = [gsb("wga%d" % i, [128, 2, NCH, 128], BF16) for i in range(2)]
                wba = [gsb("wba%d" % i, [64, 4, 128], BF16) for i in range(2)]
                wbb = [gsb("wbb%d" % i, [128, 8, 128], BF16) for i in range(2)]
                wB = [Buf(), Buf()]
                wo = [gsb("wo%d" % i, [128, 8, 128], BF16) for i in range(2)]
                woB = [Buf(), Buf()]
                one_c = eps_t[:, 1:2]
                sa = gsb("sa", [128, 512], F32)
                sb_ = gsb("sb", [128, 512], F32)
                ta = gsb("ta", [128, 512], F32)
                tb = gsb("tb", [128, 512], F32)
                saB, sbB, taB, tbB = Buf(), Buf(), Buf(), Buf()

                def loadw(dt):
                    i = dt % 2
                    T.dma("pool", wga[i][:].rearrange("p w c f -> p (w c f)"), dt_in["wgab"][dt], writes=[wB[i]])
                    T.dma("pool", wba[i][:].rearrange("p s f -> p (s f)"), dt_in["wba"][dt], writes=[wB[i]])
                    T.dma("pool", wbb[i][:].rearrange("p s f -> p (s f)"), dt_in["wbb"][dt], writes=[wB[i]])
                loadw(0)
                loadw(1)
                for dt in range(8):
                    wi = dt % 2
                    for n in range(NB):
                        tk = slice(n * 512, (n + 1) * 512)
                        bga, bgb, ba, bb = bank(), bank(), bank(), bank()

                        def mmg(e, which, b):
                            ins = None
                            for c in range(NCH):
                                ins = e.matmul(psum[:, b, :], lhsT=wga[wi][:, which, c, :], rhs=hT[:, c, tk], start=(c == 0), stop=(c == NCH - 1))
                            return ins
                        T.op("pe", lambda e: mmg(e, 0, bga), reads=[wB[wi], hB[n]], writes=[pB[bga]])
                        T.op("pe", lambda e: mmg(e, 1, bgb), reads=[wB[wi], hB[n]], writes=[pB[bgb]])

                        def mma(e):
                            ins = None
                            for s4 in range(4):
                                ins = e.matmul(psum[:, ba, :], lhsT=wba[wi][:, s4, :], rhs=yaT[:, s4, tk], start=(s4 == 0), stop=(s4 == 3))
                            return ins
                        T.op("pe", mma, reads=[wB[wi]] + yaB, writes=[pB[ba]])

                        def mmb(e):
                            ins = None
                            for h in range(8):
                                ins = e.matmul(psum[:, bb, :], lhsT=wbb[wi][:, h, :], rhs=ybT[:, h, tk], start=(h == 0), stop=(h == 7))
                            return ins
                        T.op("pe", mmb, reads=[wB[wi]] + ybB, writes=[pB[bb]])
                        T.op("act", lambda e: e.activation(out=sa[:, :], in_=psum[:, bga, :], func=AF.Sigmoid), reads=[pB[bga]], writes=[saB])
                        T.op("act", lambda e: e.activation(out=sb_[:, :], in_=psum[:, bgb, :], func=AF.Sigmoid), reads=[pB[bgb]], writes=[sbB])
                        T.op("dve", lambda e: e.tensor_tensor(out=ta[:, :], in0=psum[:, ba, :], in1=sa[:, :], op=ALU.mult), reads=[pB[ba], saB], writes=[taB])
                        T.op("dve", lambda e: e.tensor_tensor(out=tb[:, :], in0=psum[:, bb, :], in1=sb_[:, :], op=ALU.mult), reads=[pB[bb], sbB], writes=[tbB])
                        T.op("pool", lambda e: e.tensor_tensor(out=mg[:, dt, tk], in0=ta[:, :], in1=tb[:, :], op=ALU.add), reads=[taB, tbB], writes=[mgB[dt][n]])
                    if dt + 2 < 8:
                        loadw(dt + 2)

                def loado(dt):
                    T.dma("pool", wo[dt % 2][:].rearrange("p s f -> p (s f)"), dt_in["wout"][dt], writes=[woB[dt % 2]])
                loado(0)
                loado(1)
                for dt in range(8):
                    wi = dt % 2
                    for n in range(NB):
                        tk = slice(n * 512, (n + 1) * 512)
                        b = bank()

                        def mmo(e):
                            ins = None
                            for k in range(8):
                                ins = e.matmul(psum[:, b, :], lhsT=wo[wi][:, k, :], rhs=mg[:, k, tk], start=(k == 0), stop=(k == 7))
                            return ins
                        T.op("pe", mmo, reads=[woB[wi]] + [mgB[k][n] for k in range(8)], writes=[pB[b]])
                        T.op("dve", lambda e: e.tensor_tensor(out=xT[:, dt, tk], in0=psum[:, b, :], in1=xT[:, dt, tk], op=ALU.add),
                             reads=[pB[b]], writes=[xB[n]])
                    if dt + 2 < 8:
                        loado(dt + 2)
                T.barrier()

        def dbg_dump(slabs):
            with contextlib.ExitStack() as ds:
                dtile = [ds.enter_context(nc.sbuf_tensor("dbg%d" % i, [128, S], F32)) for i in range(2)]
                dB = [Buf(), Buf()]
                for k, (ap, bufs, row0, P) in enumerate(slabs):
                    i = k % 2
                    T.op("act", lambda e: e.activation(out=dtile[i][0:P, :], in_=ap, func=AF.Copy), reads=bufs, writes=[dB[i]])
                    T.dma("sp", out_d[row0:row0 + P, :], dtile[i][0:P, :], reads=[dB[i]])
                T.barrier()

        ffn(0, 0)
        done = False
        if stop_after not in ("ffn1_raw", "ffn1"):
            r = mixer()
            if r is not None:
                done = True
        if stop_after in ("all", "x3_raw"):
            ffn(1, 2)
        if not done:
          with contextlib.ExitStack() as fs:
            yT = [fs.enter_context(nc.sbuf_tensor("yT%d" % i, [128, NCH, 512], F32)) for i in range(2)]
            yB = [Buf(), Buf()]
            sq_t = fs.enter_context(nc.sbuf_tensor("fsq", [128, NCH, 512], BF16))
            sq_b = [Buf() for _ in range(NCH)]
            rs_t = fs.enter_context(nc.sbuf_tensor("frs", [128, 512], F32))
            rs_b = Buf()
            out_dv = out_d.rearrange("(c p) t -> p c t", p=128)
            for n in range(NB):
                i = n % 2
                if stop_after.endswith("_raw"):
                    for c in range(NCH):
                        T.op("dve", lambda e: e.tensor_copy(out=yT[i][:, c, :], in_=xT[:, c, n * 512:(n + 1) * 512]),
                             reads=[xB[n]], writes=[yB[i]])
                else:
                    rmsnorm_block(n, 3, lambda c: yT[i][:, c, :], yB[i], sq_t, sq_b, rs_t, rs_b)
                T.dma("sp", out_dv[:, :, n * 512:(n + 1) * 512], yT[i][:], reads=[yB[i]])
            T.barrier()
    return nc, set(dt_in.keys())


def _consts():
    c = {}
    c["ident_f"] = np.eye(128, dtype=np.float32)
    c["ones_f"] = np.ones((128, 128), dtype=np.float32)
    half = 32
    inv_freq = 10000.0 ** (-np.arange(half, dtype=np.float64) / half)
    pos = np.arange(S, dtype=np.float64)
    ang = pos[None, :] * inv_freq[:, None]
    cosv, sinv = np.cos(ang).astype(np.float32), np.sin(ang).astype(np.float32)
    cos_t = np.zeros((128, S), np.float32)
    sin_t = np.zeros((128, S), np.float32)
    rm = np.zeros((128, 128), np.float32)
    for p in range(128):
        d = p % 64
        cos_t[p] = cosv[d % 32]
        sin_t[p] = -sinv[d % 32] if d < 32 else sinv[d % 32]
        partner = p + 32 if d < 32 else p - 32
        rm[partner, p] = 1.0
    c["rope_cos"], c["rope_sin"], c["rm"] = cos_t, sin_t, rm
    j = np.arange(128)[:, None]
    i = np.arange(128)[None, :]
    mprev = (j >= i).astype(np.float32)
    mcur = (j <= i).astype(np.float32)
    c["amask"] = np.concatenate([mprev, mcur, mprev, mcur], axis=1)
    em = np.zeros((128, 64), np.float32)
    em[64, :] = 1.0
    c["em"] = em
    c["triu"] = (j <= i).astype(np.float32)
    c["smask"] = np.where(i > j, 0.0, -30000.0).astype(np.float32)
    c["odiv"] = np.full((128, 128), 1.0 / 128.0, np.float32)
    return c


def _prep_inputs(inp):
    f = lambda a: np.ascontiguousarray(a, dtype=np.float32)
    shared = {}

    def ncol(v):
        return f(np.asarray(v).reshape(NCH, 128).T)
    shared["ffn1_norm"] = ncol(inp["ffn1_norm"][0])
    shared["ffn2_norm"] = ncol(inp["ffn2_norm"][0])
    shared["mix_norm"] = ncol(inp["mix_norm"][0])
    shared["final_norm"] = ncol(inp["final_norm"])
    ffn_in = {1: (inp["ffn1_w_gate"], inp["ffn1_w_up"], inp["ffn1_w_down"]),
              2: (inp["ffn2_w_gate"], inp["ffn2_w_up"], inp["ffn2_w_down"])}
    for i in (1, 2):
        wg = np.asarray(ffn_in[i][0][0])
        wu = np.asarray(ffn_in[i][1][0])
        wd = np.asarray(ffn_in[i][2][0])
        shared["wg%d" % i] = f(wg.reshape(NCH, 128, 11, 256).transpose(2, 1, 0, 3).reshape(11, 128, NCH * 256))
        shared["wu%d" % i] = f(wu.reshape(NCH, 128, 11, 256).transpose(2, 1, 0, 3).reshape(11, 128, NCH * 256))
        shared["wd%d" % i] = f(wd.reshape(NF, 128, 8, 128).transpose(2, 1, 0, 3).reshape(8, 128, NF * 128))
    w_in = np.asarray(inp["w_in"][0])

    def cols(c0, n):
        return w_in[:, c0:c0 + n].reshape(NCH, 128, n).transpose(1, 0, 2).reshape(128, NCH * n)
    wa = np.zeros((6, 3, 128, NCH * 128), np.float32)
    for g in range(3):
        for hp in range(2):
            for i in range(3):
                wa[g * 2 + hp, i] = cols(i * 768 + (4 * g + 2 * hp) * 64, 128)
    shared["wqkv_a"] = wa
    wg_ = np.zeros((8, 3, 128, NCH * 128), np.float32)
    wgg = np.zeros((8, 128, NCH * 128), np.float32)
    for h in range(8):
        for i in range(3):
            wg_[h, i] = cols(2304 + i * 1024 + h * 128, 128)
        wgg[h] = cols(5392 + h * 128, 128)
    shared["wgdn"] = wg_
    shared["wgg"] = wgg
    shared["wbd"] = f(cols(5376, 16))
    wgab = np.zeros((8, 128, 2 * NCH * 128), np.float32)
    for dt in range(8):
        a = cols(6416 + dt * 128, 128).reshape(128, 1, NCH * 128)
        b = cols(7440 + dt * 128, 128).reshape(128, 1, NCH * 128)
        wgab[dt] = np.concatenate([a, b], axis=1).reshape(128, 2 * NCH * 128)
    shared["wgab"] = wgab
    wba = np.asarray(inp["w_branch_a"][0])
    shared["wba"] = f(wba.reshape(4, 64, 8, 128).transpose(2, 1, 0, 3).reshape(8, 64, 4 * 128))
    wbb = np.asarray(inp["w_branch_b"][0])
    shared["wbb"] = f(wbb.reshape(8, 128, 8, 128).transpose(2, 1, 0, 3).reshape(8, 128, 8 * 128))
    wo = np.asarray(inp["w_out"][0])
    shared["wout"] = f(wo.reshape(8, 128, 8, 128).transpose(2, 1, 0, 3).reshape(8, 128, 8 * 128))
    cw = np.asarray(inp["gdn_conv_w"][0])
    shared["convw"] = f(cw.reshape(4, 3, 8, 128).transpose(3, 2, 1, 0).reshape(128, 96))
    shared["alog"] = f(np.broadcast_to(np.asarray(inp["gdn_a_log"][0])[None, :], (128, 8)))
    shared["dtb"] = f(np.broadcast_to(np.asarray(inp["gdn_dt_bias"][0])[None, :], (128, 8)))
    shared["onorm"] = f(np.asarray(inp["gdn_out_norm"][0]).reshape(128, 1))
    shared.update(_consts())
    x = np.asarray(inp["x"])
    maps = []
    for b in range(8):
        m = dict(shared)
        m["xT"] = f(x[b].T)
        maps.append(m)
    return maps


_NC_CACHE = {}


def kernel(**inputs):
    stop_after = inputs.pop("_stop_after", "all")
    if stop_after not in _NC_CACHE:
        _NC_CACHE[stop_after] = build_program(stop_after)
    nc, names = _NC_CACHE[stop_after]
    maps = _prep_inputs(inputs)
    maps = [{k: v for k, v in m.items() if k in names} for m in maps]
    res = run_bass_kernel_spmd(nc, maps, core_ids=list(range(8)))
    out = np.stack([np.ascontiguousarray(r["outT"].T) for r in res.results], axis=0)
    return out.astype(np.float32)
```

```python
import contextlib
import numpy as np
import concourse.bass as bass
import concourse.mybir as mybir
from concourse.bass_utils import run_bass_kernel_spmd

F32 = mybir.dt.float32
BF16 = mybir.dt.bfloat16
AF = mybir.ActivationFunctionType
ALU = mybir.AluOpType
AX = mybir.AxisListType

S = 2048
D = 1024
NCH = 8
DFF = 2816
NF = 22
NB = 4
EPS = 1e-6
D_IN = 8464
NDS = 16
import os
ROPE_ADD_ENG = os.environ.get("ROPE_ADD_ENG", "pool")
FILLN = int(os.environ.get("FILLN", "1"))


class _Stop(Exception):
    pass


class Buf:
    __slots__ = ("w", "r", "excl")

    def __init__(self, excl=False):
        self.w = None
        self.r = []
        self.excl = excl


class Desc:
    __slots__ = ("eng", "cost", "run")

    def __init__(self, eng, cost, run):
        self.eng, self.cost, self.run = eng, cost, run


def run_streams(gens, lat=float(os.environ.get("SLAT", "1.5"))):
    st = []
    for g in gens:
        try:
            st.append([g, next(g), 0.0])
        except StopIteration:
            pass
    efree = {}
    while st:
        best = None
        for s_ in st:
            start = max(s_[2], efree.get(s_[1].eng, 0.0))
            if best is None or start < best[0]:
                best = (start, s_)
        start, s_ = best
        d = s_[1]
        d.run()
        fin = start + d.cost
        efree[d.eng] = fin
        s_[2] = fin + lat
        try:
            s_[1] = next(s_[0])
        except StopIteration:
            st.remove(s_)


class Trk:
    def __init__(self, nc, es):
        self.nc = nc
        self.eng = {"pe": nc.tensor, "act": nc.scalar, "dve": nc.vector, "pool": nc.gpsimd, "sp": nc.sync}
        self.sem = {k: es.enter_context(nc.semaphore("sem_" + k)) for k in self.eng}
        self.cnt = {k: 0 for k in self.eng}
        self.seen = {k: {} for k in self.eng}
        self.dsem = [es.enter_context(nc.semaphore("dsem%d" % i)) for i in range(NDS)]
        self.dcnt = [0] * NDS
        self.dnext = {"pool": 0, "sp": NDS // 2, "act": NDS // 2}
        self.same = {"pe": False, "act": True, "dve": True, "pool": True, "sp": True}

    def _semof(self, k):
        return self.sem[k] if isinstance(k, str) else self.dsem[k[1]]

    def _wait(self, e, deps):
        best = {}
        for (k, v) in deps:
            if v > best.get(k, 0):
                best[k] = v
        for k, v in best.items():
            if k == e and not self.same[e]:
                continue
            if self.seen[e].get(k, 0) >= v:
                continue
            self.eng[e].wait_ge(self._semof(k), v)
            self.seen[e][k] = v

    @staticmethod
    def _deps(reads, writes):
        deps = []
        for b in reads:
            if b.w is not None:
                deps.append(b.w)
        for b in writes:
            if b.w is not None:
                deps.append(b.w)
            deps.extend(b.r)
        return deps

    @staticmethod
    def _upd(d, reads, writes):
        for b in reads:
            b.r.append(d)
            if len(b.r) > 64:
                best = {}
                for (k, v) in b.r:
                    if v > best.get(k, 0):
                        best[k] = v
                b.r = list(best.items())
        for b in writes:
            b.w = d
            b.r = []

    filler = None

    def op(self, e, fn, reads=(), writes=()):
        if e == "pe" and self.filler is not None:
            self.fcnt = getattr(self, "fcnt", 0) + 1
            if self.fcnt % FILLN == 0:
                self.filler(self.eng["pe"])
        if any(b.excl for b in reads):
            writes = list(writes) + [b for b in reads if b.excl]
            reads = [b for b in reads if not b.excl]
        self._wait(e, self._deps(reads, writes))
        ins = fn(self.eng[e])
        self.cnt[e] += 1
        ins.then_inc(self.sem[e], 1)
        self._upd((e, self.cnt[e]), reads, writes)

    def dma(self, q, out, in_, reads=(), writes=()):
        deps = self._deps(reads, writes)
        i = self.dnext[q]
        lo = 0 if q == "pool" else NDS // 2
        self.dnext[q] = lo + (i - lo + 1) % (NDS // 2)
        if self.dcnt[i] > 0:
            deps.append((("d", i), self.dcnt[i]))
        self._wait(q, deps)
        self.eng[q].dma_start(out=out, in_=in_).then_inc(self.dsem[i], 16)
        self.dcnt[i] += 16
        self._upd((("d", i), self.dcnt[i]), reads, writes)

    def defer(self, e, fn, reads=(), writes=(), cost=None):
        return Desc(e, cost if cost is not None else {"pe": 0.3, "dve": 0.3, "act": 0.3, "pool": 0.6, "sp": 0.1}[e],
                    lambda: self.op(e, fn, reads, writes))

    def defer5(self, e, fn, reads=(), writes=(), cost=None):
        return Desc(e, cost if cost is not None else {"pe": 0.3, "dve": 0.6, "act": 0.6, "pool": 1.1, "sp": 0.1}[e],
                    lambda: self.op(e, fn, reads, writes))

    def defer_dma(self, q, out, in_, reads=(), writes=(), cost=0.2):
        return Desc(q, cost, lambda: self.dma(q, out, in_, reads, writes))

    def barrier(self):
        deps = [(k, v) for k, v in self.cnt.items() if v > 0]
        deps += [(("d", i), c) for i, c in enumerate(self.dcnt) if c > 0]
        for e in self.eng:
            self._wait(e, [d for d in deps if d[0] != e])


def build_program(stop_after="all", debug=False):
    nc = bass.Bass("TRN2", target_bir_lowering=False)
    dt_in = {}

    def din(name, shape):
        dt_in[name] = nc.dram_tensor(name, list(shape), F32, kind="ExternalInput").ap()
        return dt_in[name]

    xT_d = din("xT", [D, S])
    n1_d = din("ffn1_norm", [128, NCH])
    n2_d = din("ffn2_norm", [128, NCH])
    nm_d = din("mix_norm", [128, NCH])
    nf_d = din("final_norm", [128, NCH])
    ffn_w = []
    for i in (1, 2):
        ffn_w.append((din("wg%d" % i, [11, 128, NCH * 256]), din("wu%d" % i, [11, 128, NCH * 256]),
                      din("wd%d" % i, [8, 128, NF * 128])))
    din("wqkv_a", [6, 3, 128, NCH * 128])
    din("rope_cos", [128, S]); din("rope_sin", [128, S]); din("rm", [128, 128]); din("amask", [128, 512]); din("em", [128, 64])
    din("wgdn", [8, 3, 128, NCH * 128]); din("wgg", [8, 128, NCH * 128]); din("wbd", [128, NCH * 16])
    din("convw", [128, 96]); din("alog", [128, 8]); din("dtb", [128, 8]); din("onorm", [128, 1])
    din("triu", [128, 128]); din("smask", [128, 128]); din("odiv", [128, 128])
    din("wgab", [8, 128, 2 * NCH * 128]); din("wba", [8, 64, 4 * 128]); din("wbb", [8, 128, 8 * 128]); din("wout", [8, 128, 8 * 128])
    ident_d = din("ident_f", [128, 128])
    ones_d = din("ones_f", [128, 128])
    out_d = nc.dram_tensor("outT", [D, S], F32, kind="ExternalOutput").ap()
    dbg = {}

    with contextlib.ExitStack() as es:
        T = Trk(nc, es)

        def sb(name, shape, dt):
            return es.enter_context(nc.sbuf_tensor(name, list(shape), dt))

        xT = sb("xT_sb", [128, NCH, S], F32)
        xB = [Buf() for _ in range(NB)]
        psum = es.enter_context(nc.psum_tensor("psum", [128, 8, 512], F32))
        pB = [Buf(True) for _ in range(8)]
        pstate = {"n": 0}

        def bank():
            b = pstate["n"]
            pstate["n"] = (b + 1) % 8
            return b

        def bank2():
            b = pstate["n"]
            if b % 2:
                b = (b + 1) % 8
            pstate["n"] = (b + 2) % 8
            return b

        ident_f = sb("ident_f_sb", [128, 128], F32)
        ident_b = sb("ident_b_sb", [128, 128], BF16)
        ones_b = sb("ones_b_sb", [128, 128], BF16)
        ones_f = sb("ones_f_sb", [128, 128], F32)
        ident2_f = sb("ident2_f_sb", [128, 128], F32)
        cB = Buf()
        norms = sb("norms_sb", [128, 4, NCH], F32)
        eps_t = sb("eps_sb", [128, 2], F32)
        T.op("dve", lambda e: e.memset(eps_t[:, 0:1], EPS), writes=[cB])
        T.op("dve", lambda e: e.memset(eps_t[:, 1:2], 1.0), writes=[cB])
        T.dma("sp", ident_f[:], ident_d, writes=[cB])
        T.dma("sp", ones_f[:], ones_d, writes=[cB])
        T.op("dve", lambda e: e.tensor_single_scalar(out=ident2_f[:, :], in_=ident_f[:, :], scalar=2.0, op=ALU.mult), reads=[cB], writes=[cB])
        T.dma("pool", ident_b[:], ident_d, writes=[cB])
        T.dma("pool", ones_b[:], ones_d, writes=[cB])
        for i, nd in enumerate((n1_d, nm_d, n2_d, nf_d)):
            T.dma("sp", norms[:, i, :], nd, writes=[cB])
        xT_dv = xT_d.rearrange("(c p) t -> p c t", p=128)
        for n in range(NB):
            T.dma("sp", xT[:, :, n * 512:(n + 1) * 512], xT_dv[:, :, n * 512:(n + 1) * 512], writes=[xB[n]])

        def rmsnorm_block(n, gi, out_ap_fn, out_buf, sq_t, sq_b, rs_t, rs_b):
            tok = slice(n * 512, (n + 1) * 512)
            b = bank()
            for c in range(NCH):
                T.op("act", lambda e, c=c: e.activation(out=sq_t[:, c, :], in_=xT[:, c, tok], func=AF.Square),
                     reads=[xB[n]], writes=[sq_b[c]])
            def mm(e):
                ins = None
                for c in range(NCH):
                    ins = e.matmul(psum[:, b, :], lhsT=ones_b[:, :], rhs=sq_t[:, c, :], start=(c == 0), stop=(c == NCH - 1))
                return ins
            T.op("pe", mm, reads=sq_b + [cB], writes=[pB[b]])
            T.op("act", lambda e: e.activation(out=rs_t[:, :], in_=psum[:, b, :], func=AF.Ln, scale=1.0 / D, bias=eps_t[:, 0:1]),
                 reads=[pB[b], cB], writes=[rs_b])
            T.op("act", lambda e: e.activation(out=rs_t[:, :], in_=rs_t[:, :], func=AF.Exp, scale=-0.5), reads=[rs_b], writes=[rs_b])
            for c in range(NCH):
                T.op("dve", lambda e, c=c: e.scalar_tensor_tensor(out=out_ap_fn(c), in0=xT[:, c, tok],
                                                                  scalar=norms[:, gi, c:c + 1], in1=rs_t[:, :],
                                                                  op0=ALU.mult, op1=ALU.mult),
                     reads=[xB[n], rs_b, cB], writes=[out_buf])

        def ffn(which, gi):
            wg_d, wu_d, wd_d = ffn_w[which]
            with contextlib.ExitStack() as fs:
                def fsb(name, shape, dt):
                    return fs.enter_context(nc.sbuf_tensor("f%d_%s" % (which, name), list(shape), dt))
                hT = fsb("hT", [128, NCH, 1024], BF16)
                hB = [Buf(), Buf()]
                actT = fsb("actT", [128, NF, 1024], BF16)
                aB = [[Buf(), Buf()] for _ in range(NF)]
                sq_t = fsb("sq", [128, NCH, 512], BF16)
                sq_b = [Buf() for _ in range(NCH)]
                rs_t = fsb("rs", [128, 512], F32)
                rs_b = Buf()
                NWB = 3
                wgb = [fsb("wg%d" % i, [128, NCH, 256], BF16) for i in range(NWB)]
                wub = [fsb("wu%d" % i, [128, NCH, 256], BF16) for i in range(NWB)]
                wgB = [Buf() for _ in range(NWB)]
                wuB = [Buf() for _ in range(NWB)]
                wdb = [fsb("wd%d" % i, [128, NF, 128], BF16) for i in range(2)]
                wdB = [Buf() for _ in range(2)]
                sg = [fsb("sg%d" % i, [128, 512], F32) for i in range(3)]
                sgB = [Buf() for _ in range(3)]
                sgn = 0

                def load_gu(g):
                    i = g % NWB
                    T.dma("pool", wgb[i][:].rearrange("p c f -> p (c f)"), wg_d[g], writes=[wgB[i]])
                    T.dma("pool", wub[i][:].rearrange("p c f -> p (c f)"), wu_d[g], writes=[wuB[i]])

                def load_d(dt):
                    i = dt % 2
                    T.dma("pool", wdb[i][:].rearrange("p f j -> p (f j)"), wd_d[dt], writes=[wdB[i]])

                for half in range(2):
                    load_gu(0)
                    load_gu(1)
                    for nn in range(2):
                        n = half * 2 + nn
                        rmsnorm_block(n, gi, lambda c, nn=nn: hT[:, c, nn * 512:(nn + 1) * 512], hB[nn],
                                      sq_t, sq_b, rs_t, rs_b)
                    for g in range(11):
                        if g + 2 < 11:
                            load_gu(g + 2)
                        if g == 9:
                            load_d(0)
                        if g == 10:
                            load_d(1)
                        wi = g % NWB
                        for j in range(2):
                            f = 2 * g + j
                            for nn in range(2):
                                tk = slice(nn * 512, (nn + 1) * 512)
                                bg = bank()
                                bu = bank()

                                def mmg(e, w=wgb[wi], b=bg, j=j, tk=tk):
                                    ins = None
                                    for c in range(NCH):
                                        ins = e.matmul(psum[:, b, :], lhsT=w[:, c, j * 128:(j + 1) * 128],
                                                       rhs=hT[:, c, tk], start=(c == 0), stop=(c == NCH - 1))
                                    return ins
                                T.op("pe", mmg, reads=[wgB[wi], hB[nn]], writes=[pB[bg]])
                                T.op("pe", lambda e, b=bu, j=j, tk=tk: mmg(e, wub[wi], b, j, tk),
                                     reads=[wuB[wi], hB[nn]], writes=[pB[bu]])
                                si = sgn % 3
                                sgn += 1
                                T.op("act", lambda e, si=si, b=bg: e.activation(out=sg[si][:, :], in_=psum[:, b, :],
                                                                               func=AF.Silu),
                                     reads=[pB[bg]], writes=[sgB[si]])
                                T.op("dve", lambda e, si=si, b=bu, f=f, tk=tk: e.tensor_tensor(
                                    out=actT[:, f, tk], in0=psum[:, b, :], in1=sg[si][:, :], op=ALU.mult),
                                     reads=[pB[bu], sgB[si]], writes=[aB[f][nn]])
                    for dt in range(8):
                        if dt + 2 < 8:
                            pass
                        wi = dt % 2
                        for nn in range(2):
                            n = half * 2 + nn
                            tk = slice(nn * 512, (nn + 1) * 512)
                            b = bank()

                            def mmd(e, b=b, wi=wi, tk=tk):
                                ins = None
                                for f in range(NF):
                                    ins = e.matmul(psum[:, b, :], lhsT=wdb[wi][:, f, :], rhs=actT[:, f, tk],
                                                   start=(f == 0), stop=(f == NF - 1))
                                return ins
                            T.op("pe", mmd, reads=[wdB[wi]] + [aB[f][nn] for f in range(NF)], writes=[pB[b]])
                            T.op("dve", lambda e, b=b, dt=dt, n=n: e.scalar_tensor_tensor(
                                out=xT[:, dt, n * 512:(n + 1) * 512], in0=psum[:, b, :], scalar=0.5,
                                in1=xT[:, dt, n * 512:(n + 1) * 512], op0=ALU.mult, op1=ALU.add),
                                 reads=[pB[b]], writes=[xB[n]])
                        if dt + 2 < 8:
                            load_d(dt + 2)
                T.barrier()

        def mixer():
            with contextlib.ExitStack() as ms:
                def msb(name, shape, dt):
                    return ms.enter_context(nc.sbuf_tensor("m_" + name, list(shape), dt))
                hT = msb("hT", [128, NCH, S], BF16)
                hB = [Buf() for _ in range(NB)]
                with contextlib.ExitStack() as ns:
                    sq_t = ns.enter_context(nc.sbuf_tensor("m_sq", [128, NCH, 512], BF16))
                    sq_b = [Buf() for _ in range(NCH)]
                    rs_t = ns.enter_context(nc.sbuf_tensor("m_rs", [128, 512], F32))
                    rs_b = Buf()
                    for n in range(NB):
                        rmsnorm_block(n, 1, lambda c, n=n: hT[:, c, n * 512:(n + 1) * 512], hB[n], sq_t, sq_b, rs_t, rs_b)
                    T.barrier()
                if stop_after == "dbg_h":
                    dbg_dump([(hT[:, c, :], hB, c * 128, 128) for c in range(NCH)])
                    return True
                ybT = msb("ybT", [128, 8, S], BF16)
                ybB = [Buf() for _ in range(8)]
                gdn(hT, hB, ybT, ybB)
                T.barrier()
                if stop_after == "dbg_yb":
                    dbg_dump([(ybT[:, c, :], ybB, c * 128, 128) for c in range(8)])
                    return True
                yaT = msb("yaT", [64, 4, S], BF16)
                yaB = [Buf() for _ in range(4)]
                if attention(hT, hB, yaT, yaB):
                    return True
                T.barrier()
                if stop_after == "dbg_ya":
                    dbg_dump([(yaT[:, c, :], yaB, c * 64, 64) for c in range(4)])
                    return True
                merge(hT, hB, yaT, yaB, ybT, ybB)
                T.barrier()
            return None

        def nset(lo, hi):
            return list(range(lo // 512, hi // 512 + 1))

        def attention(hT, hB, yaT, yaB):
            wqkv_d = dt_in["wqkv_a"]
            with contextlib.ExitStack() as as_:
                def asb(name, shape, dt):
                    return as_.enter_context(nc.sbuf_tensor("a_" + name, list(shape), dt))
                cos_t = asb("cos", [128, S], BF16)
                sin_t = asb("sin", [128, S], BF16)
                rm_t = asb("rm", [128, 128], BF16)
                am_t = asb("am", [128, 512], BF16)
                em_t = asb("em", [128, 64], F32)
                acB = Buf()
                T.dma("pool", cos_t[:], dt_in["rope_cos"], writes=[acB])
                T.dma("pool", sin_t[:], dt_in["rope_sin"], writes=[acB])
                T.dma("pool", rm_t[:], dt_in["rm"], writes=[acB])
                T.dma("pool", am_t[:], dt_in["amask"], writes=[acB])
                T.dma("sp", em_t[:], dt_in["em"], writes=[acB])
                wq = [asb("w%d" % i, [128, NCH, 128], BF16) for i in range(3)]
                wB = [Buf() for _ in range(3)]
                qk = [asb("qk%d" % i, [128, S], BF16) for i in range(2)]
                qkB = [[Buf() for _ in range(NB)] for _ in range(2)]
                vT = asb("vT", [128, S], BF16)
                vTB = [Buf() for _ in range(NB)]
                Vt = asb("Vt", [128, 16, 2, 65], BF16)
                VtB = [Buf() for _ in range(16)]
                VoneB = Buf()
                T.op(ROPE_ADD_ENG, lambda e: e.memset(Vt[:, :, :, 64:65], 1.0), writes=[VoneB])
                acc = asb("acc", [128, 2, S], F32)
                accB = Buf()
                qb = asb("qb", [128, 512], BF16)
                qbB = Buf()
                t1 = asb("t1", [128, 512], F32)
                t2 = asb("t2", [128, 512], F32)
                t1B, t2B = Buf(), Buf()
                ex = [asb("ex%d" % i, [128, 512], BF16) for i in range(3)]
                exB = [Buf(), Buf(), Buf()]
                exm = [asb("exm%d" % i, [128, 512], BF16) for i in range(3)]
                exmB = [Buf(), Buf(), Buf()]
                rl = asb("rl", [64, 512], F32)
                rlB = Buf()
                exn = 0
                if stop_after == "dbg_att0":
                    dbg_dump([(cos_t[:, :], [acB], 0, 128), (sin_t[:, :], [acB, VoneB], 128, 128)])
                    return True
                for hp in range(2):
                    for g in range(3):
                        dil = (1, 4, 16)[g]
                        nbk = 16 // dil
                        pi = g * 2 + hp
                        for i in range(3):
                            T.dma("pool", wq[i][:].rearrange("p c f -> p (c f)"), wqkv_d[pi, i], writes=[wB[i]])
                        for i in range(2):
                            for n in range(NB):
                                tk = slice(n * 512, (n + 1) * 512)
                                b = bank()

                                def mm(e):
                                    ins = None
                                    for c in range(NCH):
                                        ins = e.matmul(psum[:, b, :], lhsT=wq[i][:, c, :], rhs=hT[:, c, tk],
                                                       start=(c == 0), stop=(c == NCH - 1))
                                    return ins
                                CUT = int(os.environ.get("CUT", "9"))
                                if CUT >= 1:
                                    T.op("pe", mm, reads=[wB[i], hB[n]], writes=[pB[b]])
                                if CUT >= 2:
                                    T.op("act", lambda e: e.activation(out=qb[:, :], in_=psum[:, b, :], func=AF.Copy),
                                         reads=[pB[b]], writes=[qbB])
                                b2 = bank()
                                if CUT >= 3:
                                    T.op("pe", lambda e: e.matmul(psum[:, b2, :], lhsT=rm_t[:, :], rhs=qb[:, :], start=True, stop=True),
                                         reads=[qbB, acB], writes=[pB[b2]])
                                if CUT >= 4:
                                    T.op("dve", lambda e: e.tensor_tensor(out=t1[:, :], in0=psum[:, b, :], in1=cos_t[:, tk], op=ALU.mult),
                                         reads=[pB[b], acB], writes=[t1B])
                                if CUT >= 5:
                                    T.op("dve", lambda e: e.tensor_tensor(out=t2[:, :], in0=psum[:, b2, :], in1=sin_t[:, tk], op=ALU.mult),
                                         reads=[pB[b2], acB], writes=[t2B])
                                if CUT >= 6:
                                    T.op(ROPE_ADD_ENG, lambda e: e.tensor_tensor(out=qk[i][:, tk], in0=t1[:, :], in1=t2[:, :], op=ALU.add),
                                         reads=[t1B, t2B], writes=[qkB[i][n]])
                                if stop_after == "dbg_att1a":
                                    T.barrier()
                                    with contextlib.ExitStack() as ds:
                                        dtl = ds.enter_context(nc.sbuf_tensor("dbgx", [128, 512], F32))
                                        dxB = Buf()
                                        T.op("act", lambda e: e.activation(out=dtl[:, :], in_=qk[0][:, 0:512], func=AF.Copy), reads=[qkB[0][0]], writes=[dxB])
                                        T.dma("sp", out_d[0:128, 0:512], t1[:, :], reads=[t1B])
                                        T.dma("sp", out_d[128:256, 0:512], t2[:, :], reads=[t2B])
                                        T.dma("sp", out_d[256:384, 0:512], dtl[:, :], reads=[dxB])
                                        T.barrier()
                                    return True
                        if stop_after == "dbg_att1":
                            dbg_dump([(qk[0][:, :], qkB[0], 0, 128), (qk[1][:, :], qkB[1], 128, 128)])
                            return True
                        def tokslice(bi):
                            r, nk = bi // nbk, bi % nbk
                            st = r + dil * 128 * nk
                            return st, slice(st, st + 127 * dil + 1, dil)
                        for n in range(NB):
                            tk = slice(n * 512, (n + 1) * 512)
                            b = bank()

                            def mmvf(e):
                                ins = None
                                for c in range(NCH):
                                    ins = e.matmul(psum[:, b, :], lhsT=wq[2][:, c, :], rhs=hT[:, c, tk], start=(c == 0), stop=(c == NCH - 1))
                                return ins
                            T.op("pe", mmvf, reads=[wB[2], hB[n]], writes=[pB[b]])
                            T.op("act", lambda e: e.activation(out=vT[:, tk], in_=psum[:, b, :], func=AF.Copy), reads=[pB[b]], writes=[vTB[n]])
                        for b4 in range(4):
                            b = bank()

                            def mmv(e):
                                ins = None
                                for k in range(4):
                                    st, sl = tokslice(b4 * 4 + k)
                                    ins = e.matmul(psum[:, b, k * 128:(k + 1) * 128], lhsT=vT[:, sl], rhs=ident_b[:, :], start=True, stop=True)
                                return ins
                            T.op("pe", mmv, reads=vTB + [cB], writes=[pB[b]])
                            T.op("act", lambda e: e.activation(
                                out=Vt[:, b4 * 4:(b4 + 1) * 4, :, 0:64],
                                in_=psum[:, b, :].rearrange("p (k e d) -> p k e d", k=4, e=2), func=AF.Copy),
                                 reads=[pB[b]], writes=[VtB[b4 * 4 + k] for k in range(4)])
                        if stop_after == "dbg_att2":
                            dbg_dump([(Vt[:, :, :, :].rearrange("p a b c -> p (a b c)")[:, 0:2048], VtB + [VoneB], 0, 128)])
                            return True
                        def stage1(bi):
                            nonlocal exn
                            r, nk = bi // nbk, bi % nbk
                            st, qsl = tokslice(bi)
                            hi_tok = st + 127 * dil
                            qn_set = nset(st, hi_tok)
                            b = bank2()
                            if nk > 0:
                                stp, ksl_p = tokslice(bi - 1)
                                kn_set = nset(stp, hi_tok)
                            else:
                                kn_set = qn_set

                            def mms(e):
                                ins = None
                                for hd in range(2):
                                    hs = slice(hd * 64, (hd + 1) * 64)
                                    if nk > 0:
                                        ins = e.matmul(psum[:, b + hd, 0:128], lhsT=qk[1][hs, ksl_p], rhs=qk[0][hs, qsl],
                                                       start=True, stop=True)
                                    ins = e.matmul(psum[:, b + hd, 128:256], lhsT=qk[1][hs, qsl], rhs=qk[0][hs, qsl],
                                                   start=True, stop=True)
                                return ins
                            T.op("pe", mms, reads=[qkB[0][n] for n in qn_set] + [qkB[1][n] for n in kn_set], writes=[pB[b], pB[b + 1]])
                            xi = exn % 3
                            exn += 1
                            lo = 0 if nk > 0 else 128
                            src = psum[:, b:b + 2, lo:256]
                            dst = ex[xi][:, :].rearrange("p (e w) -> p e w", e=2)[:, :, lo:256]
                            dstm = exm[xi][:, :].rearrange("p (e w) -> p e w", e=2)[:, :, lo:256]
                            msk = am_t[:, :].rearrange("p (e w) -> p e w", e=2)[:, :, lo:256]
                            T.op("act", lambda e: e.activation(out=dst, in_=src, func=AF.Exp, scale=0.125),
                                 reads=[pB[b], pB[b + 1]], writes=[exB[xi]])
                            T.op("pool", lambda e: e.tensor_tensor(out=dstm, in0=dst, in1=msk, op=ALU.mult),
                                 reads=[exB[xi], acB], writes=[exmB[xi]])
                            return xi

                        def stage2(bi, xi):
                            r, nk = bi // nbk, bi % nbk
                            st, qsl = tokslice(bi)
                            b2 = bank()

                            def mmu(e):
                                ins = None
                                for hd in range(2):
                                    o = psum[0:65, b2, hd * 128:(hd + 1) * 128]
                                    if nk > 0:
                                        e.matmul(o, lhsT=Vt[:, bi - 1, hd, :], rhs=exm[xi][:, hd * 256:hd * 256 + 128], start=True, stop=False)
                                    ins = e.matmul(o, lhsT=Vt[:, bi, hd, :], rhs=exm[xi][:, hd * 256 + 128:hd * 256 + 256],
                                                   start=(nk == 0), stop=True)
                                return ins
                            T.op("pe", mmu, reads=[exmB[xi], VtB[bi], VoneB] + ([VtB[bi - 1]] if nk > 0 else []), writes=[pB[b2]])
                            src2 = psum[0:65, b2, 0:256].rearrange("p (e q) -> p e q", e=2)
                            if g == 0:
                                T.op("dve", lambda e: e.tensor_copy(out=acc[0:65, :, qsl], in_=src2), reads=[pB[b2]], writes=[accB])
                            else:
                                T.op("dve", lambda e: e.tensor_tensor(out=acc[0:65, :, qsl], in0=src2, in1=acc[0:65, :, qsl], op=ALU.add),
                                     reads=[pB[b2]], writes=[accB])
                        pend = []
                        for bi in range(16):
                            pend.append((bi, stage1(bi)))
                            if len(pend) > 2:
                                stage2(*pend.pop(0))
                        while pend:
                            stage2(*pend.pop(0))
                        if stop_after == "dbg_att3":
                            dbg_dump([(acc[:, 0, :], [accB], 0, 128), (acc[:, 1, :], [accB], 128, 128)])
                            return True
                    for hd in range(2):
                        for n in range(NB):
                            tk = slice(n * 512, (n + 1) * 512)
                            b = bank()
                            T.op("pe", lambda e: e.matmul(psum[0:64, b, :], lhsT=em_t[0:65, :], rhs=acc[0:65, hd, tk], start=True, stop=True),
                                 reads=[accB, acB], writes=[pB[b]])
                            T.op("dve", lambda e: e.reciprocal(out=rl[:, :], in_=psum[0:64, b, :]), reads=[pB[b]], writes=[rlB])
                            T.op("dve", lambda e: e.tensor_tensor(out=yaT[:, 2 * hp + hd, tk], in0=acc[0:64, hd, tk], in1=rl[:, :], op=ALU.mult),
                                 reads=[rlB, accB], writes=[yaB[2 * hp + hd]])
                T.barrier()

        def gdn(hT, hB, ybT, ybB):
            with contextlib.ExitStack() as gs:
                def gsb(name, shape, dt):
                    return gs.enter_context(nc.sbuf_tensor("g_" + name, list(shape), dt))
                gcB = Buf()
                triu = gsb("triu", [128, 128], F32)
                smask = gsb("smask", [128, 128], F32)
                odiv = gsb("odiv", [128, 128], BF16)
                convw = gsb("convw", [128, 8, 3, 4], F32)
                alog = gsb("alog", [128, 8], F32)
                dtb = gsb("dtb", [128, 8], F32)
                onorm = gsb("onorm", [128, 1], F32)
                wbd = gsb("wbd", [128, NCH, 16], BF16)
                T.dma("sp", triu[:], dt_in["triu"], writes=[gcB])
                T.dma("sp", smask[:], dt_in["smask"], writes=[gcB])
                T.dma("pool", odiv[:], dt_in["odiv"], writes=[gcB])
                T.dma("sp", convw[:].rearrange("p h w t -> p (h w t)"), dt_in["convw"], writes=[gcB])
                T.dma("sp", alog[:], dt_in["alog"], writes=[gcB])
                T.dma("sp", dtb[:], dt_in["dtb"], writes=[gcB])
                T.dma("sp", onorm[:], dt_in["onorm"], writes=[gcB])
                T.dma("pool", wbd[:].rearrange("p c f -> p (c f)"), dt_in["wbd"], writes=[gcB])
                braw = gsb("braw", [128, 16, 16], F32)
                beta = gsb("beta", [128, 16, 8], F32)
                negb = gsb("negb", [128, 16, 8], F32)
                gt = gsb("gt", [128, 16, 8], F32)
                gcum = gsb("gcum", [128, 16, 8], F32)
                glast = gsb("glast", [128, 16, 8], F32)
                kds = gsb("kds", [128, 16, 8], F32)
                sdec = gsb("sdec", [128, 16, 8], F32)
                egc = gsb("egc", [128, 16, 8], F32)
                tmpg = gsb("tmpg", [128, 16, 8], F32)
                negA = gsb("negA", [128, 8], F32)
                gB = Buf()
                b = bank()

                def mmb(e):
                    ins = None
                    for t in range(16):
                        for c in range(NCH):
                            ins = e.matmul(psum[:, b, t * 16:(t + 1) * 16], lhsT=hT[:, c, t * 128:(t + 1) * 128], rhs=wbd[:, c, :],
                                           start=(c == 0), stop=(c == NCH - 1))
                    return ins
                T.op("pe", mmb, reads=hB + [gcB], writes=[pB[b]])
                T.op("act", lambda e: e.activation(out=braw[:].rearrange("p t k -> p (t k)"), in_=psum[:, b, 0:256], func=AF.Copy),
                     reads=[pB[b]], writes=[gB])
                T.op("act", lambda e: e.activation(out=tmpg[:], in_=braw[:, :, 0:8], func=AF.Exp, scale=-1.0), reads=[gB], writes=[gB])
                T.op("dve", lambda e: e.tensor_single_scalar(out=tmpg[:], in_=tmpg[:], scalar=1.0, op=ALU.add), reads=[gB], writes=[gB])
                T.op("dve", lambda e: e.reciprocal(out=beta[:], in_=tmpg[:]), reads=[gB], writes=[gB])
                T.op("dve", lambda e: e.tensor_single_scalar(out=negb[:], in_=beta[:], scalar=-1.0, op=ALU.mult), reads=[gB], writes=[gB])
                T.op("dve", lambda e: e.tensor_tensor(out=tmpg[:], in0=braw[:, :, 8:16], in1=dtb[:].unsqueeze(1).to_broadcast([128, 16, 8]), op=ALU.add),
                     reads=[gB, gcB], writes=[gB])
                T.op("act", lambda e: e.activation(out=tmpg[:], in_=tmpg[:], func=AF.Exp), reads=[gB], writes=[gB])
                T.op("act", lambda e: e.activation(out=tmpg[:], in_=tmpg[:], func=AF.Ln, bias=eps_t[:, 1:2]), reads=[gB, cB], writes=[gB])
                T.op("act", lambda e: e.activation(out=negA[:], in_=alog[:], func=AF.Exp), reads=[gcB], writes=[gB])
                T.op("dve", lambda e: e.tensor_single_scalar(out=negA[:], in_=negA[:], scalar=-1.0, op=ALU.mult), reads=[gB], writes=[gB])
                T.op("dve", lambda e: e.tensor_tensor(out=gt[:], in0=tmpg[:], in1=negA[:].unsqueeze(1).to_broadcast([128, 16, 8]), op=ALU.mult),
                     reads=[gB], writes=[gB])
                b = bank()
                T.op("pe", lambda e: e.matmul(psum[:, b, 0:128], lhsT=triu[:, :], rhs=gt[:].rearrange("p t k -> p (t k)"), start=True, stop=True),
                     reads=[gB, gcB], writes=[pB[b]])
                T.op("act", lambda e: e.activation(out=gcum[:].rearrange("p t k -> p (t k)"), in_=psum[:, b, 0:128], func=AF.Copy),
                     reads=[pB[b]], writes=[gB])
                b = bank()
                T.op("pe", lambda e: e.matmul(psum[:, b, 0:128], lhsT=ones_f[:, :], rhs=gt[:].rearrange("p t k -> p (t k)"), start=True, stop=True),
                     reads=[gB, cB], writes=[pB[b]])
                T.op("act", lambda e: e.activation(out=glast[:].rearrange("p t k -> p (t k)"), in_=psum[:, b, 0:128], func=AF.Copy),
                     reads=[pB[b]], writes=[gB])
                T.op("dve", lambda e: e.tensor_tensor(out=tmpg[:], in0=glast[:], in1=gcum[:], op=ALU.subtract), reads=[gB], writes=[gB])
                T.op("act", lambda e: e.activation(out=kds[:], in_=tmpg[:], func=AF.Exp), reads=[gB], writes=[gB])
                T.op("act", lambda e: e.activation(out=sdec[:], in_=glast[:], func=AF.Exp), reads=[gB], writes=[gB])
                T.op("act", lambda e: e.activation(out=egc[:], in_=gcum[:], func=AF.Exp), reads=[gB], writes=[gB])

                wh = [gsb("wh%d" % i, [128, NCH, 128], BF16) for i in range(4)]
                whB = [Buf() for _ in range(4)]
                cv = gsb("cv", [128, 512], F32)
                cvB = Buf()
                CONV_PE = int(os.environ.get("CONV_PE", "1"))
                Dg = gsb("Dg", [128, 3, 4, 128], BF16)
                DgB = Buf()
                xb = [gsb("xb%d" % i, [128, 4 + 512], BF16) for i in range(3)]
                xbB = [Buf() for _ in range(3)]
                kqt = [gsb("kq%d" % st_, [128, 2, 512], BF16) for st_ in range(2)]
                vts = [gsb("vt%d" % st_, [128, 512], BF16) for st_ in range(2)]
                qkvs = [[kqt[st_][:, 1, :], kqt[st_][:, 0, :], vts[st_][:, :]] for st_ in range(2)]
                qkvsB = [[Buf() for _ in range(3)] for _ in range(2)]
                gsils = [gsb("gsil%d" % st_, [128, 512], BF16) for st_ in range(3)]
                gsilsB = [Buf(), Buf(), Buf()]
                HsS = [gsb("HsS%d" % i, [128, 4, 128], F32) for i in range(2)]
                ATsS = [gsb("ATsS%d" % i, [128, 4, 128], F32) for i in range(2)]
                O0sS = [gsb("O0sS%d" % i, [128, 4, 128], F32) for i in range(2)]
                QeffS = [gsb("QeffS%d" % i, [128, 4, 128], BF16) for i in range(2)]
                HsSB = [[Buf() for _ in range(4)] for _ in range(2)]
                ATsSB = [[Buf() for _ in range(4)] for _ in range(2)]
                O0sSB = [[Buf() for _ in range(4)] for _ in range(2)]
                QeffSB = [[Buf() for _ in range(4)] for _ in range(2)]
                osum = gsb("osum", [128, 4, 128], F32)
                osumB = Buf()
                rr2 = gsb("rr2", [128, 4, 128], F32)
                rr2B = Buf()
                g1bk = {"n": 0}

                FILL = int(os.environ.get("FILL", "0"))

                def g1bank():
                    if FILL:
                        return 4
                    g1bk["n"] = 1 - g1bk["n"]
                    return 4 + g1bk["n"]
                if FILL:
                    T.filler = lambda e: e.matmul(psum[:, 5, 0:FILL], lhsT=ident_b[:, :], rhs=ones_b[:, :].unsqueeze(1).to_broadcast([128, FILL // 128, 128]) if FILL > 128 else ident_b[:, :], start=True, stop=True)
                sqb = gsb("sqb", [128, 512], BF16)
                sqbB = Buf()
                rt = gsb("rt", [128, 512], F32)
                rtB = Buf()
                Fm = [gsb("F%d" % i, [128, 4, 128], F32) for i in range(3)]
                FB = [[Buf() for _ in range(4)] for _ in range(3)]
                QT = gsb("QT", [128, 4, 128], BF16)
                QTB = [Buf() for _ in range(4)]
                PQ = [gsb("PQ%d" % i, [128, 2, 4, 128], BF16) for i in range(2)]
                PQB = [[Buf() for _ in range(4)] for _ in range(2)]
                Wt = gsb("Wt", [128, 4, 128], BF16)
                WtB = [Buf() for _ in range(4)]
                ATt = gsb("ATt", [128, 4, 128], BF16)
                ATtB = [Buf() for _ in range(4)]
                Aqk = gsb("Aqk", [128, 4, 128], BF16)
                AqkB = [Buf() for _ in range(4)]
                kdec = gsb("kdec", [128, 4, 128], BF16)
                kdecB = [Buf() for _ in range(4)]
                VK = gsb("VK", [128, 4, 256], BF16)
                VKB = [Buf() for _ in range(4)]
                Zt = gsb("Zt", [128, 4, 256], BF16)
                ZtB = [Buf() for _ in range(4)]
                osq = gsb("osq", [128, 512], BF16)
                osqB = Buf()
                Sf = gsb("Sf", [128, 128], F32)
                Sb = gsb("Sb", [128, 128], BF16)
                SfB, SbB = Buf(), Buf()
                QSCALE = 128.0 ** -0.5
                hbk = {"n": 0}

                def hbank():
                    b_ = 4 + hbk["n"]
                    hbk["n"] = (hbk["n"] + 1) % 3
                    return b_

                def pv(bk):
                    return psum[:, bk, :].rearrange("p (c i) -> p c i", c=4)

                def fl(tl):
                    return tl[:].rearrange("p c i -> p (c i)")

                def g1b(h, n, st, sg):
                    tk = slice(n * 512, (n + 1) * 512)
                    if n == 0:
                        for i in range(3):
                            yield T.defer_dma("pool", wh[i][:].rearrange("p c f -> p (c f)"), dt_in["wgdn"][h, i], writes=[whB[i]])
                        yield T.defer_dma("pool", wh[3][:].rearrange("p c f -> p (c f)"), dt_in["wgg"][h], writes=[whB[3]])
                        if CONV_PE:
                            for w3 in range(3):
                                for tp in range(4):
                                    yield T.defer("dve", lambda e: e.tensor_scalar_mul(out=Dg[:, w3, tp, :], in0=ident_f[:, :], scalar1=convw[:, h, w3, tp:tp + 1]),
                                                  reads=[gcB, cB], writes=[DgB])
                    for w3 in range(3):
                        if n == 0:
                            yield T.defer5("pool", lambda e: e.memset(xb[w3][:, 0:4], 0.0), writes=[xbB[w3]])
                        else:
                            yield T.defer5("pool", lambda e: e.tensor_copy(out=xb[w3][:, 0:4], in_=xb[w3][:, 512:516]), reads=[xbB[w3]], writes=[xbB[w3]])
                        b = g1bank()

                        def mm(e):
                            ins = None
                            for c in range(NCH):
                                ins = e.matmul(psum[:, b, :], lhsT=wh[w3][:, c, :], rhs=hT[:, c, tk], start=(c == 0), stop=(c == NCH - 1))
                            return ins
                        yield T.defer5("pe", mm, reads=[whB[w3], hB[n]], writes=[pB[b]], cost=2.1)
                        yield T.defer5("act", lambda e: e.activation(out=xb[w3][:, 4:516], in_=psum[:, b, :], func=AF.Copy), reads=[pB[b]], writes=[xbB[w3]])
                        if CONV_PE:
                            b = g1bank()

                            def mmc(e):
                                ins = None
                                for tp in range(4):
                                    ins = e.matmul(psum[:, b, :], lhsT=Dg[:, w3, tp, :], rhs=xb[w3][:, 1 + tp:1 + tp + 512], start=(tp == 0), stop=(tp == 3))
                                return ins
                            yield T.defer5("pe", mmc, reads=[DgB, xbB[w3]], writes=[pB[b]], cost=1.0)
                            yield T.defer5("act", lambda e: e.activation(out=qkvs[st][w3][:, :], in_=psum[:, b, :], func=AF.Silu), reads=[pB[b]], writes=[qkvsB[st][w3]])
                        else:
                            yield T.defer5("dve", lambda e: e.tensor_scalar_mul(out=cv[:, :], in0=xb[w3][:, 1:513], scalar1=convw[:, h, w3, 0:1]),
                                 reads=[xbB[w3], gcB], writes=[cvB])
                            for tp in range(1, 4):
                                yield T.defer5("dve", lambda e: e.scalar_tensor_tensor(out=cv[:, :], in0=xb[w3][:, 1 + tp:513 + tp], scalar=convw[:, h, w3, tp:tp + 1],
                                                                             in1=cv[:, :], op0=ALU.mult, op1=ALU.add), reads=[xbB[w3], gcB, cvB], writes=[cvB])
                            yield T.defer5("act", lambda e: e.activation(out=qkvs[st][w3][:, :], in_=cv[:, :], func=AF.Silu), reads=[cvB], writes=[qkvsB[st][w3]])
                        if w3 < 2:
                            yield T.defer5("act", lambda e: e.activation(out=sqb[:, :], in_=qkvs[st][w3][:, :], func=AF.Square), reads=[qkvsB[st][w3]], writes=[sqbB])
                            b2 = g1bank()
                            yield T.defer5("pe", lambda e: e.matmul(psum[:, b2, :], lhsT=ones_b[:, :], rhs=sqb[:, :], start=True, stop=True),
                                 reads=[sqbB, cB], writes=[pB[b2]])
                            yield T.defer5("act", lambda e: e.activation(out=rt[:, :], in_=psum[:, b2, :], func=AF.Ln, bias=eps_t[:, 0:1]), reads=[pB[b2], cB], writes=[rtB])
                            yield T.defer5("act", lambda e: e.activation(out=rt[:, :], in_=rt[:, :], func=AF.Exp, scale=-0.5), reads=[rtB], writes=[rtB])
                            yield T.defer5("dve", lambda e: e.scalar_tensor_tensor(out=qkvs[st][w3][:, :], in0=qkvs[st][w3][:, :], scalar=(QSCALE if w3 == 0 else 1.0),
                                                                         in1=rt[:, :], op0=ALU.mult, op1=ALU.mult),
                                 reads=[rtB, qkvsB[st][w3]], writes=[qkvsB[st][w3]])
                    b = g1bank()

                    def mm(e):
                        ins = None
                        for c in range(NCH):
                            ins = e.matmul(psum[:, b, :], lhsT=wh[3][:, c, :], rhs=hT[:, c, tk], start=(c == 0), stop=(c == NCH - 1))
                        return ins
                    yield T.defer5("pe", mm, reads=[whB[3], hB[n]], writes=[pB[b]], cost=2.1)
                    yield T.defer5("act", lambda e: e.activation(out=rt[:, :], in_=psum[:, b, :], func=AF.Silu), reads=[pB[b]], writes=[rtB])
                    yield T.defer5("dve", lambda e: e.tensor_scalar_mul(out=gsils[sg][:, :], in0=rt[:, :], scalar1=onorm[:, 0:1]), reads=[rtB, gcB], writes=[gsilsB[sg]])

                def scanpost(h, gi, pg, sg):
                    t0 = gi * 4
                    gk = slice(gi * 512, (gi + 1) * 512)
                    Hs, ATs, O0s, Qf = HsS[pg], ATsS[pg], O0sS[pg], QeffS[pg]
                    bOo, bS = 7, 6
                    for c in range(4):
                        t = t0 + c
                        if t > 0:
                            yield T.defer("pe", lambda e: e.matmul(psum[:, bOo, c * 128:(c + 1) * 128], lhsT=Sb[:, :], rhs=Qf[:, c, :], start=True, stop=True),
                                          reads=[SbB, QeffSB[pg][c]], writes=[pB[bOo]])
                        if t == 15:
                            continue
                        if t == 0:
                            yield T.defer("dve", lambda e: e.tensor_copy(out=Sf[:, :], in_=Hs[:, 0, :]), reads=[HsSB[pg][0]], writes=[SfB])
                        else:
                            yield T.defer("pe", lambda e: e.matmul(psum[:, bS, 0:128], lhsT=ATs[:, c, :], rhs=Sf[:, :], start=True, stop=True),
                                          reads=[ATsSB[pg][c], SfB], writes=[pB[bS]], cost=0.5)
                            yield T.defer("dve", lambda e: e.tensor_tensor(out=Sf[:, :], in0=psum[:, bS, 0:128], in1=Hs[:, c, :], op=ALU.add),
                                          reads=[pB[bS], HsSB[pg][c]], writes=[SfB])
                        yield T.defer("act", lambda e: e.activation(out=Sb[:, :], in_=Sf[:, :], func=AF.Copy), reads=[SfB], writes=[SbB])
                    if gi == 0:
                        yield T.defer("dve", lambda e: e.tensor_copy(out=osum[:, 0, :], in_=O0s[:, 0, :]), reads=[O0sSB[pg][0]], writes=[osumB])
                        yield T.defer("dve", lambda e: e.tensor_tensor(out=osum[:, 1:4, :], in0=pv(bOo)[:, 1:4, :], in1=O0s[:, 1:4, :], op=ALU.add),
                                      reads=[pB[bOo]] + O0sSB[pg][1:4], writes=[osumB], cost=0.5)
                    else:
                        yield T.defer("dve", lambda e: e.tensor_tensor(out=osum[:], in0=pv(bOo), in1=O0s[:], op=ALU.add), reads=[pB[bOo]] + O0sSB[pg], writes=[osumB], cost=0.6)
                    yield T.defer("act", lambda e: e.activation(out=osq[:, :], in_=fl(osum), func=AF.Square), reads=[osumB], writes=[osqB], cost=0.6)
                    yield T.defer("pe", lambda e: e.matmul(psum[:, bS, :], lhsT=odiv[:, :], rhs=osq[:, :], start=True, stop=True), reads=[osqB, gcB], writes=[pB[bS]])
                    yield T.defer("act", lambda e: e.activation(out=fl(rr2), in_=psum[:, bS, :], func=AF.Ln, bias=eps_t[:, 0:1]), reads=[pB[bS], cB], writes=[rr2B], cost=0.6)
                    yield T.defer("act", lambda e: e.activation(out=fl(rr2), in_=fl(rr2), func=AF.Exp, scale=-0.5), reads=[rr2B], writes=[rr2B], cost=0.6)
                    yield T.defer("dve", lambda e: e.tensor_tensor(out=osum[:], in0=osum[:], in1=rr2[:], op=ALU.mult), reads=[osumB, rr2B], writes=[osumB], cost=0.6)
                    yield T.defer("pool", lambda e: e.tensor_tensor(out=ybT[:, h, gk], in0=fl(osum), in1=gsils[sg][:, :], op=ALU.mult),
                                  reads=[osumB, gsilsB[sg]], writes=[ybB[h]], cost=1.1)

                blocks = [(h_, n_) for h_ in range(8) for n_ in range(4)]
                NBK = len(blocks)
                run_streams([g1b(0, 0, 0, 0)])
                for r in range(NBK + 1):
                    gens = []
                    if r < NBK:
                        h, gi = blocks[r]
                        st = r % 2
                        pg = r % 2
                        t0 = gi * 4
                        cks = [slice(c * 128, (c + 1) * 128) for c in range(4)]
                        qn_, kn_, vn_ = qkvs[st][0], qkvs[st][1], qkvs[st][2]
                        kq_ = kqt[st]
                        qB, kB_, vB_ = qkvsB[st][0], qkvsB[st][1], qkvsB[st][2]

                        def prep(c):
                            t = t0 + c
                            ck = cks[c]
                            ba = bb = c
                            DT, decF, EG = Fm[0][:, c, :], Fm[1][:, c, :], Fm[2][:, c, :]
                            gc = gcum[:, t, h:h + 1]
                            yield T.defer("pe", lambda e: e.matmul(psum[:, ba, 0:128], lhsT=gt[:, t, h:h + 1].to_broadcast([128, 128]), rhs=triu[:, :],
                                                          start=True, stop=True), reads=[gB, gcB], writes=[pB[ba]])
                            yield T.defer("dve", lambda e: e.scalar_tensor_tensor(out=DT, in0=psum[:, ba, 0:128], scalar=gc, in1=smask[:, :],
                                                                         op0=ALU.subtract, op1=ALU.add), reads=[pB[ba], gB, gcB], writes=[FB[0][c]])
                            yield T.defer("act", lambda e: e.activation(out=EG, in_=psum[:, ba, 0:128], func=AF.Exp), reads=[pB[ba]], writes=[FB[2][c]])
                            yield T.defer("act", lambda e: e.activation(out=DT, in_=DT, func=AF.Exp), reads=[FB[0][c]], writes=[FB[0][c]])
                            yield T.defer("pool", lambda e: e.tensor_tensor(out=decF, in0=DT, in1=ident_f[:, :], op=ALU.add), reads=[FB[0][c], cB], writes=[FB[1][c]])
                            yield T.defer("pe", lambda e: e.matmul(psum[:, ba, 0:256], lhsT=kn_[:, ck], rhs=kq_[:, :, ck], start=True, stop=True),
                                 reads=[kB_, qB], writes=[pB[ba]])
                            yield T.defer("dve", lambda e: e.scalar_tensor_tensor(out=QT[:, c, :], in0=psum[:, ba, 0:128], scalar=negb[:, t, h:h + 1], in1=DT,
                                                                         op0=ALU.mult, op1=ALU.mult), reads=[pB[ba], gB, FB[0][c]], writes=[QTB[c]])
                            yield T.defer("dve", lambda e: e.tensor_tensor(out=Aqk[:, c, :], in0=psum[:, ba, 128:256], in1=decF, op=ALU.mult),
                                 reads=[pB[ba], FB[1][c]], writes=[AqkB[c]])
                            yield T.defer("pe", lambda e: e.matmul(psum[:, bb, 128:256], lhsT=QT[:, c, :], rhs=ident_b[:, :], start=True, stop=True),
                                 reads=[QTB[c], cB], writes=[pB[bb]])
                            yield T.defer("dve", lambda e: e.tensor_tensor(out=PQ[0][:, 0, c, :], in0=psum[:, bb, 128:256], in1=ident_f[:, :], op=ALU.add),
                                 reads=[pB[bb], cB], writes=[PQB[0][c]])
                            yield T.defer("pool", lambda e: e.tensor_tensor(out=PQ[0][:, 1, c, :], in0=QT[:, c, :], in1=ident_b[:, :], op=ALU.add),
                                 reads=[QTB[c], cB], writes=[PQB[0][c]])
                            yield T.defer("pool", lambda e: e.tensor_tensor(out=ATt[:, c, :], in0=ident_b[:, :], in1=QT[:, c, :], op=ALU.subtract),
                                 reads=[QTB[c], cB], writes=[ATtB[c]])
                            yield T.defer("pe", lambda e: e.matmul(psum[:, ba, 0:128], lhsT=kn_[:, ck], rhs=ident_b[:, :], start=True, stop=True),
                                 reads=[kB_, cB], writes=[pB[ba]])
                            yield T.defer("dve", lambda e: e.tensor_scalar_mul(out=kdec[:, c, :], in0=psum[:, ba, 0:128], scalar1=kds[:, t, h:h + 1]),
                                 reads=[pB[ba], gB], writes=[kdecB[c]])
                            yield T.defer("dve", lambda e: e.tensor_scalar_mul(out=VK[:, c, 128:256], in0=psum[:, ba, 0:128], scalar1=egc[:, t, h:h + 1]),
                                 reads=[pB[ba], gB], writes=[VKB[c]])
                            yield T.defer("pe", lambda e: e.matmul(psum[:, bb, 128:256], lhsT=vn_[:, ck], rhs=ident_b[:, :], start=True, stop=True),
                                 reads=[vB_, cB], writes=[pB[bb]])
                            yield T.defer("act", lambda e: e.activation(out=VK[:, c, 0:128], in_=psum[:, bb, 128:256], func=AF.Copy), reads=[pB[bb]], writes=[VKB[c]])
                            cur = 0
                            for k in range(2, 8):
                                nxt = 1 - cur
                                last = (k == 7)
                                yield T.defer("pe", lambda e: e.matmul(psum[:, ba, 0:128], lhsT=ATt[:, c, :], rhs=PQ[cur][:, 0, c, :], start=True, stop=True),
                                     reads=[ATtB[c], PQB[cur][c]], writes=[pB[ba]])
                                yield T.defer("dve", lambda e: e.scalar_tensor_tensor(out=Wt[:, c, :], in0=psum[:, ba, 0:128], scalar=-1.0, in1=ident2_f[:, :],
                                                                             op0=ALU.mult, op1=ALU.add), reads=[pB[ba], cB], writes=[WtB[c]])

                                def mmn(e):
                                    ins = e.matmul(psum[:, bb, 128:256], lhsT=Wt[:, c, :], rhs=PQ[cur][:, 1, c, :], start=True, stop=True)
                                    if not last:
                                        ins = e.matmul(psum[:, ba, 0:128], lhsT=PQ[cur][:, 1, c, :], rhs=Wt[:, c, :], start=True, stop=True)
                                    return ins
                                yield T.defer("pe", mmn, reads=[WtB[c], PQB[cur][c]], writes=[pB[ba]])
                                if last:
                                    yield T.defer("act", lambda e: e.activation(out=QT[:, c, :], in_=psum[:, bb, 128:256], func=AF.Copy), reads=[pB[bb]], writes=[QTB[c]])
                                else:
                                    yield T.defer("act", lambda e: e.activation(out=PQ[nxt][:, :, c, :], in_=psum[:, ba, 0:256].rearrange("p (a i) -> p a i", a=2), func=AF.Copy),
                                         reads=[pB[ba]], writes=[PQB[nxt][c]])
                                cur = nxt
                            yield T.defer("pe", lambda e: e.matmul(psum[:, ba, 0:256], lhsT=QT[:, c, :], rhs=VK[:, c, :], start=True, stop=True),
                                 reads=[QTB[c], VKB[c]], writes=[pB[ba]])
                            yield T.defer("dve", lambda e: e.tensor_scalar_mul(out=Zt[:, c, :], in0=psum[:, ba, 0:256], scalar1=beta[:, t, h:h + 1]),
                                 reads=[pB[ba], gB], writes=[ZtB[c]])
                            yield T.defer("pe", lambda e: e.matmul(psum[:, bb, 128:256], lhsT=kdec[:, c, :], rhs=Zt[:, c, 0:128], start=True, stop=True),
                                 reads=[kdecB[c], ZtB[c]], writes=[pB[bb]])
                            yield T.defer("act", lambda e: e.activation(out=HsS[pg][:, c, :], in_=psum[:, bb, 128:256], func=AF.Copy), reads=[pB[bb]], writes=[HsSB[pg][c]])
                            yield T.defer("pe", lambda e: e.matmul(psum[:, ba, 0:128], lhsT=Zt[:, c, 128:256], rhs=kdec[:, c, :], start=True, stop=True),
                                 reads=[kdecB[c], ZtB[c]], writes=[pB[ba]])
                            yield T.defer("dve", lambda e: e.scalar_tensor_tensor(out=ATsS[pg][:, c, :], in0=ident_f[:, :], scalar=sdec[:, t, h:h + 1], in1=psum[:, ba, 0:128],
                                                                         op0=ALU.mult, op1=ALU.subtract), reads=[pB[ba], gB, cB], writes=[ATsSB[pg][c]])
                            yield T.defer("pe", lambda e: e.matmul(psum[:, bb, 128:256], lhsT=Zt[:, c, 0:128], rhs=Aqk[:, c, :], start=True, stop=True),
                                 reads=[AqkB[c], ZtB[c]], writes=[pB[bb]])
                            yield T.defer("act", lambda e: e.activation(out=O0sS[pg][:, c, :], in_=psum[:, bb, 128:256], func=AF.Copy), reads=[pB[bb]], writes=[O0sSB[pg][c]])
                            yield T.defer("pe", lambda e: e.matmul(psum[:, ba, 0:128], lhsT=Zt[:, c, 128:256], rhs=Aqk[:, c, :], start=True, stop=True),
                                 reads=[AqkB[c], ZtB[c]], writes=[pB[ba]])
                            yield T.defer("pool", lambda e: e.tensor_tensor(out=decF, in0=qn_[:, ck], in1=EG, op=ALU.mult), reads=[qB, FB[2][c], FB[1][c]], writes=[FB[1][c]])
                            yield T.defer("dve", lambda e: e.tensor_tensor(out=QeffS[pg][:, c, :], in0=decF, in1=psum[:, ba, 0:128], op=ALU.subtract),
                                 reads=[pB[ba], FB[1][c]], writes=[QeffSB[pg][c]])

                        gens += [prep(c) for c in range(4)]
                    if r + 1 < NBK:
                        hn, nn = blocks[r + 1]
                        gens.append(g1b(hn, nn, (r + 1) % 2, (r + 1) % 3))
                    if r >= 1:
                        hp, gp = blocks[r - 1]
                        gens.append(scanpost(hp, gp, (r - 1) % 2, (r - 1) % 3))
                    run_streams(gens)
                T.filler = None
                T.barrier()

        def merge(hT, hB, yaT, yaB, ybT, ybB):
            with contextlib.ExitStack() as gs:
                def gsb(name, shape, dt):
                    return gs.enter_context(nc.sbuf_tensor("x_" + name, list(shape), dt))
                mg = gsb("mg", [128, 8, S], BF16)
                mgB = [[Buf() for _ in range(NB)] for _ in range(8)]
                wga = [gsb("wga%d" % i, [128, 2, NCH, 128], BF16) for i in range(2)]
                wba = [gsb("wba%d" % i, [64, 4, 128], BF16) for i in range(2)]
                wbb = [gsb("wbb%d" % i, [128, 8, 128], BF16) for i in range(2)]
                wB = [Buf(), Buf()]
                wo = [gsb("wo%d" % i, [128, 8, 128], BF16) for i in range(2)]
                woB = [Buf(), Buf()]
                one_c = eps_t[:, 1:2]
                sa = gsb("sa", [128, 512], F32)
                sb_ = gsb("sb", [128, 512], F32)
                ta = gsb("ta", [128, 512], F32)
                tb = gsb("tb", [128, 512], F32)
                saB, sbB, taB, tbB = Buf(), Buf(), Buf(), Buf()

                def loadw(dt):
                    i = dt % 2
                    T.dma("pool", wga[i][:].rearrange("p w c f -> p (w c f)"), dt_in["wgab"][dt], writes=[wB[i]])
                    T.dma("pool", wba[i][:].rearrange("p s f -> p (s f)"), dt_in["wba"][dt], writes=[wB[i]])
                    T.dma("pool", wbb[i][:].rearrange("p s f -> p (s f)"), dt_in["wbb"][dt], writes=[wB[i]])
                loadw(0)
                loadw(1)
                for dt in range(8):
                    wi = dt % 2
                    for n in range(NB):
                        tk = slice(n * 512, (n + 1) * 512)
                        bga, bgb, ba, bb = bank(), bank(), bank(), bank()

                        def mmg(e, which, b):
                            ins = None
                            for c in range(NCH):
                                ins = e.matmul(psum[:, b, :], lhsT=wga[wi][:, which, c, :], rhs=hT[:, c, tk], start=(c == 0), stop=(c == NCH - 1))
                            return ins
                        T.op("pe", lambda e: mmg(e, 0, bga), reads=[wB[wi], hB[n]], writes=[pB[bga]])
                        T.op("pe", lambda e: mmg(e, 1, bgb), reads=[wB[wi], hB[n]], writes=[pB[bgb]])

                        def mma(e):
                            ins = None
                            for s4 in range(4):
                                ins = e.matmul(psum[:, ba, :], lhsT=wba[wi][:, s4, :], rhs=yaT[:, s4, tk], start=(s4 == 0), stop=(s4 == 3))
                            return ins
                        T.op("pe", mma, reads=[wB[wi]] + yaB, writes=[pB[ba]])

                        def mmb(e):
                            ins = None
                            for h in range(8):
                                ins = e.matmul(psum[:, bb, :], lhsT=wbb[wi][:, h, :], rhs=ybT[:, h, tk], start=(h == 0), stop=(h == 7))
                            return ins
                        T.op("pe", mmb, reads=[wB[wi]] + ybB, writes=[pB[bb]])
                        T.op("act", lambda e: e.activation(out=sa[:, :], in_=psum[:, bga, :], func=AF.Sigmoid), reads=[pB[bga]], writes=[saB])
                        T.op("act", lambda e: e.activation(out=sb_[:, :], in_=psum[:, bgb, :], func=AF.Sigmoid), reads=[pB[bgb]], writes=[sbB])
                        T.op("dve", lambda e: e.tensor_tensor(out=ta[:, :], in0=psum[:, ba, :], in1=sa[:, :], op=ALU.mult), reads=[pB[ba], saB], writes=[taB])
                        T.op("dve", lambda e: e.tensor_tensor(out=tb[:, :], in0=psum[:, bb, :], in1=sb_[:, :], op=ALU.mult), reads=[pB[bb], sbB], writes=[tbB])
                        T.op("pool", lambda e: e.tensor_tensor(out=mg[:, dt, tk], in0=ta[:, :], in1=tb[:, :], op=ALU.add), reads=[taB, tbB], writes=[mgB[dt][n]])
                    if dt + 2 < 8:
                        loadw(dt + 2)

                def loado(dt):
                    T.dma("pool", wo[dt % 2][:].rearrange("p s f -> p (s f)"), dt_in["wout"][dt], writes=[woB[dt % 2]])
                loado(0)
                loado(1)
                for dt in range(8):
                    wi = dt % 2
                    for n in range(NB):
                        tk = slice(n * 512, (n + 1) * 512)
                        b = bank()

                        def mmo(e):
                            ins = None
                            for k in range(8):
                                ins = e.matmul(psum[:, b, :], lhsT=wo[wi][:, k, :], rhs=mg[:, k, tk], start=(k == 0), stop=(k == 7))
                            return ins
                        T.op("pe", mmo, reads=[woB[wi]] + [mgB[k][n] for k in range(8)], writes=[pB[b]])
                        T.op("dve", lambda e: e.tensor_tensor(out=xT[:, dt, tk], in0=psum[:, b, :], in1=xT[:, dt, tk], op=ALU.add),
                             reads=[pB[b]], writes=[xB[n]])
                    if dt + 2 < 8:
                        loado(dt + 2)
                T.barrier()

        def dbg_dump(slabs):
            with contextlib.ExitStack() as ds:
                dtile = [ds.enter_context(nc.sbuf_tensor("dbg%d" % i, [128, S], F32)) for i in range(2)]
                dB = [Buf(), Buf()]
                for k, (ap, bufs, row0, P) in enumerate(slabs):
                    i = k % 2
                    T.op("act", lambda e: e.activation(out=dtile[i][0:P, :], in_=ap, func=AF.Copy), reads=bufs, writes=[dB[i]])
                    T.dma("sp", out_d[row0:row0 + P, :], dtile[i][0:P, :], reads=[dB[i]])
                T.barrier()

        ffn(0, 0)
        done = False
        if stop_after not in ("ffn1_raw", "ffn1"):
            r = mixer()
            if r is not None:
                done = True
        if stop_after in ("all", "x3_raw"):
            ffn(1, 2)
        if not done:
          with contextlib.ExitStack() as fs:
            yT = [fs.enter_context(nc.sbuf_tensor("yT%d" % i, [128, NCH, 512], F32)) for i in range(2)]
            yB = [Buf(), Buf()]
            sq_t = fs.enter_context(nc.sbuf_tensor("fsq", [128, NCH, 512], BF16))
            sq_b = [Buf() for _ in range(NCH)]
            rs_t = fs.enter_context(nc.sbuf_tensor("frs", [128, 512], F32))
            rs_b = Buf()
            out_dv = out_d.rearrange("(c p) t -> p c t", p=128)
            for n in range(NB):
                i = n % 2
                if stop_after.endswith("_raw"):
                    for c in range(NCH):
                        T.op("dve", lambda e: e.tensor_copy(out=yT[i][:, c, :], in_=xT[:, c, n * 512:(n + 1) * 512]),
                             reads=[xB[n]], writes=[yB[i]])
                else:
                    rmsnorm_block(n, 3, lambda c: yT[i][:, c, :], yB[i], sq_t, sq_b, rs_t, rs_b)
                T.dma("sp", out_dv[:, :, n * 512:(n + 1) * 512], yT[i][:], reads=[yB[i]])
            T.barrier()
    return nc, set(dt_in.keys())


def _consts():
    c = {}
    c["ident_f"] = np.eye(128, dtype=np.float32)
    c["ones_f"] = np.ones((128, 128), dtype=np.float32)
    half = 32
    inv_freq = 10000.0 ** (-np.arange(half, dtype=np.float64) / half)
    pos = np.arange(S, dtype=np.float64)
    ang = pos[None, :] * inv_freq[:, None]
    cosv, sinv = np.cos(ang).astype(np.float32), np.sin(ang).astype(np.float32)
    cos_t = np.zeros((128, S), np.float32)
    sin_t = np.zeros((128, S), np.float32)
    rm = np.zeros((128, 128), np.float32)
    for p in range(128):
        d = p % 64
        cos_t[p] = cosv[d % 32]
        sin_t[p] = -sinv[d % 32] if d < 32 else sinv[d % 32]
        partner = p + 32 if d < 32 else p - 32
        rm[partner, p] = 1.0
    c["rope_cos"], c["rope_sin"], c["rm"] = cos_t, sin_t, rm
    j = np.arange(128)[:, None]
    i = np.arange(128)[None, :]
    mprev = (j >= i).astype(np.float32)
    mcur = (j <= i).astype(np.float32)
    c["amask"] = np.concatenate([mprev, mcur, mprev, mcur], axis=1)
    em = np.zeros((128, 64), np.float32)
    em[64, :] = 1.0
    c["em"] = em
    c["triu"] = (j <= i).astype(np.float32)
    c["smask"] = np.where(i > j, 0.0, -30000.0).astype(np.float32)
    c["odiv"] = np.full((128, 128), 1.0 / 128.0, np.float32)
    return c


def _prep_inputs(inp):
    f = lambda a: np.ascontiguousarray(a, dtype=np.float32)
    shared = {}

    def ncol(v):
        return f(np.asarray(v).reshape(NCH, 128).T)
    shared["ffn1_norm"] = ncol(inp["ffn1_norm"][0])
    shared["ffn2_norm"] = ncol(inp["ffn2_norm"][0])
    shared["mix_norm"] = ncol(inp["mix_norm"][0])
    shared["final_norm"] = ncol(inp["final_norm"])
    ffn_in = {1: (inp["ffn1_w_gate"], inp["ffn1_w_up"], inp["ffn1_w_down"]),
              2: (inp["ffn2_w_gate"], inp["ffn2_w_up"], inp["ffn2_w_down"])}
    for i in (1, 2):
        wg = np.asarray(ffn_in[i][0][0])
        wu = np.asarray(ffn_in[i][1][0])
        wd = np.asarray(ffn_in[i][2][0])
        shared["wg%d" % i] = f(wg.reshape(NCH, 128, 11, 256).transpose(2, 1, 0, 3).reshape(11, 128, NCH * 256))
        shared["wu%d" % i] = f(wu.reshape(NCH, 128, 11, 256).transpose(2, 1, 0, 3).reshape(11, 128, NCH * 256))
        shared["wd%d" % i] = f(wd.reshape(NF, 128, 8, 128).transpose(2, 1, 0, 3).reshape(8, 128, NF * 128))
    w_in = np.asarray(inp["w_in"][0])

    def cols(c0, n):
        return w_in[:, c0:c0 + n].reshape(NCH, 128, n).transpose(1, 0, 2).reshape(128, NCH * n)
    wa = np.zeros((6, 3, 128, NCH * 128), np.float32)
    for g in range(3):
        for hp in range(2):
            for i in range(3):
                wa[g * 2 + hp, i] = cols(i * 768 + (4 * g + 2 * hp) * 64, 128)
    shared["wqkv_a"] = wa
    wg_ = np.zeros((8, 3, 128, NCH * 128), np.float32)
    wgg = np.zeros((8, 128, NCH * 128), np.float32)
    for h in range(8):
        for i in range(3):
            wg_[h, i] = cols(2304 + i * 1024 + h * 128, 128)
        wgg[h] = cols(5392 + h * 128, 128)
    shared["wgdn"] = wg_
    shared["wgg"] = wgg
    shared["wbd"] = f(cols(5376, 16))
    wgab = np.zeros((8, 128, 2 * NCH * 128), np.float32)
    for dt in range(8):
        a = cols(6416 + dt * 128, 128).reshape(128, 1, NCH * 128)
        b = cols(7440 + dt * 128, 128).reshape(128, 1, NCH * 128)
        wgab[dt] = np.concatenate([a, b], axis=1).reshape(128, 2 * NCH * 128)
    shared["wgab"] = wgab
    wba = np.asarray(inp["w_branch_a"][0])
    shared["wba"] = f(wba.reshape(4, 64, 8, 128).transpose(2, 1, 0, 3).reshape(8, 64, 4 * 128))
    wbb = np.asarray(inp["w_branch_b"][0])
    shared["wbb"] = f(wbb.reshape(8, 128, 8, 128).transpose(2, 1, 0, 3).reshape(8, 128, 8 * 128))
    wo = np.asarray(inp["w_out"][0])
    shared["wout"] = f(wo.reshape(8, 128, 8, 128).transpose(2, 1, 0, 3).reshape(8, 128, 8 * 128))
    cw = np.asarray(inp["gdn_conv_w"][0])
    shared["convw"] = f(cw.reshape(4, 3, 8, 128).transpose(3, 2, 1, 0).reshape(128, 96))
    shared["alog"] = f(np.broadcast_to(np.asarray(inp["gdn_a_log"][0])[None, :], (128, 8)))
    shared["dtb"] = f(np.broadcast_to(np.asarray(inp["gdn_dt_bias"][0])[None, :], (128, 8)))
    shared["onorm"] = f(np.asarray(inp["gdn_out_norm"][0]).reshape(128, 1))
    shared.update(_consts())
    x = np.asarray(inp["x"])
    maps = []
    for b in range(8):
        m = dict(shared)
        m["xT"] = f(x[b].T)
        maps.append(m)
    return maps


_NC_CACHE = {}


def kernel(**inputs):
    stop_after = inputs.pop("_stop_after", "all")
    if stop_after not in _NC_CACHE:
        _NC_CACHE[stop_after] = build_program(stop_after)
    nc, names = _NC_CACHE[stop_after]
    maps = _prep_inputs(inputs)
    maps = [{k: v for k, v in m.items() if k in names} for m in maps]
    res = run_bass_kernel_spmd(nc, maps, core_ids=list(range(8)))
    out = np.stack([np.ascontiguousarray(r["outT"].T) for r in res.results], axis=0)
    return out.astype(np.float32)
```
